# Optimizing a Trainium2 kernel written in Bass

```python
import jax, jax.numpy as jnp
from jax import lax
import numpy as np

D_MODEL = 1024
BATCH = 16
SEQ = 2048
DEPTH = 1

REC_HEADS = D_MODEL // 256
REC_DK = 128
REC_DV = 128
REC_WIDTH = REC_HEADS * REC_DK
REC_VWIDTH = REC_HEADS * REC_DV
REC_CHUNK = 64
ATT_Q_HEADS = D_MODEL // 128
ATT_KV_HEADS = 2
ATT_HEAD_DIM = 64
ATT_QWIDTH = ATT_Q_HEADS * ATT_HEAD_DIM
ATT_KVWIDTH = ATT_KV_HEADS * ATT_HEAD_DIM
ATT_WINDOW = 128
ATT_BLOCK = 128
ROPE_THETA = 10000.0
MAX_POS_OFFSET = 512
D_FF = ((8 * D_MODEL // 3 + 127) // 128) * 128
CONV_K = 3
LN_EPS = 1e-5
RMS_EPS = 1e-6
DEEPNORM_ALPHA = (2 * DEPTH) ** 0.25
DEEPNORM_BETA = (8 * DEPTH) ** -0.25
SPLITS = [REC_WIDTH, REC_WIDTH, REC_WIDTH, REC_VWIDTH, REC_VWIDTH,
          ATT_QWIDTH, ATT_KVWIDTH, ATT_KVWIDTH, D_MODEL, D_MODEL]
D_IN = sum(SPLITS)

kernel_name = "hybrid_hgrn2_swa_sink_convffn_deepnorm_adaln"


def layer_norm(x, g, b):
    xf = x.astype(jnp.float32)
    mu = jnp.mean(xf, axis=-1, keepdims=True)
    xc = xf - mu
    var = jnp.mean(xc * xc, axis=-1, keepdims=True)
    return (xc * lax.rsqrt(var + LN_EPS) * g + b).astype(x.dtype)


def chunked_gated_scan(q, k, v, log_f):
    B, H, S, dk = q.shape
    dv = v.shape[-1]
    n = S // REC_CHUNK

    def to_chunks(t):
        return t.reshape(B, H, n, REC_CHUNK, t.shape[-1]).transpose(2, 0, 1, 3, 4)

    qc, kc, vc, gc = to_chunks(q), to_chunks(k), to_chunks(v), to_chunks(log_f)
    lower = jnp.tril(jnp.ones((REC_CHUNK, REC_CHUNK), dtype=bool))[:, :, None]

    def step(state, inp):
        qi, ki, vi, gi = inp
        b = jnp.cumsum(gi, axis=2)
        diff = b[:, :, :, None, :] - b[:, :, None, :, :]
        decay = jnp.exp(jnp.where(lower, diff, -jnp.inf))
        scores = jnp.einsum('bhtk,bhsk,bhtsk->bhts', qi, ki, decay)
        o = (jnp.einsum('bhts,bhsv->bhtv', scores, vi)
             + jnp.einsum('bhtk,bhkv->bhtv', qi * jnp.exp(b), state))
        b_last = b[:, :, -1:, :]
        state = (jnp.exp(b_last[:, :, 0, :])[..., None] * state
                 + jnp.einsum('bhsk,bhsv->bhkv', ki * jnp.exp(b_last - b), vi))
        return state, o

    state0 = jnp.zeros((B, H, dk, dv), jnp.float32)
    _, o = lax.scan(step, state0, (qc, kc, vc, gc))
    return o.transpose(1, 2, 0, 3, 4).reshape(B, H, S, dv)


def hgrn2_bidirectional(q_raw, f_fwd_raw, f_bwd_raw, i_raw, g_raw, lb_fwd, lb_bwd, norm_g):
    B, S, _ = q_raw.shape

    def heads(t, d):
        return t.reshape(B, S, REC_HEADS, d).transpose(0, 2, 1, 3).astype(jnp.float32)

    q = heads(jax.nn.silu(q_raw), REC_DK) * (REC_DK ** -0.5)
    v = heads(i_raw, REC_DV)

    def gates(f_raw, lb):
        lb = lb.astype(jnp.float32).reshape(REC_HEADS, 1, REC_DK)
        f = lb + (1.0 - lb) * jax.nn.sigmoid(heads(f_raw, REC_DK))
        return 1.0 - f, jnp.log(f)

    k_f, logf_f = gates(f_fwd_raw, lb_fwd)
    k_b, logf_b = gates(f_bwd_raw, lb_bwd)
    o_fwd = chunked_gated_scan(q, k_f, v, logf_f)
    flip = lambda t: jnp.flip(t, axis=2)
    o_bwd = flip(chunked_gated_scan(flip(q), flip(k_b), flip(v), flip(logf_b)))
    o = o_fwd + o_bwd
    o = o * lax.rsqrt(jnp.mean(o * o, axis=-1, keepdims=True) + RMS_EPS) * norm_g.astype(jnp.float32)
    o = o.transpose(0, 2, 1, 3).reshape(B, S, REC_VWIDTH).astype(g_raw.dtype)
    return o * jax.nn.silu(g_raw)


def rotary(t, positions):
    half = t.shape[-1] // 2
    inv_freq = ROPE_THETA ** (-jnp.arange(half, dtype=jnp.float32) / half)
    ang = positions.astype(jnp.float32)[..., None] * inv_freq
    cos = jnp.cos(ang)[:, :, None, :]
    sin = jnp.sin(ang)[:, :, None, :]
    t1 = t[..., :half].astype(jnp.float32)
    t2 = t[..., half:].astype(jnp.float32)
    return jnp.concatenate([t1 * cos - t2 * sin, t2 * cos + t1 * sin], axis=-1).astype(t.dtype)


def windowed_gqa_with_sink(q, k, v, sink):
    B, S, Hq, hd = q.shape
    Hkv = k.shape[2]
    G = Hq // Hkv
    nb = S // ATT_BLOCK
    qb = q.reshape(B, nb, ATT_BLOCK, Hkv, G, hd)

    def band(t):
        tp = jnp.pad(t, ((0, 0), (ATT_BLOCK, ATT_BLOCK), (0, 0), (0, 0)))
        tp = tp.reshape(B, nb + 2, ATT_BLOCK, Hkv, hd)
        return jnp.concatenate([tp[:, :-2], tp[:, 1:-1], tp[:, 2:]], axis=2)

    kb, vb = band(k), band(v)
    scores = jnp.einsum('bnqhgd,bnkhd->bnhgqk', qb, kb).astype(jnp.float32) * (hd ** -0.5)
    q_idx = jnp.arange(ATT_BLOCK)[:, None]
    k_idx = jnp.arange(3 * ATT_BLOCK)[None, :] - ATT_BLOCK
    k_abs = jnp.arange(nb)[:, None, None] * ATT_BLOCK + k_idx[None]
    valid = (jnp.abs(k_idx - q_idx)[None] <= ATT_WINDOW) & (k_abs >= 0) & (k_abs < S)
    scores = jnp.where(valid[None, :, None, None], scores, -jnp.inf)
    sink_logit = jnp.broadcast_to(sink.astype(jnp.float32).reshape(1, 1, Hkv, G, 1, 1),
                                  scores.shape[:-1] + (1,))
    probs = jax.nn.softmax(jnp.concatenate([scores, sink_logit], axis=-1), axis=-1)[..., :-1]
    out = jnp.einsum('bnhgqk,bnkhd->bnqhgd', probs.astype(v.dtype), vb)
    return out.reshape(B, S, Hq * hd)


def depthwise_conv_centred(h, w, b):
    pad = CONV_K // 2
    hp = jnp.pad(h, ((0, 0), (pad, pad), (0, 0)))
    S = h.shape[1]
    out = b
    for tap in range(CONV_K):
        out = out + hp[:, tap:tap + S] * w[tap]
    return out


def setup_inputs(seed: int = 0) -> dict:
    key = jax.random.key(seed)
    ks = jax.random.split(key, 21)
    nrm = lambda k, shape, scale: jax.random.normal(k, shape, jnp.float32) * scale
    x = nrm(ks[0], (BATCH, SEQ, D_MODEL), 1.0)
    c = nrm(ks[1], (BATCH, D_MODEL), 1.0)
    positions = (jnp.arange(SEQ, dtype=jnp.int32)[None, :]
                 + jax.random.randint(ks[2], (BATCH, 1), 0, MAX_POS_OFFSET, dtype=jnp.int32))
    w_ada = nrm(ks[3], (DEPTH, D_MODEL, 6 * D_MODEL), 0.5 * D_MODEL ** -0.5)
    b_ada = nrm(ks[4], (DEPTH, 6 * D_MODEL), 0.02)
    w_in = nrm(ks[5], (DEPTH, D_MODEL, D_IN), D_MODEL ** -0.5)
    rec_lower_bound = nrm(ks[6], (2, DEPTH + 1, REC_WIDTH), 1.0)
    rec_norm_g = 1.0 + nrm(ks[7], (DEPTH, REC_DV), 0.02)
    w_rec_branch = nrm(ks[8], (DEPTH, REC_VWIDTH, D_MODEL), DEEPNORM_BETA * REC_VWIDTH ** -0.5)
    attn_sink = nrm(ks[9], (DEPTH, ATT_Q_HEADS), 0.5)
    w_attn_branch = nrm(ks[10], (DEPTH, ATT_QWIDTH, D_MODEL), DEEPNORM_BETA * ATT_QWIDTH ** -0.5)
    w_out = nrm(ks[11], (DEPTH, D_MODEL, D_MODEL), DEEPNORM_BETA * D_MODEL ** -0.5)
    ln1_g = 1.0 + nrm(ks[12], (DEPTH, D_MODEL), 0.02)
    ln1_b = nrm(ks[13], (DEPTH, D_MODEL), 0.02)
    w_up = nrm(ks[14], (DEPTH, D_MODEL, 2 * D_FF), D_MODEL ** -0.5)
    conv_w = nrm(ks[15], (DEPTH, CONV_K, 2 * D_FF), CONV_K ** -0.5)
    conv_b = nrm(ks[16], (DEPTH, 2 * D_FF), 0.02)
    w_down = nrm(ks[17], (DEPTH, D_FF, D_MODEL), DEEPNORM_BETA * D_FF ** -0.5)
    ln2_g = 1.0 + nrm(ks[18], (DEPTH, D_MODEL), 0.02)
    ln2_b = nrm(ks[19], (DEPTH, D_MODEL), 0.02)
    return {"x": x, "c": c, "positions": positions, "w_ada": w_ada, "b_ada": b_ada,
            "w_in": w_in, "rec_lower_bound": rec_lower_bound, "rec_norm_g": rec_norm_g,
            "w_rec_branch": w_rec_branch, "attn_sink": attn_sink, "w_attn_branch": w_attn_branch,
            "w_out": w_out, "ln1_g": ln1_g, "ln1_b": ln1_b, "w_up": w_up, "conv_w": conv_w,
            "conv_b": conv_b, "w_down": w_down, "ln2_g": ln2_g, "ln2_b": ln2_b}


def reference(x, c, positions, w_ada, b_ada, w_in, rec_lower_bound, rec_norm_g, w_rec_branch,
              attn_sink, w_attn_branch, w_out, ln1_g, ln1_b, w_up, conv_w, conv_b, w_down,
              ln2_g, ln2_b):
    B, S, D = x.shape
    lb_all = jnp.cumsum(jax.nn.softmax(rec_lower_bound.astype(jnp.float32), axis=1), axis=1)
    split_points = [int(p) for p in np.cumsum(SPLITS)[:-1]]
    for l in range(DEPTH):
        mods = jax.nn.silu(c) @ w_ada[l] + b_ada[l]
        sh1, sc1, ga1, sh2, sc2, ga2 = [m[:, None, :] for m in jnp.split(mods, 6, axis=-1)]

        u = x * (1.0 + sc1) + sh1
        proj = u @ w_in[l]
        (rq, rff, rfb, ri, rg, aq, ak, av, gate_rec, gate_att) = jnp.split(proj, split_points, axis=-1)
        y_rec = hgrn2_bidirectional(rq, rff, rfb, ri, rg, lb_all[0, l], lb_all[1, l],
                                    rec_norm_g[l]) @ w_rec_branch[l]
        q = rotary(aq.reshape(B, S, ATT_Q_HEADS, ATT_HEAD_DIM), positions)
        k = rotary(ak.reshape(B, S, ATT_KV_HEADS, ATT_HEAD_DIM), positions)
        v = av.reshape(B, S, ATT_KV_HEADS, ATT_HEAD_DIM)
        y_att = windowed_gqa_with_sink(q, k, v, attn_sink[l]) @ w_attn_branch[l]
        merged = jax.nn.sigmoid(gate_rec) * y_rec + jax.nn.sigmoid(gate_att) * y_att
        x = layer_norm(DEEPNORM_ALPHA * x + (1.0 + ga1) * (merged @ w_out[l]), ln1_g[l], ln1_b[l])

        u = x * (1.0 + sc2) + sh2
        h = depthwise_conv_centred(u @ w_up[l], conv_w[l], conv_b[l])
        h_val, h_gate = jnp.split(h, 2, axis=-1)
        ffn = (jax.nn.gelu(h_gate) * h_val) @ w_down[l]
        x = layer_norm(DEEPNORM_ALPHA * x + (1.0 + ga2) * ffn, ln2_g[l], ln2_b[l])
    return x
```

```python
import numpy as np
from contextlib import ExitStack
import concourse.bass as bass
import concourse.mybir as mybir
from concourse.bass_utils import run_bass_kernel_spmd

F32 = mybir.dt.float32
BF16 = mybir.dt.bfloat16
I32 = mybir.dt.int32
U8 = mybir.dt.uint8
AF = mybir.ActivationFunctionType
ALU = mybir.AluOpType
AX = mybir.AxisListType

D = 1024
NK = 8
DIN = 5376
DFF = 2816
NFC = DFF // 128
ALPHA = 2.0 ** 0.25
LN_EPS = 1e-5
RMS_EPS = 1e-6
N_CORES = 8
O_RQ, O_RFF, O_RFB, O_RI, O_RG, O_AQ, O_AK, O_AV, O_GR, O_GA = 0, 512, 1024, 1536, 2048, 2560, 3072, 3200, 3328, 4352


class Res:
    __slots__ = ("w", "rd")

    def __init__(self):
        self.w = None
        self.rd = []


class Tracker:
    def __init__(self, nc, stack):
        self.nc = nc
        self.stack = stack
        self.eng = {}
        self.sems = {}
        for name, h in (("pe", nc.tensor), ("act", nc.scalar), ("dve", nc.vector),
                        ("pool", nc.gpsimd), ("sp", nc.sync)):
            sem = stack.enter_context(nc.semaphore("sem_" + name))
            self.eng[name] = {"h": h, "sem": sem, "cnt": 0, "seen": {}, "name": name}
            self.sems[id(sem)] = sem
        self.dma_sems = []
        self.n_wait = 0
        self.n_ins = 0

    def new_dma_sem(self, name):
        sem = self.stack.enter_context(self.nc.semaphore(name))
        d = {"sem": sem, "cnt": 0}
        self.sems[id(sem)] = sem
        self.dma_sems.append(d)
        return d

    def _wait_deps(self, E, reads, writes):
        deps = {}

        def add(tok):
            if tok is not None and deps.get(tok[0], 0) < tok[1]:
                deps[tok[0]] = tok[1]

        for r in reads:
            add(r.w)
        for w in writes:
            add(w.w)
            for t in w.rd:
                add(t)
        own = id(E["sem"])
        for k, v in deps.items():
            if E["name"] == "pe" and k == own:
                continue
            if E["seen"].get(k, 0) >= v:
                continue
            E["h"].wait_ge(self.sems[k], v)
            E["seen"][k] = v
            self.n_wait += 1

    def _commit(self, tok, reads, writes):
        for r in reads:
            r.rd.append(tok)
            if len(r.rd) > 48:
                best = {}
                for k, v in r.rd:
                    if best.get(k, 0) < v:
                        best[k] = v
                r.rd = list(best.items())
        for w in writes:
            w.w = tok
            w.rd = []
        self.n_ins += 1

    def op(self, eng, fn, reads=(), writes=()):
        E = self.eng[eng]
        self._wait_deps(E, reads, writes)
        ins = fn(E["h"])
        E["cnt"] += 1
        ins.then_inc(E["sem"], 1)
        self._commit((id(E["sem"]), E["cnt"]), reads, writes)
        return ins

    def dma(self, q, dsem, out, in_, reads=(), writes=(), **kw):
        E = self.eng[q]
        self._wait_deps(E, reads, writes)
        ins = E["h"].dma_start(out=out, in_=in_, **kw)
        dsem["cnt"] += 16
        ins.then_inc(dsem["sem"], 16)
        self._commit((id(dsem["sem"]), dsem["cnt"]), reads, writes)
        return ins

    def barrier(self):
        targets = [(id(e["sem"]), e["cnt"]) for e in self.eng.values() if e["cnt"] > 0]
        targets += [(id(d["sem"]), d["cnt"]) for d in self.dma_sems if d["cnt"] > 0]
        for E in self.eng.values():
            for k, v in targets:
                if E["seen"].get(k, 0) >= v:
                    continue
                E["h"].wait_ge(self.sems[k], v)
                E["seen"][k] = v
                self.n_wait += 1


class Rot:
    def __init__(self, items):
        self.items = items
        self.i = 0

    def next(self):
        it = self.items[self.i % len(self.items)]
        self.i += 1
        return it


def build_nc(S=2048, NSEQ=2, dbg=False):
    NT, NG, NCH = S // 128, S // 512, S // 64
    nc = bass.Bass("TRN2", target_bir_lowering=False)

    def din(name, shape, dt=F32):
        return nc.dram_tensor(name, list(shape), dt, kind="ExternalInput").ap()

    x_d = din("x", [NSEQ, S, D])
    c_d = din("c", [NSEQ, D])
    pos_d = din("positions", [NSEQ, S], I32)
    w_ada_d = din("w_ada", [D, 6 * D])
    b_ada_d = din("b_ada", [1, 6 * D])
    w_in_d = din("w_in", [D, DIN])
    lbr_d = din("rec_lower_bound", [2, 2, 512])
    ng_d = din("rec_norm_g", [1, 128])
    w_rec_d = din("w_rec_branch", [512, D])
    sink_d = din("attn_sink", [1, 8])
    w_att_d = din("w_attn_branch", [512, D])
    w_out_d = din("w_out", [D, D])
    ln1g_d = din("ln1_g", [1, D]); ln1b_d = din("ln1_b", [1, D])
    w_up_d = din("w_up", [D, 2 * DFF])
    cw_d = din("conv_w", [3, 2 * DFF])
    cb_d = din("conv_b", [1, 2 * DFF])
    w_dn_d = din("w_down", [DFF, D])
    ln2g_d = din("ln2_g", [1, D]); ln2b_d = din("ln2_b", [1, D])
    identf_d = din("k_ident", [128, 128])
    mskf_d = din("k_maskf", [64, 64], U8)
    mskb_d = din("k_maskb", [64, 64], U8)
    amask_d = din("k_amask", [128, 384])
    rope_d = din("k_rope", [128, 2])
    out_d = nc.dram_tensor("out", [NSEQ, S, D], F32, kind="ExternalOutput").ap()
    x1_d = nc.dram_tensor("x1_scr", [NSEQ, S, D], F32).ap()
    u2T_d = nc.dram_tensor("u2T_scr", [NSEQ, D, S + 2], BF16).ap()
    dbg_d = {}
    if dbg:
        dbg_d["mods"] = nc.dram_tensor("dbg_mods", [128, 4, 8], F32, kind="ExternalOutput").ap()
        dbg_d["QT"] = nc.dram_tensor("dbg_QT", [2, 128, 2, S], BF16, kind="ExternalOutput").ap()
        dbg_d["KT"] = nc.dram_tensor("dbg_KT", [2, 128, 2, S], BF16, kind="ExternalOutput").ap()
        dbg_d["V"] = nc.dram_tensor("dbg_V", [64, NCH, 256], BF16, kind="ExternalOutput").ap()
        dbg_d["oT"] = nc.dram_tensor("dbg_oT", [128, 4, S], F32, kind="ExternalOutput").ap()
        dbg_d["kT"] = nc.dram_tensor("dbg_kT", [128, S], BF16, kind="ExternalOutput").ap()
        dbg_d["x1"] = x1_d

    w_in_k = w_in_d.rearrange("(j p) n -> p j n", p=128)
    w_ada_k = w_ada_d.rearrange("(j p) n -> p j n", p=128)
    w_up_k = w_up_d.rearrange("(j p) n -> p j n", p=128)
    w_out_k = w_out_d.rearrange("(j p) n -> p j n", p=128)
    w_rec_k = w_rec_d.rearrange("(j p) n -> p j n", p=128)
    w_att_k = w_att_d.rearrange("(j p) n -> p j n", p=128)
    w_dn_k = w_dn_d.rearrange("(j p) n -> p j n", p=128)

    with ExitStack() as top:
        T = Tracker(nc, top)
        op, dma = T.op, T.dma

        uniq = [0]

        def SB(st, name, shape, dt):
            uniq[0] += 1
            return st.enter_context(nc.sbuf_tensor("%s_%d" % (name, uniq[0]), list(shape), dt))

        PB = [top.enter_context(nc.psum_tensor("pb%d" % i, [128, 512], F32)) for i in range(8)]
        rPB = [Res() for _ in range(8)]
        ds_const = T.new_dma_sem("ds_const")
        ds_w = T.new_dma_sem("ds_w")
        ds_x = [T.new_dma_sem("ds_x%d" % i) for i in range(4)]
        ds_o = [T.new_dma_sem("ds_o%d" % i) for i in range(4)]
        ds_misc = [T.new_dma_sem("ds_m%d" % i) for i in range(4)]
        ds_dbg = T.new_dma_sem("ds_dbg")

        identf = SB(top, "identf", [128, 128], F32); r_identf = Res()
        identb = SB(top, "identb", [128, 128], BF16); r_identb = Res()
        onesf = SB(top, "onesf", [128, 128], F32); r_onesf = Res()
        maskf = SB(top, "maskf", [64, 64], U8); maskb = SB(top, "maskb", [64, 64], U8); r_mask = Res()
        zer = SB(top, "zer", [128, 128], F32); r_zer = Res()
        rmask = SB(top, "rmask", [128, 512], F32); r_rmask = Res()
        lbt = SB(top, "lbt", [128, 3, 2, 4], F32); r_lbt = Res()
        lraw = SB(top, "lraw", [128, 2, 2, 4], F32); r_lraw = Res()
        ngc = SB(top, "ngc", [128, 1], F32); r_ngc = Res()
        sinkb = SB(top, "sinkb", [128, 2, 8], F32); r_sink = Res()
        rope = SB(top, "rope", [128, 2], F32); r_rope = Res()
        epsc = SB(top, "epsc", [128, 2], F32); r_eps = Res()
        modsT = SB(top, "modsT", [128, 4, 8], F32); r_modsT = Res()
        gaB = SB(top, "gaB", [128, 2, D], F32); r_gaB = [Res(), Res()]
        amask = SB(top, "amask", [128, 384], F32); r_amask = Res()

        dma("sp", ds_const, identf[:], identf_d, writes=[r_identf])
        dma("sp", ds_const, maskf[:], mskf_d, writes=[r_mask])
        dma("sp", ds_const, maskb[:], mskb_d, writes=[r_mask])
        dma("sp", ds_const, rope[:], rope_d, writes=[r_rope])
        dma("sp", ds_const, amask[:], amask_d, writes=[r_amask])
        dma("sp", ds_const, ngc[:], ng_d.rearrange("o p -> p o"), writes=[r_ngc], allow_slow_non_contiguous=True)
        dma("sp", ds_const, sinkb[:, 0, :], sink_d[0:1, :].broadcast_to([128, 8]), writes=[r_sink])
        dma("sp", ds_const, lraw[:].rearrange("p a b h -> p (a b) h"),
            lbr_d.rearrange("a b (h p) -> p (a b) h", p=128), writes=[r_lraw], allow_slow_non_contiguous=True)
        T.barrier()
        op("dve", lambda e: e.tensor_copy(out=identb[:], in_=identf[:]), reads=[r_identf], writes=[r_identb])
        op("pool", lambda e: e.memset(onesf[:], 1.0), writes=[r_onesf])
        op("pool", lambda e: e.memset(zer[:], 0.0), writes=[r_zer])
        op("pool", lambda e: e.memset(rmask[:], 1.0), writes=[r_rmask])
        op("pool", lambda e: e.memset(rmask[:].rearrange("p (c t) -> p c t", t=64)[:, :, 0:1], 0.0), writes=[r_rmask])
        op("pool", lambda e: e.memset(epsc[:, 0:1], LN_EPS), writes=[r_eps])
        op("pool", lambda e: e.memset(epsc[:, 1:2], RMS_EPS), writes=[r_eps])
        op("dve", lambda e: e.tensor_scalar(out=sinkb[:, 1, :], in0=sinkb[:, 0, :], scalar1=-1.0, scalar2=None, op0=ALU.mult),
           reads=[r_sink], writes=[r_sink])
        op("dve", lambda e: e.tensor_tensor(out=lbt[:, 2, :, :], in0=lraw[:, :, 0, :], in1=lraw[:, :, 1, :], op=ALU.subtract),
           reads=[r_lraw], writes=[r_lbt])
        op("act", lambda e: e.activation(out=lbt[:, 0, :, :], in_=lbt[:, 2, :, :], func=AF.Sigmoid), reads=[r_lbt], writes=[r_lbt])
        op("act", lambda e: e.activation(out=lbt[:, 1, :, :], in_=lbt[:, 2, :, :], func=AF.Sigmoid, scale=-1.0), reads=[r_lbt], writes=[r_lbt])
        op("dve", lambda e: e.tensor_scalar(out=lbt[:, 2, :, :], in0=lbt[:, 1, :, :], scalar1=-1.0, scalar2=None, op0=ALU.mult),
           reads=[r_lbt], writes=[r_lbt])

        def make_uT(xt, r_xt, dst, r_dst, scol, bcol, banks=(0, 1)):
            for hb in range(2):
                b = banks[hb]
                for i in range(4):
                    k = hb * 4 + i
                    op("pe", lambda e, k=k, i=i, b=b: e.transpose(out=PB[b][:, i * 128:(i + 1) * 128], in_=xt[:, k * 128:(k + 1) * 128],
                                                                   identity=identf[:]),
                       reads=[r_xt, r_identf], writes=[rPB[b]])
                for i in range(4):
                    k = hb * 4 + i
                    if i % 2 == 0:
                        op("act", lambda e, k=k, i=i, b=b: e.activation(out=dst[:, k, :], in_=PB[b][:, i * 128:(i + 1) * 128], func=AF.Identity,
                                                                         scale=modsT[:, scol, k:k + 1], bias=modsT[:, bcol, k:k + 1]),
                           reads=[rPB[b], r_modsT], writes=[r_dst])
                    else:
                        op("dve", lambda e, k=k, i=i, b=b: e.tensor_scalar(out=dst[:, k, :], in0=PB[b][:, i * 128:(i + 1) * 128],
                                                                            scalar1=modsT[:, scol, k:k + 1], scalar2=modsT[:, bcol, k:k + 1],
                                                                            op0=ALU.mult, op1=ALU.add),
                           reads=[rPB[b], r_modsT], writes=[r_dst])

        def ln_epilogue(pz, r_pz, xres, r_xres, which, gB, bB, r_gb, y, r_y, st6, r_st):
            for h in range(2):
                op("dve", lambda e, h=h: e.tensor_tensor(out=y[:, h * 512:(h + 1) * 512], in0=PB[pz[h]][:], in1=gaB[:, which, h * 512:(h + 1) * 512], op=ALU.mult),
                   reads=[r_pz[h], r_gaB[which]], writes=[r_y])
            op("dve", lambda e: e.scalar_tensor_tensor(out=y[:], in0=xres, scalar=ALPHA, in1=y[:], op0=ALU.mult, op1=ALU.add),
               reads=[r_xres, r_y], writes=[r_y])
            for h in range(2):
                op("dve", lambda e, h=h: e.bn_stats(out=st6[:, h * 6:(h + 1) * 6], in_=y[:, h * 512:(h + 1) * 512]), reads=[r_y], writes=[r_st])
            op("dve", lambda e: e.bn_aggr(out=st6[:, 12:14], in_=st6[:, 0:12]), reads=[r_st], writes=[r_st])
            op("act", lambda e: e.activation(out=st6[:, 14:15], in_=st6[:, 13:14], func=AF.Ln, bias=epsc[:, 0:1], scale=1.0), reads=[r_st, r_eps], writes=[r_st])
            op("act", lambda e: e.activation(out=st6[:, 15:16], in_=st6[:, 14:15], func=AF.Exp, scale=-0.5), reads=[r_st], writes=[r_st])
            op("dve", lambda e: e.tensor_scalar(out=st6[:, 16:17], in0=st6[:, 12:13], scalar1=st6[:, 15:16], scalar2=-1.0, op0=ALU.mult, op1=ALU.mult),
               reads=[r_st], writes=[r_st])
            op("act", lambda e: e.activation(out=y[:], in_=y[:], func=AF.Identity, scale=st6[:, 15:16], bias=st6[:, 16:17]),
               reads=[r_y, r_st], writes=[r_y])
            op("pool", lambda e: e.tensor_tensor(out=y[:], in0=y[:], in1=gB, op=ALU.mult), reads=[r_y, r_gb], writes=[r_y])
            op("pool", lambda e: e.tensor_tensor(out=y[:], in0=y[:], in1=bB, op=ALU.add), reads=[r_y, r_gb], writes=[r_y])

        for s in range(NSEQ):
            with ExitStack() as ph:
                cT = SB(ph, "cT", [128, 8], F32); r_cT = Res()
                cbt = SB(ph, "cbt", [128, 8, 128], F32); r_cbt = Res()
                wst = [SB(ph, "wst%d" % i, [128, 8, 512], F32) for i in range(2)]; r_wst = [Res(), Res()]
                bst = [SB(ph, "bst%d" % i, [1, 512], F32) for i in range(2)]; r_bst = [Res(), Res()]
                mtmp = SB(ph, "mtmp", [128, 512], F32); r_mtmp = Res()
                dma("sp", ds_misc[0], cT[:], c_d[s].rearrange("(j p) -> p j", p=128), writes=[r_cT], allow_slow_non_contiguous=True)
                op("act", lambda e: e.activation(out=cT[:], in_=cT[:], func=AF.Silu), reads=[r_cT], writes=[r_cT])
                op("dve", lambda e: e.tensor_copy(out=cbt[:], in_=cT[:].unsqueeze(2).broadcast_to([128, 8, 128])), reads=[r_cT], writes=[r_cbt])
                for cg in range(12):
                    b2 = cg % 2
                    dma("sp", ds_misc[1 + b2], wst[b2][:], w_ada_k[:, :, cg * 512:(cg + 1) * 512], writes=[r_wst[b2]])
                    dma("sp", ds_o[b2], bst[b2][:], b_ada_d[0:1, cg * 512:(cg + 1) * 512], writes=[r_bst[b2]])
                    pb = 2 + b2
                    for k in range(8):
                        op("pe", lambda e, k=k: e.matmul(PB[pb][:], lhsT=cbt[:, k, :], rhs=wst[b2][:, k, :], start=(k == 0), stop=False),
                           reads=[r_cbt, r_wst[b2]], writes=[rPB[pb]])
                    op("pe", lambda e: e.matmul(PB[pb][:], lhsT=onesf[0:1, :], rhs=bst[b2][0:1, :], start=False, stop=True),
                       reads=[r_onesf, r_bst[b2]], writes=[rPB[pb]])
                    which, half = cg // 2, cg % 2
                    if which in (2, 5):
                        gi = 0 if which == 2 else 1
                        op("dve", lambda e: e.tensor_scalar(out=gaB[:, gi, half * 512:(half + 1) * 512], in0=PB[pb][:], scalar1=1.0, scalar2=None, op0=ALU.add),
                           reads=[rPB[pb]], writes=[r_gaB[gi]])
                    else:
                        widx = {0: 0, 1: 1, 3: 2, 4: 3}[which]
                        addc = 1.0 if which in (1, 4) else 0.0
                        op("dve", lambda e: e.tensor_scalar(out=mtmp[:], in0=PB[pb][:], scalar1=addc, scalar2=None, op0=ALU.add),
                           reads=[rPB[pb]], writes=[r_mtmp])
                        for i in range(4):
                            op("pe", lambda e, i=i: e.transpose(out=PB[4][:, i * 128:(i + 1) * 128], in_=mtmp[:, i * 128:(i + 1) * 128], identity=identf[:]),
                               reads=[r_mtmp, r_identf], writes=[rPB[4]])
                        op("dve", lambda e: e.tensor_copy(out=modsT[:, widx, half * 4:(half + 1) * 4],
                                                          in_=PB[4][:].rearrange("p (i m) -> p i m", m=128)[:, :, 0]),
                           reads=[rPB[4]], writes=[r_modsT])
                if dbg and s == 0:
                    dma("sp", ds_dbg, dbg_d["mods"], modsT[:], reads=[r_modsT])
            T.barrier()

            with ExitStack() as pab:
                oT = SB(pab, "oT", [128, 4, S], F32)
                r_oT = [[Res() for _ in range(NCH)] for _ in range(4)]
                kTt = SB(pab, "kTt", [128, S], BF16); r_kT = [Res() for _ in range(NG)]
                vtk = SB(pab, "vtk", [128, NT, 128], BF16); r_v = [Res() for _ in range(NT)]
                cosT = SB(pab, "cosT", [128, S], F32); sinT = SB(pab, "sinT", [128, S], F32); r_cs = Res()
                with ExitStack() as pr:
                    posi = SB(pr, "posi", [128, S], I32); r_posi = Res()
                    ang = SB(pr, "ang", [128, S], F32); r_ang = Res()
                    kq = SB(pr, "kq", [128, S], F32); r_kq = Res()
                    ki = SB(pr, "ki", [128, S], I32); r_ki = Res()
                    dma("sp", ds_misc[0], posi[:], pos_d[s:s + 1, :].broadcast_to([128, S]), writes=[r_posi])
                    op("dve", lambda e: e.tensor_copy(out=ang[:], in_=posi[:]), reads=[r_posi], writes=[r_ang])
                    op("dve", lambda e: e.tensor_scalar(out=ang[:], in0=ang[:], scalar1=rope[:, 0:1], scalar2=None, op0=ALU.mult),
                       reads=[r_ang, r_rope], writes=[r_ang])
                    TWO_PI = 6.283185307179586
                    C1 = 6.28125
                    C2 = TWO_PI - C1
                    for (dst, shift) in ((sinT, 0.0), (cosT, np.pi / 2)):
                        op("dve", lambda e, shift=shift: e.tensor_scalar(out=kq[:], in0=ang[:], scalar1=float(shift), scalar2=1.0 / TWO_PI, op0=ALU.add, op1=ALU.mult),
                           reads=[r_ang], writes=[r_kq])
                        op("dve", lambda e: e.tensor_copy(out=ki[:], in_=kq[:]), reads=[r_kq], writes=[r_ki])
                        op("dve", lambda e: e.tensor_copy(out=kq[:], in_=ki[:]), reads=[r_ki], writes=[r_kq])
                        op("dve", lambda e, shift=shift, dst=dst: e.tensor_scalar(out=dst[:], in0=ang[:], scalar1=float(shift), scalar2=None, op0=ALU.add),
                           reads=[r_ang], writes=[r_cs])
                        op("dve", lambda e, dst=dst: e.scalar_tensor_tensor(out=dst[:], in0=kq[:], scalar=-C1, in1=dst[:], op0=ALU.mult, op1=ALU.add),
                           reads=[r_kq, r_cs], writes=[r_cs])
                        op("dve", lambda e, dst=dst: e.scalar_tensor_tensor(out=dst[:], in0=kq[:], scalar=-C2, in1=dst[:], op0=ALU.mult, op1=ALU.add),
                           reads=[r_kq, r_cs], writes=[r_cs])
                        op("dve", lambda e, dst=dst: e.tensor_scalar(out=kq[:], in0=dst[:], scalar1=float(np.pi), scalar2=-TWO_PI, op0=ALU.is_gt, op1=ALU.mult),
                           reads=[r_cs], writes=[r_kq])
                        op("dve", lambda e, dst=dst: e.tensor_tensor(out=dst[:], in0=dst[:], in1=kq[:], op=ALU.add), reads=[r_cs, r_kq], writes=[r_cs])
                        op("dve", lambda e, dst=dst: e.tensor_scalar(out=kq[:], in0=dst[:], scalar1=float(-np.pi), scalar2=TWO_PI, op0=ALU.is_lt, op1=ALU.mult),
                           reads=[r_cs], writes=[r_kq])
                        op("dve", lambda e, dst=dst: e.tensor_tensor(out=dst[:], in0=dst[:], in1=kq[:], op=ALU.add), reads=[r_cs, r_kq], writes=[r_cs])
                        op("dve", lambda e, dst=dst: e.tensor_scalar(out=dst[:], in0=dst[:], scalar1=3.1415925, scalar2=-3.1415925, op0=ALU.min, op1=ALU.max),
                           reads=[r_cs], writes=[r_cs])
                        op("act", lambda e, dst=dst: e.activation(out=dst[:], in_=dst[:], func=AF.Sin), reads=[r_cs], writes=[r_cs])
                    op("dve", lambda e: e.tensor_scalar(out=sinT[:], in0=sinT[:], scalar1=rope[:, 1:2], scalar2=None, op0=ALU.mult),
                       reads=[r_cs, r_rope], writes=[r_cs])
                T.barrier()

                for hp in range(2):
                    with ExitStack() as pa:
                        NWA = 1024 + (384 if hp == 0 else 0)
                        wA = SB(pa, "wA", [128, 8, NWA], BF16); r_wA = Res()
                        QT = [SB(pa, "QT%d" % d, [128, 2, S], BF16) for d in range(2)]
                        KT = [SB(pa, "KT%d" % d, [128, 2, S], BF16) for d in range(2)]
                        r_QK = [[[Res() for _ in range(NG)] for _ in range(2)] for _ in range(2)]
                        Vt = SB(pa, "Vt", [64, NCH, 256], BF16); r_V = [Res() for _ in range(NG)]
                        Bref = SB(pa, "Bref", [128, 2, 2, NCH], F32); Bend = SB(pa, "Bend", [128, 2, 2, NCH], F32); r_B = Res()
                        Gx = SB(pa, "Gx", [128, 2, 2, NCH], F32); r_G = Res()
                        for seg, base in enumerate((O_RQ, O_RFF, O_RFB, O_RI)):
                            dma("pool", ds_w, wA[:, :, seg * 256:(seg + 1) * 256], w_in_k[:, :, base + hp * 256: base + hp * 256 + 256], writes=[r_wA])
                        if hp == 0:
                            dma("pool", ds_w, wA[:, :, 1024:1152], w_in_k[:, :, O_AK:O_AK + 128], writes=[r_wA])
                            for kv in range(2):
                                dma("pool", ds_w, wA[:, :, 1152 + kv * 64:1152 + kv * 64 + 32], w_in_k[:, :, O_AK + kv * 64 + 32:O_AK + kv * 64 + 64], writes=[r_wA])
                                dma("pool", ds_w, wA[:, :, 1152 + kv * 64 + 32:1152 + kv * 64 + 64], w_in_k[:, :, O_AK + kv * 64:O_AK + kv * 64 + 32], writes=[r_wA])
                            dma("pool", ds_w, wA[:, :, 1280:1408], w_in_k[:, :, O_AV:O_AV + 128], writes=[r_wA])
                        with ExitStack() as pa2:
                            xs = [SB(pa2, "xs%d" % i, [128, D], F32) for i in range(2)]; r_xs = [Res() for _ in range(2)]
                            uT = [SB(pa2, "uT%d" % i, [128, 8, 512], BF16) for i in range(1)]; r_uT = [Res()]
                            qs = [SB(pa2, "qs%d" % i, [128, 512], F32) for i in range(2)]; r_qs = [Res(), Res()]
                            NTMP = 2
                            tnames = ("sig", "lg", "Bf", "Eq", "Ek")
                            tmps = [{n: SB(pa2, "t_%s%d" % (n, i), [128, 512], F32) for n in tnames} for i in range(NTMP)]
                            r_tmps = [{n: Res() for n in tnames} for i in range(NTMP)]
                            for i in range(NTMP):
                                tmps[i]["kk"] = tmps[i]["sig"]; r_tmps[i]["kk"] = r_tmps[i]["sig"]
                                tmps[i]["dd"] = tmps[i]["lg"]; r_tmps[i]["dd"] = r_tmps[i]["lg"]
                            tctr = 0
                            bank_rot = Rot([2, 3, 4, 5, 6, 7])
                            for g in range(NG):
                                ub = 0
                                for t in range(4):
                                    tile_i = g * 4 + t
                                    sl = tile_i % 2
                                    dma("sp", ds_x[sl], xs[sl][:], x_d[s, tile_i * 128:(tile_i + 1) * 128, :], writes=[r_xs[sl]])
                                    make_uT(xs[sl], r_xs[sl], uT[ub][:, :, t * 128:(t + 1) * 128], r_uT[ub], 1, 0)
                                gs = slice(g * 512, (g + 1) * 512)
                                for cp in range(4):
                                    pb = bank_rot.next()
                                    for cc in range(2):
                                        ch = cp * 2 + cc
                                        for k in range(8):
                                            op("pe", lambda e, k=k, cc=cc, ch=ch, pb=pb: e.matmul(PB[pb][0:64, cc * 256:(cc + 1) * 256], lhsT=uT[ub][:, k, ch * 64:(ch + 1) * 64],
                                                                                                   rhs=wA[:, k, 768:1024], start=(k == 0), stop=(k == 7)),
                                               reads=[r_uT[ub], r_wA], writes=[rPB[pb]])
                                    op("act", lambda e, pb=pb, cp=cp: e.activation(out=Vt[:, g * 8 + cp * 2:g * 8 + cp * 2 + 2, :],
                                                                                    in_=PB[pb][0:64, :].rearrange("p (c n) -> p c n", n=256), func=AF.Copy, scale=128.0 ** -0.5),
                                       reads=[rPB[pb]], writes=[r_V[g]])
                                for hl in range(2):
                                    h = hp * 2 + hl
                                    pb = bank_rot.next()
                                    for k in range(8):
                                        op("pe", lambda e, k=k, pb=pb, hl=hl: e.matmul(PB[pb][:], lhsT=wA[:, k, hl * 128:(hl + 1) * 128], rhs=uT[ub][:, k, :],
                                                                                         start=(k == 0), stop=(k == 7)), reads=[r_uT[ub], r_wA], writes=[rPB[pb]])
                                    op("act", lambda e, pb=pb, hl=hl: e.activation(out=qs[hl][:], in_=PB[pb][:], func=AF.Silu), reads=[rPB[pb]], writes=[r_qs[hl]])
                                    for d in range(2):
                                        tm = tmps[tctr % NTMP]; rt = r_tmps[tctr % NTMP]; tctr += 1
                                        pb = bank_rot.next()
                                        cb = 256 * (1 + d) + hl * 128
                                        for k in range(8):
                                            op("pe", lambda e, k=k, pb=pb, cb=cb: e.matmul(PB[pb][:], lhsT=wA[:, k, cb:cb + 128], rhs=uT[ub][:, k, :],
                                                                                             start=(k == 0), stop=(k == 7)), reads=[r_uT[ub], r_wA], writes=[rPB[pb]])
                                        lb_c = lbt[:, 0, d, h:h + 1]; om_c = lbt[:, 1, d, h:h + 1]; nom_c = lbt[:, 2, d, h:h + 1]
                                        op("act", lambda e, pb=pb, tm=tm: e.activation(out=tm["sig"][:], in_=PB[pb][:], func=AF.Sigmoid), reads=[rPB[pb]], writes=[rt["sig"]])
                                        op("act", lambda e, tm=tm: e.activation(out=tm["lg"][:], in_=tm["sig"][:], func=AF.Ln, scale=om_c, bias=lb_c),
                                           reads=[rt["sig"], r_lbt], writes=[rt["lg"]])
                                        op("dve", lambda e, tm=tm: e.tensor_scalar(out=tm["kk"][:], in0=tm["sig"][:], scalar1=nom_c, scalar2=om_c, op0=ALU.mult, op1=ALU.add),
                                           reads=[rt["sig"], r_lbt], writes=[rt["kk"]])
                                        op("dve", lambda e, tm=tm: e.tensor_tensor_scan(out=tm["Bf"][:], data0=rmask[:], data1=tm["lg"][:], initial=0.0, op0=ALU.mult, op1=ALU.add),
                                           reads=[r_rmask, rt["lg"]], writes=[rt["Bf"]])
                                        B3 = tm["Bf"][:].rearrange("p (c t) -> p c t", t=64)
                                        if d == 1:
                                            lg3 = tm["lg"][:].rearrange("p (c t) -> p c t", t=64)
                                            op("dve", lambda e, B3=B3, lg3=lg3: e.tensor_tensor(out=lg3, in0=lg3, in1=B3, op=ALU.subtract), reads=[rt["lg"], rt["Bf"]], writes=[rt["lg"]])
                                            op("dve", lambda e, B3=B3, lg3=lg3: e.tensor_tensor(out=B3, in0=lg3, in1=B3[:, :, 63:64].broadcast_to([128, 8, 64]), op=ALU.add),
                                               reads=[rt["lg"], rt["Bf"]], writes=[rt["Bf"]])
                                            iref, iend = 32, 0
                                        else:
                                            iref, iend = 31, 63
                                        op("pool", lambda e, B3=B3, iref=iref: e.tensor_copy(out=Bref[:, d, hl, g * 8:(g + 1) * 8], in_=B3[:, :, iref]), reads=[rt["Bf"]], writes=[r_B])
                                        op("pool", lambda e, B3=B3, iend=iend: e.tensor_copy(out=Bend[:, d, hl, g * 8:(g + 1) * 8], in_=B3[:, :, iend]), reads=[rt["Bf"]], writes=[r_B])
                                        op("dve", lambda e, B3=B3, tm=tm, iref=iref: e.tensor_tensor(out=tm["dd"][:].rearrange("p (c t) -> p c t", t=64), in0=B3,
                                                                                                       in1=B3[:, :, iref:iref + 1].broadcast_to([128, 8, 64]), op=ALU.subtract),
                                           reads=[rt["Bf"]], writes=[rt["dd"]])
                                        op("act", lambda e, tm=tm: e.activation(out=tm["Eq"][:], in_=tm["dd"][:], func=AF.Exp), reads=[rt["dd"]], writes=[rt["Eq"]])
                                        op("act", lambda e, tm=tm: e.activation(out=tm["Ek"][:], in_=tm["dd"][:], func=AF.Exp, scale=-1.0), reads=[rt["dd"]], writes=[rt["Ek"]])
                                        op("pool", lambda e, tm=tm, d=d, hl=hl: e.tensor_tensor(out=KT[d][:, hl, gs], in0=tm["kk"][:], in1=tm["Ek"][:], op=ALU.mult),
                                           reads=[rt["kk"], rt["Ek"]], writes=[r_QK[d][hl][g]])
                                        op("pool", lambda e, tm=tm, d=d, hl=hl: e.tensor_tensor(out=QT[d][:, hl, gs], in0=qs[hl][:], in1=tm["Eq"][:], op=ALU.mult),
                                           reads=[r_qs[hl], rt["Eq"]], writes=[r_QK[d][hl][g]])
                                if hp == 0:
                                    pb1 = bank_rot.next()
                                    pb2 = bank_rot.next()
                                    for (pbx, cb) in ((pb1, 1024), (pb2, 1152)):
                                        for k in range(8):
                                            op("pe", lambda e, k=k, pbx=pbx, cb=cb: e.matmul(PB[pbx][:], lhsT=wA[:, k, cb:cb + 128], rhs=uT[ub][:, k, :], start=(k == 0), stop=(k == 7)),
                                               reads=[r_uT[ub], r_wA], writes=[rPB[pbx]])
                                    tm = tmps[tctr % NTMP]; rt = r_tmps[tctr % NTMP]; tctr += 1
                                    op("dve", lambda e, tm=tm: e.tensor_tensor(out=tm["sig"][:], in0=PB[pb1][:], in1=cosT[:, gs], op=ALU.mult), reads=[rPB[pb1], r_cs], writes=[rt["sig"]])
                                    op("dve", lambda e, tm=tm: e.tensor_tensor(out=tm["lg"][:], in0=PB[pb2][:], in1=sinT[:, gs], op=ALU.mult), reads=[rPB[pb2], r_cs], writes=[rt["lg"]])
                                    op("pool", lambda e, tm=tm: e.tensor_tensor(out=kTt[:, gs], in0=tm["sig"][:], in1=tm["lg"][:], op=ALU.add), reads=[rt["sig"], rt["lg"]], writes=[r_kT[g]])
                                    for t in range(4):
                                        pb = bank_rot.next()
                                        for k in range(8):
                                            op("pe", lambda e, k=k, pb=pb, t=t: e.matmul(PB[pb][:, 0:128], lhsT=uT[ub][:, k, t * 128:(t + 1) * 128], rhs=wA[:, k, 1280:1408],
                                                                                          start=(k == 0), stop=(k == 7)), reads=[r_uT[ub], r_wA], writes=[rPB[pb]])
                                        op("act", lambda e, pb=pb, t=t: e.activation(out=vtk[:, g * 4 + t, :], in_=PB[pb][:, 0:128], func=AF.Copy), reads=[rPB[pb]], writes=[r_v[g * 4 + t]])
                            op("dve", lambda e: e.tensor_tensor(out=Gx[:], in0=Bend[:], in1=Bref[:], op=ALU.subtract), reads=[r_B], writes=[r_G])
                            if NCH > 1:
                                op("dve", lambda e: e.tensor_tensor(out=Gx[:, 0, :, 0:NCH - 1], in0=Gx[:, 0, :, 0:NCH - 1], in1=Bref[:, 0, :, 1:NCH], op=ALU.add), reads=[r_B, r_G], writes=[r_G])
                                op("dve", lambda e: e.tensor_tensor(out=Gx[:, 1, :, 1:NCH], in0=Gx[:, 1, :, 1:NCH], in1=Bref[:, 1, :, 0:NCH - 1], op=ALU.add), reads=[r_B, r_G], writes=[r_G])
                            op("act", lambda e: e.activation(out=Gx[:], in_=Gx[:], func=AF.Exp), reads=[r_G], writes=[r_G])
                            if dbg and s == 0 and hp == 0:
                                for d in range(2):
                                    dma("sp", ds_dbg, dbg_d["QT"][d], QT[d][:], reads=[r_QK[d][hl][g] for hl in range(2) for g in range(NG)])
                                    dma("sp", ds_dbg, dbg_d["KT"][d], KT[d][:], reads=[r_QK[d][hl][g] for hl in range(2) for g in range(NG)])
                                dma("sp", ds_dbg, dbg_d["V"], Vt[:], reads=r_V)
                                dma("sp", ds_dbg, dbg_d["kT"], kTt[:], reads=r_kT)
                        T.barrier()

                        with ExitStack() as pb_:
                            Sm32 = SB(pb_, "Sm32", [128, 4, 128], F32); Smb = SB(pb_, "Smb", [128, 4, 128], BF16); tmp32 = SB(pb_, "tmp32", [128, 4, 128], F32)
                            r_Sm32 = [Res() for _ in range(4)]; r_Smb = [Res() for _ in range(4)]; r_tmp32 = [Res() for _ in range(4)]
                            scm = SB(pb_, "scm", [64, 2, 4, 64], BF16); r_scm = [[Res() for _ in range(4)] for _ in range(2)]
                            ktok = SB(pb_, "ktok", [64, 2, 4, 128], BF16); r_ktok = [[Res() for _ in range(4)] for _ in range(2)]
                            rB = [[None] * 4 for _ in range(2)]
                            for _p in range(2):
                                for _c in range(4):
                                    _r = Res()
                                    rB[_p][_c] = {n: _r for n in ("sc", "kt", "o", "dl")}
                            op("pool", lambda e: e.memset(Sm32[:], 0.0), writes=r_Sm32)
                            op("pool", lambda e: e.memset(Smb[:], 0.0), writes=r_Smb)
                            allQK = lambda d, hl: [r_QK[d][hl][g] for g in range(NG)]

                            def chunk_of(j, d):
                                return j if d == 0 else NCH - 1 - j

                            def stage1(j):
                                par = j % 2
                                for ci in range(4):
                                    hl, d = ci // 2, ci % 2
                                    c = chunk_of(j, d); tk = slice(c * 64, (c + 1) * 64)
                                    bk = ci + 4 * par
                                    op("pe", lambda e, bk=bk, d=d, hl=hl, tk=tk: e.matmul(PB[bk][0:64, 0:64], lhsT=KT[d][:, hl, tk], rhs=QT[d][:, hl, tk], start=True, stop=True),
                                       reads=allQK(d, hl), writes=[rB[par][ci]["sc"]])
                                    op("pe", lambda e, bk=bk, d=d, hl=hl, tk=tk: e.transpose(out=PB[bk][:].bitcast(BF16)[0:64, 128:256], in_=KT[d][:, hl, tk], identity=identb[:]),
                                       reads=allQK(d, hl) + [r_identb], writes=[rB[par][ci]["kt"]])

                            def stage2(j):
                                par = j % 2
                                for ci in range(4):
                                    hl, d = ci // 2, ci % 2
                                    bk = ci + 4 * par
                                    mk = maskf if d == 0 else maskb
                                    op("pool", lambda e, ci=ci: e.memset(scm[:, par, ci, :], 0.0), writes=[r_scm[par][ci]])
                                    op("dve", lambda e, bk=bk, ci=ci, mk=mk: e.copy_predicated(out=scm[:, par, ci, :], mask=mk[:], data=PB[bk][0:64, 0:64]),
                                       reads=[rB[par][ci]["sc"], r_mask], writes=[r_scm[par][ci]])
                                    op("act", lambda e, bk=bk, ci=ci: e.activation(out=ktok[:, par, ci, :], in_=PB[bk][:].bitcast(BF16)[0:64, 128:256], func=AF.Copy),
                                       reads=[rB[par][ci]["kt"]], writes=[r_ktok[par][ci]])

                            def stage3(j):
                                par = j % 2
                                for ci in range(4):
                                    hl, d = ci // 2, ci % 2
                                    c = chunk_of(j, d); tk = slice(c * 64, (c + 1) * 64)
                                    bk = ci + 4 * par
                                    vv = Vt[:, c, hl * 128:(hl + 1) * 128]
                                    op("pe", lambda e, bk=bk, vv=vv, ci=ci: e.matmul(PB[bk][:, 128:192], lhsT=vv, rhs=scm[:, par, ci, :], start=True, stop=False),
                                       reads=[r_V[c // 8], r_scm[par][ci]], writes=[rB[par][ci]["o"]])
                                    op("pe", lambda e, bk=bk, ci=ci, d=d, hl=hl, tk=tk: e.matmul(PB[bk][:, 128:192], lhsT=Smb[:, ci, :], rhs=QT[d][:, hl, tk], start=False, stop=True),
                                       reads=[r_Smb[ci]] + allQK(d, hl), writes=[rB[par][ci]["o"]])
                                    op("pe", lambda e, bk=bk, vv=vv, ci=ci: e.matmul(PB[bk][:, 192:320], lhsT=ktok[:, par, ci, :], rhs=vv, start=True, stop=True),
                                       reads=[r_ktok[par][ci], r_V[c // 8]], writes=[rB[par][ci]["dl"]])

                            def stage4(j):
                                par = j % 2
                                for ci in range(4):
                                    hl, d = ci // 2, ci % 2
                                    h = hp * 2 + hl
                                    c = chunk_of(j, d); tk = slice(c * 64, (c + 1) * 64)
                                    bk = ci + 4 * par
                                    if j < NCH // 2:
                                        op("act", lambda e, bk=bk, h=h, tk=tk: e.activation(out=oT[:, h, tk], in_=PB[bk][:, 128:192], func=AF.Copy),
                                           reads=[rB[par][ci]["o"]], writes=[r_oT[h][c]])
                                    else:
                                        op("dve", lambda e, bk=bk, h=h, tk=tk: e.tensor_tensor(out=oT[:, h, tk], in0=PB[bk][:, 128:192], in1=oT[:, h, tk], op=ALU.add),
                                           reads=[rB[par][ci]["o"], r_oT[h][c]], writes=[r_oT[h][c]])
                                    if j < NCH - 1:
                                        gcol = Gx[:, d, hl, c:c + 1]
                                        op("dve", lambda e, bk=bk, ci=ci: e.tensor_tensor(out=tmp32[:, ci, :], in0=PB[bk][:, 192:320], in1=Sm32[:, ci, :], op=ALU.add),
                                           reads=[rB[par][ci]["dl"], r_Sm32[ci]], writes=[r_tmp32[ci]])
                                        op("act", lambda e, ci=ci, gcol=gcol: e.activation(out=Sm32[:, ci, :], in_=tmp32[:, ci, :], func=AF.Identity, scale=gcol),
                                           reads=[r_tmp32[ci], r_G], writes=[r_Sm32[ci]])
                                        op("pool", lambda e, ci=ci, gcol=gcol: e.tensor_scalar(out=Smb[:, ci, :], in0=tmp32[:, ci, :], scalar1=gcol, scalar2=None, op0=ALU.mult),
                                           reads=[r_tmp32[ci], r_G], writes=[r_Smb[ci]])

                            stage1(0)
                            stage2(0)
                            for j in range(NCH):
                                if j + 1 < NCH:
                                    stage1(j + 1)
                                stage3(j)
                                if j + 1 < NCH:
                                    stage2(j + 1)
                                stage4(j)
                        T.barrier()
                if dbg and s == 0:
                    dma("sp", ds_dbg, dbg_d["oT"], oT[:], reads=[r for rr in r_oT for r in rr])

                ogA = SB(pab, "ogA", [128, 4, S], BF16); atA = SB(pab, "atA", [128, 4, S], BF16)
                r_ogA = [Res() for _ in range(NG)]; r_atA = [Res() for _ in range(NT)]
                with ExitStack() as pc:
                    wC = SB(pc, "wC", [128, 8, 1536], BF16); r_wC = Res()
                    dma("pool", ds_w, wC[:, :, 0:512], w_in_k[:, :, O_RG:O_RG + 512], writes=[r_wC])
                    for m in range(4):
                        for hf in range(2):
                            hq = m + 4 * hf
                            cdst = 512 + m * 128 + hf * 64
                            dma("pool", ds_w, wC[:, :, cdst:cdst + 64], w_in_k[:, :, O_AQ + hq * 64:O_AQ + hq * 64 + 64], writes=[r_wC])
                            dma("pool", ds_w, wC[:, :, 512 + cdst:512 + cdst + 32], w_in_k[:, :, O_AQ + hq * 64 + 32:O_AQ + hq * 64 + 64], writes=[r_wC])
                            dma("pool", ds_w, wC[:, :, 512 + cdst + 32:512 + cdst + 64], w_in_k[:, :, O_AQ + hq * 64:O_AQ + hq * 64 + 32], writes=[r_wC])
                    xs = [SB(pc, "cxs%d" % i, [128, D], F32) for i in range(2)]; r_xs = [Res() for _ in range(2)]
                    uT = SB(pc, "cuT", [128, 8, 512], BF16); r_uT = Res()
                    qT = SB(pc, "qT", [128, 4, 512], BF16); r_qT = Res()
                    ta = [SB(pc, "cta%d" % i, [128, 512], F32) for i in range(4)]; r_ta = [Res() for _ in range(4)]
                    msc = [SB(pc, "msc%d" % i, [128, 384], F32) for i in range(2)]; r_msc = [Res(), Res()]
                    pex = [SB(pc, "pex%d" % i, [128, 384], BF16) for i in range(2)]; r_pex = [Res(), Res()]
                    pT = [SB(pc, "pT%d" % i, [128, 3, 128], BF16) for i in range(2)]; r_pT = [Res(), Res()]
                    sm = [SB(pc, "smx%d" % i, [128, 8], F32) for i in range(2)]; r_sm = [Res(), Res()]
                    ao = SB(pc, "ao", [128, 512], BF16); r_ao = Res()
                    hctr = 0
                    for g in range(NG):
                        gs = slice(g * 512, (g + 1) * 512)
                        for t in range(4):
                            tile_i = g * 4 + t
                            sl = tile_i % 2
                            dma("sp", ds_x[sl], xs[sl][:], x_d[s, tile_i * 128:(tile_i + 1) * 128, :], writes=[r_xs[sl]])
                            make_uT(xs[sl], r_xs[sl], uT[:, :, t * 128:(t + 1) * 128], r_uT, 1, 0)
                        for m in range(4):
                            for (pbx, cb) in ((2, 512 + m * 128), (3, 1024 + m * 128)):
                                for k in range(8):
                                    op("pe", lambda e, k=k, pbx=pbx, cb=cb: e.matmul(PB[pbx][:], lhsT=wC[:, k, cb:cb + 128], rhs=uT[:, k, :], start=(k == 0), stop=(k == 7)),
                                       reads=[r_uT, r_wC], writes=[rPB[pbx]])
                            op("dve", lambda e: e.tensor_tensor(out=ta[0][:], in0=PB[2][:], in1=cosT[:, gs], op=ALU.mult), reads=[rPB[2], r_cs], writes=[r_ta[0]])
                            op("dve", lambda e: e.tensor_tensor(out=ta[1][:], in0=PB[3][:], in1=sinT[:, gs], op=ALU.mult), reads=[rPB[3], r_cs], writes=[r_ta[1]])
                            op("pool", lambda e, m=m: e.tensor_tensor(out=qT[:, m, :], in0=ta[0][:], in1=ta[1][:], op=ALU.add), reads=[r_ta[0], r_ta[1]], writes=[r_qT])
                        for h in range(4):
                            for k in range(8):
                                op("pe", lambda e, k=k, h=h: e.matmul(PB[4][:], lhsT=wC[:, k, h * 128:(h + 1) * 128], rhs=uT[:, k, :], start=(k == 0), stop=(k == 7)),
                                   reads=[r_uT, r_wC], writes=[rPB[4]])
                            op("act", lambda e: e.activation(out=ta[0][:], in_=PB[4][:], func=AF.Silu), reads=[rPB[4]], writes=[r_ta[0]])
                            rr = [r_oT[h][c] for c in range(g * 8, (g + 1) * 8)]
                            op("act", lambda e, h=h: e.activation(out=ta[2][:], in_=oT[:, h, gs], func=AF.Square), reads=rr, writes=[r_ta[2]])
                            op("pe", lambda e: e.matmul(PB[5][:], lhsT=onesf[:], rhs=ta[2][:], start=True, stop=True), reads=[r_onesf, r_ta[2]], writes=[rPB[5]])
                            op("act", lambda e: e.activation(out=ta[3][:], in_=PB[5][:], func=AF.Ln, scale=1.0 / 128.0, bias=epsc[:, 1:2]), reads=[rPB[5], r_eps], writes=[r_ta[3]])
                            op("act", lambda e: e.activation(out=ta[3][:], in_=ta[3][:], func=AF.Exp, scale=-0.5), reads=[r_ta[3]], writes=[r_ta[3]])
                            op("dve", lambda e, h=h: e.tensor_tensor(out=ta[2][:], in0=oT[:, h, gs], in1=ta[3][:], op=ALU.mult), reads=rr + [r_ta[3]], writes=[r_ta[2]])
                            op("dve", lambda e, h=h: e.scalar_tensor_tensor(out=ogA[:, h, gs], in0=ta[0][:], scalar=ngc[:, 0:1], in1=ta[2][:], op0=ALU.mult, op1=ALU.mult),
                               reads=[r_ta[0], r_ngc, r_ta[2]], writes=[r_ogA[g]])
                        for t in range(4):
                            j = g * 4 + t
                            k0 = max(0, j - 1); k1 = min(NT, j + 2)
                            nkb = k1 - k0; nk = nkb * 128
                            mc0 = 128 if j == 0 else 0
                            for h in range(8):
                                m, hf = h % 4, h // 4
                                ps = slice(hf * 64, (hf + 1) * 64)
                                ib = hctr % 2; hctr += 1
                                pbs = 2 + ib
                                kgs = [r_kT[gg] for gg in range((k0 * 128) // 512, ((k1 * 128) - 1) // 512 + 1)]
                                op("pe", lambda e, pbs=pbs, ps=ps, m=m, t=t: e.matmul(PB[pbs][:, 0:nk], lhsT=qT[ps, m, t * 128:(t + 1) * 128], rhs=kTt[ps, k0 * 128:k1 * 128], start=True, stop=True),
                                   reads=[r_qT] + kgs, writes=[rPB[pbs]])
                                op("dve", lambda e, pbs=pbs, ib=ib: e.tensor_tensor(out=msc[ib][:, 0:nk], in0=PB[pbs][:, 0:nk], in1=amask[:, mc0:mc0 + nk], op=ALU.add),
                                   reads=[rPB[pbs], r_amask], writes=[r_msc[ib]])
                                op("dve", lambda e, ib=ib: e.reduce_max(out=sm[ib][:, 0:1], in_=msc[ib][:, 0:nk], axis=AX.X), reads=[r_msc[ib]], writes=[r_sm[ib]])
                                op("dve", lambda e, ib=ib, h=h: e.tensor_scalar(out=sm[ib][:, 1:2], in0=sm[ib][:, 0:1], scalar1=-0.125, scalar2=sinkb[:, 1, h:h + 1], op0=ALU.mult, op1=ALU.min),
                                   reads=[r_sm[ib], r_sink], writes=[r_sm[ib]])
                                op("act", lambda e, ib=ib: e.activation(out=pex[ib][:, 0:nk], in_=msc[ib][:, 0:nk], func=AF.Exp, scale=0.125, bias=sm[ib][:, 1:2], accum_out=sm[ib][:, 2:3]),
                                   reads=[r_msc[ib], r_sm[ib]], writes=[r_pex[ib], r_sm[ib]])
                                op("act", lambda e, ib=ib, h=h: e.activation(out=sm[ib][:, 3:4], in_=sinkb[:, 0, h:h + 1], func=AF.Exp, bias=sm[ib][:, 1:2], scale=1.0),
                                   reads=[r_sm[ib], r_sink], writes=[r_sm[ib]])
                                op("dve", lambda e, ib=ib: e.tensor_tensor(out=sm[ib][:, 4:5], in0=sm[ib][:, 2:3], in1=sm[ib][:, 3:4], op=ALU.add), reads=[r_sm[ib]], writes=[r_sm[ib]])
                                op("dve", lambda e, ib=ib: e.reciprocal(out=sm[ib][:, 5:6], in_=sm[ib][:, 4:5]), reads=[r_sm[ib]], writes=[r_sm[ib]])
                                pbt = 4 + ib
                                for b in range(nkb):
                                    op("pe", lambda e, b=b, pbt=pbt, ib=ib: e.transpose(out=PB[pbt][:].bitcast(BF16)[:, b * 128:(b + 1) * 128], in_=pex[ib][:, b * 128:(b + 1) * 128], identity=identb[:]),
                                       reads=[r_pex[ib], r_identb], writes=[rPB[pbt]])
                                if ib == 0:
                                    op("act", lambda e, pbt=pbt, ib=ib: e.activation(out=pT[ib][:, 0:nkb, :], in_=PB[pbt][:].bitcast(BF16)[:, 0:nk].rearrange("p (b q) -> p b q", q=128), func=AF.Copy),
                                       reads=[rPB[pbt]], writes=[r_pT[ib]])
                                else:
                                    op("dve", lambda e, pbt=pbt, ib=ib: e.tensor_copy(out=pT[ib][:, 0:nkb, :], in_=PB[pbt][:].bitcast(BF16)[:, 0:nk].rearrange("p (b q) -> p b q", q=128)),
                                       reads=[rPB[pbt]], writes=[r_pT[ib]])
                                pbo = 6 + ib
                                for b in range(nkb):
                                    op("pe", lambda e, b=b, pbo=pbo, ib=ib, hf=hf: e.matmul(PB[pbo][:, 0:64], lhsT=pT[ib][:, b, :], rhs=vtk[:, k0 + b, hf * 64:(hf + 1) * 64], start=(b == 0), stop=(b == nkb - 1)),
                                       reads=[r_pT[ib], r_v[k0 + b]], writes=[rPB[pbo]])
                                op("act", lambda e, pbo=pbo, ib=ib, h=h: e.activation(out=ao[:, h * 64:(h + 1) * 64], in_=PB[pbo][:, 0:64], func=AF.Identity, scale=sm[ib][:, 5:6]),
                                   reads=[rPB[pbo], r_sm[ib]], writes=[r_ao])
                            for m in range(4):
                                op("pe", lambda e, m=m: e.transpose(out=PB[5][:].bitcast(BF16)[:, m * 128:(m + 1) * 128], in_=ao[:, m * 128:(m + 1) * 128], identity=identb[:]),
                                   reads=[r_ao, r_identb], writes=[rPB[5]])
                            op("dve", lambda e, j=j: e.tensor_copy(out=atA[:, :, j * 128:(j + 1) * 128], in_=PB[5][:].bitcast(BF16)[:, 0:512].rearrange("p (m q) -> p m q", q=128)),
                               reads=[rPB[5]], writes=[r_atA[j]])
                T.barrier()

                with ExitStack() as pc:
                    if S >= 2048:
                        wG = oT[:].rearrange("p h s -> p (h s)").bitcast(BF16)
                        wGv = wG[:, 0:8 * 2048].rearrange("p (k n) -> p k n", n=2048)
                    else:
                        wGv = SB(pc, "wGt", [128, 8, 2048], BF16)[:]
                    r_wG = Res()
                    wR = SB(pc, "wR", [128, 4, D], BF16); wT = SB(pc, "wT", [128, 4, D], BF16); wO = SB(pc, "wO", [128, 8, D], BF16)
                    lnB = SB(pc, "lnB1", [128, 2, D], F32); r_lnB = Res()
                    for q4 in range(4):
                        dma("pool", ds_w, wGv[:, :, q4 * 512:(q4 + 1) * 512], w_in_k[:, :, O_GR + q4 * 512:O_GR + (q4 + 1) * 512], writes=[r_wG])
                    dma("pool", ds_w, wR[:], w_rec_k, writes=[r_wG])
                    dma("pool", ds_w, wT[:], w_att_k, writes=[r_wG])
                    for q2 in range(2):
                        dma("pool", ds_w, wO[:, :, q2 * 512:(q2 + 1) * 512], w_out_k[:, :, q2 * 512:(q2 + 1) * 512], writes=[r_wG])
                    dma("sp", ds_const, lnB[:, 0, :], ln1g_d[0:1, :].broadcast_to([128, D]), writes=[r_lnB])
                    dma("sp", ds_const, lnB[:, 1, :], ln1b_d[0:1, :].broadcast_to([128, D]), writes=[r_lnB])
                    zb = zer[:, 0:8].bitcast(BF16)[:, 0:8].unsqueeze(2)
                    dma("sp", ds_misc[3], u2T_d[s, :, 0:1].rearrange("(j p) o -> p j o", p=128), zb, reads=[r_zer], allow_slow_non_contiguous=True)
                    dma("sp", ds_misc[3], u2T_d[s, :, S + 1:S + 2].rearrange("(j p) o -> p j o", p=128), zb, reads=[r_zer], allow_slow_non_contiguous=True)
                    r_x1d = [Res() for _ in range(NT)]
                    r_u2d = [Res() for _ in range(NT)]
                    xs = [SB(pc, "bxs%d" % i, [128, D], F32) for i in range(4)]; r_xs = [Res() for _ in range(4)]
                    uT = SB(pc, "buT", [128, 8, 512], BF16); r_uT = Res()
                    mT = SB(pc, "mT", [128, 8, 512], BF16); r_mT = Res()
                    ta = [SB(pc, "bta%d" % i, [128, 512], F32) for i in range(2)]; r_ta = [Res() for _ in range(2)]
                    yt = [SB(pc, "yt%d" % i, [128, D], F32) for i in range(2)]; r_yt = [Res(), Res()]
                    st6 = [SB(pc, "st6%d" % i, [128, 20], F32) for i in range(2)]; r_st6 = [Res(), Res()]
                    u2t = [SB(pc, "u2t%d" % i, [128, 8, 128], BF16) for i in range(2)]; r_u2t = [Res(), Res()]
                    for g in range(NG):
                        gs = slice(g * 512, (g + 1) * 512)
                        for t in range(4):
                            tile_i = g * 4 + t
                            dma("sp", ds_x[t], xs[t][:], x_d[s, tile_i * 128:(tile_i + 1) * 128, :], writes=[r_xs[t]])
                            make_uT(xs[t], r_xs[t], uT[:, :, t * 128:(t + 1) * 128], r_uT, 1, 0)
                        for dc in range(8):
                            dcs = slice(dc * 128, (dc + 1) * 128)
                            for k in range(8):
                                op("pe", lambda e, k=k, dc=dc: e.matmul(PB[2][:], lhsT=wGv[:, k, dc * 128:(dc + 1) * 128], rhs=uT[:, k, :], start=(k == 0), stop=(k == 7)),
                                   reads=[r_uT, r_wG], writes=[rPB[2]])
                            for k in range(8):
                                op("pe", lambda e, k=k, dc=dc: e.matmul(PB[3][:], lhsT=wGv[:, k, 1024 + dc * 128:1024 + (dc + 1) * 128], rhs=uT[:, k, :], start=(k == 0), stop=(k == 7)),
                                   reads=[r_uT, r_wG], writes=[rPB[3]])
                            for h in range(4):
                                op("pe", lambda e, h=h, dcs=dcs: e.matmul(PB[4][:], lhsT=wR[:, h, dcs], rhs=ogA[:, h, gs], start=(h == 0), stop=(h == 3)),
                                   reads=[r_wG, r_ogA[g]], writes=[rPB[4]])
                            for m in range(4):
                                op("pe", lambda e, m=m, dcs=dcs: e.matmul(PB[5][:], lhsT=wT[:, m, dcs], rhs=atA[:, m, gs], start=(m == 0), stop=(m == 3)),
                                   reads=[r_wG] + r_atA[g * 4:(g + 1) * 4], writes=[rPB[5]])
                            op("act", lambda e: e.activation(out=ta[0][:], in_=PB[2][:], func=AF.Sigmoid), reads=[rPB[2]], writes=[r_ta[0]])
                            op("act", lambda e: e.activation(out=ta[1][:], in_=PB[3][:], func=AF.Sigmoid), reads=[rPB[3]], writes=[r_ta[1]])
                            op("dve", lambda e: e.tensor_tensor(out=ta[0][:], in0=PB[4][:], in1=ta[0][:], op=ALU.mult), reads=[rPB[4], r_ta[0]], writes=[r_ta[0]])
                            op("dve", lambda e: e.tensor_tensor(out=ta[1][:], in0=PB[5][:], in1=ta[1][:], op=ALU.mult), reads=[rPB[5], r_ta[1]], writes=[r_ta[1]])
                            op("pool", lambda e, dc=dc: e.tensor_tensor(out=mT[:, dc, :], in0=ta[0][:], in1=ta[1][:], op=ALU.add), reads=[r_ta[0], r_ta[1]], writes=[r_mT])
                        for t in range(4):
                            tile_i = g * 4 + t
                            ib = tile_i % 2
                            for hh in range(2):
                                for k in range(8):
                                    op("pe", lambda e, k=k, hh=hh, t=t: e.matmul(PB[6 + hh][:], lhsT=mT[:, k, t * 128:(t + 1) * 128], rhs=wO[:, k, hh * 512:(hh + 1) * 512], start=(k == 0), stop=(k == 7)),
                                       reads=[r_mT, r_wG], writes=[rPB[6 + hh]])
                            ln_epilogue((6, 7), (rPB[6], rPB[7]), xs[t][:], r_xs[t], 0, lnB[:, 0, :], lnB[:, 1, :], r_lnB, yt[ib], r_yt[ib], st6[ib], r_st6[ib])
                            dma("sp", ds_o[ib], x1_d[s, tile_i * 128:(tile_i + 1) * 128, :], yt[ib][:], reads=[r_yt[ib]], writes=[r_x1d[tile_i]])
                            make_uT(yt[ib], r_yt[ib], u2t[ib][:], r_u2t[ib], 3, 2, banks=(0, 1))
                            dma("sp", ds_o[2 + ib], u2T_d[s, :, 1 + tile_i * 128:1 + (tile_i + 1) * 128].rearrange("(j p) q -> p j q", p=128), u2t[ib][:],
                                reads=[r_u2t[ib]], writes=[r_u2d[tile_i]])
                T.barrier()
            T.barrier()

            with ExitStack() as pf:
                wU = SB(pf, "wU", [128, 8, 2 * DFF], BF16); wD = SB(pf, "wD", [128, NFC, D], BF16); r_wU = Res()
                lnB = SB(pf, "lnB2", [128, 2, D], F32); r_lnB = Res()
                cvw = SB(pf, "cvw", [128, 4, 2 * NFC], F32); r_cvw = r_lnB
                for q in range(11):
                    dma("pool", ds_w, wU[:, :, q * 512:(q + 1) * 512], w_up_k[:, :, q * 512:(q + 1) * 512], writes=[r_wU])
                for q in range(NFC):
                    dma("pool", ds_w, wD[:, q, :], w_dn_d[q * 128:(q + 1) * 128, :], writes=[r_wU])
                dma("sp", ds_const, lnB[:, 0, :], ln2g_d[0:1, :].broadcast_to([128, D]), writes=[r_lnB])
                dma("sp", ds_const, lnB[:, 1, :], ln2b_d[0:1, :].broadcast_to([128, D]), writes=[r_lnB])
                dma("sp", ds_const, cvw[:, 0:3, :], cw_d.rearrange("k (j p) -> p k j", p=128), writes=[r_cvw], allow_slow_non_contiguous=True)
                dma("sp", ds_const, cvw[:, 3, :], cb_d.rearrange("o (j p) -> p (o j)", p=128), writes=[r_cvw], allow_slow_non_contiguous=True)
                NSG = S // 256
                u2 = [SB(pf, "fu2%d" % i, [128, 8, 258], BF16) for i in range(2)]; r_u2 = [Res(), Res()]
                x1t = [SB(pf, "fx1%d" % i, [128, D], F32) for i in range(4)]; r_x1t = [Res() for _ in range(4)]
                cv = [SB(pf, "fcv%d" % i, [128, 256], F32) for i in range(2)]; r_cv = [Res(), Res()]
                cgt = [SB(pf, "fcg%d" % i, [128, 256], F32) for i in range(2)]; r_cg = [Res(), Res()]
                gz = [SB(pf, "fgz%d" % i, [128, 256], F32) for i in range(2)]; r_gz = [Res(), Res()]
                actb = [SB(pf, "fact%d" % i, [128, 256], BF16) for i in range(2)]; r_act = [Res(), Res()]
                yt = [SB(pf, "fyt%d" % i, [128, D], F32) for i in range(2)]; r_yt = [Res(), Res()]
                st6 = [SB(pf, "fst6%d" % i, [128, 20], F32) for i in range(2)]; r_st6 = [Res(), Res()]
                cctr = 0
                for sg in range(NSG):
                    a = sg * 256
                    ub = sg % 2
                    dma("sp", ds_misc[ub], u2[ub][:], u2T_d[s, :, a:a + 258].rearrange("(j p) q -> p j q", p=128), writes=[r_u2[ub]])
                    for tt in range(2):
                        sl = (sg * 2 + tt) % 4
                        dma("sp", ds_x[sl], x1t[sl][:], x1_d[s, a + tt * 128:a + (tt + 1) * 128, :], writes=[r_x1t[sl]])
                    for c in range(NFC):
                        ib = cctr % 2; cctr += 1
                        pv, pg = 4 + 2 * ib, 5 + 2 * ib
                        for (pbx, cb) in ((pv, c * 128), (pg, DFF + c * 128)):
                            for k in range(8):
                                op("pe", lambda e, k=k, pbx=pbx, cb=cb: e.matmul(PB[pbx][:, 0:258], lhsT=wU[:, k, cb:cb + 128], rhs=u2[ub][:, k, :], start=(k == 0), stop=(k == 7)),
                                   reads=[r_wU, r_u2[ub]], writes=[rPB[pbx]])
                        for (pbx, dst, r_dst, col) in ((pv, cv[ib], r_cv[ib], c), (pg, cgt[ib], r_cg[ib], NFC + c)):
                            op("act", lambda e, pbx=pbx, dst=dst, col=col: e.activation(out=dst[:], in_=PB[pbx][:, 0:256], func=AF.Identity, scale=cvw[:, 0, col:col + 1], bias=cvw[:, 3, col:col + 1]),
                               reads=[rPB[pbx], r_cvw], writes=[r_dst])
                            op("dve", lambda e, pbx=pbx, dst=dst, col=col: e.scalar_tensor_tensor(out=dst[:], in0=PB[pbx][:, 1:257], scalar=cvw[:, 1, col:col + 1], in1=dst[:], op0=ALU.mult, op1=ALU.add),
                               reads=[rPB[pbx], r_cvw, r_dst], writes=[r_dst])
                            op("dve", lambda e, pbx=pbx, dst=dst, col=col: e.scalar_tensor_tensor(out=dst[:], in0=PB[pbx][:, 2:258], scalar=cvw[:, 2, col:col + 1], in1=dst[:], op0=ALU.mult, op1=ALU.add),
                               reads=[rPB[pbx], r_cvw, r_dst], writes=[r_dst])
                        op("act", lambda e, ib=ib: e.activation(out=gz[ib][:], in_=cgt[ib][:], func=AF.Square), reads=[r_cg[ib]], writes=[r_gz[ib]])
                        op("pool", lambda e, ib=ib: e.tensor_scalar(out=gz[ib][:], in0=gz[ib][:], scalar1=0.044715, scalar2=1.0, op0=ALU.mult, op1=ALU.add), reads=[r_gz[ib]], writes=[r_gz[ib]])
                        op("pool", lambda e, ib=ib: e.tensor_tensor(out=gz[ib][:], in0=gz[ib][:], in1=cgt[ib][:], op=ALU.mult), reads=[r_gz[ib], r_cg[ib]], writes=[r_gz[ib]])
                        op("act", lambda e, ib=ib: e.activation(out=gz[ib][:], in_=gz[ib][:], func=AF.Sigmoid, scale=1.5957691216057308), reads=[r_gz[ib]], writes=[r_gz[ib]])
                        op("dve", lambda e, ib=ib: e.tensor_tensor(out=cv[ib][:], in0=cv[ib][:], in1=cgt[ib][:], op=ALU.mult), reads=[r_cv[ib], r_cg[ib]], writes=[r_cv[ib]])
                        op("pool", lambda e, ib=ib: e.tensor_tensor(out=actb[ib][:], in0=gz[ib][:], in1=cv[ib][:], op=ALU.mult), reads=[r_gz[ib], r_cv[ib]], writes=[r_act[ib]])
                        for tt in range(2):
                            for hh in range(2):
                                pa_ = tt * 2 + hh
                                op("pe", lambda e, tt=tt, hh=hh, pa_=pa_, c=c, ib=ib: e.matmul(PB[pa_][:], lhsT=actb[ib][:, tt * 128:(tt + 1) * 128], rhs=wD[:, c, hh * 512:(hh + 1) * 512],
                                                                                               start=(c == 0), stop=(c == NFC - 1)),
                                   reads=[r_act[ib], r_wU], writes=[rPB[pa_]])
                    for tt in range(2):
                        tile_i = sg * 2 + tt
                        sl = tile_i % 4
                        ib = tile_i % 2
                        ln_epilogue((tt * 2, tt * 2 + 1), (rPB[tt * 2], rPB[tt * 2 + 1]), x1t[sl][:], r_x1t[sl], 1, lnB[:, 0, :], lnB[:, 1, :], r_lnB,
                                    yt[ib], r_yt[ib], st6[ib], r_st6[ib])
                        dma("sp", ds_o[ib], out_d[s, tile_i * 128:(tile_i + 1) * 128, :], yt[ib][:], reads=[r_yt[ib]])
            T.barrier()
        print("build: instructions=%d waits=%d" % (T.n_ins, T.n_wait))
    return nc


def _consts():
    ident = np.eye(128, dtype=np.float32)
    i = np.arange(64)
    maskf = (i[:, None] <= i[None, :]).astype(np.uint8)
    maskb = (i[:, None] >= i[None, :]).astype(np.uint8)
    q = np.arange(128)[:, None]
    cc = np.arange(128)[None, :]
    NEG = -30000.0
    am = np.zeros((128, 384), np.float32)
    am[:, 0:128] = np.where(cc >= q, 0.0, NEG)
    am[:, 256:384] = np.where(cc <= q, 0.0, NEG)
    half = 32
    inv = (np.float32(10000.0) ** (-np.arange(half, dtype=np.float32) / np.float32(half))).astype(np.float32)
    p = np.arange(128)
    rope = np.stack([inv[p % 32], np.where((p % 64) < 32, -1.0, 1.0).astype(np.float32)], axis=1).astype(np.float32)
    return dict(k_ident=ident, k_maskf=maskf, k_maskb=maskb, k_amask=am, k_rope=rope)


def make_in_maps(inputs, n_cores, nseq):
    f = lambda a: np.ascontiguousarray(np.asarray(a))
    shared = dict(
        w_ada=f(inputs["w_ada"][0]), b_ada=f(inputs["b_ada"][0:1]), w_in=f(inputs["w_in"][0]),
        rec_lower_bound=f(inputs["rec_lower_bound"]), rec_norm_g=f(inputs["rec_norm_g"][0:1]),
        w_rec_branch=f(inputs["w_rec_branch"][0]), attn_sink=f(inputs["attn_sink"][0:1]),
        w_attn_branch=f(inputs["w_attn_branch"][0]), w_out=f(inputs["w_out"][0]),
        ln1_g=f(inputs["ln1_g"][0:1]), ln1_b=f(inputs["ln1_b"][0:1]), w_up=f(inputs["w_up"][0]),
        conv_w=f(inputs["conv_w"][0]), conv_b=f(inputs["conv_b"][0:1]), w_down=f(inputs["w_down"][0]),
        ln2_g=f(inputs["ln2_g"][0:1]), ln2_b=f(inputs["ln2_b"][0:1]))
    shared.update(_consts())
    maps = []
    for i in range(n_cores):
        m = dict(shared)
        m["x"] = f(inputs["x"][i * nseq:(i + 1) * nseq])
        m["c"] = f(inputs["c"][i * nseq:(i + 1) * nseq])
        m["positions"] = f(inputs["positions"][i * nseq:(i + 1) * nseq]).astype(np.int32)
        maps.append(m)
    return maps


def kernel(**inputs):
    B, S, _ = inputs["x"].shape
    nseq = B // N_CORES
    nc = build_nc(S=S, NSEQ=nseq)
    in_maps = make_in_maps(inputs, N_CORES, nseq)
    res = run_bass_kernel_spmd(nc, in_maps, core_ids=list(range(N_CORES)))
    return np.concatenate([np.asarray(r["out"]) for r in res.results], axis=0).astype(np.float32)
```

```python
import numpy as np
from contextlib import ExitStack
import concourse.bass as bass
import concourse.mybir as mybir
from concourse.bass_utils import run_bass_kernel_spmd

F32 = mybir.dt.float32
BF16 = mybir.dt.bfloat16
I32 = mybir.dt.int32
U8 = mybir.dt.uint8
AF = mybir.ActivationFunctionType
ALU = mybir.AluOpType
AX = mybir.AxisListType

D = 1024
NK = 8
DIN = 5376
DFF = 2816
NFC = DFF // 128
ALPHA = 2.0 ** 0.25
LN_EPS = 1e-5
RMS_EPS = 1e-6
N_CORES = 8
O_RQ, O_RFF, O_RFB, O_RI, O_RG, O_AQ, O_AK, O_AV, O_GR, O_GA = 0, 512, 1024, 1536, 2048, 2560, 3072, 3200, 3328, 4352


class Res:
    __slots__ = ("w", "rd")

    def __init__(self):
        self.w = None
        self.rd = []


class Tracker:
    def __init__(self, nc, stack):
        self.nc = nc
        self.stack = stack
        self.eng = {}
        self.sems = {}
        for name, h in (("pe", nc.tensor), ("act", nc.scalar), ("dve", nc.vector),
                        ("pool", nc.gpsimd), ("sp", nc.sync)):
            sem = stack.enter_context(nc.semaphore("sem_" + name))
            self.eng[name] = {"h": h, "sem": sem, "cnt": 0, "seen": {}, "name": name}
            self.sems[id(sem)] = sem
        self.dma_sems = []
        self.n_wait = 0
        self.n_ins = 0

    def new_dma_sem(self, name):
        sem = self.stack.enter_context(self.nc.semaphore(name))
        d = {"sem": sem, "cnt": 0}
        self.sems[id(sem)] = sem
        self.dma_sems.append(d)
        return d

    def _wait_deps(self, E, reads, writes):
        deps = {}

        def add(tok):
            if tok is not None and deps.get(tok[0], 0) < tok[1]:
                deps[tok[0]] = tok[1]

        for r in reads:
            add(r.w)
        for w in writes:
            add(w.w)
            for t in w.rd:
                add(t)
        own = id(E["sem"])
        for k, v in deps.items():
            if E["name"] == "pe" and k == own:
                continue
            if E["seen"].get(k, 0) >= v:
                continue
            E["h"].wait_ge(self.sems[k], v)
            E["seen"][k] = v
            self.n_wait += 1

    def _commit(self, tok, reads, writes):
        for r in reads:
            r.rd.append(tok)
            if len(r.rd) > 48:
                best = {}
                for k, v in r.rd:
                    if best.get(k, 0) < v:
                        best[k] = v
                r.rd = list(best.items())
        for w in writes:
            w.w = tok
            w.rd = []
        self.n_ins += 1

    def op(self, eng, fn, reads=(), writes=()):
        E = self.eng[eng]
        self._wait_deps(E, reads, writes)
        ins = fn(E["h"])
        E["cnt"] += 1
        ins.then_inc(E["sem"], 1)
        self._commit((id(E["sem"]), E["cnt"]), reads, writes)
        return ins

    def dma(self, q, dsem, out, in_, reads=(), writes=(), **kw):
        E = self.eng[q]
        self._wait_deps(E, reads, writes)
        ins = E["h"].dma_start(out=out, in_=in_, **kw)
        dsem["cnt"] += 16
        ins.then_inc(dsem["sem"], 16)
        self._commit((id(dsem["sem"]), dsem["cnt"]), reads, writes)
        return ins

    def barrier(self):
        targets = [(id(e["sem"]), e["cnt"]) for e in self.eng.values() if e["cnt"] > 0]
        targets += [(id(d["sem"]), d["cnt"]) for d in self.dma_sems if d["cnt"] > 0]
        for E in self.eng.values():
            for k, v in targets:
                if E["seen"].get(k, 0) >= v:
                    continue
                E["h"].wait_ge(self.sems[k], v)
                E["seen"][k] = v
                self.n_wait += 1


class Rot:
    def __init__(self, items):
        self.items = items
        self.i = 0

    def next(self):
        it = self.items[self.i % len(self.items)]
        self.i += 1
        return it


def build_nc(S=2048, NSEQ=2, dbg=False):
    NT, NG, NCH = S // 128, S // 512, S // 64
    nc = bass.Bass("TRN2", target_bir_lowering=False)

    def din(name, shape, dt=F32):
        return nc.dram_tensor(name, list(shape), dt, kind="ExternalInput").ap()

    x_d = din("x", [NSEQ, S, D])
    c_d = din("c", [NSEQ, D])
    pos_d = din("positions", [NSEQ, S], I32)
    w_ada_d = din("w_ada", [D, 6 * D])
    b_ada_d = din("b_ada", [1, 6 * D])
    w_in_d = din("w_in", [D, DIN])
    lbr_d = din("rec_lower_bound", [2, 2, 512])
    ng_d = din("rec_norm_g", [1, 128])
    w_rec_d = din("w_rec_branch", [512, D])
    sink_d = din("attn_sink", [1, 8])
    w_att_d = din("w_attn_branch", [512, D])
    w_out_d = din("w_out", [D, D])
    ln1g_d = din("ln1_g", [1, D]); ln1b_d = din("ln1_b", [1, D])
    w_up_d = din("w_up", [D, 2 * DFF])
    cw_d = din("conv_w", [3, 2 * DFF])
    cb_d = din("conv_b", [1, 2 * DFF])
    w_dn_d = din("w_down", [DFF, D])
    ln2g_d = din("ln2_g", [1, D]); ln2b_d = din("ln2_b", [1, D])
    identf_d = din("k_ident", [128, 128])
    mskf_d = din("k_maskf", [64, 64], U8)
    mskb_d = din("k_maskb", [64, 64], U8)
    amask_d = din("k_amask", [128, 384])
    rope_d = din("k_rope", [128, 2])
    out_d = nc.dram_tensor("out", [NSEQ, S, D], F32, kind="ExternalOutput").ap()
    x1_d = nc.dram_tensor("x1_scr", [NSEQ, S, D], F32).ap()
    u2T_d = nc.dram_tensor("u2T_scr", [NSEQ, D, S + 2], BF16).ap()
    dbg_d = {}
    if dbg:
        dbg_d["mods"] = nc.dram_tensor("dbg_mods", [128, 4, 8], F32, kind="ExternalOutput").ap()
        dbg_d["QT"] = nc.dram_tensor("dbg_QT", [2, 128, 2, S], BF16, kind="ExternalOutput").ap()
        dbg_d["KT"] = nc.dram_tensor("dbg_KT", [2, 128, 2, S], BF16, kind="ExternalOutput").ap()
        dbg_d["V"] = nc.dram_tensor("dbg_V", [64, NCH, 256], BF16, kind="ExternalOutput").ap()
        dbg_d["oT"] = nc.dram_tensor("dbg_oT", [128, 4, S], F32, kind="ExternalOutput").ap()
        dbg_d["kT"] = nc.dram_tensor("dbg_kT", [128, S], BF16, kind="ExternalOutput").ap()
        dbg_d["x1"] = x1_d

    w_in_k = w_in_d.rearrange("(j p) n -> p j n", p=128)
    w_ada_k = w_ada_d.rearrange("(j p) n -> p j n", p=128)
    w_up_k = w_up_d.rearrange("(j p) n -> p j n", p=128)
    w_out_k = w_out_d.rearrange("(j p) n -> p j n", p=128)
    w_rec_k = w_rec_d.rearrange("(j p) n -> p j n", p=128)
    w_att_k = w_att_d.rearrange("(j p) n -> p j n", p=128)
    w_dn_k = w_dn_d.rearrange("(j p) n -> p j n", p=128)

    with ExitStack() as top:
        T = Tracker(nc, top)
        op, dma = T.op, T.dma

        uniq = [0]

        def SB(st, name, shape, dt):
            uniq[0] += 1
            return st.enter_context(nc.sbuf_tensor("%s_%d" % (name, uniq[0]), list(shape), dt))

        PB = [top.enter_context(nc.psum_tensor("pb%d" % i, [128, 512], F32)) for i in range(8)]
        rPB = [Res() for _ in range(8)]
        ds_const = T.new_dma_sem("ds_const")
        ds_w = T.new_dma_sem("ds_w")
        ds_x = [T.new_dma_sem("ds_x%d" % i) for i in range(4)]
        ds_o = [T.new_dma_sem("ds_o%d" % i) for i in range(4)]
        ds_misc = [T.new_dma_sem("ds_m%d" % i) for i in range(4)]
        ds_dbg = T.new_dma_sem("ds_dbg")

        identf = SB(top, "identf", [128, 128], F32); r_identf = Res()
        identb = SB(top, "identb", [128, 128], BF16); r_identb = Res()
        onesf = SB(top, "onesf", [128, 128], F32); r_onesf = Res()
        maskf = SB(top, "maskf", [64, 64], U8); maskb = SB(top, "maskb", [64, 64], U8); r_mask = Res()
        zer = SB(top, "zer", [128, 128], F32); r_zer = Res()
        rmask = SB(top, "rmask", [128, 512], F32); r_rmask = Res()
        lbt = SB(top, "lbt", [128, 3, 2, 4], F32); r_lbt = Res()
        lraw = SB(top, "lraw", [128, 2, 2, 4], F32); r_lraw = Res()
        ngc = SB(top, "ngc", [128, 1], F32); r_ngc = Res()
        sinkb = SB(top, "sinkb", [128, 2, 8], F32); r_sink = Res()
        rope = SB(top, "rope", [128, 2], F32); r_rope = Res()
        epsc = SB(top, "epsc", [128, 2], F32); r_eps = Res()
        modsT = SB(top, "modsT", [128, 4, 8], F32); r_modsT = Res()
        gaB = SB(top, "gaB", [128, 2, D], F32); r_gaB = [Res(), Res()]
        amask = SB(top, "amask", [128, 384], F32); r_amask = Res()

        dma("sp", ds_const, identf[:], identf_d, writes=[r_identf])
        dma("sp", ds_const, maskf[:], mskf_d, writes=[r_mask])
        dma("sp", ds_const, maskb[:], mskb_d, writes=[r_mask])
        dma("sp", ds_const, rope[:], rope_d, writes=[r_rope])
        dma("sp", ds_const, amask[:], amask_d, writes=[r_amask])
        dma("sp", ds_const, ngc[:], ng_d.rearrange("o p -> p o"), writes=[r_ngc], allow_slow_non_contiguous=True)
        dma("sp", ds_const, sinkb[:, 0, :], sink_d[0:1, :].broadcast_to([128, 8]), writes=[r_sink])
        dma("sp", ds_const, lraw[:].rearrange("p a b h -> p (a b) h"),
            lbr_d.rearrange("a b (h p) -> p (a b) h", p=128), writes=[r_lraw], allow_slow_non_contiguous=True)
        T.barrier()
        op("dve", lambda e: e.tensor_copy(out=identb[:], in_=identf[:]), reads=[r_identf], writes=[r_identb])
        op("pool", lambda e: e.memset(onesf[:], 1.0), writes=[r_onesf])
        op("pool", lambda e: e.memset(zer[:], 0.0), writes=[r_zer])
        op("pool", lambda e: e.memset(rmask[:], 1.0), writes=[r_rmask])
        op("pool", lambda e: e.memset(rmask[:].rearrange("p (c t) -> p c t", t=64)[:, :, 0:1], 0.0), writes=[r_rmask])
        op("pool", lambda e: e.memset(epsc[:, 0:1], LN_EPS), writes=[r_eps])
        op("pool", lambda e: e.memset(epsc[:, 1:2], RMS_EPS), writes=[r_eps])
        op("dve", lambda e: e.tensor_scalar(out=sinkb[:, 1, :], in0=sinkb[:, 0, :], scalar1=-1.0, scalar2=None, op0=ALU.mult),
           reads=[r_sink], writes=[r_sink])
        op("dve", lambda e: e.tensor_tensor(out=lbt[:, 2, :, :], in0=lraw[:, :, 0, :], in1=lraw[:, :, 1, :], op=ALU.subtract),
           reads=[r_lraw], writes=[r_lbt])
        op("act", lambda e: e.activation(out=lbt[:, 0, :, :], in_=lbt[:, 2, :, :], func=AF.Sigmoid), reads=[r_lbt], writes=[r_lbt])
        op("act", lambda e: e.activation(out=lbt[:, 1, :, :], in_=lbt[:, 2, :, :], func=AF.Sigmoid, scale=-1.0), reads=[r_lbt], writes=[r_lbt])
        op("dve", lambda e: e.tensor_scalar(out=lbt[:, 2, :, :], in0=lbt[:, 1, :, :], scalar1=-1.0, scalar2=None, op0=ALU.mult),
           reads=[r_lbt], writes=[r_lbt])

        def make_uT(xt, r_xt, dst, r_dst, scol, bcol, banks=(0, 1)):
            for hb in range(2):
                b = banks[hb]
                for i in range(4):
                    k = hb * 4 + i
                    op("pe", lambda e, k=k, i=i, b=b: e.transpose(out=PB[b][:, i * 128:(i + 1) * 128], in_=xt[:, k * 128:(k + 1) * 128],
                                                                   identity=identf[:]),
                       reads=[r_xt, r_identf], writes=[rPB[b]])
                for i in range(4):
                    k = hb * 4 + i
                    if i % 2 == 0:
                        op("act", lambda e, k=k, i=i, b=b: e.activation(out=dst[:, k, :], in_=PB[b][:, i * 128:(i + 1) * 128], func=AF.Identity,
                                                                         scale=modsT[:, scol, k:k + 1], bias=modsT[:, bcol, k:k + 1]),
                           reads=[rPB[b], r_modsT], writes=[r_dst])
                    else:
                        op("dve", lambda e, k=k, i=i, b=b: e.tensor_scalar(out=dst[:, k, :], in0=PB[b][:, i * 128:(i + 1) * 128],
                                                                            scalar1=modsT[:, scol, k:k + 1], scalar2=modsT[:, bcol, k:k + 1],
                                                                            op0=ALU.mult, op1=ALU.add),
                           reads=[rPB[b], r_modsT], writes=[r_dst])

        def ln_part1(pz, r_pz, which, y, r_y):
            for h in range(2):
                op("dve", lambda e, h=h: e.tensor_tensor(out=y[:, h * 512:(h + 1) * 512], in0=PB[pz[h]][:], in1=gaB[:, which, h * 512:(h + 1) * 512], op=ALU.mult),
                   reads=[r_pz[h], r_gaB[which]], writes=[r_y])

        def ln_part2(xres, r_xres, gB, bB, r_gb, y, r_y, st6, r_st):
            op("dve", lambda e: e.scalar_tensor_tensor(out=y[:], in0=xres, scalar=ALPHA, in1=y[:], op0=ALU.mult, op1=ALU.add),
               reads=[r_xres, r_y], writes=[r_y])
            for h in range(2):
                op("dve", lambda e, h=h: e.bn_stats(out=st6[:, h * 6:(h + 1) * 6], in_=y[:, h * 512:(h + 1) * 512]), reads=[r_y], writes=[r_st])
            op("dve", lambda e: e.bn_aggr(out=st6[:, 12:14], in_=st6[:, 0:12]), reads=[r_st], writes=[r_st])
            op("act", lambda e: e.activation(out=st6[:, 14:15], in_=st6[:, 13:14], func=AF.Ln, bias=epsc[:, 0:1], scale=1.0), reads=[r_st, r_eps], writes=[r_st])
            op("act", lambda e: e.activation(out=st6[:, 15:16], in_=st6[:, 14:15], func=AF.Exp, scale=-0.5), reads=[r_st], writes=[r_st])
            op("dve", lambda e: e.tensor_scalar(out=st6[:, 16:17], in0=st6[:, 12:13], scalar1=st6[:, 15:16], scalar2=-1.0, op0=ALU.mult, op1=ALU.mult),
               reads=[r_st], writes=[r_st])
            op("act", lambda e: e.activation(out=y[:], in_=y[:], func=AF.Identity, scale=st6[:, 15:16], bias=st6[:, 16:17]),
               reads=[r_y, r_st], writes=[r_y])
            op("dve", lambda e: e.tensor_tensor(out=y[:], in0=y[:], in1=gB, op=ALU.mult), reads=[r_y, r_gb], writes=[r_y])
            op("dve", lambda e: e.tensor_tensor(out=y[:], in0=y[:], in1=bB, op=ALU.add), reads=[r_y, r_gb], writes=[r_y])

        def ln_epilogue(pz, r_pz, xres, r_xres, which, gB, bB, r_gb, y, r_y, st6, r_st):
            ln_part1(pz, r_pz, which, y, r_y)
            ln_part2(xres, r_xres, gB, bB, r_gb, y, r_y, st6, r_st)

        for s in range(NSEQ):
            with ExitStack() as ph:
                cT = SB(ph, "cT", [128, 8], F32); r_cT = Res()
                cbt = SB(ph, "cbt", [128, 8, 128], F32); r_cbt = Res()
                wst = [SB(ph, "wst%d" % i, [128, 8, 512], F32) for i in range(2)]; r_wst = [Res(), Res()]
                bst = [SB(ph, "bst%d" % i, [1, 512], F32) for i in range(2)]; r_bst = [Res(), Res()]
                mtmp = SB(ph, "mtmp", [128, 512], F32); r_mtmp = Res()
                dma("sp", ds_misc[0], cT[:], c_d[s].rearrange("(j p) -> p j", p=128), writes=[r_cT], allow_slow_non_contiguous=True)
                op("act", lambda e: e.activation(out=cT[:], in_=cT[:], func=AF.Silu), reads=[r_cT], writes=[r_cT])
                op("dve", lambda e: e.tensor_copy(out=cbt[:], in_=cT[:].unsqueeze(2).broadcast_to([128, 8, 128])), reads=[r_cT], writes=[r_cbt])
                for cg in range(12):
                    b2 = cg % 2
                    dma("sp", ds_misc[1 + b2], wst[b2][:], w_ada_k[:, :, cg * 512:(cg + 1) * 512], writes=[r_wst[b2]])
                    dma("sp", ds_o[b2], bst[b2][:], b_ada_d[0:1, cg * 512:(cg + 1) * 512], writes=[r_bst[b2]])
                    pb = 2 + b2
                    for k in range(8):
                        op("pe", lambda e, k=k: e.matmul(PB[pb][:], lhsT=cbt[:, k, :], rhs=wst[b2][:, k, :], start=(k == 0), stop=False),
                           reads=[r_cbt, r_wst[b2]], writes=[rPB[pb]])
                    op("pe", lambda e: e.matmul(PB[pb][:], lhsT=onesf[0:1, :], rhs=bst[b2][0:1, :], start=False, stop=True),
                       reads=[r_onesf, r_bst[b2]], writes=[rPB[pb]])
                    which, half = cg // 2, cg % 2
                    if which in (2, 5):
                        gi = 0 if which == 2 else 1
                        op("dve", lambda e: e.tensor_scalar(out=gaB[:, gi, half * 512:(half + 1) * 512], in0=PB[pb][:], scalar1=1.0, scalar2=None, op0=ALU.add),
                           reads=[rPB[pb]], writes=[r_gaB[gi]])
                    else:
                        widx = {0: 0, 1: 1, 3: 2, 4: 3}[which]
                        addc = 1.0 if which in (1, 4) else 0.0
                        op("dve", lambda e: e.tensor_scalar(out=mtmp[:], in0=PB[pb][:], scalar1=addc, scalar2=None, op0=ALU.add),
                           reads=[rPB[pb]], writes=[r_mtmp])
                        for i in range(4):
                            op("pe", lambda e, i=i: e.transpose(out=PB[4][:, i * 128:(i + 1) * 128], in_=mtmp[:, i * 128:(i + 1) * 128], identity=identf[:]),
                               reads=[r_mtmp, r_identf], writes=[rPB[4]])
                        op("dve", lambda e: e.tensor_copy(out=modsT[:, widx, half * 4:(half + 1) * 4],
                                                          in_=PB[4][:].rearrange("p (i m) -> p i m", m=128)[:, :, 0]),
                           reads=[rPB[4]], writes=[r_modsT])
                if dbg and s == 0:
                    dma("sp", ds_dbg, dbg_d["mods"], modsT[:], reads=[r_modsT])
            T.barrier()

            with ExitStack() as pab:
                oT = SB(pab, "oT", [128, 4, S], F32)
                r_oT = [[Res() for _ in range(NCH)] for _ in range(4)]
                kTt = SB(pab, "kTt", [128, S], BF16); r_kT = [Res() for _ in range(NG)]
                vtk = SB(pab, "vtk", [128, NT, 128], BF16); r_v = [Res() for _ in range(NT)]
                cosT = SB(pab, "cosT", [128, S], F32); sinT = SB(pab, "sinT", [128, S], F32); r_cs = Res()
                with ExitStack() as pr:
                    posi = SB(pr, "posi", [128, S], I32); r_posi = Res()
                    ang = SB(pr, "ang", [128, S], F32); r_ang = Res()
                    kq = SB(pr, "kq", [128, S], F32); r_kq = Res()
                    ki = SB(pr, "ki", [128, S], I32); r_ki = Res()
                    dma("sp", ds_misc[0], posi[:], pos_d[s:s + 1, :].broadcast_to([128, S]), writes=[r_posi])
                    op("dve", lambda e: e.tensor_copy(out=ang[:], in_=posi[:]), reads=[r_posi], writes=[r_ang])
                    op("dve", lambda e: e.tensor_scalar(out=ang[:], in0=ang[:], scalar1=rope[:, 0:1], scalar2=None, op0=ALU.mult),
                       reads=[r_ang, r_rope], writes=[r_ang])
                    TWO_PI = 6.283185307179586
                    C1 = 6.28125
                    C2 = TWO_PI - C1
                    for (dst, shift) in ((sinT, 0.0), (cosT, np.pi / 2)):
                        op("dve", lambda e, shift=shift: e.tensor_scalar(out=kq[:], in0=ang[:], scalar1=float(shift), scalar2=1.0 / TWO_PI, op0=ALU.add, op1=ALU.mult),
                           reads=[r_ang], writes=[r_kq])
                        op("dve", lambda e: e.tensor_copy(out=ki[:], in_=kq[:]), reads=[r_kq], writes=[r_ki])
                        op("dve", lambda e: e.tensor_copy(out=kq[:], in_=ki[:]), reads=[r_ki], writes=[r_kq])
                        op("dve", lambda e, shift=shift, dst=dst: e.tensor_scalar(out=dst[:], in0=ang[:], scalar1=float(shift), scalar2=None, op0=ALU.add),
                           reads=[r_ang], writes=[r_cs])
                        op("dve", lambda e, dst=dst: e.scalar_tensor_tensor(out=dst[:], in0=kq[:], scalar=-C1, in1=dst[:], op0=ALU.mult, op1=ALU.add),
                           reads=[r_kq, r_cs], writes=[r_cs])
                        op("dve", lambda e, dst=dst: e.scalar_tensor_tensor(out=dst[:], in0=kq[:], scalar=-C2, in1=dst[:], op0=ALU.mult, op1=ALU.add),
                           reads=[r_kq, r_cs], writes=[r_cs])
                        op("dve", lambda e, dst=dst: e.tensor_scalar(out=kq[:], in0=dst[:], scalar1=float(np.pi), scalar2=-TWO_PI, op0=ALU.is_gt, op1=ALU.mult),
                           reads=[r_cs], writes=[r_kq])
                        op("dve", lambda e, dst=dst: e.tensor_tensor(out=dst[:], in0=dst[:], in1=kq[:], op=ALU.add), reads=[r_cs, r_kq], writes=[r_cs])
                        op("dve", lambda e, dst=dst: e.tensor_scalar(out=kq[:], in0=dst[:], scalar1=float(-np.pi), scalar2=TWO_PI, op0=ALU.is_lt, op1=ALU.mult),
                           reads=[r_cs], writes=[r_kq])
                        op("dve", lambda e, dst=dst: e.tensor_tensor(out=dst[:], in0=dst[:], in1=kq[:], op=ALU.add), reads=[r_cs, r_kq], writes=[r_cs])
                        op("dve", lambda e, dst=dst: e.tensor_scalar(out=dst[:], in0=dst[:], scalar1=3.1415925, scalar2=-3.1415925, op0=ALU.min, op1=ALU.max),
                           reads=[r_cs], writes=[r_cs])
                        op("act", lambda e, dst=dst: e.activation(out=dst[:], in_=dst[:], func=AF.Sin), reads=[r_cs], writes=[r_cs])
                    op("dve", lambda e: e.tensor_scalar(out=sinT[:], in0=sinT[:], scalar1=rope[:, 1:2], scalar2=None, op0=ALU.mult),
                       reads=[r_cs, r_rope], writes=[r_cs])
                T.barrier()

                for hp in range(2):
                    with ExitStack() as pa:
                        NWA = 1024 + (384 if hp == 0 else 0)
                        wA = SB(pa, "wA", [128, 8, NWA], BF16); r_wA = Res()
                        QT = [SB(pa, "QT%d" % d, [128, 2, S], BF16) for d in range(2)]
                        KT = [SB(pa, "KT%d" % d, [128, 2, S], BF16) for d in range(2)]
                        r_QK = [[[Res() for _ in range(NG)] for _ in range(2)] for _ in range(2)]
                        Vt = SB(pa, "Vt", [64, NCH, 256], BF16); r_V = [Res() for _ in range(NG)]
                        Bref = SB(pa, "Bref", [128, 2, 2, NCH], F32); Bend = SB(pa, "Bend", [128, 2, 2, NCH], F32); r_B = Res()
                        Gx = SB(pa, "Gx", [128, 2, 2, NCH], F32); r_G = Res()
                        for seg, base in enumerate((O_RQ, O_RFF, O_RFB, O_RI)):
                            dma("pool", ds_w, wA[:, :, seg * 256:(seg + 1) * 256], w_in_k[:, :, base + hp * 256: base + hp * 256 + 256], writes=[r_wA])
                        if hp == 0:
                            dma("pool", ds_w, wA[:, :, 1024:1152], w_in_k[:, :, O_AK:O_AK + 128], writes=[r_wA])
                            for kv in range(2):
                                dma("pool", ds_w, wA[:, :, 1152 + kv * 64:1152 + kv * 64 + 32], w_in_k[:, :, O_AK + kv * 64 + 32:O_AK + kv * 64 + 64], writes=[r_wA])
                                dma("pool", ds_w, wA[:, :, 1152 + kv * 64 + 32:1152 + kv * 64 + 64], w_in_k[:, :, O_AK + kv * 64:O_AK + kv * 64 + 32], writes=[r_wA])
                            dma("pool", ds_w, wA[:, :, 1280:1408], w_in_k[:, :, O_AV:O_AV + 128], writes=[r_wA])
                        with ExitStack() as pa2:
                            xs = [SB(pa2, "xs%d" % i, [128, D], F32) for i in range(2)]; r_xs = [Res() for _ in range(2)]
                            uT = [SB(pa2, "uT%d" % i, [128, 8, 512], BF16) for i in range(1)]; r_uT = [Res()]
                            qs = [SB(pa2, "qs%d" % i, [128, 512], F32) for i in range(2)]; r_qs = [Res(), Res()]
                            NTMP = 2
                            tnames = ("sig", "lg", "Bf", "Eq", "Ek")
                            tmps = [{n: SB(pa2, "t_%s%d" % (n, i), [128, 512], F32) for n in tnames} for i in range(NTMP)]
                            r_tmps = [{n: Res() for n in tnames} for i in range(NTMP)]
                            for i in range(NTMP):
                                tmps[i]["kk"] = tmps[i]["sig"]; r_tmps[i]["kk"] = r_tmps[i]["sig"]
                                tmps[i]["dd"] = tmps[i]["lg"]; r_tmps[i]["dd"] = r_tmps[i]["lg"]
                            tctr = 0
                            bank_rot = Rot([2, 3, 4, 5, 6, 7])
                            for g in range(NG):
                                ub = 0
                                for t in range(4):
                                    tile_i = g * 4 + t
                                    sl = tile_i % 2
                                    dma("sp", ds_x[sl], xs[sl][:], x_d[s, tile_i * 128:(tile_i + 1) * 128, :], writes=[r_xs[sl]])
                                    make_uT(xs[sl], r_xs[sl], uT[ub][:, :, t * 128:(t + 1) * 128], r_uT[ub], 1, 0)
                                gs = slice(g * 512, (g + 1) * 512)
                                for cp in range(4):
                                    pb = bank_rot.next()
                                    for cc in range(2):
                                        ch = cp * 2 + cc
                                        for k in range(8):
                                            op("pe", lambda e, k=k, cc=cc, ch=ch, pb=pb: e.matmul(PB[pb][0:64, cc * 256:(cc + 1) * 256], lhsT=uT[ub][:, k, ch * 64:(ch + 1) * 64],
                                                                                                   rhs=wA[:, k, 768:1024], start=(k == 0), stop=(k == 7)),
                                               reads=[r_uT[ub], r_wA], writes=[rPB[pb]])
                                    op("act", lambda e, pb=pb, cp=cp: e.activation(out=Vt[:, g * 8 + cp * 2:g * 8 + cp * 2 + 2, :],
                                                                                    in_=PB[pb][0:64, :].rearrange("p (c n) -> p c n", n=256), func=AF.Copy, scale=128.0 ** -0.5),
                                       reads=[rPB[pb]], writes=[r_V[g]])
                                for hl in range(2):
                                    h = hp * 2 + hl
                                    pb = bank_rot.next()
                                    for k in range(8):
                                        op("pe", lambda e, k=k, pb=pb, hl=hl: e.matmul(PB[pb][:], lhsT=wA[:, k, hl * 128:(hl + 1) * 128], rhs=uT[ub][:, k, :],
                                                                                         start=(k == 0), stop=(k == 7)), reads=[r_uT[ub], r_wA], writes=[rPB[pb]])
                                    op("act", lambda e, pb=pb, hl=hl: e.activation(out=qs[hl][:], in_=PB[pb][:], func=AF.Silu), reads=[rPB[pb]], writes=[r_qs[hl]])
                                    for d in range(2):
                                        tm = tmps[tctr % NTMP]; rt = r_tmps[tctr % NTMP]; tctr += 1
                                        pb = bank_rot.next()
                                        cb = 256 * (1 + d) + hl * 128
                                        for k in range(8):
                                            op("pe", lambda e, k=k, pb=pb, cb=cb: e.matmul(PB[pb][:], lhsT=wA[:, k, cb:cb + 128], rhs=uT[ub][:, k, :],
                                                                                             start=(k == 0), stop=(k == 7)), reads=[r_uT[ub], r_wA], writes=[rPB[pb]])
                                        lb_c = lbt[:, 0, d, h:h + 1]; om_c = lbt[:, 1, d, h:h + 1]; nom_c = lbt[:, 2, d, h:h + 1]
                                        op("act", lambda e, pb=pb, tm=tm: e.activation(out=tm["sig"][:], in_=PB[pb][:], func=AF.Sigmoid), reads=[rPB[pb]], writes=[rt["sig"]])
                                        op("act", lambda e, tm=tm: e.activation(out=tm["lg"][:], in_=tm["sig"][:], func=AF.Ln, scale=om_c, bias=lb_c),
                                           reads=[rt["sig"], r_lbt], writes=[rt["lg"]])
                                        op("dve", lambda e, tm=tm: e.tensor_scalar(out=tm["kk"][:], in0=tm["sig"][:], scalar1=nom_c, scalar2=om_c, op0=ALU.mult, op1=ALU.add),
                                           reads=[rt["sig"], r_lbt], writes=[rt["kk"]])
                                        op("dve", lambda e, tm=tm: e.tensor_tensor_scan(out=tm["Bf"][:], data0=rmask[:], data1=tm["lg"][:], initial=0.0, op0=ALU.mult, op1=ALU.add),
                                           reads=[r_rmask, rt["lg"]], writes=[rt["Bf"]])
                                        B3 = tm["Bf"][:].rearrange("p (c t) -> p c t", t=64)
                                        if d == 1:
                                            lg3 = tm["lg"][:].rearrange("p (c t) -> p c t", t=64)
                                            op("dve", lambda e, B3=B3, lg3=lg3: e.tensor_tensor(out=lg3, in0=lg3, in1=B3, op=ALU.subtract), reads=[rt["lg"], rt["Bf"]], writes=[rt["lg"]])
                                            op("dve", lambda e, B3=B3, lg3=lg3: e.tensor_tensor(out=B3, in0=lg3, in1=B3[:, :, 63:64].broadcast_to([128, 8, 64]), op=ALU.add),
                                               reads=[rt["lg"], rt["Bf"]], writes=[rt["Bf"]])
                                            iref, iend = 32, 0
                                        else:
                                            iref, iend = 31, 63
                                        op("pool", lambda e, B3=B3, iref=iref: e.tensor_copy(out=Bref[:, d, hl, g * 8:(g + 1) * 8], in_=B3[:, :, iref]), reads=[rt["Bf"]], writes=[r_B])
                                        op("pool", lambda e, B3=B3, iend=iend: e.tensor_copy(out=Bend[:, d, hl, g * 8:(g + 1) * 8], in_=B3[:, :, iend]), reads=[rt["Bf"]], writes=[r_B])
                                        op("dve", lambda e, B3=B3, tm=tm, iref=iref: e.tensor_tensor(out=tm["dd"][:].rearrange("p (c t) -> p c t", t=64), in0=B3,
                                                                                                       in1=B3[:, :, iref:iref + 1].broadcast_to([128, 8, 64]), op=ALU.subtract),
                                           reads=[rt["Bf"]], writes=[rt["dd"]])
                                        op("act", lambda e, tm=tm: e.activation(out=tm["Eq"][:], in_=tm["dd"][:], func=AF.Exp), reads=[rt["dd"]], writes=[rt["Eq"]])
                                        op("act", lambda e, tm=tm: e.activation(out=tm["Ek"][:], in_=tm["dd"][:], func=AF.Exp, scale=-1.0), reads=[rt["dd"]], writes=[rt["Ek"]])
                                        op("dve", lambda e, tm=tm, d=d, hl=hl: e.tensor_tensor(out=KT[d][:, hl, gs], in0=tm["kk"][:], in1=tm["Ek"][:], op=ALU.mult),
                                           reads=[rt["kk"], rt["Ek"]], writes=[r_QK[d][hl][g]])
                                        op("dve", lambda e, tm=tm, d=d, hl=hl: e.tensor_tensor(out=QT[d][:, hl, gs], in0=qs[hl][:], in1=tm["Eq"][:], op=ALU.mult),
                                           reads=[r_qs[hl], rt["Eq"]], writes=[r_QK[d][hl][g]])
                                if hp == 0:
                                    pb1 = bank_rot.next()
                                    pb2 = bank_rot.next()
                                    for (pbx, cb) in ((pb1, 1024), (pb2, 1152)):
                                        for k in range(8):
                                            op("pe", lambda e, k=k, pbx=pbx, cb=cb: e.matmul(PB[pbx][:], lhsT=wA[:, k, cb:cb + 128], rhs=uT[ub][:, k, :], start=(k == 0), stop=(k == 7)),
                                               reads=[r_uT[ub], r_wA], writes=[rPB[pbx]])
                                    tm = tmps[tctr % NTMP]; rt = r_tmps[tctr % NTMP]; tctr += 1
                                    op("dve", lambda e, tm=tm: e.tensor_tensor(out=tm["sig"][:], in0=PB[pb1][:], in1=cosT[:, gs], op=ALU.mult), reads=[rPB[pb1], r_cs], writes=[rt["sig"]])
                                    op("dve", lambda e, tm=tm: e.tensor_tensor(out=tm["lg"][:], in0=PB[pb2][:], in1=sinT[:, gs], op=ALU.mult), reads=[rPB[pb2], r_cs], writes=[rt["lg"]])
                                    op("dve", lambda e, tm=tm: e.tensor_tensor(out=kTt[:, gs], in0=tm["sig"][:], in1=tm["lg"][:], op=ALU.add), reads=[rt["sig"], rt["lg"]], writes=[r_kT[g]])
                                    for t in range(4):
                                        pb = bank_rot.next()
                                        for k in range(8):
                                            op("pe", lambda e, k=k, pb=pb, t=t: e.matmul(PB[pb][:, 0:128], lhsT=uT[ub][:, k, t * 128:(t + 1) * 128], rhs=wA[:, k, 1280:1408],
                                                                                          start=(k == 0), stop=(k == 7)), reads=[r_uT[ub], r_wA], writes=[rPB[pb]])
                                        op("act", lambda e, pb=pb, t=t: e.activation(out=vtk[:, g * 4 + t, :], in_=PB[pb][:, 0:128], func=AF.Copy), reads=[rPB[pb]], writes=[r_v[g * 4 + t]])
                            op("dve", lambda e: e.tensor_tensor(out=Gx[:], in0=Bend[:], in1=Bref[:], op=ALU.subtract), reads=[r_B], writes=[r_G])
                            if NCH > 1:
                                op("dve", lambda e: e.tensor_tensor(out=Gx[:, 0, :, 0:NCH - 1], in0=Gx[:, 0, :, 0:NCH - 1], in1=Bref[:, 0, :, 1:NCH], op=ALU.add), reads=[r_B, r_G], writes=[r_G])
                                op("dve", lambda e: e.tensor_tensor(out=Gx[:, 1, :, 1:NCH], in0=Gx[:, 1, :, 1:NCH], in1=Bref[:, 1, :, 0:NCH - 1], op=ALU.add), reads=[r_B, r_G], writes=[r_G])
                            op("act", lambda e: e.activation(out=Gx[:], in_=Gx[:], func=AF.Exp), reads=[r_G], writes=[r_G])
                            if dbg and s == 0 and hp == 0:
                                for d in range(2):
                                    dma("sp", ds_dbg, dbg_d["QT"][d], QT[d][:], reads=[r_QK[d][hl][g] for hl in range(2) for g in range(NG)])
                                    dma("sp", ds_dbg, dbg_d["KT"][d], KT[d][:], reads=[r_QK[d][hl][g] for hl in range(2) for g in range(NG)])
                                dma("sp", ds_dbg, dbg_d["V"], Vt[:], reads=r_V)
                                dma("sp", ds_dbg, dbg_d["kT"], kTt[:], reads=r_kT)
                        T.barrier()

                        with ExitStack() as pb_:
                            Sm32 = SB(pb_, "Sm32", [128, 4, 128], F32); Smb = SB(pb_, "Smb", [128, 4, 128], BF16); tmp32 = SB(pb_, "tmp32", [128, 4, 128], F32)
                            r_Sm32 = [Res() for _ in range(4)]; r_Smb = [Res() for _ in range(4)]; r_tmp32 = [Res() for _ in range(4)]
                            scm = SB(pb_, "scm", [64, 2, 4, 64], BF16); r_scm = [[Res() for _ in range(4)] for _ in range(2)]
                            ktok = SB(pb_, "ktok", [64, 2, 4, 128], BF16); r_ktok = [[Res() for _ in range(4)] for _ in range(2)]
                            rB = [[None] * 4 for _ in range(2)]
                            for _p in range(2):
                                for _c in range(4):
                                    _r = Res()
                                    rB[_p][_c] = {n: _r for n in ("sc", "kt", "o", "dl")}
                            op("pool", lambda e: e.memset(Sm32[:], 0.0), writes=r_Sm32)
                            op("pool", lambda e: e.memset(Smb[:], 0.0), writes=r_Smb)
                            allQK = lambda d, hl: [r_QK[d][hl][g] for g in range(NG)]

                            def chunk_of(j, d):
                                return j if d == 0 else NCH - 1 - j

                            def stage1(j):
                                par = j % 2
                                for ci in range(4):
                                    hl, d = ci // 2, ci % 2
                                    c = chunk_of(j, d); tk = slice(c * 64, (c + 1) * 64)
                                    bk = ci + 4 * par
                                    op("pe", lambda e, bk=bk, d=d, hl=hl, tk=tk: e.matmul(PB[bk][0:64, 0:64], lhsT=KT[d][:, hl, tk], rhs=QT[d][:, hl, tk], start=True, stop=True),
                                       reads=allQK(d, hl), writes=[rB[par][ci]["sc"]])
                                    op("pe", lambda e, bk=bk, d=d, hl=hl, tk=tk: e.transpose(out=PB[bk][:].bitcast(BF16)[0:64, 128:256], in_=KT[d][:, hl, tk], identity=identb[:]),
                                       reads=allQK(d, hl) + [r_identb], writes=[rB[par][ci]["kt"]])

                            def stage2(j):
                                par = j % 2
                                for ci in range(4):
                                    hl, d = ci // 2, ci % 2
                                    bk = ci + 4 * par
                                    mk = maskf if d == 0 else maskb
                                    op("pool", lambda e, ci=ci: e.memset(scm[:, par, ci, :], 0.0), writes=[r_scm[par][ci]])
                                    op("dve", lambda e, bk=bk, ci=ci, mk=mk: e.copy_predicated(out=scm[:, par, ci, :], mask=mk[:], data=PB[bk][0:64, 0:64]),
                                       reads=[rB[par][ci]["sc"], r_mask], writes=[r_scm[par][ci]])
                                    op("act", lambda e, bk=bk, ci=ci: e.activation(out=ktok[:, par, ci, :], in_=PB[bk][:].bitcast(BF16)[0:64, 128:256], func=AF.Copy),
                                       reads=[rB[par][ci]["kt"]], writes=[r_ktok[par][ci]])

                            def stage3(j):
                                par = j % 2
                                for ci in range(4):
                                    hl, d = ci // 2, ci % 2
                                    c = chunk_of(j, d); tk = slice(c * 64, (c + 1) * 64)
                                    bk = ci + 4 * par
                                    vv = Vt[:, c, hl * 128:(hl + 1) * 128]
                                    op("pe", lambda e, bk=bk, vv=vv, ci=ci: e.matmul(PB[bk][:, 128:192], lhsT=vv, rhs=scm[:, par, ci, :], start=True, stop=False),
                                       reads=[r_V[c // 8], r_scm[par][ci]], writes=[rB[par][ci]["o"]])
                                    op("pe", lambda e, bk=bk, ci=ci, d=d, hl=hl, tk=tk: e.matmul(PB[bk][:, 128:192], lhsT=Smb[:, ci, :], rhs=QT[d][:, hl, tk], start=False, stop=True),
                                       reads=[r_Smb[ci]] + allQK(d, hl), writes=[rB[par][ci]["o"]])
                                    op("pe", lambda e, bk=bk, vv=vv, ci=ci: e.matmul(PB[bk][:, 192:320], lhsT=ktok[:, par, ci, :], rhs=vv, start=True, stop=True),
                                       reads=[r_ktok[par][ci], r_V[c // 8]], writes=[rB[par][ci]["dl"]])

                            def stage4(j):
                                par = j % 2
                                for ci in range(4):
                                    hl, d = ci // 2, ci % 2
                                    h = hp * 2 + hl
                                    c = chunk_of(j, d); tk = slice(c * 64, (c + 1) * 64)
                                    bk = ci + 4 * par
                                    if j < NCH // 2:
                                        op("act", lambda e, bk=bk, h=h, tk=tk: e.activation(out=oT[:, h, tk], in_=PB[bk][:, 128:192], func=AF.Copy),
                                           reads=[rB[par][ci]["o"]], writes=[r_oT[h][c]])
                                    else:
                                        op("dve", lambda e, bk=bk, h=h, tk=tk: e.tensor_tensor(out=oT[:, h, tk], in0=PB[bk][:, 128:192], in1=oT[:, h, tk], op=ALU.add),
                                           reads=[rB[par][ci]["o"], r_oT[h][c]], writes=[r_oT[h][c]])
                                    if j < NCH - 1:
                                        gcol = Gx[:, d, hl, c:c + 1]
                                        op("dve", lambda e, bk=bk, ci=ci: e.tensor_tensor(out=tmp32[:, ci, :], in0=PB[bk][:, 192:320], in1=Sm32[:, ci, :], op=ALU.add),
                                           reads=[rB[par][ci]["dl"], r_Sm32[ci]], writes=[r_tmp32[ci]])
                                        op("act", lambda e, ci=ci, gcol=gcol: e.activation(out=Sm32[:, ci, :], in_=tmp32[:, ci, :], func=AF.Identity, scale=gcol),
                                           reads=[r_tmp32[ci], r_G], writes=[r_Sm32[ci]])
                                        op("pool", lambda e, ci=ci, gcol=gcol: e.tensor_scalar(out=Smb[:, ci, :], in0=tmp32[:, ci, :], scalar1=gcol, scalar2=None, op0=ALU.mult),
                                           reads=[r_tmp32[ci], r_G], writes=[r_Smb[ci]])

                            stage1(0)
                            stage2(0)
                            for j in range(NCH):
                                if j + 1 < NCH:
                                    stage1(j + 1)
                                stage3(j)
                                if j + 1 < NCH:
                                    stage2(j + 1)
                                stage4(j)
                        T.barrier()
                if dbg and s == 0:
                    dma("sp", ds_dbg, dbg_d["oT"], oT[:], reads=[r for rr in r_oT for r in rr])

                ogA = SB(pab, "ogA", [128, 4, S], BF16); atA = SB(pab, "atA", [128, 4, S], BF16)
                r_ogA = [Res() for _ in range(NG)]; r_atA = [Res() for _ in range(NT)]
                with ExitStack() as pc:
                    wC = SB(pc, "wC", [128, 8, 1536], BF16); r_wC = Res()
                    dma("pool", ds_w, wC[:, :, 0:512], w_in_k[:, :, O_RG:O_RG + 512], writes=[r_wC])
                    for m in range(4):
                        for hf in range(2):
                            hq = m + 4 * hf
                            cdst = 512 + m * 128 + hf * 64
                            dma("pool", ds_w, wC[:, :, cdst:cdst + 64], w_in_k[:, :, O_AQ + hq * 64:O_AQ + hq * 64 + 64], writes=[r_wC])
                            dma("pool", ds_w, wC[:, :, 512 + cdst:512 + cdst + 32], w_in_k[:, :, O_AQ + hq * 64 + 32:O_AQ + hq * 64 + 64], writes=[r_wC])
                            dma("pool", ds_w, wC[:, :, 512 + cdst + 32:512 + cdst + 64], w_in_k[:, :, O_AQ + hq * 64:O_AQ + hq * 64 + 32], writes=[r_wC])
                    xs = [SB(pc, "cxs%d" % i, [128, D], F32) for i in range(2)]; r_xs = [Res() for _ in range(2)]
                    uT = SB(pc, "cuT", [128, 8, 512], BF16); r_uT = Res()
                    qT = SB(pc, "qT", [128, 4, 512], BF16); r_qT = Res()
                    ta = [SB(pc, "cta%d" % i, [128, 512], F32) for i in range(4)]; r_ta = [Res() for _ in range(4)]
                    msc = [SB(pc, "msc%d" % i, [128, 384], F32) for i in range(2)]; r_msc = [Res(), Res()]
                    pex = [SB(pc, "pex%d" % i, [128, 384], BF16) for i in range(2)]; r_pex = [Res(), Res()]
                    pT = [SB(pc, "pT%d" % i, [128, 3, 128], BF16) for i in range(2)]; r_pT = [Res(), Res()]
                    sm = [SB(pc, "smx%d" % i, [128, 8], F32) for i in range(2)]; r_sm = [Res(), Res()]
                    ao = SB(pc, "ao", [128, 512], BF16); r_ao = Res()
                    hctr = 0
                    for g in range(NG):
                        gs = slice(g * 512, (g + 1) * 512)
                        for t in range(4):
                            tile_i = g * 4 + t
                            sl = tile_i % 2
                            dma("sp", ds_x[sl], xs[sl][:], x_d[s, tile_i * 128:(tile_i + 1) * 128, :], writes=[r_xs[sl]])
                            make_uT(xs[sl], r_xs[sl], uT[:, :, t * 128:(t + 1) * 128], r_uT, 1, 0)
                        for m in range(4):
                            for (pbx, cb) in ((2, 512 + m * 128), (3, 1024 + m * 128)):
                                for k in range(8):
                                    op("pe", lambda e, k=k, pbx=pbx, cb=cb: e.matmul(PB[pbx][:], lhsT=wC[:, k, cb:cb + 128], rhs=uT[:, k, :], start=(k == 0), stop=(k == 7)),
                                       reads=[r_uT, r_wC], writes=[rPB[pbx]])
                            op("dve", lambda e: e.tensor_tensor(out=ta[0][:], in0=PB[2][:], in1=cosT[:, gs], op=ALU.mult), reads=[rPB[2], r_cs], writes=[r_ta[0]])
                            op("dve", lambda e: e.tensor_tensor(out=ta[1][:], in0=PB[3][:], in1=sinT[:, gs], op=ALU.mult), reads=[rPB[3], r_cs], writes=[r_ta[1]])
                            op("dve", lambda e, m=m: e.tensor_tensor(out=qT[:, m, :], in0=ta[0][:], in1=ta[1][:], op=ALU.add), reads=[r_ta[0], r_ta[1]], writes=[r_qT])
                        for h in range(4):
                            for k in range(8):
                                op("pe", lambda e, k=k, h=h: e.matmul(PB[4][:], lhsT=wC[:, k, h * 128:(h + 1) * 128], rhs=uT[:, k, :], start=(k == 0), stop=(k == 7)),
                                   reads=[r_uT, r_wC], writes=[rPB[4]])
                            op("act", lambda e: e.activation(out=ta[0][:], in_=PB[4][:], func=AF.Silu), reads=[rPB[4]], writes=[r_ta[0]])
                            rr = [r_oT[h][c] for c in range(g * 8, (g + 1) * 8)]
                            op("act", lambda e, h=h: e.activation(out=ta[2][:], in_=oT[:, h, gs], func=AF.Square), reads=rr, writes=[r_ta[2]])
                            op("pe", lambda e: e.matmul(PB[5][:], lhsT=onesf[:], rhs=ta[2][:], start=True, stop=True), reads=[r_onesf, r_ta[2]], writes=[rPB[5]])
                            op("act", lambda e: e.activation(out=ta[3][:], in_=PB[5][:], func=AF.Ln, scale=1.0 / 128.0, bias=epsc[:, 1:2]), reads=[rPB[5], r_eps], writes=[r_ta[3]])
                            op("act", lambda e: e.activation(out=ta[3][:], in_=ta[3][:], func=AF.Exp, scale=-0.5), reads=[r_ta[3]], writes=[r_ta[3]])
                            op("dve", lambda e, h=h: e.tensor_tensor(out=ta[2][:], in0=oT[:, h, gs], in1=ta[3][:], op=ALU.mult), reads=rr + [r_ta[3]], writes=[r_ta[2]])
                            op("dve", lambda e, h=h: e.scalar_tensor_tensor(out=ogA[:, h, gs], in0=ta[0][:], scalar=ngc[:, 0:1], in1=ta[2][:], op0=ALU.mult, op1=ALU.mult),
                               reads=[r_ta[0], r_ngc, r_ta[2]], writes=[r_ogA[g]])
                        for t in range(4):
                            j = g * 4 + t
                            k0 = max(0, j - 1); k1 = min(NT, j + 2)
                            nkb = k1 - k0; nk = nkb * 128
                            mc0 = 128 if j == 0 else 0
                            for h in range(8):
                                m, hf = h % 4, h // 4
                                ps = slice(hf * 64, (hf + 1) * 64)
                                ib = hctr % 2; hctr += 1
                                pbs = 2 + ib
                                kgs = [r_kT[gg] for gg in range((k0 * 128) // 512, ((k1 * 128) - 1) // 512 + 1)]
                                op("pe", lambda e, pbs=pbs, ps=ps, m=m, t=t: e.matmul(PB[pbs][:, 0:nk], lhsT=qT[ps, m, t * 128:(t + 1) * 128], rhs=kTt[ps, k0 * 128:k1 * 128], start=True, stop=True),
                                   reads=[r_qT] + kgs, writes=[rPB[pbs]])
                                op("dve", lambda e, pbs=pbs, ib=ib: e.tensor_tensor(out=msc[ib][:, 0:nk], in0=PB[pbs][:, 0:nk], in1=amask[:, mc0:mc0 + nk], op=ALU.add),
                                   reads=[rPB[pbs], r_amask], writes=[r_msc[ib]])
                                op("dve", lambda e, ib=ib: e.reduce_max(out=sm[ib][:, 0:1], in_=msc[ib][:, 0:nk], axis=AX.X), reads=[r_msc[ib]], writes=[r_sm[ib]])
                                op("dve", lambda e, ib=ib, h=h: e.tensor_scalar(out=sm[ib][:, 1:2], in0=sm[ib][:, 0:1], scalar1=-0.125, scalar2=sinkb[:, 1, h:h + 1], op0=ALU.mult, op1=ALU.min),
                                   reads=[r_sm[ib], r_sink], writes=[r_sm[ib]])
                                op("act", lambda e, ib=ib: e.activation(out=pex[ib][:, 0:nk], in_=msc[ib][:, 0:nk], func=AF.Exp, scale=0.125, bias=sm[ib][:, 1:2], accum_out=sm[ib][:, 2:3]),
                                   reads=[r_msc[ib], r_sm[ib]], writes=[r_pex[ib], r_sm[ib]])
                                op("act", lambda e, ib=ib, h=h: e.activation(out=sm[ib][:, 3:4], in_=sinkb[:, 0, h:h + 1], func=AF.Exp, bias=sm[ib][:, 1:2], scale=1.0),
                                   reads=[r_sm[ib], r_sink], writes=[r_sm[ib]])
                                op("dve", lambda e, ib=ib: e.tensor_tensor(out=sm[ib][:, 4:5], in0=sm[ib][:, 2:3], in1=sm[ib][:, 3:4], op=ALU.add), reads=[r_sm[ib]], writes=[r_sm[ib]])
                                op("dve", lambda e, ib=ib: e.reciprocal(out=sm[ib][:, 5:6], in_=sm[ib][:, 4:5]), reads=[r_sm[ib]], writes=[r_sm[ib]])
                                pbt = 4 + ib
                                for b in range(nkb):
                                    op("pe", lambda e, b=b, pbt=pbt, ib=ib: e.transpose(out=PB[pbt][:].bitcast(BF16)[:, b * 128:(b + 1) * 128], in_=pex[ib][:, b * 128:(b + 1) * 128], identity=identb[:]),
                                       reads=[r_pex[ib], r_identb], writes=[rPB[pbt]])
                                if ib == 0:
                                    op("act", lambda e, pbt=pbt, ib=ib: e.activation(out=pT[ib][:, 0:nkb, :], in_=PB[pbt][:].bitcast(BF16)[:, 0:nk].rearrange("p (b q) -> p b q", q=128), func=AF.Copy),
                                       reads=[rPB[pbt]], writes=[r_pT[ib]])
                                else:
                                    op("dve", lambda e, pbt=pbt, ib=ib: e.tensor_copy(out=pT[ib][:, 0:nkb, :], in_=PB[pbt][:].bitcast(BF16)[:, 0:nk].rearrange("p (b q) -> p b q", q=128)),
                                       reads=[rPB[pbt]], writes=[r_pT[ib]])
                                pbo = 6 + ib
                                for b in range(nkb):
                                    op("pe", lambda e, b=b, pbo=pbo, ib=ib, hf=hf: e.matmul(PB[pbo][:, 0:64], lhsT=pT[ib][:, b, :], rhs=vtk[:, k0 + b, hf * 64:(hf + 1) * 64], start=(b == 0), stop=(b == nkb - 1)),
                                       reads=[r_pT[ib], r_v[k0 + b]], writes=[rPB[pbo]])
                                op("act", lambda e, pbo=pbo, ib=ib, h=h: e.activation(out=ao[:, h * 64:(h + 1) * 64], in_=PB[pbo][:, 0:64], func=AF.Identity, scale=sm[ib][:, 5:6]),
                                   reads=[rPB[pbo], r_sm[ib]], writes=[r_ao])
                            for m in range(4):
                                op("pe", lambda e, m=m: e.transpose(out=PB[5][:].bitcast(BF16)[:, m * 128:(m + 1) * 128], in_=ao[:, m * 128:(m + 1) * 128], identity=identb[:]),
                                   reads=[r_ao, r_identb], writes=[rPB[5]])
                            op("dve", lambda e, j=j: e.tensor_copy(out=atA[:, :, j * 128:(j + 1) * 128], in_=PB[5][:].bitcast(BF16)[:, 0:512].rearrange("p (m q) -> p m q", q=128)),
                               reads=[rPB[5]], writes=[r_atA[j]])
                T.barrier()

                with ExitStack() as pc:
                    if S >= 2048:
                        wG = oT[:].rearrange("p h s -> p (h s)").bitcast(BF16)
                        wGv = wG[:, 0:8 * 2048].rearrange("p (k n) -> p k n", n=2048)
                    else:
                        wGv = SB(pc, "wGt", [128, 8, 2048], BF16)[:]
                    r_wG = Res()
                    wR = SB(pc, "wR", [128, 4, D], BF16); wT = SB(pc, "wT", [128, 4, D], BF16); wO = SB(pc, "wO", [128, 8, D], BF16)
                    lnB = SB(pc, "lnB1", [128, 2, D], F32); r_lnB = Res()
                    for q4 in range(4):
                        dma("pool", ds_w, wGv[:, :, q4 * 512:(q4 + 1) * 512], w_in_k[:, :, O_GR + q4 * 512:O_GR + (q4 + 1) * 512], writes=[r_wG])
                    dma("pool", ds_w, wR[:], w_rec_k, writes=[r_wG])
                    dma("pool", ds_w, wT[:], w_att_k, writes=[r_wG])
                    for q2 in range(2):
                        dma("pool", ds_w, wO[:, :, q2 * 512:(q2 + 1) * 512], w_out_k[:, :, q2 * 512:(q2 + 1) * 512], writes=[r_wG])
                    dma("sp", ds_const, lnB[:, 0, :], ln1g_d[0:1, :].broadcast_to([128, D]), writes=[r_lnB])
                    dma("sp", ds_const, lnB[:, 1, :], ln1b_d[0:1, :].broadcast_to([128, D]), writes=[r_lnB])
                    zb = zer[:, 0:8].bitcast(BF16)[:, 0:8].unsqueeze(2)
                    dma("sp", ds_misc[3], u2T_d[s, :, 0:1].rearrange("(j p) o -> p j o", p=128), zb, reads=[r_zer], allow_slow_non_contiguous=True)
                    dma("sp", ds_misc[3], u2T_d[s, :, S + 1:S + 2].rearrange("(j p) o -> p j o", p=128), zb, reads=[r_zer], allow_slow_non_contiguous=True)
                    r_x1d = [Res() for _ in range(NT)]
                    r_u2d = [Res() for _ in range(NT)]
                    xs = [SB(pc, "bxs%d" % i, [128, D], F32) for i in range(4)]; r_xs = [Res() for _ in range(4)]
                    uT = SB(pc, "buT", [128, 8, 512], BF16); r_uT = Res()
                    mT = SB(pc, "mT", [128, 8, 512], BF16); r_mT = Res()
                    ta = [SB(pc, "bta%d" % i, [128, 512], F32) for i in range(4)]; r_ta = [Res() for _ in range(4)]
                    yt = [SB(pc, "yt%d" % i, [128, D], F32) for i in range(2)]; r_yt = [Res(), Res()]
                    st6 = [SB(pc, "st6%d" % i, [128, 20], F32) for i in range(2)]; r_st6 = [Res(), Res()]
                    u2t = [SB(pc, "u2t%d" % i, [128, 8, 128], BF16) for i in range(2)]; r_u2t = [Res(), Res()]
                    for g in range(NG):
                        gs = slice(g * 512, (g + 1) * 512)
                        for t in range(4):
                            tile_i = g * 4 + t
                            dma("sp", ds_x[t], xs[t][:], x_d[s, tile_i * 128:(tile_i + 1) * 128, :], writes=[r_xs[t]])
                            make_uT(xs[t], r_xs[t], uT[:, :, t * 128:(t + 1) * 128], r_uT, 1, 0)
                        BSETS = ((2, 3, 4, 5), (0, 1, 6, 7))

                        def dc_pe(dc):
                            b0, b1, b2, b3 = BSETS[dc % 2]
                            dcs = slice(dc * 128, (dc + 1) * 128)
                            for k in range(8):
                                op("pe", lambda e, k=k: e.matmul(PB[b0][:], lhsT=wGv[:, k, dc * 128:(dc + 1) * 128], rhs=uT[:, k, :], start=(k == 0), stop=(k == 7)),
                                   reads=[r_uT, r_wG], writes=[rPB[b0]])
                            for k in range(8):
                                op("pe", lambda e, k=k: e.matmul(PB[b1][:], lhsT=wGv[:, k, 1024 + dc * 128:1024 + (dc + 1) * 128], rhs=uT[:, k, :], start=(k == 0), stop=(k == 7)),
                                   reads=[r_uT, r_wG], writes=[rPB[b1]])
                            for h in range(4):
                                op("pe", lambda e, h=h: e.matmul(PB[b2][:], lhsT=wR[:, h, dcs], rhs=ogA[:, h, gs], start=(h == 0), stop=(h == 3)),
                                   reads=[r_wG, r_ogA[g]], writes=[rPB[b2]])
                            for m in range(4):
                                op("pe", lambda e, m=m: e.matmul(PB[b3][:], lhsT=wT[:, m, dcs], rhs=atA[:, m, gs], start=(m == 0), stop=(m == 3)),
                                   reads=[r_wG] + r_atA[g * 4:(g + 1) * 4], writes=[rPB[b3]])

                        def dc_ew(dc):
                            b0, b1, b2, b3 = BSETS[dc % 2]
                            a0, a1 = ta[2 * (dc % 2)], ta[2 * (dc % 2) + 1]
                            ra0, ra1 = r_ta[2 * (dc % 2)], r_ta[2 * (dc % 2) + 1]
                            op("act", lambda e: e.activation(out=a0[:], in_=PB[b0][:], func=AF.Sigmoid), reads=[rPB[b0]], writes=[ra0])
                            op("act", lambda e: e.activation(out=a1[:], in_=PB[b1][:], func=AF.Sigmoid), reads=[rPB[b1]], writes=[ra1])
                            op("dve", lambda e: e.tensor_tensor(out=a0[:], in0=PB[b2][:], in1=a0[:], op=ALU.mult), reads=[rPB[b2], ra0], writes=[ra0])
                            op("dve", lambda e: e.tensor_tensor(out=a1[:], in0=PB[b3][:], in1=a1[:], op=ALU.mult), reads=[rPB[b3], ra1], writes=[ra1])
                            op("dve", lambda e: e.tensor_tensor(out=mT[:, dc, :], in0=a0[:], in1=a1[:], op=ALU.add), reads=[ra0, ra1], writes=[r_mT])

                        dc_pe(0)
                        for dc in range(8):
                            if dc + 1 < 8:
                                dc_pe(dc + 1)
                            dc_ew(dc)
                        for t in range(4):
                            tile_i = g * 4 + t
                            ib = tile_i % 2
                            wb = 2 + 2 * (tile_i % 2)
                            for hh in range(2):
                                for k in range(8):
                                    op("pe", lambda e, k=k, hh=hh, t=t, wb=wb: e.matmul(PB[wb + hh][:], lhsT=mT[:, k, t * 128:(t + 1) * 128], rhs=wO[:, k, hh * 512:(hh + 1) * 512], start=(k == 0), stop=(k == 7)),
                                       reads=[r_mT, r_wG], writes=[rPB[wb + hh]])
                            ln_epilogue((wb, wb + 1), (rPB[wb], rPB[wb + 1]), xs[t][:], r_xs[t], 0, lnB[:, 0, :], lnB[:, 1, :], r_lnB, yt[ib], r_yt[ib], st6[ib], r_st6[ib])
                            dma("sp", ds_o[ib], x1_d[s, tile_i * 128:(tile_i + 1) * 128, :], yt[ib][:], reads=[r_yt[ib]], writes=[r_x1d[tile_i]])
                            make_uT(yt[ib], r_yt[ib], u2t[ib][:], r_u2t[ib], 3, 2, banks=(6, 7))
                            dma("sp", ds_o[2 + ib], u2T_d[s, :, 1 + tile_i * 128:1 + (tile_i + 1) * 128].rearrange("(j p) q -> p j q", p=128), u2t[ib][:],
                                reads=[r_u2t[ib]], writes=[r_u2d[tile_i]])
                T.barrier()
            T.barrier()

            with ExitStack() as pf:
                wU = SB(pf, "wU", [128, 8, 2 * DFF], BF16); wD = SB(pf, "wD", [128, NFC, D], BF16); r_wU = Res()
                lnB = SB(pf, "lnB2", [128, 2, D], F32); r_lnB = Res()
                cvw = SB(pf, "cvw", [128, 4, 2 * NFC], F32); r_cvw = r_lnB
                for q in range(11):
                    dma("pool", ds_w, wU[:, :, q * 512:(q + 1) * 512], w_up_k[:, :, q * 512:(q + 1) * 512], writes=[r_wU])
                for q in range(NFC):
                    dma("pool", ds_w, wD[:, q, :], w_dn_d[q * 128:(q + 1) * 128, :], writes=[r_wU])
                dma("sp", ds_const, lnB[:, 0, :], ln2g_d[0:1, :].broadcast_to([128, D]), writes=[r_lnB])
                dma("sp", ds_const, lnB[:, 1, :], ln2b_d[0:1, :].broadcast_to([128, D]), writes=[r_lnB])
                dma("sp", ds_const, cvw[:, 0:3, :], cw_d.rearrange("k (j p) -> p k j", p=128), writes=[r_cvw], allow_slow_non_contiguous=True)
                dma("sp", ds_const, cvw[:, 3, :], cb_d.rearrange("o (j p) -> p (o j)", p=128), writes=[r_cvw], allow_slow_non_contiguous=True)
                NSG = S // 256
                u2 = [SB(pf, "fu2%d" % i, [128, 8, 258], BF16) for i in range(2)]; r_u2 = [Res(), Res()]
                x1t = [SB(pf, "fx1%d" % i, [128, D], F32) for i in range(4)]; r_x1t = [Res() for _ in range(4)]
                cv = [SB(pf, "fcv%d" % i, [128, 256], F32) for i in range(2)]; r_cv = [Res(), Res()]
                cgt = [SB(pf, "fcg%d" % i, [128, 256], F32) for i in range(2)]; r_cg = [Res(), Res()]
                gz = [SB(pf, "fgz%d" % i, [128, 256], F32) for i in range(2)]; r_gz = [Res(), Res()]
                actb = [SB(pf, "fact%d" % i, [128, 256], BF16) for i in range(2)]; r_act = [Res(), Res()]
                yt = [SB(pf, "fyt%d" % i, [128, D], F32) for i in range(2)]; r_yt = [Res(), Res()]
                st6 = [SB(pf, "fst6%d" % i, [128, 20], F32) for i in range(2)]; r_st6 = [Res(), Res()]
                def c2_up(sg, c):
                    ub = sg % 2; ib = c % 2
                    pv, pg = 4 + 2 * ib, 5 + 2 * ib
                    for (pbx, cb) in ((pv, c * 128), (pg, DFF + c * 128)):
                        for k in range(8):
                            op("pe", lambda e, k=k, pbx=pbx, cb=cb: e.matmul(PB[pbx][:, 0:258], lhsT=wU[:, k, cb:cb + 128], rhs=u2[ub][:, k, :], start=(k == 0), stop=(k == 7)),
                               reads=[r_wU, r_u2[ub]], writes=[rPB[pbx]])

                def c2_s1(c):
                    ib = c % 2
                    pv, pg = 4 + 2 * ib, 5 + 2 * ib
                    for (pbx, dst, r_dst, col) in ((pv, cv[ib], r_cv[ib], c), (pg, cgt[ib], r_cg[ib], NFC + c)):
                        op("act", lambda e, pbx=pbx, dst=dst, col=col: e.activation(out=dst[:], in_=PB[pbx][:, 0:256], func=AF.Identity, scale=cvw[:, 0, col:col + 1], bias=cvw[:, 3, col:col + 1]),
                           reads=[rPB[pbx], r_cvw], writes=[r_dst])

                def c2_s2(c):
                    ib = c % 2
                    pv, pg = 4 + 2 * ib, 5 + 2 * ib
                    for (pbx, dst, r_dst, col) in ((pg, cgt[ib], r_cg[ib], NFC + c), (pv, cv[ib], r_cv[ib], c)):
                        op("dve", lambda e, pbx=pbx, dst=dst, col=col: e.scalar_tensor_tensor(out=dst[:], in0=PB[pbx][:, 1:257], scalar=cvw[:, 1, col:col + 1], in1=dst[:], op0=ALU.mult, op1=ALU.add),
                           reads=[rPB[pbx], r_cvw, r_dst], writes=[r_dst])
                        op("dve", lambda e, pbx=pbx, dst=dst, col=col: e.scalar_tensor_tensor(out=dst[:], in0=PB[pbx][:, 2:258], scalar=cvw[:, 2, col:col + 1], in1=dst[:], op0=ALU.mult, op1=ALU.add),
                           reads=[rPB[pbx], r_cvw, r_dst], writes=[r_dst])

                def c2_s3(c):
                    ib = c % 2
                    op("act", lambda e: e.activation(out=gz[ib][:], in_=cgt[ib][:], func=AF.Gelu_apprx_tanh), reads=[r_cg[ib]], writes=[r_gz[ib]])

                def c2_s4(c):
                    ib = c % 2
                    op("dve", lambda e: e.tensor_tensor(out=actb[ib][:], in0=gz[ib][:], in1=cv[ib][:], op=ALU.mult), reads=[r_gz[ib], r_cv[ib]], writes=[r_act[ib]])

                def c2_down(c):
                    ib = c % 2
                    for tt in range(2):
                        for hh in range(2):
                            pa_ = tt * 2 + hh
                            op("pe", lambda e, tt=tt, hh=hh, pa_=pa_: e.matmul(PB[pa_][:], lhsT=actb[ib][:, tt * 128:(tt + 1) * 128], rhs=wD[:, c, hh * 512:(hh + 1) * 512],
                                                                             start=(c == 0), stop=(c == NFC - 1)),
                               reads=[r_act[ib], r_wU], writes=[rPB[pa_]])

                def c2_load(sg):
                    a = sg * 256
                    ub = sg % 2
                    dma("sp", ds_misc[ub], u2[ub][:], u2T_d[s, :, a:a + 258].rearrange("(j p) q -> p j q", p=128), writes=[r_u2[ub]])
                    for tt in range(2):
                        sl = (sg * 2 + tt) % 4
                        dma("sp", ds_x[sl], x1t[sl][:], x1_d[s, a + tt * 128:a + (tt + 1) * 128, :], writes=[r_x1t[sl]])

                def c2_prologue(sg):
                    c2_up(sg, 0); c2_up(sg, 1); c2_s1(0); c2_s2(0); c2_s1(1)

                def c2_ln2(sg):
                    for tt in range(2):
                        tile_i = sg * 2 + tt
                        sl = tile_i % 4
                        ib = tile_i % 2
                        ln_part2(x1t[sl][:], r_x1t[sl], lnB[:, 0, :], lnB[:, 1, :], r_lnB, yt[ib], r_yt[ib], st6[ib], r_st6[ib])
                        dma("sp", ds_o[ib], out_d[s, tile_i * 128:(tile_i + 1) * 128, :], yt[ib][:], reads=[r_yt[ib]])

                c2_load(0)
                if NSG > 1:
                    c2_load(1)
                c2_prologue(0)
                for sg in range(NSG):
                    for c in range(NFC):
                        c2_s3(c)
                        if c + 1 < NFC:
                            c2_s2(c + 1)
                        c2_s4(c)
                        if c + 2 < NFC:
                            c2_up(sg, c + 2)
                            c2_s1(c + 2)
                        c2_down(c)
                    for tt in range(2):
                        tile_i = sg * 2 + tt
                        ib = tile_i % 2
                        ln_part1((tt * 2, tt * 2 + 1), (rPB[tt * 2], rPB[tt * 2 + 1]), 1, yt[ib], r_yt[ib])
                    if sg + 1 < NSG:
                        c2_prologue(sg + 1)
                    c2_ln2(sg)
                    if sg + 2 < NSG:
                        c2_load(sg + 2)
            T.barrier()
        print("build: instructions=%d waits=%d" % (T.n_ins, T.n_wait))
    return nc


def _consts():
    ident = np.eye(128, dtype=np.float32)
    i = np.arange(64)
    maskf = (i[:, None] <= i[None, :]).astype(np.uint8)
    maskb = (i[:, None] >= i[None, :]).astype(np.uint8)
    q = np.arange(128)[:, None]
    cc = np.arange(128)[None, :]
    NEG = -30000.0
    am = np.zeros((128, 384), np.float32)
    am[:, 0:128] = np.where(cc >= q, 0.0, NEG)
    am[:, 256:384] = np.where(cc <= q, 0.0, NEG)
    half = 32
    inv = (np.float32(10000.0) ** (-np.arange(half, dtype=np.float32) / np.float32(half))).astype(np.float32)
    p = np.arange(128)
    rope = np.stack([inv[p % 32], np.where((p % 64) < 32, -1.0, 1.0).astype(np.float32)], axis=1).astype(np.float32)
    return dict(k_ident=ident, k_maskf=maskf, k_maskb=maskb, k_amask=am, k_rope=rope)


def make_in_maps(inputs, n_cores, nseq):
    f = lambda a: np.ascontiguousarray(np.asarray(a))
    shared = dict(
        w_ada=f(inputs["w_ada"][0]), b_ada=f(inputs["b_ada"][0:1]), w_in=f(inputs["w_in"][0]),
        rec_lower_bound=f(inputs["rec_lower_bound"]), rec_norm_g=f(inputs["rec_norm_g"][0:1]),
        w_rec_branch=f(inputs["w_rec_branch"][0]), attn_sink=f(inputs["attn_sink"][0:1]),
        w_attn_branch=f(inputs["w_attn_branch"][0]), w_out=f(inputs["w_out"][0]),
        ln1_g=f(inputs["ln1_g"][0:1]), ln1_b=f(inputs["ln1_b"][0:1]), w_up=f(inputs["w_up"][0]),
        conv_w=f(inputs["conv_w"][0]), conv_b=f(inputs["conv_b"][0:1]), w_down=f(inputs["w_down"][0]),
        ln2_g=f(inputs["ln2_g"][0:1]), ln2_b=f(inputs["ln2_b"][0:1]))
    shared.update(_consts())
    maps = []
    for i in range(n_cores):
        m = dict(shared)
        m["x"] = f(inputs["x"][i * nseq:(i + 1) * nseq])
        m["c"] = f(inputs["c"][i * nseq:(i + 1) * nseq])
        m["positions"] = f(inputs["positions"][i * nseq:(i + 1) * nseq]).astype(np.int32)
        maps.append(m)
    return maps


def kernel(**inputs):
    B, S, _ = inputs["x"].shape
    nseq = B // N_CORES
    nc = build_nc(S=S, NSEQ=nseq)
    in_maps = make_in_maps(inputs, N_CORES, nseq)
    res = run_bass_kernel_spmd(nc, in_maps, core_ids=list(range(N_CORES)))
    return np.concatenate([np.asarray(r["out"]) for r in res.results], axis=0).astype(np.float32)
```

```python
import numpy as np
from contextlib import ExitStack
import concourse.bass as bass
import concourse.mybir as mybir
from concourse.bass_utils import run_bass_kernel_spmd

F32 = mybir.dt.float32
BF16 = mybir.dt.bfloat16
I32 = mybir.dt.int32
U8 = mybir.dt.uint8
AF = mybir.ActivationFunctionType
ALU = mybir.AluOpType
AX = mybir.AxisListType

D = 1024
NK = 8
DIN = 5376
DFF = 2816
NFC = DFF // 128
ALPHA = 2.0 ** 0.25
LN_EPS = 1e-5
RMS_EPS = 1e-6
N_CORES = 8
O_RQ, O_RFF, O_RFB, O_RI, O_RG, O_AQ, O_AK, O_AV, O_GR, O_GA = 0, 512, 1024, 1536, 2048, 2560, 3072, 3200, 3328, 4352


class Res:
    __slots__ = ("w", "rd")

    def __init__(self):
        self.w = None
        self.rd = []


class Tracker:
    def __init__(self, nc, stack):
        self.nc = nc
        self.stack = stack
        self.eng = {}
        self.sems = {}
        for name, h in (("pe", nc.tensor), ("act", nc.scalar), ("dve", nc.vector),
                        ("pool", nc.gpsimd), ("sp", nc.sync)):
            sem = stack.enter_context(nc.semaphore("sem_" + name))
            self.eng[name] = {"h": h, "sem": sem, "cnt": 0, "seen": {}, "name": name}
            self.sems[id(sem)] = sem
        self.dma_sems = []
        self.n_wait = 0
        self.n_ins = 0

    def new_dma_sem(self, name):
        sem = self.stack.enter_context(self.nc.semaphore(name))
        d = {"sem": sem, "cnt": 0}
        self.sems[id(sem)] = sem
        self.dma_sems.append(d)
        return d

    def _wait_deps(self, E, reads, writes):
        deps = {}

        def add(tok):
            if tok is not None and deps.get(tok[0], 0) < tok[1]:
                deps[tok[0]] = tok[1]

        for r in reads:
            add(r.w)
        for w in writes:
            add(w.w)
            for t in w.rd:
                add(t)
        own = id(E["sem"])
        for k, v in deps.items():
            if E["name"] == "pe" and k == own:
                continue
            if E["seen"].get(k, 0) >= v:
                continue
            E["h"].wait_ge(self.sems[k], v)
            E["seen"][k] = v
            self.n_wait += 1

    def _commit(self, tok, reads, writes):
        for r in reads:
            r.rd.append(tok)
            if len(r.rd) > 48:
                best = {}
                for k, v in r.rd:
                    if best.get(k, 0) < v:
                        best[k] = v
                r.rd = list(best.items())
        for w in writes:
            w.w = tok
            w.rd = []
        self.n_ins += 1

    def op(self, eng, fn, reads=(), writes=()):
        E = self.eng[eng]
        self._wait_deps(E, reads, writes)
        ins = fn(E["h"])
        E["cnt"] += 1
        ins.then_inc(E["sem"], 1)
        self._commit((id(E["sem"]), E["cnt"]), reads, writes)
        return ins

    def dma(self, q, dsem, out, in_, reads=(), writes=(), **kw):
        E = self.eng[q]
        self._wait_deps(E, reads, writes)
        ins = E["h"].dma_start(out=out, in_=in_, **kw)
        dsem["cnt"] += 16
        ins.then_inc(dsem["sem"], 16)
        self._commit((id(dsem["sem"]), dsem["cnt"]), reads, writes)
        return ins

    def barrier(self):
        targets = [(id(e["sem"]), e["cnt"]) for e in self.eng.values() if e["cnt"] > 0]
        targets += [(id(d["sem"]), d["cnt"]) for d in self.dma_sems if d["cnt"] > 0]
        for E in self.eng.values():
            for k, v in targets:
                if E["seen"].get(k, 0) >= v:
                    continue
                E["h"].wait_ge(self.sems[k], v)
                E["seen"][k] = v
                self.n_wait += 1


class Rot:
    def __init__(self, items):
        self.items = items
        self.i = 0

    def next(self):
        it = self.items[self.i % len(self.items)]
        self.i += 1
        return it


def build_nc(S=2048, NSEQ=2, dbg=False):
    NT, NG, NCH = S // 128, S // 512, S // 64
    nc = bass.Bass("TRN2", target_bir_lowering=False)

    def din(name, shape, dt=F32):
        return nc.dram_tensor(name, list(shape), dt, kind="ExternalInput").ap()

    x_d = din("x", [NSEQ, S, D])
    c_d = din("c", [NSEQ, D])
    pos_d = din("positions", [NSEQ, S], I32)
    w_ada_d = din("w_ada", [D, 6 * D])
    b_ada_d = din("b_ada", [1, 6 * D])
    w_in_d = din("w_in", [D, DIN])
    lbr_d = din("rec_lower_bound", [2, 2, 512])
    ng_d = din("rec_norm_g", [1, 128])
    w_rec_d = din("w_rec_branch", [512, D])
    sink_d = din("attn_sink", [1, 8])
    w_att_d = din("w_attn_branch", [512, D])
    w_out_d = din("w_out", [D, D])
    ln1g_d = din("ln1_g", [1, D]); ln1b_d = din("ln1_b", [1, D])
    w_up_d = din("w_up", [D, 2 * DFF])
    cw_d = din("conv_w", [3, 2 * DFF])
    cb_d = din("conv_b", [1, 2 * DFF])
    w_dn_d = din("w_down", [DFF, D])
    ln2g_d = din("ln2_g", [1, D]); ln2b_d = din("ln2_b", [1, D])
    identf_d = din("k_ident", [128, 128])
    mskf_d = din("k_maskf", [64, 64], U8)
    mskb_d = din("k_maskb", [64, 64], U8)
    amask_d = din("k_amask", [128, 384])
    rope_d = din("k_rope", [128, 2])
    out_d = nc.dram_tensor("out", [NSEQ, S, D], F32, kind="ExternalOutput").ap()
    x1_d = nc.dram_tensor("x1_scr", [NSEQ, S, D], F32).ap()
    u2T_d = nc.dram_tensor("u2T_scr", [NSEQ, D, S + 2], BF16).ap()
    dbg_d = {}
    if dbg:
        dbg_d["mods"] = nc.dram_tensor("dbg_mods", [128, 4, 8], F32, kind="ExternalOutput").ap()
        dbg_d["QT"] = nc.dram_tensor("dbg_QT", [2, 128, 2, S], BF16, kind="ExternalOutput").ap()
        dbg_d["KT"] = nc.dram_tensor("dbg_KT", [2, 128, 2, S], BF16, kind="ExternalOutput").ap()
        dbg_d["V"] = nc.dram_tensor("dbg_V", [64, NCH, 256], BF16, kind="ExternalOutput").ap()
        dbg_d["oT"] = nc.dram_tensor("dbg_oT", [128, 4, S], F32, kind="ExternalOutput").ap()
        dbg_d["kT"] = nc.dram_tensor("dbg_kT", [128, S], BF16, kind="ExternalOutput").ap()
        dbg_d["x1"] = x1_d

    w_in_k = w_in_d.rearrange("(j p) n -> p j n", p=128)
    w_ada_k = w_ada_d.rearrange("(j p) n -> p j n", p=128)
    w_up_k = w_up_d.rearrange("(j p) n -> p j n", p=128)
    w_out_k = w_out_d.rearrange("(j p) n -> p j n", p=128)
    w_rec_k = w_rec_d.rearrange("(j p) n -> p j n", p=128)
    w_att_k = w_att_d.rearrange("(j p) n -> p j n", p=128)
    w_dn_k = w_dn_d.rearrange("(j p) n -> p j n", p=128)

    with ExitStack() as top:
        T = Tracker(nc, top)
        op, dma = T.op, T.dma

        uniq = [0]

        def SB(st, name, shape, dt):
            uniq[0] += 1
            return st.enter_context(nc.sbuf_tensor("%s_%d" % (name, uniq[0]), list(shape), dt))

        PB = [top.enter_context(nc.psum_tensor("pb%d" % i, [128, 512], F32)) for i in range(8)]
        rPB = [Res() for _ in range(8)]
        ds_const = T.new_dma_sem("ds_const")
        ds_w = T.new_dma_sem("ds_w")
        ds_x = [T.new_dma_sem("ds_x%d" % i) for i in range(4)]
        ds_o = [T.new_dma_sem("ds_o%d" % i) for i in range(4)]
        ds_misc = [T.new_dma_sem("ds_m%d" % i) for i in range(4)]
        ds_dbg = T.new_dma_sem("ds_dbg")
        ds_wc = [T.new_dma_sem("ds_wc%d" % i) for i in range(11)]

        identf = SB(top, "identf", [128, 128], F32); r_identf = Res()
        identb = SB(top, "identb", [128, 128], BF16); r_identb = Res()
        onesf = SB(top, "onesf", [128, 128], F32); r_onesf = Res()
        maskf = SB(top, "maskf", [64, 64], U8); maskb = SB(top, "maskb", [64, 64], U8); r_mask = Res()
        zer = SB(top, "zer", [128, 128], F32); r_zer = Res()
        rmask = SB(top, "rmask", [128, 512], F32); r_rmask = Res()
        lbt = SB(top, "lbt", [128, 3, 2, 4], F32); r_lbt = Res()
        lraw = SB(top, "lraw", [128, 2, 2, 4], F32); r_lraw = Res()
        ngc = SB(top, "ngc", [128, 1], F32); r_ngc = Res()
        sinkb = SB(top, "sinkb", [128, 2, 8], F32); r_sink = Res()
        rope = SB(top, "rope", [128, 2], F32); r_rope = Res()
        epsc = SB(top, "epsc", [128, 2], F32); r_eps = Res()
        modsT = SB(top, "modsT", [128, 4, 8], F32); r_modsT = Res()
        gaB = SB(top, "gaB", [128, 2, D], F32); r_gaB = [Res(), Res()]
        amask = SB(top, "amask", [128, 384], F32); r_amask = Res()

        dma("sp", ds_const, identf[:], identf_d, writes=[r_identf])
        dma("sp", ds_const, maskf[:], mskf_d, writes=[r_mask])
        dma("sp", ds_const, maskb[:], mskb_d, writes=[r_mask])
        dma("sp", ds_const, rope[:], rope_d, writes=[r_rope])
        dma("sp", ds_const, amask[:], amask_d, writes=[r_amask])
        dma("sp", ds_const, ngc[:], ng_d.rearrange("o p -> p o"), writes=[r_ngc], allow_slow_non_contiguous=True)
        dma("sp", ds_const, sinkb[:, 0, :], sink_d[0:1, :].broadcast_to([128, 8]), writes=[r_sink])
        dma("sp", ds_const, lraw[:].rearrange("p a b h -> p (a b) h"),
            lbr_d.rearrange("a b (h p) -> p (a b) h", p=128), writes=[r_lraw], allow_slow_non_contiguous=True)
        T.barrier()
        op("dve", lambda e: e.tensor_copy(out=identb[:], in_=identf[:]), reads=[r_identf], writes=[r_identb])
        op("pool", lambda e: e.memset(onesf[:], 1.0), writes=[r_onesf])
        op("pool", lambda e: e.memset(zer[:], 0.0), writes=[r_zer])
        op("pool", lambda e: e.memset(rmask[:], 1.0), writes=[r_rmask])
        op("pool", lambda e: e.memset(rmask[:].rearrange("p (c t) -> p c t", t=64)[:, :, 0:1], 0.0), writes=[r_rmask])
        op("pool", lambda e: e.memset(epsc[:, 0:1], LN_EPS), writes=[r_eps])
        op("pool", lambda e: e.memset(epsc[:, 1:2], RMS_EPS), writes=[r_eps])
        op("dve", lambda e: e.tensor_scalar(out=sinkb[:, 1, :], in0=sinkb[:, 0, :], scalar1=-1.0, scalar2=None, op0=ALU.mult),
           reads=[r_sink], writes=[r_sink])
        op("dve", lambda e: e.tensor_tensor(out=lbt[:, 2, :, :], in0=lraw[:, :, 0, :], in1=lraw[:, :, 1, :], op=ALU.subtract),
           reads=[r_lraw], writes=[r_lbt])
        op("act", lambda e: e.activation(out=lbt[:, 0, :, :], in_=lbt[:, 2, :, :], func=AF.Sigmoid), reads=[r_lbt], writes=[r_lbt])
        op("act", lambda e: e.activation(out=lbt[:, 1, :, :], in_=lbt[:, 2, :, :], func=AF.Sigmoid, scale=-1.0), reads=[r_lbt], writes=[r_lbt])
        op("dve", lambda e: e.tensor_scalar(out=lbt[:, 2, :, :], in0=lbt[:, 1, :, :], scalar1=-1.0, scalar2=None, op0=ALU.mult),
           reads=[r_lbt], writes=[r_lbt])

        def make_uT(xt, r_xt, dst, r_dst, scol, bcol, banks=(0, 1)):
            for hb in range(2):
                b = banks[hb]
                for i in range(4):
                    k = hb * 4 + i
                    op("pe", lambda e, k=k, i=i, b=b: e.transpose(out=PB[b][:, i * 128:(i + 1) * 128], in_=xt[:, k * 128:(k + 1) * 128],
                                                                   identity=identf[:]),
                       reads=[r_xt, r_identf], writes=[rPB[b]])
                for i in range(4):
                    k = hb * 4 + i
                    if i % 2 == 0:
                        op("act", lambda e, k=k, i=i, b=b: e.activation(out=dst[:, k, :], in_=PB[b][:, i * 128:(i + 1) * 128], func=AF.Identity,
                                                                         scale=modsT[:, scol, k:k + 1], bias=modsT[:, bcol, k:k + 1]),
                           reads=[rPB[b], r_modsT], writes=[r_dst])
                    else:
                        op("dve", lambda e, k=k, i=i, b=b: e.tensor_scalar(out=dst[:, k, :], in0=PB[b][:, i * 128:(i + 1) * 128],
                                                                            scalar1=modsT[:, scol, k:k + 1], scalar2=modsT[:, bcol, k:k + 1],
                                                                            op0=ALU.mult, op1=ALU.add),
                           reads=[rPB[b], r_modsT], writes=[r_dst])

        def ln_part1(pz, r_pz, which, y, r_y):
            for h in range(2):
                op("dve", lambda e, h=h: e.tensor_tensor(out=y[:, h * 512:(h + 1) * 512], in0=PB[pz[h]][:], in1=gaB[:, which, h * 512:(h + 1) * 512], op=ALU.mult),
                   reads=[r_pz[h], r_gaB[which]], writes=[r_y])

        def ln_part2(xres, r_xres, gB, bB, r_gb, y, r_y, st6, r_st):
            op("dve", lambda e: e.scalar_tensor_tensor(out=y[:], in0=xres, scalar=ALPHA, in1=y[:], op0=ALU.mult, op1=ALU.add),
               reads=[r_xres, r_y], writes=[r_y])
            for h in range(2):
                op("dve", lambda e, h=h: e.bn_stats(out=st6[:, h * 6:(h + 1) * 6], in_=y[:, h * 512:(h + 1) * 512]), reads=[r_y], writes=[r_st])
            op("dve", lambda e: e.bn_aggr(out=st6[:, 12:14], in_=st6[:, 0:12]), reads=[r_st], writes=[r_st])
            op("act", lambda e: e.activation(out=st6[:, 14:15], in_=st6[:, 13:14], func=AF.Ln, bias=epsc[:, 0:1], scale=1.0), reads=[r_st, r_eps], writes=[r_st])
            op("act", lambda e: e.activation(out=st6[:, 15:16], in_=st6[:, 14:15], func=AF.Exp, scale=-0.5), reads=[r_st], writes=[r_st])
            op("dve", lambda e: e.tensor_scalar(out=st6[:, 16:17], in0=st6[:, 12:13], scalar1=st6[:, 15:16], scalar2=-1.0, op0=ALU.mult, op1=ALU.mult),
               reads=[r_st], writes=[r_st])
            op("act", lambda e: e.activation(out=y[:], in_=y[:], func=AF.Identity, scale=st6[:, 15:16], bias=st6[:, 16:17]),
               reads=[r_y, r_st], writes=[r_y])
            op("dve", lambda e: e.tensor_tensor(out=y[:], in0=y[:], in1=gB, op=ALU.mult), reads=[r_y, r_gb], writes=[r_y])
            op("dve", lambda e: e.tensor_tensor(out=y[:], in0=y[:], in1=bB, op=ALU.add), reads=[r_y, r_gb], writes=[r_y])

        def ln_epilogue(pz, r_pz, xres, r_xres, which, gB, bB, r_gb, y, r_y, st6, r_st):
            ln_part1(pz, r_pz, which, y, r_y)
            ln_part2(xres, r_xres, gB, bB, r_gb, y, r_y, st6, r_st)

        for s in range(NSEQ):
            with ExitStack() as ph:
                cT = SB(ph, "cT", [128, 8], F32); r_cT = Res()
                cbt = SB(ph, "cbt", [128, 8, 128], F32); r_cbt = Res()
                wst = [SB(ph, "wst%d" % i, [128, 8, 512], F32) for i in range(2)]; r_wst = [Res(), Res()]
                bst = [SB(ph, "bst%d" % i, [1, 512], F32) for i in range(2)]; r_bst = [Res(), Res()]
                mtmp = SB(ph, "mtmp", [128, 512], F32); r_mtmp = Res()
                dma("sp", ds_misc[0], cT[:], c_d[s].rearrange("(j p) -> p j", p=128), writes=[r_cT], allow_slow_non_contiguous=True)
                op("act", lambda e: e.activation(out=cT[:], in_=cT[:], func=AF.Silu), reads=[r_cT], writes=[r_cT])
                op("dve", lambda e: e.tensor_copy(out=cbt[:], in_=cT[:].unsqueeze(2).broadcast_to([128, 8, 128])), reads=[r_cT], writes=[r_cbt])
                for cg in range(12):
                    b2 = cg % 2
                    dma("sp", ds_misc[1 + b2], wst[b2][:], w_ada_k[:, :, cg * 512:(cg + 1) * 512], writes=[r_wst[b2]])
                    dma("sp", ds_o[b2], bst[b2][:], b_ada_d[0:1, cg * 512:(cg + 1) * 512], writes=[r_bst[b2]])
                    pb = 2 + b2
                    for k in range(8):
                        op("pe", lambda e, k=k: e.matmul(PB[pb][:], lhsT=cbt[:, k, :], rhs=wst[b2][:, k, :], start=(k == 0), stop=False),
                           reads=[r_cbt, r_wst[b2]], writes=[rPB[pb]])
                    op("pe", lambda e: e.matmul(PB[pb][:], lhsT=onesf[0:1, :], rhs=bst[b2][0:1, :], start=False, stop=True),
                       reads=[r_onesf, r_bst[b2]], writes=[rPB[pb]])
                    which, half = cg // 2, cg % 2
                    if which in (2, 5):
                        gi = 0 if which == 2 else 1
                        op("dve", lambda e: e.tensor_scalar(out=gaB[:, gi, half * 512:(half + 1) * 512], in0=PB[pb][:], scalar1=1.0, scalar2=None, op0=ALU.add),
                           reads=[rPB[pb]], writes=[r_gaB[gi]])
                    else:
                        widx = {0: 0, 1: 1, 3: 2, 4: 3}[which]
                        addc = 1.0 if which in (1, 4) else 0.0
                        op("dve", lambda e: e.tensor_scalar(out=mtmp[:], in0=PB[pb][:], scalar1=addc, scalar2=None, op0=ALU.add),
                           reads=[rPB[pb]], writes=[r_mtmp])
                        for i in range(4):
                            op("pe", lambda e, i=i: e.transpose(out=PB[4][:, i * 128:(i + 1) * 128], in_=mtmp[:, i * 128:(i + 1) * 128], identity=identf[:]),
                               reads=[r_mtmp, r_identf], writes=[rPB[4]])
                        op("dve", lambda e: e.tensor_copy(out=modsT[:, widx, half * 4:(half + 1) * 4],
                                                          in_=PB[4][:].rearrange("p (i m) -> p i m", m=128)[:, :, 0]),
                           reads=[rPB[4]], writes=[r_modsT])
                if dbg and s == 0:
                    dma("sp", ds_dbg, dbg_d["mods"], modsT[:], reads=[r_modsT])
            T.barrier()

            with ExitStack() as pab:
                oT = SB(pab, "oT", [128, 4, S], F32)
                r_oT = [[Res() for _ in range(NCH)] for _ in range(4)]
                kTt = SB(pab, "kTt", [128, S], BF16); r_kT = [Res() for _ in range(NG)]
                vtk = SB(pab, "vtk", [128, NT, 128], BF16); r_v = [Res() for _ in range(NT)]
                cosT = SB(pab, "cosT", [128, S], F32); sinT = SB(pab, "sinT", [128, S], F32); r_cs = Res()
                with ExitStack() as pr:
                    posi = SB(pr, "posi", [128, S], I32); r_posi = Res()
                    ang = SB(pr, "ang", [128, S], F32); r_ang = Res()
                    kq = SB(pr, "kq", [128, S], F32); r_kq = Res()
                    ki = SB(pr, "ki", [128, S], I32); r_ki = Res()
                    dma("sp", ds_misc[0], posi[:], pos_d[s:s + 1, :].broadcast_to([128, S]), writes=[r_posi])
                    op("dve", lambda e: e.tensor_copy(out=ang[:], in_=posi[:]), reads=[r_posi], writes=[r_ang])
                    op("dve", lambda e: e.tensor_scalar(out=ang[:], in0=ang[:], scalar1=rope[:, 0:1], scalar2=None, op0=ALU.mult),
                       reads=[r_ang, r_rope], writes=[r_ang])
                    TWO_PI = 6.283185307179586
                    C1 = 6.28125
                    C2 = TWO_PI - C1
                    for (dst, shift) in ((sinT, 0.0), (cosT, np.pi / 2)):
                        op("dve", lambda e, shift=shift: e.tensor_scalar(out=kq[:], in0=ang[:], scalar1=float(shift), scalar2=1.0 / TWO_PI, op0=ALU.add, op1=ALU.mult),
                           reads=[r_ang], writes=[r_kq])
                        op("dve", lambda e: e.tensor_copy(out=ki[:], in_=kq[:]), reads=[r_kq], writes=[r_ki])
                        op("dve", lambda e: e.tensor_copy(out=kq[:], in_=ki[:]), reads=[r_ki], writes=[r_kq])
                        op("dve", lambda e, shift=shift, dst=dst: e.tensor_scalar(out=dst[:], in0=ang[:], scalar1=float(shift), scalar2=None, op0=ALU.add),
                           reads=[r_ang], writes=[r_cs])
                        op("dve", lambda e, dst=dst: e.scalar_tensor_tensor(out=dst[:], in0=kq[:], scalar=-C1, in1=dst[:], op0=ALU.mult, op1=ALU.add),
                           reads=[r_kq, r_cs], writes=[r_cs])
                        op("dve", lambda e, dst=dst: e.scalar_tensor_tensor(out=dst[:], in0=kq[:], scalar=-C2, in1=dst[:], op0=ALU.mult, op1=ALU.add),
                           reads=[r_kq, r_cs], writes=[r_cs])
                        op("dve", lambda e, dst=dst: e.tensor_scalar(out=kq[:], in0=dst[:], scalar1=float(np.pi), scalar2=-TWO_PI, op0=ALU.is_gt, op1=ALU.mult),
                           reads=[r_cs], writes=[r_kq])
                        op("dve", lambda e, dst=dst: e.tensor_tensor(out=dst[:], in0=dst[:], in1=kq[:], op=ALU.add), reads=[r_cs, r_kq], writes=[r_cs])
                        op("dve", lambda e, dst=dst: e.tensor_scalar(out=kq[:], in0=dst[:], scalar1=float(-np.pi), scalar2=TWO_PI, op0=ALU.is_lt, op1=ALU.mult),
                           reads=[r_cs], writes=[r_kq])
                        op("dve", lambda e, dst=dst: e.tensor_tensor(out=dst[:], in0=dst[:], in1=kq[:], op=ALU.add), reads=[r_cs, r_kq], writes=[r_cs])
                        op("dve", lambda e, dst=dst: e.tensor_scalar(out=dst[:], in0=dst[:], scalar1=3.1415925, scalar2=-3.1415925, op0=ALU.min, op1=ALU.max),
                           reads=[r_cs], writes=[r_cs])
                        op("act", lambda e, dst=dst: e.activation(out=dst[:], in_=dst[:], func=AF.Sin), reads=[r_cs], writes=[r_cs])
                    op("dve", lambda e: e.tensor_scalar(out=sinT[:], in0=sinT[:], scalar1=rope[:, 1:2], scalar2=None, op0=ALU.mult),
                       reads=[r_cs, r_rope], writes=[r_cs])
                T.barrier()

                for hp in range(2):
                    with ExitStack() as pa:
                        NWA = 1024 + (384 if hp == 0 else 0)
                        wA = SB(pa, "wA", [128, 8, NWA], BF16); r_wA = Res()
                        QT = [SB(pa, "QT%d" % d, [128, 2, S], BF16) for d in range(2)]
                        KT = [SB(pa, "KT%d" % d, [128, 2, S], BF16) for d in range(2)]
                        r_QK = [[[Res() for _ in range(NG)] for _ in range(2)] for _ in range(2)]
                        Vt = SB(pa, "Vt", [64, NCH, 256], BF16); r_V = [Res() for _ in range(NG)]
                        Bref = SB(pa, "Bref", [128, 2, 2, NCH], F32); Bend = SB(pa, "Bend", [128, 2, 2, NCH], F32); r_B = Res()
                        Gx = SB(pa, "Gx", [128, 2, 2, NCH], F32); r_G = Res()
                        for seg, base in enumerate((O_RQ, O_RFF, O_RFB, O_RI)):
                            dma("pool", ds_w, wA[:, :, seg * 256:(seg + 1) * 256], w_in_k[:, :, base + hp * 256: base + hp * 256 + 256], writes=[r_wA])
                        if hp == 0:
                            dma("pool", ds_w, wA[:, :, 1024:1152], w_in_k[:, :, O_AK:O_AK + 128], writes=[r_wA])
                            for kv in range(2):
                                dma("pool", ds_w, wA[:, :, 1152 + kv * 64:1152 + kv * 64 + 32], w_in_k[:, :, O_AK + kv * 64 + 32:O_AK + kv * 64 + 64], writes=[r_wA])
                                dma("pool", ds_w, wA[:, :, 1152 + kv * 64 + 32:1152 + kv * 64 + 64], w_in_k[:, :, O_AK + kv * 64:O_AK + kv * 64 + 32], writes=[r_wA])
                            dma("pool", ds_w, wA[:, :, 1280:1408], w_in_k[:, :, O_AV:O_AV + 128], writes=[r_wA])
                        with ExitStack() as pa2:
                            xs = [SB(pa2, "xs%d" % i, [128, D], F32) for i in range(2)]; r_xs = [Res() for _ in range(2)]
                            uT = [SB(pa2, "uT%d" % i, [128, 8, 512], BF16) for i in range(1)]; r_uT = [Res()]
                            qs = [SB(pa2, "qs%d" % i, [128, 512], F32) for i in range(2)]; r_qs = [Res(), Res()]
                            NTMP = 2
                            tnames = ("sig", "lg", "Bf", "Eq", "Ek")
                            tmps = [{n: SB(pa2, "t_%s%d" % (n, i), [128, 512], F32) for n in tnames} for i in range(NTMP)]
                            r_tmps = [{n: Res() for n in tnames} for i in range(NTMP)]
                            for i in range(NTMP):
                                tmps[i]["kk"] = tmps[i]["sig"]; r_tmps[i]["kk"] = r_tmps[i]["sig"]
                                tmps[i]["dd"] = tmps[i]["lg"]; r_tmps[i]["dd"] = r_tmps[i]["lg"]
                            tctr = 0
                            bank_rot = Rot([2, 3, 4, 5, 6, 7])
                            for g in range(NG):
                                ub = 0
                                for t in range(4):
                                    tile_i = g * 4 + t
                                    sl = tile_i % 2
                                    dma("sp", ds_x[sl], xs[sl][:], x_d[s, tile_i * 128:(tile_i + 1) * 128, :], writes=[r_xs[sl]])
                                    make_uT(xs[sl], r_xs[sl], uT[ub][:, :, t * 128:(t + 1) * 128], r_uT[ub], 1, 0)
                                gs = slice(g * 512, (g + 1) * 512)
                                for cp in range(4):
                                    pb = bank_rot.next()
                                    for cc in range(2):
                                        ch = cp * 2 + cc
                                        for k in range(8):
                                            op("pe", lambda e, k=k, cc=cc, ch=ch, pb=pb: e.matmul(PB[pb][0:64, cc * 256:(cc + 1) * 256], lhsT=uT[ub][:, k, ch * 64:(ch + 1) * 64],
                                                                                                   rhs=wA[:, k, 768:1024], start=(k == 0), stop=(k == 7)),
                                               reads=[r_uT[ub], r_wA], writes=[rPB[pb]])
                                    op("act", lambda e, pb=pb, cp=cp: e.activation(out=Vt[:, g * 8 + cp * 2:g * 8 + cp * 2 + 2, :],
                                                                                    in_=PB[pb][0:64, :].rearrange("p (c n) -> p c n", n=256), func=AF.Copy, scale=128.0 ** -0.5),
                                       reads=[rPB[pb]], writes=[r_V[g]])
                                for hl in range(2):
                                    h = hp * 2 + hl
                                    pb = bank_rot.next()
                                    for k in range(8):
                                        op("pe", lambda e, k=k, pb=pb, hl=hl: e.matmul(PB[pb][:], lhsT=wA[:, k, hl * 128:(hl + 1) * 128], rhs=uT[ub][:, k, :],
                                                                                         start=(k == 0), stop=(k == 7)), reads=[r_uT[ub], r_wA], writes=[rPB[pb]])
                                    op("act", lambda e, pb=pb, hl=hl: e.activation(out=qs[hl][:], in_=PB[pb][:], func=AF.Silu), reads=[rPB[pb]], writes=[r_qs[hl]])
                                    for d in range(2):
                                        tm = tmps[tctr % NTMP]; rt = r_tmps[tctr % NTMP]; tctr += 1
                                        pb = bank_rot.next()
                                        cb = 256 * (1 + d) + hl * 128
                                        for k in range(8):
                                            op("pe", lambda e, k=k, pb=pb, cb=cb: e.matmul(PB[pb][:], lhsT=wA[:, k, cb:cb + 128], rhs=uT[ub][:, k, :],
                                                                                             start=(k == 0), stop=(k == 7)), reads=[r_uT[ub], r_wA], writes=[rPB[pb]])
                                        lb_c = lbt[:, 0, d, h:h + 1]; om_c = lbt[:, 1, d, h:h + 1]; nom_c = lbt[:, 2, d, h:h + 1]
                                        op("act", lambda e, pb=pb, tm=tm: e.activation(out=tm["sig"][:], in_=PB[pb][:], func=AF.Sigmoid), reads=[rPB[pb]], writes=[rt["sig"]])
                                        op("act", lambda e, tm=tm: e.activation(out=tm["lg"][:], in_=tm["sig"][:], func=AF.Ln, scale=om_c, bias=lb_c),
                                           reads=[rt["sig"], r_lbt], writes=[rt["lg"]])
                                        op("dve", lambda e, tm=tm: e.tensor_scalar(out=tm["kk"][:], in0=tm["sig"][:], scalar1=nom_c, scalar2=om_c, op0=ALU.mult, op1=ALU.add),
                                           reads=[rt["sig"], r_lbt], writes=[rt["kk"]])
                                        op("dve", lambda e, tm=tm: e.tensor_tensor_scan(out=tm["Bf"][:], data0=rmask[:], data1=tm["lg"][:], initial=0.0, op0=ALU.mult, op1=ALU.add),
                                           reads=[r_rmask, rt["lg"]], writes=[rt["Bf"]])
                                        B3 = tm["Bf"][:].rearrange("p (c t) -> p c t", t=64)
                                        if d == 1:
                                            lg3 = tm["lg"][:].rearrange("p (c t) -> p c t", t=64)
                                            op("dve", lambda e, B3=B3, lg3=lg3: e.tensor_tensor(out=lg3, in0=lg3, in1=B3, op=ALU.subtract), reads=[rt["lg"], rt["Bf"]], writes=[rt["lg"]])
                                            op("dve", lambda e, B3=B3, lg3=lg3: e.tensor_tensor(out=B3, in0=lg3, in1=B3[:, :, 63:64].broadcast_to([128, 8, 64]), op=ALU.add),
                                               reads=[rt["lg"], rt["Bf"]], writes=[rt["Bf"]])
                                            iref, iend = 32, 0
                                        else:
                                            iref, iend = 31, 63
                                        op("pool", lambda e, B3=B3, iref=iref: e.tensor_copy(out=Bref[:, d, hl, g * 8:(g + 1) * 8], in_=B3[:, :, iref]), reads=[rt["Bf"]], writes=[r_B])
                                        op("pool", lambda e, B3=B3, iend=iend: e.tensor_copy(out=Bend[:, d, hl, g * 8:(g + 1) * 8], in_=B3[:, :, iend]), reads=[rt["Bf"]], writes=[r_B])
                                        op("dve", lambda e, B3=B3, tm=tm, iref=iref: e.tensor_tensor(out=tm["dd"][:].rearrange("p (c t) -> p c t", t=64), in0=B3,
                                                                                                       in1=B3[:, :, iref:iref + 1].broadcast_to([128, 8, 64]), op=ALU.subtract),
                                           reads=[rt["Bf"]], writes=[rt["dd"]])
                                        op("act", lambda e, tm=tm: e.activation(out=tm["Eq"][:], in_=tm["dd"][:], func=AF.Exp), reads=[rt["dd"]], writes=[rt["Eq"]])
                                        op("act", lambda e, tm=tm: e.activation(out=tm["Ek"][:], in_=tm["dd"][:], func=AF.Exp, scale=-1.0), reads=[rt["dd"]], writes=[rt["Ek"]])
                                        op("dve", lambda e, tm=tm, d=d, hl=hl: e.tensor_tensor(out=KT[d][:, hl, gs], in0=tm["kk"][:], in1=tm["Ek"][:], op=ALU.mult),
                                           reads=[rt["kk"], rt["Ek"]], writes=[r_QK[d][hl][g]])
                                        op("dve", lambda e, tm=tm, d=d, hl=hl: e.tensor_tensor(out=QT[d][:, hl, gs], in0=qs[hl][:], in1=tm["Eq"][:], op=ALU.mult),
                                           reads=[r_qs[hl], rt["Eq"]], writes=[r_QK[d][hl][g]])
                                if hp == 0:
                                    pb1 = bank_rot.next()
                                    pb2 = bank_rot.next()
                                    for (pbx, cb) in ((pb1, 1024), (pb2, 1152)):
                                        for k in range(8):
                                            op("pe", lambda e, k=k, pbx=pbx, cb=cb: e.matmul(PB[pbx][:], lhsT=wA[:, k, cb:cb + 128], rhs=uT[ub][:, k, :], start=(k == 0), stop=(k == 7)),
                                               reads=[r_uT[ub], r_wA], writes=[rPB[pbx]])
                                    tm = tmps[tctr % NTMP]; rt = r_tmps[tctr % NTMP]; tctr += 1
                                    op("dve", lambda e, tm=tm: e.tensor_tensor(out=tm["sig"][:], in0=PB[pb1][:], in1=cosT[:, gs], op=ALU.mult), reads=[rPB[pb1], r_cs], writes=[rt["sig"]])
                                    op("dve", lambda e, tm=tm: e.tensor_tensor(out=tm["lg"][:], in0=PB[pb2][:], in1=sinT[:, gs], op=ALU.mult), reads=[rPB[pb2], r_cs], writes=[rt["lg"]])
                                    op("dve", lambda e, tm=tm: e.tensor_tensor(out=kTt[:, gs], in0=tm["sig"][:], in1=tm["lg"][:], op=ALU.add), reads=[rt["sig"], rt["lg"]], writes=[r_kT[g]])
                                    for t in range(4):
                                        pb = bank_rot.next()
                                        for k in range(8):
                                            op("pe", lambda e, k=k, pb=pb, t=t: e.matmul(PB[pb][:, 0:128], lhsT=uT[ub][:, k, t * 128:(t + 1) * 128], rhs=wA[:, k, 1280:1408],
                                                                                          start=(k == 0), stop=(k == 7)), reads=[r_uT[ub], r_wA], writes=[rPB[pb]])
                                        op("act", lambda e, pb=pb, t=t: e.activation(out=vtk[:, g * 4 + t, :], in_=PB[pb][:, 0:128], func=AF.Copy), reads=[rPB[pb]], writes=[r_v[g * 4 + t]])
                            op("dve", lambda e: e.tensor_tensor(out=Gx[:], in0=Bend[:], in1=Bref[:], op=ALU.subtract), reads=[r_B], writes=[r_G])
                            if NCH > 1:
                                op("dve", lambda e: e.tensor_tensor(out=Gx[:, 0, :, 0:NCH - 1], in0=Gx[:, 0, :, 0:NCH - 1], in1=Bref[:, 0, :, 1:NCH], op=ALU.add), reads=[r_B, r_G], writes=[r_G])
                                op("dve", lambda e: e.tensor_tensor(out=Gx[:, 1, :, 1:NCH], in0=Gx[:, 1, :, 1:NCH], in1=Bref[:, 1, :, 0:NCH - 1], op=ALU.add), reads=[r_B, r_G], writes=[r_G])
                            op("act", lambda e: e.activation(out=Gx[:], in_=Gx[:], func=AF.Exp), reads=[r_G], writes=[r_G])
                            if dbg and s == 0 and hp == 0:
                                for d in range(2):
                                    dma("sp", ds_dbg, dbg_d["QT"][d], QT[d][:], reads=[r_QK[d][hl][g] for hl in range(2) for g in range(NG)])
                                    dma("sp", ds_dbg, dbg_d["KT"][d], KT[d][:], reads=[r_QK[d][hl][g] for hl in range(2) for g in range(NG)])
                                dma("sp", ds_dbg, dbg_d["V"], Vt[:], reads=r_V)
                                dma("sp", ds_dbg, dbg_d["kT"], kTt[:], reads=r_kT)
                        T.barrier()

                        with ExitStack() as pb_:
                            Sm32 = SB(pb_, "Sm32", [128, 4, 128], F32); Smb = SB(pb_, "Smb", [128, 4, 128], BF16); tmp32 = SB(pb_, "tmp32", [128, 4, 128], F32)
                            r_Sm32 = [Res() for _ in range(4)]; r_Smb = [Res() for _ in range(4)]; r_tmp32 = [Res() for _ in range(4)]
                            scm = SB(pb_, "scm", [64, 2, 4, 64], BF16); r_scm = [[Res() for _ in range(4)] for _ in range(2)]
                            ktok = SB(pb_, "ktok", [64, 2, 4, 128], BF16); r_ktok = [[Res() for _ in range(4)] for _ in range(2)]
                            rB = [[None] * 4 for _ in range(2)]
                            for _p in range(2):
                                for _c in range(4):
                                    _r = Res()
                                    rB[_p][_c] = {n: _r for n in ("sc", "kt", "o", "dl")}
                            op("pool", lambda e: e.memset(Sm32[:], 0.0), writes=r_Sm32)
                            op("pool", lambda e: e.memset(Smb[:], 0.0), writes=r_Smb)
                            allQK = lambda d, hl: [r_QK[d][hl][g] for g in range(NG)]

                            def chunk_of(j, d):
                                return j if d == 0 else NCH - 1 - j

                            def stage1(j):
                                par = j % 2
                                for ci in range(4):
                                    hl, d = ci // 2, ci % 2
                                    c = chunk_of(j, d); tk = slice(c * 64, (c + 1) * 64)
                                    bk = ci + 4 * par
                                    op("pe", lambda e, bk=bk, d=d, hl=hl, tk=tk: e.matmul(PB[bk][0:64, 0:64], lhsT=KT[d][:, hl, tk], rhs=QT[d][:, hl, tk], start=True, stop=True),
                                       reads=allQK(d, hl), writes=[rB[par][ci]["sc"]])
                                    op("pe", lambda e, bk=bk, d=d, hl=hl, tk=tk: e.transpose(out=PB[bk][:].bitcast(BF16)[0:64, 128:256], in_=KT[d][:, hl, tk], identity=identb[:]),
                                       reads=allQK(d, hl) + [r_identb], writes=[rB[par][ci]["kt"]])

                            def stage2(j):
                                par = j % 2
                                for ci in range(4):
                                    hl, d = ci // 2, ci % 2
                                    bk = ci + 4 * par
                                    mk = maskf if d == 0 else maskb
                                    op("pool", lambda e, ci=ci: e.memset(scm[:, par, ci, :], 0.0), writes=[r_scm[par][ci]])
                                    op("dve", lambda e, bk=bk, ci=ci, mk=mk: e.copy_predicated(out=scm[:, par, ci, :], mask=mk[:], data=PB[bk][0:64, 0:64]),
                                       reads=[rB[par][ci]["sc"], r_mask], writes=[r_scm[par][ci]])
                                    op("act", lambda e, bk=bk, ci=ci: e.activation(out=ktok[:, par, ci, :], in_=PB[bk][:].bitcast(BF16)[0:64, 128:256], func=AF.Copy),
                                       reads=[rB[par][ci]["kt"]], writes=[r_ktok[par][ci]])

                            def stage3(j):
                                par = j % 2
                                for ci in range(4):
                                    hl, d = ci // 2, ci % 2
                                    c = chunk_of(j, d); tk = slice(c * 64, (c + 1) * 64)
                                    bk = ci + 4 * par
                                    vv = Vt[:, c, hl * 128:(hl + 1) * 128]
                                    op("pe", lambda e, bk=bk, vv=vv, ci=ci: e.matmul(PB[bk][:, 128:192], lhsT=vv, rhs=scm[:, par, ci, :], start=True, stop=False),
                                       reads=[r_V[c // 8], r_scm[par][ci]], writes=[rB[par][ci]["o"]])
                                    op("pe", lambda e, bk=bk, ci=ci, d=d, hl=hl, tk=tk: e.matmul(PB[bk][:, 128:192], lhsT=Smb[:, ci, :], rhs=QT[d][:, hl, tk], start=False, stop=True),
                                       reads=[r_Smb[ci]] + allQK(d, hl), writes=[rB[par][ci]["o"]])
                                    op("pe", lambda e, bk=bk, vv=vv, ci=ci: e.matmul(PB[bk][:, 192:320], lhsT=ktok[:, par, ci, :], rhs=vv, start=True, stop=True),
                                       reads=[r_ktok[par][ci], r_V[c // 8]], writes=[rB[par][ci]["dl"]])

                            def stage4(j):
                                par = j % 2
                                for ci in range(4):
                                    hl, d = ci // 2, ci % 2
                                    h = hp * 2 + hl
                                    c = chunk_of(j, d); tk = slice(c * 64, (c + 1) * 64)
                                    bk = ci + 4 * par
                                    if j < NCH // 2:
                                        op("act", lambda e, bk=bk, h=h, tk=tk: e.activation(out=oT[:, h, tk], in_=PB[bk][:, 128:192], func=AF.Copy),
                                           reads=[rB[par][ci]["o"]], writes=[r_oT[h][c]])
                                    else:
                                        op("dve", lambda e, bk=bk, h=h, tk=tk: e.tensor_tensor(out=oT[:, h, tk], in0=PB[bk][:, 128:192], in1=oT[:, h, tk], op=ALU.add),
                                           reads=[rB[par][ci]["o"], r_oT[h][c]], writes=[r_oT[h][c]])
                                    if j < NCH - 1:
                                        gcol = Gx[:, d, hl, c:c + 1]
                                        op("dve", lambda e, bk=bk, ci=ci: e.tensor_tensor(out=tmp32[:, ci, :], in0=PB[bk][:, 192:320], in1=Sm32[:, ci, :], op=ALU.add),
                                           reads=[rB[par][ci]["dl"], r_Sm32[ci]], writes=[r_tmp32[ci]])
                                        op("act", lambda e, ci=ci, gcol=gcol: e.activation(out=Sm32[:, ci, :], in_=tmp32[:, ci, :], func=AF.Identity, scale=gcol),
                                           reads=[r_tmp32[ci], r_G], writes=[r_Sm32[ci]])
                                        op("pool", lambda e, ci=ci, gcol=gcol: e.tensor_scalar(out=Smb[:, ci, :], in0=tmp32[:, ci, :], scalar1=gcol, scalar2=None, op0=ALU.mult),
                                           reads=[r_tmp32[ci], r_G], writes=[r_Smb[ci]])

                            stage1(0)
                            stage2(0)
                            for j in range(NCH):
                                if j + 1 < NCH:
                                    stage1(j + 1)
                                stage3(j)
                                if j + 1 < NCH:
                                    stage2(j + 1)
                                stage4(j)
                        T.barrier()
                if dbg and s == 0:
                    dma("sp", ds_dbg, dbg_d["oT"], oT[:], reads=[r for rr in r_oT for r in rr])

                ogA = SB(pab, "ogA", [128, 4, S], BF16); atA = SB(pab, "atA", [128, 4, S], BF16)
                r_ogA = [Res() for _ in range(NG)]; r_atA = [Res() for _ in range(NT)]
                with ExitStack() as pc:
                    wC = SB(pc, "wC", [128, 8, 1536], BF16); r_wC = Res()
                    dma("pool", ds_w, wC[:, :, 0:512], w_in_k[:, :, O_RG:O_RG + 512], writes=[r_wC])
                    for m in range(4):
                        for hf in range(2):
                            hq = m + 4 * hf
                            cdst = 512 + m * 128 + hf * 64
                            dma("pool", ds_w, wC[:, :, cdst:cdst + 64], w_in_k[:, :, O_AQ + hq * 64:O_AQ + hq * 64 + 64], writes=[r_wC])
                            dma("pool", ds_w, wC[:, :, 512 + cdst:512 + cdst + 32], w_in_k[:, :, O_AQ + hq * 64 + 32:O_AQ + hq * 64 + 64], writes=[r_wC])
                            dma("pool", ds_w, wC[:, :, 512 + cdst + 32:512 + cdst + 64], w_in_k[:, :, O_AQ + hq * 64:O_AQ + hq * 64 + 32], writes=[r_wC])
                    xs = [SB(pc, "cxs%d" % i, [128, D], F32) for i in range(2)]; r_xs = [Res() for _ in range(2)]
                    uT = SB(pc, "cuT", [128, 8, 512], BF16); r_uT = Res()
                    qT = SB(pc, "qT", [128, 4, 512], BF16); r_qT = Res()
                    ta = [SB(pc, "cta%d" % i, [128, 512], F32) for i in range(4)]; r_ta = [Res() for _ in range(4)]
                    msc = [SB(pc, "msc%d" % i, [128, 384], F32) for i in range(2)]; r_msc = [Res(), Res()]
                    pex = [SB(pc, "pex%d" % i, [128, 384], BF16) for i in range(2)]; r_pex = [Res(), Res()]
                    pT = [SB(pc, "pT%d" % i, [128, 3, 128], BF16) for i in range(2)]; r_pT = [Res(), Res()]
                    sm = [SB(pc, "smx%d" % i, [128, 8], F32) for i in range(2)]; r_sm = [Res(), Res()]
                    ao = [SB(pc, "ao%d" % i, [128, 512], BF16) for i in range(2)]; r_ao = [Res(), Res()]
                    hctr = 0
                    for g in range(NG):
                        gs = slice(g * 512, (g + 1) * 512)
                        for t in range(4):
                            tile_i = g * 4 + t
                            sl = tile_i % 2
                            dma("sp", ds_x[sl], xs[sl][:], x_d[s, tile_i * 128:(tile_i + 1) * 128, :], writes=[r_xs[sl]])
                            make_uT(xs[sl], r_xs[sl], uT[:, :, t * 128:(t + 1) * 128], r_uT, 1, 0)
                        for m in range(4):
                            for (pbx, cb) in ((2, 512 + m * 128), (3, 1024 + m * 128)):
                                for k in range(8):
                                    op("pe", lambda e, k=k, pbx=pbx, cb=cb: e.matmul(PB[pbx][:], lhsT=wC[:, k, cb:cb + 128], rhs=uT[:, k, :], start=(k == 0), stop=(k == 7)),
                                       reads=[r_uT, r_wC], writes=[rPB[pbx]])
                            op("dve", lambda e: e.tensor_tensor(out=ta[0][:], in0=PB[2][:], in1=cosT[:, gs], op=ALU.mult), reads=[rPB[2], r_cs], writes=[r_ta[0]])
                            op("dve", lambda e: e.tensor_tensor(out=ta[1][:], in0=PB[3][:], in1=sinT[:, gs], op=ALU.mult), reads=[rPB[3], r_cs], writes=[r_ta[1]])
                            op("dve", lambda e, m=m: e.tensor_tensor(out=qT[:, m, :], in0=ta[0][:], in1=ta[1][:], op=ALU.add), reads=[r_ta[0], r_ta[1]], writes=[r_qT])
                        for h in range(4):
                            for k in range(8):
                                op("pe", lambda e, k=k, h=h: e.matmul(PB[4][:], lhsT=wC[:, k, h * 128:(h + 1) * 128], rhs=uT[:, k, :], start=(k == 0), stop=(k == 7)),
                                   reads=[r_uT, r_wC], writes=[rPB[4]])
                            op("act", lambda e: e.activation(out=ta[0][:], in_=PB[4][:], func=AF.Silu), reads=[rPB[4]], writes=[r_ta[0]])
                            rr = [r_oT[h][c] for c in range(g * 8, (g + 1) * 8)]
                            op("act", lambda e, h=h: e.activation(out=ta[2][:], in_=oT[:, h, gs], func=AF.Square), reads=rr, writes=[r_ta[2]])
                            op("pe", lambda e: e.matmul(PB[5][:], lhsT=onesf[:], rhs=ta[2][:], start=True, stop=True), reads=[r_onesf, r_ta[2]], writes=[rPB[5]])
                            op("act", lambda e: e.activation(out=ta[3][:], in_=PB[5][:], func=AF.Ln, scale=1.0 / 128.0, bias=epsc[:, 1:2]), reads=[rPB[5], r_eps], writes=[r_ta[3]])
                            op("act", lambda e: e.activation(out=ta[3][:], in_=ta[3][:], func=AF.Exp, scale=-0.5), reads=[r_ta[3]], writes=[r_ta[3]])
                            op("dve", lambda e, h=h: e.tensor_tensor(out=ta[2][:], in0=oT[:, h, gs], in1=ta[3][:], op=ALU.mult), reads=rr + [r_ta[3]], writes=[r_ta[2]])
                            op("dve", lambda e, h=h: e.scalar_tensor_tensor(out=ogA[:, h, gs], in0=ta[0][:], scalar=ngc[:, 0:1], in1=ta[2][:], op0=ALU.mult, op1=ALU.mult),
                               reads=[r_ta[0], r_ngc, r_ta[2]], writes=[r_ogA[g]])
                        def att_geom(t):
                            j = g * 4 + t
                            k0 = max(0, j - 1); k1 = min(NT, j + 2)
                            return j, k0, k1, k1 - k0, (k1 - k0) * 128, (128 if j == 0 else 0)

                        def att_s1(idx):
                            t, h = idx // 8, idx % 8
                            j, k0, k1, nkb, nk, mc0 = att_geom(t)
                            m, hf = h % 4, h // 4
                            ps = slice(hf * 64, (hf + 1) * 64)
                            ib = idx % 2
                            pbs = 2 + ib
                            kgs = [r_kT[gg] for gg in range((k0 * 128) // 512, ((k1 * 128) - 1) // 512 + 1)]
                            op("pe", lambda e: e.matmul(PB[pbs][:, 0:nk], lhsT=qT[ps, m, t * 128:(t + 1) * 128], rhs=kTt[ps, k0 * 128:k1 * 128], start=True, stop=True),
                               reads=[r_qT] + kgs, writes=[rPB[pbs]])
                            op("dve", lambda e: e.tensor_tensor(out=msc[ib][:, 0:nk], in0=PB[pbs][:, 0:nk], in1=amask[:, mc0:mc0 + nk], op=ALU.add),
                               reads=[rPB[pbs], r_amask], writes=[r_msc[ib]])
                            op("dve", lambda e: e.reduce_max(out=sm[ib][:, 0:1], in_=msc[ib][:, 0:nk], axis=AX.X), reads=[r_msc[ib]], writes=[r_sm[ib]])
                            op("dve", lambda e: e.tensor_scalar(out=sm[ib][:, 1:2], in0=sm[ib][:, 0:1], scalar1=-0.125, scalar2=sinkb[:, 1, h:h + 1], op0=ALU.mult, op1=ALU.min),
                               reads=[r_sm[ib], r_sink], writes=[r_sm[ib]])
                            op("act", lambda e: e.activation(out=pex[ib][:, 0:nk], in_=msc[ib][:, 0:nk], func=AF.Exp, scale=0.125, bias=sm[ib][:, 1:2], accum_out=sm[ib][:, 2:3]),
                               reads=[r_msc[ib], r_sm[ib]], writes=[r_pex[ib], r_sm[ib]])
                            op("act", lambda e: e.activation(out=sm[ib][:, 3:4], in_=sinkb[:, 0, h:h + 1], func=AF.Exp, bias=sm[ib][:, 1:2], scale=1.0),
                               reads=[r_sm[ib], r_sink], writes=[r_sm[ib]])
                            op("dve", lambda e: e.tensor_tensor(out=sm[ib][:, 4:5], in0=sm[ib][:, 2:3], in1=sm[ib][:, 3:4], op=ALU.add), reads=[r_sm[ib]], writes=[r_sm[ib]])
                            op("dve", lambda e: e.reciprocal(out=sm[ib][:, 5:6], in_=sm[ib][:, 4:5]), reads=[r_sm[ib]], writes=[r_sm[ib]])

                        def att_s2(idx):
                            t, h = idx // 8, idx % 8
                            j, k0, k1, nkb, nk, mc0 = att_geom(t)
                            hf = h // 4
                            ib = idx % 2
                            pbt = 4 + ib
                            for b in range(nkb):
                                op("pe", lambda e, b=b: e.transpose(out=PB[pbt][:].bitcast(BF16)[:, b * 128:(b + 1) * 128], in_=pex[ib][:, b * 128:(b + 1) * 128], identity=identb[:]),
                                   reads=[r_pex[ib], r_identb], writes=[rPB[pbt]])
                            if ib == 0:
                                op("act", lambda e: e.activation(out=pT[ib][:, 0:nkb, :], in_=PB[pbt][:].bitcast(BF16)[:, 0:nk].rearrange("p (b q) -> p b q", q=128), func=AF.Copy),
                                   reads=[rPB[pbt]], writes=[r_pT[ib]])
                            else:
                                op("dve", lambda e: e.tensor_copy(out=pT[ib][:, 0:nkb, :], in_=PB[pbt][:].bitcast(BF16)[:, 0:nk].rearrange("p (b q) -> p b q", q=128)),
                                   reads=[rPB[pbt]], writes=[r_pT[ib]])
                            pbo = 6 + ib
                            for b in range(nkb):
                                op("pe", lambda e, b=b: e.matmul(PB[pbo][:, 0:64], lhsT=pT[ib][:, b, :], rhs=vtk[:, k0 + b, hf * 64:(hf + 1) * 64], start=(b == 0), stop=(b == nkb - 1)),
                                   reads=[r_pT[ib], r_v[k0 + b]], writes=[rPB[pbo]])
                            aot = ao[t % 2]; r_aot = r_ao[t % 2]
                            op("act", lambda e: e.activation(out=aot[:, h * 64:(h + 1) * 64], in_=PB[pbo][:, 0:64], func=AF.Identity, scale=sm[ib][:, 5:6]),
                               reads=[rPB[pbo], r_sm[ib]], writes=[r_aot])
                            if h == 7:
                                pba = t % 2
                                for m in range(4):
                                    op("pe", lambda e, m=m: e.transpose(out=PB[pba][:].bitcast(BF16)[:, m * 128:(m + 1) * 128], in_=aot[:, m * 128:(m + 1) * 128], identity=identb[:]),
                                       reads=[r_aot, r_identb], writes=[rPB[pba]])
                                op("dve", lambda e: e.tensor_copy(out=atA[:, :, j * 128:(j + 1) * 128], in_=PB[pba][:].bitcast(BF16)[:, 0:512].rearrange("p (m q) -> p m q", q=128)),
                                   reads=[rPB[pba]], writes=[r_atA[j]])

                        att_s1(0)
                        for idx in range(32):
                            if idx + 1 < 32:
                                att_s1(idx + 1)
                            att_s2(idx)
                T.barrier()

                with ExitStack() as pc:
                    if S >= 2048:
                        wG = oT[:].rearrange("p h s -> p (h s)").bitcast(BF16)
                        wGv = wG[:, 0:8 * 2048].rearrange("p (k n) -> p k n", n=2048)
                    else:
                        wGv = SB(pc, "wGt", [128, 8, 2048], BF16)[:]
                    r_wG = Res()
                    wR = SB(pc, "wR", [128, 4, D], BF16); wT = SB(pc, "wT", [128, 4, D], BF16); wO = SB(pc, "wO", [128, 8, D], BF16)
                    lnB = SB(pc, "lnB1", [128, 2, D], F32); r_lnB = Res()
                    for q4 in range(4):
                        dma("pool", ds_w, wGv[:, :, q4 * 512:(q4 + 1) * 512], w_in_k[:, :, O_GR + q4 * 512:O_GR + (q4 + 1) * 512], writes=[r_wG])
                    dma("pool", ds_w, wR[:], w_rec_k, writes=[r_wG])
                    dma("pool", ds_w, wT[:], w_att_k, writes=[r_wG])
                    for q2 in range(2):
                        dma("pool", ds_w, wO[:, :, q2 * 512:(q2 + 1) * 512], w_out_k[:, :, q2 * 512:(q2 + 1) * 512], writes=[r_wG])
                    dma("sp", ds_const, lnB[:, 0, :], ln1g_d[0:1, :].broadcast_to([128, D]), writes=[r_lnB])
                    dma("sp", ds_const, lnB[:, 1, :], ln1b_d[0:1, :].broadcast_to([128, D]), writes=[r_lnB])
                    zb = zer[:, 0:8].bitcast(BF16)[:, 0:8].unsqueeze(2)
                    dma("sp", ds_misc[3], u2T_d[s, :, 0:1].rearrange("(j p) o -> p j o", p=128), zb, reads=[r_zer], allow_slow_non_contiguous=True)
                    dma("sp", ds_misc[3], u2T_d[s, :, S + 1:S + 2].rearrange("(j p) o -> p j o", p=128), zb, reads=[r_zer], allow_slow_non_contiguous=True)
                    r_x1d = [Res() for _ in range(NT)]
                    r_u2d = [Res() for _ in range(NT)]
                    xs = [SB(pc, "bxs%d" % i, [128, D], F32) for i in range(4)]; r_xs = [Res() for _ in range(4)]
                    uT = SB(pc, "buT", [128, 8, 512], BF16); r_uT = Res()
                    mT = SB(pc, "mT", [128, 8, 512], BF16); r_mT = Res()
                    ta = [SB(pc, "bta%d" % i, [128, 512], F32) for i in range(4)]; r_ta = [Res() for _ in range(4)]
                    yt = [SB(pc, "yt%d" % i, [128, D], F32) for i in range(2)]; r_yt = [Res(), Res()]
                    st6 = [SB(pc, "st6%d" % i, [128, 20], F32) for i in range(2)]; r_st6 = [Res(), Res()]
                    u2t = [SB(pc, "u2t%d" % i, [128, 8, 128], BF16) for i in range(2)]; r_u2t = [Res(), Res()]
                    for g in range(NG):
                        gs = slice(g * 512, (g + 1) * 512)
                        for t in range(4):
                            tile_i = g * 4 + t
                            dma("sp", ds_x[t], xs[t][:], x_d[s, tile_i * 128:(tile_i + 1) * 128, :], writes=[r_xs[t]])
                            make_uT(xs[t], r_xs[t], uT[:, :, t * 128:(t + 1) * 128], r_uT, 1, 0)
                        BSETS = ((2, 3, 4, 5), (0, 1, 6, 7))

                        def dc_pe(dc):
                            b0, b1, b2, b3 = BSETS[dc % 2]
                            dcs = slice(dc * 128, (dc + 1) * 128)
                            for k in range(8):
                                op("pe", lambda e, k=k: e.matmul(PB[b0][:], lhsT=wGv[:, k, dc * 128:(dc + 1) * 128], rhs=uT[:, k, :], start=(k == 0), stop=(k == 7)),
                                   reads=[r_uT, r_wG], writes=[rPB[b0]])
                            for k in range(8):
                                op("pe", lambda e, k=k: e.matmul(PB[b1][:], lhsT=wGv[:, k, 1024 + dc * 128:1024 + (dc + 1) * 128], rhs=uT[:, k, :], start=(k == 0), stop=(k == 7)),
                                   reads=[r_uT, r_wG], writes=[rPB[b1]])
                            for h in range(4):
                                op("pe", lambda e, h=h: e.matmul(PB[b2][:], lhsT=wR[:, h, dcs], rhs=ogA[:, h, gs], start=(h == 0), stop=(h == 3)),
                                   reads=[r_wG, r_ogA[g]], writes=[rPB[b2]])
                            for m in range(4):
                                op("pe", lambda e, m=m: e.matmul(PB[b3][:], lhsT=wT[:, m, dcs], rhs=atA[:, m, gs], start=(m == 0), stop=(m == 3)),
                                   reads=[r_wG] + r_atA[g * 4:(g + 1) * 4], writes=[rPB[b3]])

                        def dc_ew(dc):
                            b0, b1, b2, b3 = BSETS[dc % 2]
                            a0, a1 = ta[2 * (dc % 2)], ta[2 * (dc % 2) + 1]
                            ra0, ra1 = r_ta[2 * (dc % 2)], r_ta[2 * (dc % 2) + 1]
                            op("act", lambda e: e.activation(out=a0[:], in_=PB[b0][:], func=AF.Sigmoid), reads=[rPB[b0]], writes=[ra0])
                            op("act", lambda e: e.activation(out=a1[:], in_=PB[b1][:], func=AF.Sigmoid), reads=[rPB[b1]], writes=[ra1])
                            op("dve", lambda e: e.tensor_tensor(out=a0[:], in0=PB[b2][:], in1=a0[:], op=ALU.mult), reads=[rPB[b2], ra0], writes=[ra0])
                            op("dve", lambda e: e.tensor_tensor(out=a1[:], in0=PB[b3][:], in1=a1[:], op=ALU.mult), reads=[rPB[b3], ra1], writes=[ra1])
                            op("dve", lambda e: e.tensor_tensor(out=mT[:, dc, :], in0=a0[:], in1=a1[:], op=ALU.add), reads=[ra0, ra1], writes=[r_mT])

                        dc_pe(0)
                        for dc in range(8):
                            if dc + 1 < 8:
                                dc_pe(dc + 1)
                            dc_ew(dc)
                        for t in range(4):
                            tile_i = g * 4 + t
                            ib = tile_i % 2
                            wb = 2 + 2 * (tile_i % 2)
                            for hh in range(2):
                                for k in range(8):
                                    op("pe", lambda e, k=k, hh=hh, t=t, wb=wb: e.matmul(PB[wb + hh][:], lhsT=mT[:, k, t * 128:(t + 1) * 128], rhs=wO[:, k, hh * 512:(hh + 1) * 512], start=(k == 0), stop=(k == 7)),
                                       reads=[r_mT, r_wG], writes=[rPB[wb + hh]])
                            ln_epilogue((wb, wb + 1), (rPB[wb], rPB[wb + 1]), xs[t][:], r_xs[t], 0, lnB[:, 0, :], lnB[:, 1, :], r_lnB, yt[ib], r_yt[ib], st6[ib], r_st6[ib])
                            dma("sp", ds_o[ib], x1_d[s, tile_i * 128:(tile_i + 1) * 128, :], yt[ib][:], reads=[r_yt[ib]], writes=[r_x1d[tile_i]])
                            make_uT(yt[ib], r_yt[ib], u2t[ib][:], r_u2t[ib], 3, 2, banks=(6, 7))
                            dma("sp", ds_o[2 + ib], u2T_d[s, :, 1 + tile_i * 128:1 + (tile_i + 1) * 128].rearrange("(j p) q -> p j q", p=128), u2t[ib][:],
                                reads=[r_u2t[ib]], writes=[r_u2d[tile_i]])
                T.barrier()
            T.barrier()

            with ExitStack() as pf:
                wU = SB(pf, "wU", [128, 8, 2 * DFF], BF16); wD = SB(pf, "wD", [128, NFC, D], BF16); r_wU = Res()
                lnB = SB(pf, "lnB2", [128, 2, D], F32); r_lnB = Res()
                cvw = SB(pf, "cvw", [128, 4, 2 * NFC], F32); r_cvw = r_lnB
                r_wc = [Res() for _ in range(11)]
                for jq in range(11):
                    dma("pool", ds_wc[jq], wU[:, :, jq * 256:(jq + 1) * 256], w_up_k[:, :, jq * 256:(jq + 1) * 256], writes=[r_wc[jq]])
                    dma("pool", ds_wc[jq], wU[:, :, DFF + jq * 256:DFF + (jq + 1) * 256], w_up_k[:, :, DFF + jq * 256:DFF + (jq + 1) * 256], writes=[r_wc[jq]])
                    dma("pool", ds_wc[jq], wD[:, 2 * jq:2 * jq + 2, :], w_dn_k[:, 2 * jq:2 * jq + 2, :], writes=[r_wc[jq]])
                dma("sp", ds_const, lnB[:, 0, :], ln2g_d[0:1, :].broadcast_to([128, D]), writes=[r_lnB])
                dma("sp", ds_const, lnB[:, 1, :], ln2b_d[0:1, :].broadcast_to([128, D]), writes=[r_lnB])
                dma("sp", ds_const, cvw[:, 0:3, :], cw_d.rearrange("k (j p) -> p k j", p=128), writes=[r_cvw], allow_slow_non_contiguous=True)
                dma("sp", ds_const, cvw[:, 3, :], cb_d.rearrange("o (j p) -> p (o j)", p=128), writes=[r_cvw], allow_slow_non_contiguous=True)
                NSG = S // 256
                u2 = [SB(pf, "fu2%d" % i, [128, 8, 258], BF16) for i in range(2)]; r_u2 = [Res(), Res()]
                x1t = [SB(pf, "fx1%d" % i, [128, D], F32) for i in range(4)]; r_x1t = [Res() for _ in range(4)]
                cv = [SB(pf, "fcv%d" % i, [128, 256], F32) for i in range(2)]; r_cv = [Res(), Res()]
                cgt = [SB(pf, "fcg%d" % i, [128, 256], F32) for i in range(2)]; r_cg = [Res(), Res()]
                gz = [SB(pf, "fgz%d" % i, [128, 256], F32) for i in range(2)]; r_gz = [Res(), Res()]
                actb = [SB(pf, "fact%d" % i, [128, 256], BF16) for i in range(2)]; r_act = [Res(), Res()]
                yt = [SB(pf, "fyt%d" % i, [128, D], F32) for i in range(2)]; r_yt = [Res(), Res()]
                st6 = [SB(pf, "fst6%d" % i, [128, 20], F32) for i in range(2)]; r_st6 = [Res(), Res()]
                def c2_up(sg, c):
                    ub = sg % 2; ib = c % 2
                    pv, pg = 4 + 2 * ib, 5 + 2 * ib
                    for (pbx, cb) in ((pv, c * 128), (pg, DFF + c * 128)):
                        for k in range(8):
                            op("pe", lambda e, k=k, pbx=pbx, cb=cb: e.matmul(PB[pbx][:, 0:258], lhsT=wU[:, k, cb:cb + 128], rhs=u2[ub][:, k, :], start=(k == 0), stop=(k == 7)),
                               reads=[r_wc[c // 2], r_u2[ub]], writes=[rPB[pbx]])

                def c2_s1(c):
                    ib = c % 2
                    pv, pg = 4 + 2 * ib, 5 + 2 * ib
                    for (pbx, dst, r_dst, col) in ((pv, cv[ib], r_cv[ib], c), (pg, cgt[ib], r_cg[ib], NFC + c)):
                        op("act", lambda e, pbx=pbx, dst=dst, col=col: e.activation(out=dst[:], in_=PB[pbx][:, 0:256], func=AF.Identity, scale=cvw[:, 0, col:col + 1], bias=cvw[:, 3, col:col + 1]),
                           reads=[rPB[pbx], r_cvw], writes=[r_dst])

                def c2_s2(c):
                    ib = c % 2
                    pv, pg = 4 + 2 * ib, 5 + 2 * ib
                    for (pbx, dst, r_dst, col) in ((pg, cgt[ib], r_cg[ib], NFC + c), (pv, cv[ib], r_cv[ib], c)):
                        op("dve", lambda e, pbx=pbx, dst=dst, col=col: e.scalar_tensor_tensor(out=dst[:], in0=PB[pbx][:, 1:257], scalar=cvw[:, 1, col:col + 1], in1=dst[:], op0=ALU.mult, op1=ALU.add),
                           reads=[rPB[pbx], r_cvw, r_dst], writes=[r_dst])
                        op("dve", lambda e, pbx=pbx, dst=dst, col=col: e.scalar_tensor_tensor(out=dst[:], in0=PB[pbx][:, 2:258], scalar=cvw[:, 2, col:col + 1], in1=dst[:], op0=ALU.mult, op1=ALU.add),
                           reads=[rPB[pbx], r_cvw, r_dst], writes=[r_dst])

                def c2_s3(c):
                    ib = c % 2
                    op("act", lambda e: e.activation(out=gz[ib][:], in_=cgt[ib][:], func=AF.Gelu_apprx_tanh), reads=[r_cg[ib]], writes=[r_gz[ib]])

                def c2_s4(c):
                    ib = c % 2
                    op("dve", lambda e: e.tensor_tensor(out=actb[ib][:], in0=gz[ib][:], in1=cv[ib][:], op=ALU.mult), reads=[r_gz[ib], r_cv[ib]], writes=[r_act[ib]])

                def c2_down(c):
                    ib = c % 2
                    for tt in range(2):
                        for hh in range(2):
                            pa_ = tt * 2 + hh
                            op("pe", lambda e, tt=tt, hh=hh, pa_=pa_: e.matmul(PB[pa_][:], lhsT=actb[ib][:, tt * 128:(tt + 1) * 128], rhs=wD[:, c, hh * 512:(hh + 1) * 512],
                                                                             start=(c == 0), stop=(c == NFC - 1)),
                               reads=[r_act[ib], r_wc[c // 2]], writes=[rPB[pa_]])

                def c2_load(sg):
                    a = sg * 256
                    ub = sg % 2
                    dma("sp", ds_misc[ub], u2[ub][:], u2T_d[s, :, a:a + 258].rearrange("(j p) q -> p j q", p=128), writes=[r_u2[ub]])
                    for tt in range(2):
                        sl = (sg * 2 + tt) % 4
                        dma("sp", ds_x[sl], x1t[sl][:], x1_d[s, a + tt * 128:a + (tt + 1) * 128, :], writes=[r_x1t[sl]])

                def c2_prologue(sg):
                    c2_up(sg, 0); c2_up(sg, 1); c2_s1(0); c2_s2(0); c2_s1(1)

                def c2_ln2(sg):
                    for tt in range(2):
                        tile_i = sg * 2 + tt
                        sl = tile_i % 4
                        ib = tile_i % 2
                        ln_part2(x1t[sl][:], r_x1t[sl], lnB[:, 0, :], lnB[:, 1, :], r_lnB, yt[ib], r_yt[ib], st6[ib], r_st6[ib])
                        dma("sp", ds_o[ib], out_d[s, tile_i * 128:(tile_i + 1) * 128, :], yt[ib][:], reads=[r_yt[ib]])

                c2_load(0)
                if NSG > 1:
                    c2_load(1)
                c2_prologue(0)
                for sg in range(NSG):
                    for c in range(NFC):
                        c2_s3(c)
                        if c + 1 < NFC:
                            c2_s2(c + 1)
                        c2_s4(c)
                        if c + 2 < NFC:
                            c2_up(sg, c + 2)
                            c2_s1(c + 2)
                        c2_down(c)
                    for tt in range(2):
                        tile_i = sg * 2 + tt
                        ib = tile_i % 2
                        ln_part1((tt * 2, tt * 2 + 1), (rPB[tt * 2], rPB[tt * 2 + 1]), 1, yt[ib], r_yt[ib])
                    if sg + 1 < NSG:
                        c2_prologue(sg + 1)
                    c2_ln2(sg)
                    if sg + 2 < NSG:
                        c2_load(sg + 2)
            T.barrier()
        print("build: instructions=%d waits=%d" % (T.n_ins, T.n_wait))
    return nc


def _consts():
    ident = np.eye(128, dtype=np.float32)
    i = np.arange(64)
    maskf = (i[:, None] <= i[None, :]).astype(np.uint8)
    maskb = (i[:, None] >= i[None, :]).astype(np.uint8)
    q = np.arange(128)[:, None]
    cc = np.arange(128)[None, :]
    NEG = -30000.0
    am = np.zeros((128, 384), np.float32)
    am[:, 0:128] = np.where(cc >= q, 0.0, NEG)
    am[:, 256:384] = np.where(cc <= q, 0.0, NEG)
    half = 32
    inv = (np.float32(10000.0) ** (-np.arange(half, dtype=np.float32) / np.float32(half))).astype(np.float32)
    p = np.arange(128)
    rope = np.stack([inv[p % 32], np.where((p % 64) < 32, -1.0, 1.0).astype(np.float32)], axis=1).astype(np.float32)
    return dict(k_ident=ident, k_maskf=maskf, k_maskb=maskb, k_amask=am, k_rope=rope)


def make_in_maps(inputs, n_cores, nseq):
    f = lambda a: np.ascontiguousarray(np.asarray(a))
    shared = dict(
        w_ada=f(inputs["w_ada"][0]), b_ada=f(inputs["b_ada"][0:1]), w_in=f(inputs["w_in"][0]),
        rec_lower_bound=f(inputs["rec_lower_bound"]), rec_norm_g=f(inputs["rec_norm_g"][0:1]),
        w_rec_branch=f(inputs["w_rec_branch"][0]), attn_sink=f(inputs["attn_sink"][0:1]),
        w_attn_branch=f(inputs["w_attn_branch"][0]), w_out=f(inputs["w_out"][0]),
        ln1_g=f(inputs["ln1_g"][0:1]), ln1_b=f(inputs["ln1_b"][0:1]), w_up=f(inputs["w_up"][0]),
        conv_w=f(inputs["conv_w"][0]), conv_b=f(inputs["conv_b"][0:1]), w_down=f(inputs["w_down"][0]),
        ln2_g=f(inputs["ln2_g"][0:1]), ln2_b=f(inputs["ln2_b"][0:1]))
    shared.update(_consts())
    maps = []
    for i in range(n_cores):
        m = dict(shared)
        m["x"] = f(inputs["x"][i * nseq:(i + 1) * nseq])
        m["c"] = f(inputs["c"][i * nseq:(i + 1) * nseq])
        m["positions"] = f(inputs["positions"][i * nseq:(i + 1) * nseq]).astype(np.int32)
        maps.append(m)
    return maps


def kernel(**inputs):
    B, S, _ = inputs["x"].shape
    nseq = B // N_CORES
    nc = build_nc(S=S, NSEQ=nseq)
    in_maps = make_in_maps(inputs, N_CORES, nseq)
    res = run_bass_kernel_spmd(nc, in_maps, core_ids=list(range(N_CORES)))
    return np.concatenate([np.asarray(r["out"]) for r in res.results], axis=0).astype(np.float32)
```

```python
import numpy as np
from contextlib import ExitStack
import concourse.bass as bass
import concourse.mybir as mybir
from concourse.bass_utils import run_bass_kernel_spmd

F32 = mybir.dt.float32
BF16 = mybir.dt.bfloat16
I32 = mybir.dt.int32
U8 = mybir.dt.uint8
AF = mybir.ActivationFunctionType
ALU = mybir.AluOpType
AX = mybir.AxisListType

D = 1024
NK = 8
DIN = 5376
DFF = 2816
NFC = DFF // 128
ALPHA = 2.0 ** 0.25
LN_EPS = 1e-5
RMS_EPS = 1e-6
N_CORES = 8
O_RQ, O_RFF, O_RFB, O_RI, O_RG, O_AQ, O_AK, O_AV, O_GR, O_GA = 0, 512, 1024, 1536, 2048, 2560, 3072, 3200, 3328, 4352


class Res:
    __slots__ = ("w", "rd")

    def __init__(self):
        self.w = None
        self.rd = []


class Tracker:
    def __init__(self, nc, stack):
        self.nc = nc
        self.stack = stack
        self.eng = {}
        self.sems = {}
        for name, h in (("pe", nc.tensor), ("act", nc.scalar), ("dve", nc.vector),
                        ("pool", nc.gpsimd), ("sp", nc.sync)):
            sem = stack.enter_context(nc.semaphore("sem_" + name))
            self.eng[name] = {"h": h, "sem": sem, "cnt": 0, "seen": {}, "name": name}
            self.sems[id(sem)] = sem
        self.dma_sems = []
        self.n_wait = 0
        self.n_ins = 0

    def new_dma_sem(self, name):
        sem = self.stack.enter_context(self.nc.semaphore(name))
        d = {"sem": sem, "cnt": 0}
        self.sems[id(sem)] = sem
        self.dma_sems.append(d)
        return d

    def _wait_deps(self, E, reads, writes):
        deps = {}

        def add(tok):
            if tok is not None and deps.get(tok[0], 0) < tok[1]:
                deps[tok[0]] = tok[1]

        for r in reads:
            add(r.w)
        for w in writes:
            add(w.w)
            for t in w.rd:
                add(t)
        own = id(E["sem"])
        for k, v in deps.items():
            if E["name"] == "pe" and k == own:
                continue
            if E["seen"].get(k, 0) >= v:
                continue
            E["h"].wait_ge(self.sems[k], v)
            E["seen"][k] = v
            self.n_wait += 1

    def _commit(self, tok, reads, writes):
        for r in reads:
            r.rd.append(tok)
            if len(r.rd) > 48:
                best = {}
                for k, v in r.rd:
                    if best.get(k, 0) < v:
                        best[k] = v
                r.rd = list(best.items())
        for w in writes:
            w.w = tok
            w.rd = []
        self.n_ins += 1

    def op(self, eng, fn, reads=(), writes=()):
        E = self.eng[eng]
        self._wait_deps(E, reads, writes)
        ins = fn(E["h"])
        E["cnt"] += 1
        ins.then_inc(E["sem"], 1)
        self._commit((id(E["sem"]), E["cnt"]), reads, writes)
        return ins

    def dma(self, q, dsem, out, in_, reads=(), writes=(), **kw):
        E = self.eng[q]
        self._wait_deps(E, reads, writes)
        ins = E["h"].dma_start(out=out, in_=in_, **kw)
        dsem["cnt"] += 16
        ins.then_inc(dsem["sem"], 16)
        self._commit((id(dsem["sem"]), dsem["cnt"]), reads, writes)
        return ins

    def barrier(self):
        targets = [(id(e["sem"]), e["cnt"]) for e in self.eng.values() if e["cnt"] > 0]
        targets += [(id(d["sem"]), d["cnt"]) for d in self.dma_sems if d["cnt"] > 0]
        for E in self.eng.values():
            for k, v in targets:
                if E["seen"].get(k, 0) >= v:
                    continue
                E["h"].wait_ge(self.sems[k], v)
                E["seen"][k] = v
                self.n_wait += 1


class Rot:
    def __init__(self, items):
        self.items = items
        self.i = 0

    def next(self):
        it = self.items[self.i % len(self.items)]
        self.i += 1
        return it


def build_nc(S=2048, NSEQ=2, dbg=False):
    NT, NG, NCH = S // 128, S // 512, S // 64
    nc = bass.Bass("TRN2", target_bir_lowering=False)

    def din(name, shape, dt=F32):
        return nc.dram_tensor(name, list(shape), dt, kind="ExternalInput").ap()

    x_d = din("x", [NSEQ, S, D])
    c_d = din("c", [NSEQ, D])
    pos_d = din("positions", [NSEQ, S], I32)
    w_ada_d = din("w_ada", [D, 6 * D])
    b_ada_d = din("b_ada", [1, 6 * D])
    w_in_d = din("w_in", [D, DIN])
    lbr_d = din("rec_lower_bound", [2, 2, 512])
    ng_d = din("rec_norm_g", [1, 128])
    w_rec_d = din("w_rec_branch", [512, D])
    sink_d = din("attn_sink", [1, 8])
    w_att_d = din("w_attn_branch", [512, D])
    w_out_d = din("w_out", [D, D])
    ln1g_d = din("ln1_g", [1, D]); ln1b_d = din("ln1_b", [1, D])
    w_up_d = din("w_up", [D, 2 * DFF])
    cw_d = din("conv_w", [3, 2 * DFF])
    cb_d = din("conv_b", [1, 2 * DFF])
    w_dn_d = din("w_down", [DFF, D])
    ln2g_d = din("ln2_g", [1, D]); ln2b_d = din("ln2_b", [1, D])
    identf_d = din("k_ident", [128, 128])
    mskf_d = din("k_maskf", [64, 64], U8)
    mskb_d = din("k_maskb", [64, 64], U8)
    amask_d = din("k_amask", [128, 384])
    rope_d = din("k_rope", [128, 2])
    out_d = nc.dram_tensor("out", [NSEQ, S, D], F32, kind="ExternalOutput").ap()
    x1_d = nc.dram_tensor("x1_scr", [NSEQ, S, D], F32).ap()
    u2T_d = nc.dram_tensor("u2T_scr", [NSEQ, D, S + 2], BF16).ap()
    dbg_d = {}
    if dbg:
        dbg_d["mods"] = nc.dram_tensor("dbg_mods", [128, 4, 8], F32, kind="ExternalOutput").ap()
        dbg_d["QT"] = nc.dram_tensor("dbg_QT", [2, 128, 2, S], BF16, kind="ExternalOutput").ap()
        dbg_d["KT"] = nc.dram_tensor("dbg_KT", [2, 128, 2, S], BF16, kind="ExternalOutput").ap()
        dbg_d["V"] = nc.dram_tensor("dbg_V", [64, NCH, 256], BF16, kind="ExternalOutput").ap()
        dbg_d["oT"] = nc.dram_tensor("dbg_oT", [128, 4, S], F32, kind="ExternalOutput").ap()
        dbg_d["kT"] = nc.dram_tensor("dbg_kT", [128, S], BF16, kind="ExternalOutput").ap()
        dbg_d["x1"] = x1_d

    w_in_k = w_in_d.rearrange("(j p) n -> p j n", p=128)
    w_ada_k = w_ada_d.rearrange("(j p) n -> p j n", p=128)
    w_up_k = w_up_d.rearrange("(j p) n -> p j n", p=128)
    w_out_k = w_out_d.rearrange("(j p) n -> p j n", p=128)
    w_rec_k = w_rec_d.rearrange("(j p) n -> p j n", p=128)
    w_att_k = w_att_d.rearrange("(j p) n -> p j n", p=128)
    w_dn_k = w_dn_d.rearrange("(j p) n -> p j n", p=128)

    with ExitStack() as top:
        T = Tracker(nc, top)
        op, dma = T.op, T.dma

        uniq = [0]

        def SB(st, name, shape, dt):
            uniq[0] += 1
            return st.enter_context(nc.sbuf_tensor("%s_%d" % (name, uniq[0]), list(shape), dt))

        PB = [top.enter_context(nc.psum_tensor("pb%d" % i, [128, 512], F32)) for i in range(8)]
        rPB = [Res() for _ in range(8)]
        ds_const = T.new_dma_sem("ds_const")
        ds_w = T.new_dma_sem("ds_w")
        ds_x = [T.new_dma_sem("ds_x%d" % i) for i in range(4)]
        ds_o = [T.new_dma_sem("ds_o%d" % i) for i in range(4)]
        ds_misc = [T.new_dma_sem("ds_m%d" % i) for i in range(4)]
        ds_dbg = T.new_dma_sem("ds_dbg")
        ds_wc = [T.new_dma_sem("ds_wc%d" % i) for i in range(11)]

        identf = SB(top, "identf", [128, 128], F32); r_identf = Res()
        identb = SB(top, "identb", [128, 128], BF16); r_identb = Res()
        onesf = SB(top, "onesf", [128, 128], F32); r_onesf = Res()
        maskf = SB(top, "maskf", [64, 64], U8); maskb = SB(top, "maskb", [64, 64], U8); r_mask = Res()
        zer = SB(top, "zer", [128, 128], F32); r_zer = Res()
        rmask = SB(top, "rmask", [128, 512], F32); r_rmask = Res()
        lbt = SB(top, "lbt", [128, 3, 2, 4], F32); r_lbt = Res()
        lraw = SB(top, "lraw", [128, 2, 2, 4], F32); r_lraw = Res()
        ngc = SB(top, "ngc", [128, 1], F32); r_ngc = Res()
        sinkb = SB(top, "sinkb", [128, 2, 8], F32); r_sink = Res()
        rope = SB(top, "rope", [128, 2], F32); r_rope = Res()
        epsc = SB(top, "epsc", [128, 2], F32); r_eps = Res()
        modsT = SB(top, "modsT", [128, 4, 8], F32); r_modsT = Res()
        gaB = SB(top, "gaB", [128, 2, D], F32); r_gaB = [Res(), Res()]
        amask = SB(top, "amask", [128, 384], F32); r_amask = Res()

        dma("sp", ds_const, identf[:], identf_d, writes=[r_identf])
        dma("sp", ds_const, maskf[:], mskf_d, writes=[r_mask])
        dma("sp", ds_const, maskb[:], mskb_d, writes=[r_mask])
        dma("sp", ds_const, rope[:], rope_d, writes=[r_rope])
        dma("sp", ds_const, amask[:], amask_d, writes=[r_amask])
        dma("sp", ds_const, ngc[:], ng_d.rearrange("o p -> p o"), writes=[r_ngc], allow_slow_non_contiguous=True)
        dma("sp", ds_const, sinkb[:, 0, :], sink_d[0:1, :].broadcast_to([128, 8]), writes=[r_sink])
        dma("sp", ds_const, lraw[:].rearrange("p a b h -> p (a b) h"),
            lbr_d.rearrange("a b (h p) -> p (a b) h", p=128), writes=[r_lraw], allow_slow_non_contiguous=True)
        T.barrier()
        op("dve", lambda e: e.tensor_copy(out=identb[:], in_=identf[:]), reads=[r_identf], writes=[r_identb])
        op("pool", lambda e: e.memset(onesf[:], 1.0), writes=[r_onesf])
        op("pool", lambda e: e.memset(zer[:], 0.0), writes=[r_zer])
        op("pool", lambda e: e.memset(rmask[:], 1.0), writes=[r_rmask])
        op("pool", lambda e: e.memset(rmask[:].rearrange("p (c t) -> p c t", t=64)[:, :, 0:1], 0.0), writes=[r_rmask])
        op("pool", lambda e: e.memset(epsc[:, 0:1], LN_EPS), writes=[r_eps])
        op("pool", lambda e: e.memset(epsc[:, 1:2], RMS_EPS), writes=[r_eps])
        op("dve", lambda e: e.tensor_scalar(out=sinkb[:, 1, :], in0=sinkb[:, 0, :], scalar1=-1.0, scalar2=None, op0=ALU.mult),
           reads=[r_sink], writes=[r_sink])
        op("dve", lambda e: e.tensor_tensor(out=lbt[:, 2, :, :], in0=lraw[:, :, 0, :], in1=lraw[:, :, 1, :], op=ALU.subtract),
           reads=[r_lraw], writes=[r_lbt])
        op("act", lambda e: e.activation(out=lbt[:, 0, :, :], in_=lbt[:, 2, :, :], func=AF.Sigmoid), reads=[r_lbt], writes=[r_lbt])
        op("act", lambda e: e.activation(out=lbt[:, 1, :, :], in_=lbt[:, 2, :, :], func=AF.Sigmoid, scale=-1.0), reads=[r_lbt], writes=[r_lbt])
        op("dve", lambda e: e.tensor_scalar(out=lbt[:, 2, :, :], in0=lbt[:, 1, :, :], scalar1=-1.0, scalar2=None, op0=ALU.mult),
           reads=[r_lbt], writes=[r_lbt])

        def make_uT(xt, r_xt, dst, r_dst, scol, bcol, banks=(0, 1)):
            for hb in range(2):
                b = banks[hb]
                for i in range(4):
                    k = hb * 4 + i
                    op("pe", lambda e, k=k, i=i, b=b: e.transpose(out=PB[b][:, i * 128:(i + 1) * 128], in_=xt[:, k * 128:(k + 1) * 128],
                                                                   identity=identf[:]),
                       reads=[r_xt, r_identf], writes=[rPB[b]])
                for i in range(4):
                    k = hb * 4 + i
                    if i % 2 == 0:
                        op("act", lambda e, k=k, i=i, b=b: e.activation(out=dst[:, k, :], in_=PB[b][:, i * 128:(i + 1) * 128], func=AF.Identity,
                                                                         scale=modsT[:, scol, k:k + 1], bias=modsT[:, bcol, k:k + 1]),
                           reads=[rPB[b], r_modsT], writes=[r_dst])
                    else:
                        op("dve", lambda e, k=k, i=i, b=b: e.tensor_scalar(out=dst[:, k, :], in0=PB[b][:, i * 128:(i + 1) * 128],
                                                                            scalar1=modsT[:, scol, k:k + 1], scalar2=modsT[:, bcol, k:k + 1],
                                                                            op0=ALU.mult, op1=ALU.add),
                           reads=[rPB[b], r_modsT], writes=[r_dst])

        def ln_part1(pz, r_pz, which, y, r_y):
            for h in range(2):
                op("dve", lambda e, h=h: e.tensor_tensor(out=y[:, h * 512:(h + 1) * 512], in0=PB[pz[h]][:], in1=gaB[:, which, h * 512:(h + 1) * 512], op=ALU.mult),
                   reads=[r_pz[h], r_gaB[which]], writes=[r_y])

        def ln_part2(xres, r_xres, gB, bB, r_gb, y, r_y, st6, r_st):
            op("dve", lambda e: e.scalar_tensor_tensor(out=y[:], in0=xres, scalar=ALPHA, in1=y[:], op0=ALU.mult, op1=ALU.add),
               reads=[r_xres, r_y], writes=[r_y])
            for h in range(2):
                op("dve", lambda e, h=h: e.bn_stats(out=st6[:, h * 6:(h + 1) * 6], in_=y[:, h * 512:(h + 1) * 512]), reads=[r_y], writes=[r_st])
            op("dve", lambda e: e.bn_aggr(out=st6[:, 12:14], in_=st6[:, 0:12]), reads=[r_st], writes=[r_st])
            op("act", lambda e: e.activation(out=st6[:, 14:15], in_=st6[:, 13:14], func=AF.Ln, bias=epsc[:, 0:1], scale=1.0), reads=[r_st, r_eps], writes=[r_st])
            op("act", lambda e: e.activation(out=st6[:, 15:16], in_=st6[:, 14:15], func=AF.Exp, scale=-0.5), reads=[r_st], writes=[r_st])
            op("dve", lambda e: e.tensor_scalar(out=st6[:, 16:17], in0=st6[:, 12:13], scalar1=st6[:, 15:16], scalar2=-1.0, op0=ALU.mult, op1=ALU.mult),
               reads=[r_st], writes=[r_st])
            op("act", lambda e: e.activation(out=y[:], in_=y[:], func=AF.Identity, scale=st6[:, 15:16], bias=st6[:, 16:17]),
               reads=[r_y, r_st], writes=[r_y])
            op("dve", lambda e: e.tensor_tensor(out=y[:], in0=y[:], in1=gB, op=ALU.mult), reads=[r_y, r_gb], writes=[r_y])
            op("dve", lambda e: e.tensor_tensor(out=y[:], in0=y[:], in1=bB, op=ALU.add), reads=[r_y, r_gb], writes=[r_y])

        def ln_epilogue(pz, r_pz, xres, r_xres, which, gB, bB, r_gb, y, r_y, st6, r_st):
            ln_part1(pz, r_pz, which, y, r_y)
            ln_part2(xres, r_xres, gB, bB, r_gb, y, r_y, st6, r_st)

        for s in range(NSEQ):
            with ExitStack() as ph:
                cT = SB(ph, "cT", [128, 8], F32); r_cT = Res()
                cbt = SB(ph, "cbt", [128, 8, 128], F32); r_cbt = Res()
                wst = [SB(ph, "wst%d" % i, [128, 8, 512], F32) for i in range(2)]; r_wst = [Res(), Res()]
                bst = [SB(ph, "bst%d" % i, [1, 512], F32) for i in range(2)]; r_bst = [Res(), Res()]
                mtmp = SB(ph, "mtmp", [128, 512], F32); r_mtmp = Res()
                dma("sp", ds_misc[0], cT[:], c_d[s].rearrange("(j p) -> p j", p=128), writes=[r_cT], allow_slow_non_contiguous=True)
                op("act", lambda e: e.activation(out=cT[:], in_=cT[:], func=AF.Silu), reads=[r_cT], writes=[r_cT])
                op("dve", lambda e: e.tensor_copy(out=cbt[:], in_=cT[:].unsqueeze(2).broadcast_to([128, 8, 128])), reads=[r_cT], writes=[r_cbt])
                for cg in range(12):
                    b2 = cg % 2
                    dma("sp", ds_misc[1 + b2], wst[b2][:], w_ada_k[:, :, cg * 512:(cg + 1) * 512], writes=[r_wst[b2]])
                    dma("sp", ds_o[b2], bst[b2][:], b_ada_d[0:1, cg * 512:(cg + 1) * 512], writes=[r_bst[b2]])
                    pb = 2 + b2
                    for k in range(8):
                        op("pe", lambda e, k=k: e.matmul(PB[pb][:], lhsT=cbt[:, k, :], rhs=wst[b2][:, k, :], start=(k == 0), stop=False),
                           reads=[r_cbt, r_wst[b2]], writes=[rPB[pb]])
                    op("pe", lambda e: e.matmul(PB[pb][:], lhsT=onesf[0:1, :], rhs=bst[b2][0:1, :], start=False, stop=True),
                       reads=[r_onesf, r_bst[b2]], writes=[rPB[pb]])
                    which, half = cg // 2, cg % 2
                    if which in (2, 5):
                        gi = 0 if which == 2 else 1
                        op("dve", lambda e: e.tensor_scalar(out=gaB[:, gi, half * 512:(half + 1) * 512], in0=PB[pb][:], scalar1=1.0, scalar2=None, op0=ALU.add),
                           reads=[rPB[pb]], writes=[r_gaB[gi]])
                    else:
                        widx = {0: 0, 1: 1, 3: 2, 4: 3}[which]
                        addc = 1.0 if which in (1, 4) else 0.0
                        op("dve", lambda e: e.tensor_scalar(out=mtmp[:], in0=PB[pb][:], scalar1=addc, scalar2=None, op0=ALU.add),
                           reads=[rPB[pb]], writes=[r_mtmp])
                        for i in range(4):
                            op("pe", lambda e, i=i: e.transpose(out=PB[4][:, i * 128:(i + 1) * 128], in_=mtmp[:, i * 128:(i + 1) * 128], identity=identf[:]),
                               reads=[r_mtmp, r_identf], writes=[rPB[4]])
                        op("dve", lambda e: e.tensor_copy(out=modsT[:, widx, half * 4:(half + 1) * 4],
                                                          in_=PB[4][:].rearrange("p (i m) -> p i m", m=128)[:, :, 0]),
                           reads=[rPB[4]], writes=[r_modsT])
                if dbg and s == 0:
                    dma("sp", ds_dbg, dbg_d["mods"], modsT[:], reads=[r_modsT])
            T.barrier()

            with ExitStack() as pab:
                oT = SB(pab, "oT", [128, 4, S], F32)
                r_oT = [[Res() for _ in range(NCH)] for _ in range(4)]
                kTt = SB(pab, "kTt", [128, S], BF16); r_kT = [Res() for _ in range(NG)]
                vtk = SB(pab, "vtk", [128, NT, 128], BF16); r_v = [Res() for _ in range(NT)]
                cosT = SB(pab, "cosT", [128, S], F32); sinT = SB(pab, "sinT", [128, S], F32); r_cs = Res()
                with ExitStack() as pr:
                    posi = SB(pr, "posi", [128, S], I32); r_posi = Res()
                    ang = SB(pr, "ang", [128, S], F32); r_ang = Res()
                    kq = SB(pr, "kq", [128, S], F32); r_kq = Res()
                    ki = SB(pr, "ki", [128, S], I32); r_ki = Res()
                    dma("sp", ds_misc[0], posi[:], pos_d[s:s + 1, :].broadcast_to([128, S]), writes=[r_posi])
                    op("dve", lambda e: e.tensor_copy(out=ang[:], in_=posi[:]), reads=[r_posi], writes=[r_ang])
                    op("dve", lambda e: e.tensor_scalar(out=ang[:], in0=ang[:], scalar1=rope[:, 0:1], scalar2=None, op0=ALU.mult),
                       reads=[r_ang, r_rope], writes=[r_ang])
                    TWO_PI = 6.283185307179586
                    C1 = 6.28125
                    C2 = TWO_PI - C1
                    for (dst, shift) in ((sinT, 0.0), (cosT, np.pi / 2)):
                        op("dve", lambda e, shift=shift: e.tensor_scalar(out=kq[:], in0=ang[:], scalar1=float(shift), scalar2=1.0 / TWO_PI, op0=ALU.add, op1=ALU.mult),
                           reads=[r_ang], writes=[r_kq])
                        op("dve", lambda e: e.tensor_copy(out=ki[:], in_=kq[:]), reads=[r_kq], writes=[r_ki])
                        op("dve", lambda e: e.tensor_copy(out=kq[:], in_=ki[:]), reads=[r_ki], writes=[r_kq])
                        op("dve", lambda e, shift=shift, dst=dst: e.tensor_scalar(out=dst[:], in0=ang[:], scalar1=float(shift), scalar2=None, op0=ALU.add),
                           reads=[r_ang], writes=[r_cs])
                        op("dve", lambda e, dst=dst: e.scalar_tensor_tensor(out=dst[:], in0=kq[:], scalar=-C1, in1=dst[:], op0=ALU.mult, op1=ALU.add),
                           reads=[r_kq, r_cs], writes=[r_cs])
                        op("dve", lambda e, dst=dst: e.scalar_tensor_tensor(out=dst[:], in0=kq[:], scalar=-C2, in1=dst[:], op0=ALU.mult, op1=ALU.add),
                           reads=[r_kq, r_cs], writes=[r_cs])
                        op("dve", lambda e, dst=dst: e.tensor_scalar(out=kq[:], in0=dst[:], scalar1=float(np.pi), scalar2=-TWO_PI, op0=ALU.is_gt, op1=ALU.mult),
                           reads=[r_cs], writes=[r_kq])
                        op("dve", lambda e, dst=dst: e.tensor_tensor(out=dst[:], in0=dst[:], in1=kq[:], op=ALU.add), reads=[r_cs, r_kq], writes=[r_cs])
                        op("dve", lambda e, dst=dst: e.tensor_scalar(out=kq[:], in0=dst[:], scalar1=float(-np.pi), scalar2=TWO_PI, op0=ALU.is_lt, op1=ALU.mult),
                           reads=[r_cs], writes=[r_kq])
                        op("dve", lambda e, dst=dst: e.tensor_tensor(out=dst[:], in0=dst[:], in1=kq[:], op=ALU.add), reads=[r_cs, r_kq], writes=[r_cs])
                        op("dve", lambda e, dst=dst: e.tensor_scalar(out=dst[:], in0=dst[:], scalar1=3.1415925, scalar2=-3.1415925, op0=ALU.min, op1=ALU.max),
                           reads=[r_cs], writes=[r_cs])
                        op("act", lambda e, dst=dst: e.activation(out=dst[:], in_=dst[:], func=AF.Sin), reads=[r_cs], writes=[r_cs])
                    op("dve", lambda e: e.tensor_scalar(out=sinT[:], in0=sinT[:], scalar1=rope[:, 1:2], scalar2=None, op0=ALU.mult),
                       reads=[r_cs, r_rope], writes=[r_cs])
                T.barrier()

                for hp in range(2):
                    with ExitStack() as pa:
                        NWA = 1024 + (384 if hp == 0 else 0)
                        wA = SB(pa, "wA", [128, 8, NWA], BF16); r_wA = Res()
                        QT = [SB(pa, "QT%d" % d, [128, 2, S], BF16) for d in range(2)]
                        KT = [SB(pa, "KT%d" % d, [128, 2, S], BF16) for d in range(2)]
                        r_QK = [[[Res() for _ in range(NG)] for _ in range(2)] for _ in range(2)]
                        Vt = SB(pa, "Vt", [64, NCH, 256], BF16); r_V = [Res() for _ in range(NG)]
                        Bref = SB(pa, "Bref", [128, 2, 2, NCH], F32); Bend = SB(pa, "Bend", [128, 2, 2, NCH], F32); r_B = Res()
                        Gx = SB(pa, "Gx", [128, 2, 2, NCH], F32); r_G = Res()
                        for seg, base in enumerate((O_RQ, O_RFF, O_RFB, O_RI)):
                            dma("pool", ds_w, wA[:, :, seg * 256:(seg + 1) * 256], w_in_k[:, :, base + hp * 256: base + hp * 256 + 256], writes=[r_wA])
                        if hp == 0:
                            dma("pool", ds_w, wA[:, :, 1024:1152], w_in_k[:, :, O_AK:O_AK + 128], writes=[r_wA])
                            for kv in range(2):
                                dma("pool", ds_w, wA[:, :, 1152 + kv * 64:1152 + kv * 64 + 32], w_in_k[:, :, O_AK + kv * 64 + 32:O_AK + kv * 64 + 64], writes=[r_wA])
                                dma("pool", ds_w, wA[:, :, 1152 + kv * 64 + 32:1152 + kv * 64 + 64], w_in_k[:, :, O_AK + kv * 64:O_AK + kv * 64 + 32], writes=[r_wA])
                            dma("pool", ds_w, wA[:, :, 1280:1408], w_in_k[:, :, O_AV:O_AV + 128], writes=[r_wA])
                        with ExitStack() as pa2:
                            xs = [SB(pa2, "xs%d" % i, [128, D], F32) for i in range(2)]; r_xs = [Res() for _ in range(2)]
                            uT = [SB(pa2, "uT%d" % i, [128, 8, 512], BF16) for i in range(1)]; r_uT = [Res()]
                            qs = [SB(pa2, "qs%d" % i, [128, 512], F32) for i in range(2)]; r_qs = [Res(), Res()]
                            NTMP = 2
                            tnames = ("sig", "lg", "Bf", "Eq", "Ek")
                            tmps = [{n: SB(pa2, "t_%s%d" % (n, i), [128, 512], F32) for n in tnames} for i in range(NTMP)]
                            r_tmps = [{n: Res() for n in tnames} for i in range(NTMP)]
                            for i in range(NTMP):
                                tmps[i]["kk"] = tmps[i]["sig"]; r_tmps[i]["kk"] = r_tmps[i]["sig"]
                                tmps[i]["dd"] = tmps[i]["lg"]; r_tmps[i]["dd"] = r_tmps[i]["lg"]
                            tctr = 0
                            bank_rot = Rot([2, 3, 4, 5, 6, 7])
                            for g in range(NG):
                                ub = 0
                                for t in range(4):
                                    tile_i = g * 4 + t
                                    sl = tile_i % 2
                                    dma("sp", ds_x[sl], xs[sl][:], x_d[s, tile_i * 128:(tile_i + 1) * 128, :], writes=[r_xs[sl]])
                                    make_uT(xs[sl], r_xs[sl], uT[ub][:, :, t * 128:(t + 1) * 128], r_uT[ub], 1, 0)
                                gs = slice(g * 512, (g + 1) * 512)
                                for cp in range(4):
                                    pb = bank_rot.next()
                                    for cc in range(2):
                                        ch = cp * 2 + cc
                                        for k in range(8):
                                            op("pe", lambda e, k=k, cc=cc, ch=ch, pb=pb: e.matmul(PB[pb][0:64, cc * 256:(cc + 1) * 256], lhsT=uT[ub][:, k, ch * 64:(ch + 1) * 64],
                                                                                                   rhs=wA[:, k, 768:1024], start=(k == 0), stop=(k == 7)),
                                               reads=[r_uT[ub], r_wA], writes=[rPB[pb]])
                                    op("act", lambda e, pb=pb, cp=cp: e.activation(out=Vt[:, g * 8 + cp * 2:g * 8 + cp * 2 + 2, :],
                                                                                    in_=PB[pb][0:64, :].rearrange("p (c n) -> p c n", n=256), func=AF.Copy, scale=128.0 ** -0.5),
                                       reads=[rPB[pb]], writes=[r_V[g]])
                                for hl in range(2):
                                    h = hp * 2 + hl
                                    pb = bank_rot.next()
                                    for k in range(8):
                                        op("pe", lambda e, k=k, pb=pb, hl=hl: e.matmul(PB[pb][:], lhsT=wA[:, k, hl * 128:(hl + 1) * 128], rhs=uT[ub][:, k, :],
                                                                                         start=(k == 0), stop=(k == 7)), reads=[r_uT[ub], r_wA], writes=[rPB[pb]])
                                    op("act", lambda e, pb=pb, hl=hl: e.activation(out=qs[hl][:], in_=PB[pb][:], func=AF.Silu), reads=[rPB[pb]], writes=[r_qs[hl]])
                                    for d in range(2):
                                        tm = tmps[tctr % NTMP]; rt = r_tmps[tctr % NTMP]; tctr += 1
                                        pb = bank_rot.next()
                                        cb = 256 * (1 + d) + hl * 128
                                        for k in range(8):
                                            op("pe", lambda e, k=k, pb=pb, cb=cb: e.matmul(PB[pb][:], lhsT=wA[:, k, cb:cb + 128], rhs=uT[ub][:, k, :],
                                                                                             start=(k == 0), stop=(k == 7)), reads=[r_uT[ub], r_wA], writes=[rPB[pb]])
                                        lb_c = lbt[:, 0, d, h:h + 1]; om_c = lbt[:, 1, d, h:h + 1]; nom_c = lbt[:, 2, d, h:h + 1]
                                        op("act", lambda e, pb=pb, tm=tm: e.activation(out=tm["sig"][:], in_=PB[pb][:], func=AF.Sigmoid), reads=[rPB[pb]], writes=[rt["sig"]])
                                        op("act", lambda e, tm=tm: e.activation(out=tm["lg"][:], in_=tm["sig"][:], func=AF.Ln, scale=om_c, bias=lb_c),
                                           reads=[rt["sig"], r_lbt], writes=[rt["lg"]])
                                        op("dve", lambda e, tm=tm: e.tensor_scalar(out=tm["kk"][:], in0=tm["sig"][:], scalar1=nom_c, scalar2=om_c, op0=ALU.mult, op1=ALU.add),
                                           reads=[rt["sig"], r_lbt], writes=[rt["kk"]])
                                        op("dve", lambda e, tm=tm: e.tensor_tensor_scan(out=tm["Bf"][:], data0=rmask[:], data1=tm["lg"][:], initial=0.0, op0=ALU.mult, op1=ALU.add),
                                           reads=[r_rmask, rt["lg"]], writes=[rt["Bf"]])
                                        B3 = tm["Bf"][:].rearrange("p (c t) -> p c t", t=64)
                                        if d == 1:
                                            lg3 = tm["lg"][:].rearrange("p (c t) -> p c t", t=64)
                                            op("dve", lambda e, B3=B3, lg3=lg3: e.tensor_tensor(out=lg3, in0=lg3, in1=B3, op=ALU.subtract), reads=[rt["lg"], rt["Bf"]], writes=[rt["lg"]])
                                            op("dve", lambda e, B3=B3, lg3=lg3: e.tensor_tensor(out=B3, in0=lg3, in1=B3[:, :, 63:64].broadcast_to([128, 8, 64]), op=ALU.add),
                                               reads=[rt["lg"], rt["Bf"]], writes=[rt["Bf"]])
                                            iref, iend = 32, 0
                                        else:
                                            iref, iend = 31, 63
                                        op("act", lambda e, B3=B3, iref=iref: e.activation(out=Bref[:, d, hl, g * 8:(g + 1) * 8], in_=B3[:, :, iref], func=AF.Copy), reads=[rt["Bf"]], writes=[r_B])
                                        op("act", lambda e, B3=B3, iend=iend: e.activation(out=Bend[:, d, hl, g * 8:(g + 1) * 8], in_=B3[:, :, iend], func=AF.Copy), reads=[rt["Bf"]], writes=[r_B])
                                        op("dve", lambda e, B3=B3, tm=tm, iref=iref: e.tensor_tensor(out=tm["dd"][:].rearrange("p (c t) -> p c t", t=64), in0=B3,
                                                                                                       in1=B3[:, :, iref:iref + 1].broadcast_to([128, 8, 64]), op=ALU.subtract),
                                           reads=[rt["Bf"]], writes=[rt["dd"]])
                                        op("act", lambda e, tm=tm: e.activation(out=tm["Eq"][:], in_=tm["dd"][:], func=AF.Exp), reads=[rt["dd"]], writes=[rt["Eq"]])
                                        op("act", lambda e, tm=tm: e.activation(out=tm["Ek"][:], in_=tm["dd"][:], func=AF.Exp, scale=-1.0), reads=[rt["dd"]], writes=[rt["Ek"]])
                                        op("dve", lambda e, tm=tm, d=d, hl=hl: e.tensor_tensor(out=KT[d][:, hl, gs], in0=tm["kk"][:], in1=tm["Ek"][:], op=ALU.mult),
                                           reads=[rt["kk"], rt["Ek"]], writes=[r_QK[d][hl][g]])
                                        op("dve", lambda e, tm=tm, d=d, hl=hl: e.tensor_tensor(out=QT[d][:, hl, gs], in0=qs[hl][:], in1=tm["Eq"][:], op=ALU.mult),
                                           reads=[r_qs[hl], rt["Eq"]], writes=[r_QK[d][hl][g]])
                                if hp == 0:
                                    pb1 = bank_rot.next()
                                    pb2 = bank_rot.next()
                                    for (pbx, cb) in ((pb1, 1024), (pb2, 1152)):
                                        for k in range(8):
                                            op("pe", lambda e, k=k, pbx=pbx, cb=cb: e.matmul(PB[pbx][:], lhsT=wA[:, k, cb:cb + 128], rhs=uT[ub][:, k, :], start=(k == 0), stop=(k == 7)),
                                               reads=[r_uT[ub], r_wA], writes=[rPB[pbx]])
                                    tm = tmps[tctr % NTMP]; rt = r_tmps[tctr % NTMP]; tctr += 1
                                    op("dve", lambda e, tm=tm: e.tensor_tensor(out=tm["sig"][:], in0=PB[pb1][:], in1=cosT[:, gs], op=ALU.mult), reads=[rPB[pb1], r_cs], writes=[rt["sig"]])
                                    op("dve", lambda e, tm=tm: e.tensor_tensor(out=tm["lg"][:], in0=PB[pb2][:], in1=sinT[:, gs], op=ALU.mult), reads=[rPB[pb2], r_cs], writes=[rt["lg"]])
                                    op("dve", lambda e, tm=tm: e.tensor_tensor(out=kTt[:, gs], in0=tm["sig"][:], in1=tm["lg"][:], op=ALU.add), reads=[rt["sig"], rt["lg"]], writes=[r_kT[g]])
                                    for t in range(4):
                                        pb = bank_rot.next()
                                        for k in range(8):
                                            op("pe", lambda e, k=k, pb=pb, t=t: e.matmul(PB[pb][:, 0:128], lhsT=uT[ub][:, k, t * 128:(t + 1) * 128], rhs=wA[:, k, 1280:1408],
                                                                                          start=(k == 0), stop=(k == 7)), reads=[r_uT[ub], r_wA], writes=[rPB[pb]])
                                        op("act", lambda e, pb=pb, t=t: e.activation(out=vtk[:, g * 4 + t, :], in_=PB[pb][:, 0:128], func=AF.Copy), reads=[rPB[pb]], writes=[r_v[g * 4 + t]])
                            op("dve", lambda e: e.tensor_tensor(out=Gx[:], in0=Bend[:], in1=Bref[:], op=ALU.subtract), reads=[r_B], writes=[r_G])
                            if NCH > 1:
                                op("dve", lambda e: e.tensor_tensor(out=Gx[:, 0, :, 0:NCH - 1], in0=Gx[:, 0, :, 0:NCH - 1], in1=Bref[:, 0, :, 1:NCH], op=ALU.add), reads=[r_B, r_G], writes=[r_G])
                                op("dve", lambda e: e.tensor_tensor(out=Gx[:, 1, :, 1:NCH], in0=Gx[:, 1, :, 1:NCH], in1=Bref[:, 1, :, 0:NCH - 1], op=ALU.add), reads=[r_B, r_G], writes=[r_G])
                            op("act", lambda e: e.activation(out=Gx[:], in_=Gx[:], func=AF.Exp), reads=[r_G], writes=[r_G])
                            if dbg and s == 0 and hp == 0:
                                for d in range(2):
                                    dma("sp", ds_dbg, dbg_d["QT"][d], QT[d][:], reads=[r_QK[d][hl][g] for hl in range(2) for g in range(NG)])
                                    dma("sp", ds_dbg, dbg_d["KT"][d], KT[d][:], reads=[r_QK[d][hl][g] for hl in range(2) for g in range(NG)])
                                dma("sp", ds_dbg, dbg_d["V"], Vt[:], reads=r_V)
                                dma("sp", ds_dbg, dbg_d["kT"], kTt[:], reads=r_kT)
                        T.barrier()

                        with ExitStack() as pb_:
                            Sm32 = SB(pb_, "Sm32", [128, 4, 128], F32); Smb = SB(pb_, "Smb", [128, 4, 128], BF16); tmp32 = SB(pb_, "tmp32", [128, 4, 128], F32)
                            r_Sm32 = [Res() for _ in range(4)]; r_Smb = [Res() for _ in range(4)]; r_tmp32 = [Res() for _ in range(4)]
                            scm = SB(pb_, "scm", [64, 2, 4, 64], BF16); r_scm = [[Res() for _ in range(4)] for _ in range(2)]
                            ktok = SB(pb_, "ktok", [64, 2, 4, 128], BF16); r_ktok = [[Res() for _ in range(4)] for _ in range(2)]
                            rB = [[None] * 4 for _ in range(2)]
                            for _p in range(2):
                                for _c in range(4):
                                    _r = Res()
                                    rB[_p][_c] = {n: _r for n in ("sc", "kt", "o", "dl")}
                            op("pool", lambda e: e.memset(Sm32[:], 0.0), writes=r_Sm32)
                            op("pool", lambda e: e.memset(Smb[:], 0.0), writes=r_Smb)
                            allQK = lambda d, hl: [r_QK[d][hl][g] for g in range(NG)]

                            def chunk_of(j, d):
                                return j if d == 0 else NCH - 1 - j

                            def stage1(j):
                                par = j % 2
                                for ci in range(4):
                                    hl, d = ci // 2, ci % 2
                                    c = chunk_of(j, d); tk = slice(c * 64, (c + 1) * 64)
                                    bk = ci + 4 * par
                                    op("pe", lambda e, bk=bk, d=d, hl=hl, tk=tk: e.matmul(PB[bk][0:64, 0:64], lhsT=KT[d][:, hl, tk], rhs=QT[d][:, hl, tk], start=True, stop=True),
                                       reads=allQK(d, hl), writes=[rB[par][ci]["sc"]])
                                    op("pe", lambda e, bk=bk, d=d, hl=hl, tk=tk: e.transpose(out=PB[bk][:].bitcast(BF16)[0:64, 128:256], in_=KT[d][:, hl, tk], identity=identb[:]),
                                       reads=allQK(d, hl) + [r_identb], writes=[rB[par][ci]["kt"]])

                            def stage2(j):
                                par = j % 2
                                for ci in range(4):
                                    hl, d = ci // 2, ci % 2
                                    bk = ci + 4 * par
                                    mk = maskf if d == 0 else maskb
                                    op("pool", lambda e, ci=ci: e.memset(scm[:, par, ci, :], 0.0), writes=[r_scm[par][ci]])
                                    op("dve", lambda e, bk=bk, ci=ci, mk=mk: e.copy_predicated(out=scm[:, par, ci, :], mask=mk[:], data=PB[bk][0:64, 0:64]),
                                       reads=[rB[par][ci]["sc"], r_mask], writes=[r_scm[par][ci]])
                                    op("act", lambda e, bk=bk, ci=ci: e.activation(out=ktok[:, par, ci, :], in_=PB[bk][:].bitcast(BF16)[0:64, 128:256], func=AF.Copy),
                                       reads=[rB[par][ci]["kt"]], writes=[r_ktok[par][ci]])

                            def stage3(j):
                                par = j % 2
                                for ci in range(4):
                                    hl, d = ci // 2, ci % 2
                                    c = chunk_of(j, d); tk = slice(c * 64, (c + 1) * 64)
                                    bk = ci + 4 * par
                                    vv = Vt[:, c, hl * 128:(hl + 1) * 128]
                                    op("pe", lambda e, bk=bk, vv=vv, ci=ci: e.matmul(PB[bk][:, 128:192], lhsT=vv, rhs=scm[:, par, ci, :], start=True, stop=False),
                                       reads=[r_V[c // 8], r_scm[par][ci]], writes=[rB[par][ci]["o"]])
                                    op("pe", lambda e, bk=bk, ci=ci, d=d, hl=hl, tk=tk: e.matmul(PB[bk][:, 128:192], lhsT=Smb[:, ci, :], rhs=QT[d][:, hl, tk], start=False, stop=True),
                                       reads=[r_Smb[ci]] + allQK(d, hl), writes=[rB[par][ci]["o"]])
                                    op("pe", lambda e, bk=bk, vv=vv, ci=ci: e.matmul(PB[bk][:, 192:320], lhsT=ktok[:, par, ci, :], rhs=vv, start=True, stop=True),
                                       reads=[r_ktok[par][ci], r_V[c // 8]], writes=[rB[par][ci]["dl"]])

                            def stage4(j):
                                par = j % 2
                                for ci in range(4):
                                    hl, d = ci // 2, ci % 2
                                    h = hp * 2 + hl
                                    c = chunk_of(j, d); tk = slice(c * 64, (c + 1) * 64)
                                    bk = ci + 4 * par
                                    if j < NCH // 2:
                                        op("act", lambda e, bk=bk, h=h, tk=tk: e.activation(out=oT[:, h, tk], in_=PB[bk][:, 128:192], func=AF.Copy),
                                           reads=[rB[par][ci]["o"]], writes=[r_oT[h][c]])
                                    else:
                                        op("dve", lambda e, bk=bk, h=h, tk=tk: e.tensor_tensor(out=oT[:, h, tk], in0=PB[bk][:, 128:192], in1=oT[:, h, tk], op=ALU.add),
                                           reads=[rB[par][ci]["o"], r_oT[h][c]], writes=[r_oT[h][c]])
                                    if j < NCH - 1:
                                        gcol = Gx[:, d, hl, c:c + 1]
                                        op("dve", lambda e, bk=bk, ci=ci: e.tensor_tensor(out=tmp32[:, ci, :], in0=PB[bk][:, 192:320], in1=Sm32[:, ci, :], op=ALU.add),
                                           reads=[rB[par][ci]["dl"], r_Sm32[ci]], writes=[r_tmp32[ci]])
                                        op("act", lambda e, ci=ci, gcol=gcol: e.activation(out=Sm32[:, ci, :], in_=tmp32[:, ci, :], func=AF.Identity, scale=gcol),
                                           reads=[r_tmp32[ci], r_G], writes=[r_Sm32[ci]])
                                        op("pool", lambda e, ci=ci, gcol=gcol: e.tensor_scalar(out=Smb[:, ci, :], in0=tmp32[:, ci, :], scalar1=gcol, scalar2=None, op0=ALU.mult),
                                           reads=[r_tmp32[ci], r_G], writes=[r_Smb[ci]])

                            stage1(0)
                            stage2(0)
                            for j in range(NCH):
                                if j + 1 < NCH:
                                    stage1(j + 1)
                                stage3(j)
                                if j + 1 < NCH:
                                    stage2(j + 1)
                                stage4(j)
                        T.barrier()
                if dbg and s == 0:
                    dma("sp", ds_dbg, dbg_d["oT"], oT[:], reads=[r for rr in r_oT for r in rr])

                ogA = SB(pab, "ogA", [128, 4, S], BF16); atA = SB(pab, "atA", [128, 4, S], BF16)
                r_ogA = [Res() for _ in range(NG)]; r_atA = [Res() for _ in range(NT)]
                with ExitStack() as pc:
                    wC = SB(pc, "wC", [128, 8, 1536], BF16); r_wC = Res()
                    dma("pool", ds_w, wC[:, :, 0:512], w_in_k[:, :, O_RG:O_RG + 512], writes=[r_wC])
                    for m in range(4):
                        for hf in range(2):
                            hq = m + 4 * hf
                            cdst = 512 + m * 128 + hf * 64
                            dma("pool", ds_w, wC[:, :, cdst:cdst + 64], w_in_k[:, :, O_AQ + hq * 64:O_AQ + hq * 64 + 64], writes=[r_wC])
                            dma("pool", ds_w, wC[:, :, 512 + cdst:512 + cdst + 32], w_in_k[:, :, O_AQ + hq * 64 + 32:O_AQ + hq * 64 + 64], writes=[r_wC])
                            dma("pool", ds_w, wC[:, :, 512 + cdst + 32:512 + cdst + 64], w_in_k[:, :, O_AQ + hq * 64:O_AQ + hq * 64 + 32], writes=[r_wC])
                    xs = [SB(pc, "cxs%d" % i, [128, D], F32) for i in range(2)]; r_xs = [Res() for _ in range(2)]
                    uT = SB(pc, "cuT", [128, 8, 512], BF16); r_uT = Res()
                    qT = SB(pc, "qT", [128, 4, 512], BF16); r_qT = Res()
                    ta = [SB(pc, "cta%d" % i, [128, 512], F32) for i in range(4)]; r_ta = [Res() for _ in range(4)]
                    msc = [SB(pc, "msc%d" % i, [128, 384], F32) for i in range(2)]; r_msc = [Res(), Res()]
                    pex = [SB(pc, "pex%d" % i, [128, 384], BF16) for i in range(2)]; r_pex = [Res(), Res()]
                    pT = [SB(pc, "pT%d" % i, [128, 3, 128], BF16) for i in range(2)]; r_pT = [Res(), Res()]
                    sm = [SB(pc, "smx%d" % i, [128, 8], F32) for i in range(2)]; r_sm = [Res(), Res()]
                    sm2 = [SB(pc, "smy%d" % i, [128, 8], F32) for i in range(2)]; r_sm2 = [Res(), Res()]
                    ao = [SB(pc, "ao%d" % i, [128, 512], BF16) for i in range(2)]; r_ao = [Res(), Res()]
                    hctr = 0
                    for g in range(NG):
                        gs = slice(g * 512, (g + 1) * 512)
                        for t in range(4):
                            tile_i = g * 4 + t
                            sl = tile_i % 2
                            dma("sp", ds_x[sl], xs[sl][:], x_d[s, tile_i * 128:(tile_i + 1) * 128, :], writes=[r_xs[sl]])
                            make_uT(xs[sl], r_xs[sl], uT[:, :, t * 128:(t + 1) * 128], r_uT, 1, 0)
                        for m in range(4):
                            for (pbx, cb) in ((2, 512 + m * 128), (3, 1024 + m * 128)):
                                for k in range(8):
                                    op("pe", lambda e, k=k, pbx=pbx, cb=cb: e.matmul(PB[pbx][:], lhsT=wC[:, k, cb:cb + 128], rhs=uT[:, k, :], start=(k == 0), stop=(k == 7)),
                                       reads=[r_uT, r_wC], writes=[rPB[pbx]])
                            op("dve", lambda e: e.tensor_tensor(out=ta[0][:], in0=PB[2][:], in1=cosT[:, gs], op=ALU.mult), reads=[rPB[2], r_cs], writes=[r_ta[0]])
                            op("dve", lambda e: e.tensor_tensor(out=ta[1][:], in0=PB[3][:], in1=sinT[:, gs], op=ALU.mult), reads=[rPB[3], r_cs], writes=[r_ta[1]])
                            op("dve", lambda e, m=m: e.tensor_tensor(out=qT[:, m, :], in0=ta[0][:], in1=ta[1][:], op=ALU.add), reads=[r_ta[0], r_ta[1]], writes=[r_qT])
                        for h in range(4):
                            for k in range(8):
                                op("pe", lambda e, k=k, h=h: e.matmul(PB[4][:], lhsT=wC[:, k, h * 128:(h + 1) * 128], rhs=uT[:, k, :], start=(k == 0), stop=(k == 7)),
                                   reads=[r_uT, r_wC], writes=[rPB[4]])
                            op("act", lambda e: e.activation(out=ta[0][:], in_=PB[4][:], func=AF.Silu), reads=[rPB[4]], writes=[r_ta[0]])
                            rr = [r_oT[h][c] for c in range(g * 8, (g + 1) * 8)]
                            op("act", lambda e, h=h: e.activation(out=ta[2][:], in_=oT[:, h, gs], func=AF.Square), reads=rr, writes=[r_ta[2]])
                            op("pe", lambda e: e.matmul(PB[5][:], lhsT=onesf[:], rhs=ta[2][:], start=True, stop=True), reads=[r_onesf, r_ta[2]], writes=[rPB[5]])
                            op("act", lambda e: e.activation(out=ta[3][:], in_=PB[5][:], func=AF.Ln, scale=1.0 / 128.0, bias=epsc[:, 1:2]), reads=[rPB[5], r_eps], writes=[r_ta[3]])
                            op("act", lambda e: e.activation(out=ta[3][:], in_=ta[3][:], func=AF.Exp, scale=-0.5), reads=[r_ta[3]], writes=[r_ta[3]])
                            op("dve", lambda e, h=h: e.tensor_tensor(out=ta[2][:], in0=oT[:, h, gs], in1=ta[3][:], op=ALU.mult), reads=rr + [r_ta[3]], writes=[r_ta[2]])
                            op("dve", lambda e, h=h: e.scalar_tensor_tensor(out=ogA[:, h, gs], in0=ta[0][:], scalar=ngc[:, 0:1], in1=ta[2][:], op0=ALU.mult, op1=ALU.mult),
                               reads=[r_ta[0], r_ngc, r_ta[2]], writes=[r_ogA[g]])
                        def att_geom(t):
                            j = g * 4 + t
                            k0 = max(0, j - 1); k1 = min(NT, j + 2)
                            return j, k0, k1, k1 - k0, (k1 - k0) * 128, (128 if j == 0 else 0)

                        def att_s1_pe(idx):
                            t, h = idx // 8, idx % 8
                            j, k0, k1, nkb, nk, mc0 = att_geom(t)
                            m, hf = h % 4, h // 4
                            ps = slice(hf * 64, (hf + 1) * 64)
                            ib = idx % 2
                            pbs = 2 + ib
                            kgs = [r_kT[gg] for gg in range((k0 * 128) // 512, ((k1 * 128) - 1) // 512 + 1)]
                            op("pe", lambda e: e.matmul(PB[pbs][:, 0:nk], lhsT=qT[ps, m, t * 128:(t + 1) * 128], rhs=kTt[ps, k0 * 128:k1 * 128], start=True, stop=True),
                               reads=[r_qT] + kgs, writes=[rPB[pbs]])

                        def att_s1_ew(idx):
                            t, h = idx // 8, idx % 8
                            j, k0, k1, nkb, nk, mc0 = att_geom(t)
                            ib = idx % 2
                            pbs = 2 + ib
                            op("dve", lambda e: e.tensor_tensor(out=msc[ib][:, 0:nk], in0=PB[pbs][:, 0:nk], in1=amask[:, mc0:mc0 + nk], op=ALU.add),
                               reads=[rPB[pbs], r_amask], writes=[r_msc[ib]])
                            op("dve", lambda e: e.reduce_max(out=sm[ib][:, 0:1], in_=msc[ib][:, 0:nk], axis=AX.X), reads=[r_msc[ib]], writes=[r_sm[ib]])
                            op("dve", lambda e: e.tensor_scalar(out=sm[ib][:, 1:2], in0=sm[ib][:, 0:1], scalar1=-0.125, scalar2=sinkb[:, 1, h:h + 1], op0=ALU.mult, op1=ALU.min),
                               reads=[r_sm[ib], r_sink], writes=[r_sm[ib]])
                            op("act", lambda e: e.activation(out=pex[ib][:, 0:nk], in_=msc[ib][:, 0:nk], func=AF.Exp, scale=0.125, bias=sm[ib][:, 1:2], accum_out=sm2[ib][:, 0:1]),
                               reads=[r_msc[ib], r_sm[ib]], writes=[r_pex[ib], r_sm2[ib]])
                            op("act", lambda e: e.activation(out=sm2[ib][:, 1:2], in_=sinkb[:, 0, h:h + 1], func=AF.Exp, bias=sm[ib][:, 1:2], scale=1.0),
                               reads=[r_sm[ib], r_sink], writes=[r_sm2[ib]])

                        def att_s2a(idx):
                            t, h = idx // 8, idx % 8
                            j, k0, k1, nkb, nk, mc0 = att_geom(t)
                            ib = idx % 2
                            pbt = 4 + ib
                            for b in range(nkb):
                                op("pe", lambda e, b=b: e.transpose(out=PB[pbt][:].bitcast(BF16)[:, b * 128:(b + 1) * 128], in_=pex[ib][:, b * 128:(b + 1) * 128], identity=identb[:]),
                                   reads=[r_pex[ib], r_identb], writes=[rPB[pbt]])
                            if ib == 0:
                                op("act", lambda e: e.activation(out=pT[ib][:, 0:nkb, :], in_=PB[pbt][:].bitcast(BF16)[:, 0:nk].rearrange("p (b q) -> p b q", q=128), func=AF.Copy),
                                   reads=[rPB[pbt]], writes=[r_pT[ib]])
                            else:
                                op("dve", lambda e: e.tensor_copy(out=pT[ib][:, 0:nkb, :], in_=PB[pbt][:].bitcast(BF16)[:, 0:nk].rearrange("p (b q) -> p b q", q=128)),
                                   reads=[rPB[pbt]], writes=[r_pT[ib]])

                        def att_s2b(idx):
                            t, h = idx // 8, idx % 8
                            j, k0, k1, nkb, nk, mc0 = att_geom(t)
                            hf = h // 4
                            ib = idx % 2
                            op("dve", lambda e: e.tensor_tensor(out=sm2[ib][:, 2:3], in0=sm2[ib][:, 0:1], in1=sm2[ib][:, 1:2], op=ALU.add), reads=[r_sm2[ib]], writes=[r_sm2[ib]])
                            op("dve", lambda e: e.reciprocal(out=sm2[ib][:, 3:4], in_=sm2[ib][:, 2:3]), reads=[r_sm2[ib]], writes=[r_sm2[ib]])
                            pbo = 6 + ib
                            for b in range(nkb):
                                op("pe", lambda e, b=b: e.matmul(PB[pbo][:, 0:64], lhsT=pT[ib][:, b, :], rhs=vtk[:, k0 + b, hf * 64:(hf + 1) * 64], start=(b == 0), stop=(b == nkb - 1)),
                                   reads=[r_pT[ib], r_v[k0 + b]], writes=[rPB[pbo]])
                            aot = ao[t % 2]; r_aot = r_ao[t % 2]
                            op("act", lambda e: e.activation(out=aot[:, h * 64:(h + 1) * 64], in_=PB[pbo][:, 0:64], func=AF.Identity, scale=sm2[ib][:, 3:4]),
                               reads=[rPB[pbo], r_sm2[ib]], writes=[r_aot])
                            if h == 7:
                                pba = t % 2
                                for m in range(4):
                                    op("pe", lambda e, m=m: e.transpose(out=PB[pba][:].bitcast(BF16)[:, m * 128:(m + 1) * 128], in_=aot[:, m * 128:(m + 1) * 128], identity=identb[:]),
                                       reads=[r_aot, r_identb], writes=[rPB[pba]])
                                op("dve", lambda e: e.tensor_copy(out=atA[:, :, j * 128:(j + 1) * 128], in_=PB[pba][:].bitcast(BF16)[:, 0:512].rearrange("p (m q) -> p m q", q=128)),
                                   reads=[rPB[pba]], writes=[r_atA[j]])

                        att_s1_pe(0)
                        att_s1_ew(0)
                        for idx in range(32):
                            if idx + 1 < 32:
                                att_s1_pe(idx + 1)
                            att_s2a(idx)
                            if idx + 1 < 32:
                                att_s1_ew(idx + 1)
                            att_s2b(idx)
                T.barrier()

                with ExitStack() as pc:
                    if S >= 2048:
                        wG = oT[:].rearrange("p h s -> p (h s)").bitcast(BF16)
                        wGv = wG[:, 0:8 * 2048].rearrange("p (k n) -> p k n", n=2048)
                    else:
                        wGv = SB(pc, "wGt", [128, 8, 2048], BF16)[:]
                    r_wG = Res()
                    wR = SB(pc, "wR", [128, 4, D], BF16); wT = SB(pc, "wT", [128, 4, D], BF16); wO = SB(pc, "wO", [128, 8, D], BF16)
                    lnB = SB(pc, "lnB1", [128, 2, D], F32); r_lnB = Res()
                    for q4 in range(4):
                        dma("pool", ds_w, wGv[:, :, q4 * 512:(q4 + 1) * 512], w_in_k[:, :, O_GR + q4 * 512:O_GR + (q4 + 1) * 512], writes=[r_wG])
                    dma("pool", ds_w, wR[:], w_rec_k, writes=[r_wG])
                    dma("pool", ds_w, wT[:], w_att_k, writes=[r_wG])
                    for q2 in range(2):
                        dma("pool", ds_w, wO[:, :, q2 * 512:(q2 + 1) * 512], w_out_k[:, :, q2 * 512:(q2 + 1) * 512], writes=[r_wG])
                    dma("sp", ds_const, lnB[:, 0, :], ln1g_d[0:1, :].broadcast_to([128, D]), writes=[r_lnB])
                    dma("sp", ds_const, lnB[:, 1, :], ln1b_d[0:1, :].broadcast_to([128, D]), writes=[r_lnB])
                    zb = zer[:, 0:8].bitcast(BF16)[:, 0:8].unsqueeze(2)
                    dma("sp", ds_misc[3], u2T_d[s, :, 0:1].rearrange("(j p) o -> p j o", p=128), zb, reads=[r_zer], allow_slow_non_contiguous=True)
                    dma("sp", ds_misc[3], u2T_d[s, :, S + 1:S + 2].rearrange("(j p) o -> p j o", p=128), zb, reads=[r_zer], allow_slow_non_contiguous=True)
                    r_x1d = [Res() for _ in range(NT)]
                    r_u2d = [Res() for _ in range(NT)]
                    xs = [SB(pc, "bxs%d" % i, [128, D], F32) for i in range(4)]; r_xs = [Res() for _ in range(4)]
                    uT = SB(pc, "buT", [128, 8, 512], BF16); r_uT = Res()
                    mT = SB(pc, "mT", [128, 8, 512], BF16); r_mT = Res()
                    ta = [SB(pc, "bta%d" % i, [128, 512], F32) for i in range(4)]; r_ta = [Res() for _ in range(4)]
                    yt = [SB(pc, "yt%d" % i, [128, D], F32) for i in range(2)]; r_yt = [Res(), Res()]
                    st6 = [SB(pc, "st6%d" % i, [128, 20], F32) for i in range(2)]; r_st6 = [Res(), Res()]
                    u2t = [SB(pc, "u2t%d" % i, [128, 8, 128], BF16) for i in range(2)]; r_u2t = [Res(), Res()]
                    for g in range(NG):
                        gs = slice(g * 512, (g + 1) * 512)
                        for t in range(4):
                            tile_i = g * 4 + t
                            dma("sp", ds_x[t], xs[t][:], x_d[s, tile_i * 128:(tile_i + 1) * 128, :], writes=[r_xs[t]])
                            make_uT(xs[t], r_xs[t], uT[:, :, t * 128:(t + 1) * 128], r_uT, 1, 0)
                        BSETS = ((2, 3, 4, 5), (0, 1, 6, 7))

                        def dc_pe(dc):
                            b0, b1, b2, b3 = BSETS[dc % 2]
                            dcs = slice(dc * 128, (dc + 1) * 128)
                            for k in range(8):
                                op("pe", lambda e, k=k: e.matmul(PB[b0][:], lhsT=wGv[:, k, dc * 128:(dc + 1) * 128], rhs=uT[:, k, :], start=(k == 0), stop=(k == 7)),
                                   reads=[r_uT, r_wG], writes=[rPB[b0]])
                            for k in range(8):
                                op("pe", lambda e, k=k: e.matmul(PB[b1][:], lhsT=wGv[:, k, 1024 + dc * 128:1024 + (dc + 1) * 128], rhs=uT[:, k, :], start=(k == 0), stop=(k == 7)),
                                   reads=[r_uT, r_wG], writes=[rPB[b1]])
                            for h in range(4):
                                op("pe", lambda e, h=h: e.matmul(PB[b2][:], lhsT=wR[:, h, dcs], rhs=ogA[:, h, gs], start=(h == 0), stop=(h == 3)),
                                   reads=[r_wG, r_ogA[g]], writes=[rPB[b2]])
                            for m in range(4):
                                op("pe", lambda e, m=m: e.matmul(PB[b3][:], lhsT=wT[:, m, dcs], rhs=atA[:, m, gs], start=(m == 0), stop=(m == 3)),
                                   reads=[r_wG] + r_atA[g * 4:(g + 1) * 4], writes=[rPB[b3]])

                        def dc_ew(dc):
                            b0, b1, b2, b3 = BSETS[dc % 2]
                            a0, a1 = ta[2 * (dc % 2)], ta[2 * (dc % 2) + 1]
                            ra0, ra1 = r_ta[2 * (dc % 2)], r_ta[2 * (dc % 2) + 1]
                            op("act", lambda e: e.activation(out=a0[:], in_=PB[b0][:], func=AF.Sigmoid), reads=[rPB[b0]], writes=[ra0])
                            op("act", lambda e: e.activation(out=a1[:], in_=PB[b1][:], func=AF.Sigmoid), reads=[rPB[b1]], writes=[ra1])
                            op("dve", lambda e: e.tensor_tensor(out=a0[:], in0=PB[b2][:], in1=a0[:], op=ALU.mult), reads=[rPB[b2], ra0], writes=[ra0])
                            op("dve", lambda e: e.tensor_tensor(out=a1[:], in0=PB[b3][:], in1=a1[:], op=ALU.mult), reads=[rPB[b3], ra1], writes=[ra1])
                            op("dve", lambda e: e.tensor_tensor(out=mT[:, dc, :], in0=a0[:], in1=a1[:], op=ALU.add), reads=[ra0, ra1], writes=[r_mT])

                        dc_pe(0)
                        for dc in range(8):
                            if dc + 1 < 8:
                                dc_pe(dc + 1)
                            dc_ew(dc)
                        for t in range(4):
                            tile_i = g * 4 + t
                            ib = tile_i % 2
                            wb = 2 + 2 * (tile_i % 2)
                            for hh in range(2):
                                for k in range(8):
                                    op("pe", lambda e, k=k, hh=hh, t=t, wb=wb: e.matmul(PB[wb + hh][:], lhsT=mT[:, k, t * 128:(t + 1) * 128], rhs=wO[:, k, hh * 512:(hh + 1) * 512], start=(k == 0), stop=(k == 7)),
                                       reads=[r_mT, r_wG], writes=[rPB[wb + hh]])
                            ln_epilogue((wb, wb + 1), (rPB[wb], rPB[wb + 1]), xs[t][:], r_xs[t], 0, lnB[:, 0, :], lnB[:, 1, :], r_lnB, yt[ib], r_yt[ib], st6[ib], r_st6[ib])
                            dma("sp", ds_o[ib], x1_d[s, tile_i * 128:(tile_i + 1) * 128, :], yt[ib][:], reads=[r_yt[ib]], writes=[r_x1d[tile_i]])
                            make_uT(yt[ib], r_yt[ib], u2t[ib][:], r_u2t[ib], 3, 2, banks=(6, 7))
                            dma("sp", ds_o[2 + ib], u2T_d[s, :, 1 + tile_i * 128:1 + (tile_i + 1) * 128].rearrange("(j p) q -> p j q", p=128), u2t[ib][:],
                                reads=[r_u2t[ib]], writes=[r_u2d[tile_i]])
                T.barrier()
            T.barrier()

            with ExitStack() as pf:
                wU = SB(pf, "wU", [128, 8, 2 * DFF], BF16); wD = SB(pf, "wD", [128, NFC, D], BF16); r_wU = Res()
                lnB = SB(pf, "lnB2", [128, 2, D], F32); r_lnB = Res()
                cvw = SB(pf, "cvw", [128, 4, 2 * NFC], F32); r_cvw = r_lnB
                r_wc = [Res() for _ in range(11)]
                for jq in range(11):
                    dma("pool", ds_wc[jq], wU[:, :, jq * 256:(jq + 1) * 256], w_up_k[:, :, jq * 256:(jq + 1) * 256], writes=[r_wc[jq]])
                    dma("pool", ds_wc[jq], wU[:, :, DFF + jq * 256:DFF + (jq + 1) * 256], w_up_k[:, :, DFF + jq * 256:DFF + (jq + 1) * 256], writes=[r_wc[jq]])
                    dma("pool", ds_wc[jq], wD[:, 2 * jq:2 * jq + 2, :], w_dn_k[:, 2 * jq:2 * jq + 2, :], writes=[r_wc[jq]])
                dma("sp", ds_const, lnB[:, 0, :], ln2g_d[0:1, :].broadcast_to([128, D]), writes=[r_lnB])
                dma("sp", ds_const, lnB[:, 1, :], ln2b_d[0:1, :].broadcast_to([128, D]), writes=[r_lnB])
                dma("sp", ds_const, cvw[:, 0:3, :], cw_d.rearrange("k (j p) -> p k j", p=128), writes=[r_cvw], allow_slow_non_contiguous=True)
                dma("sp", ds_const, cvw[:, 3, :], cb_d.rearrange("o (j p) -> p (o j)", p=128), writes=[r_cvw], allow_slow_non_contiguous=True)
                NSG = S // 256
                u2 = [SB(pf, "fu2%d" % i, [128, 8, 258], BF16) for i in range(2)]; r_u2 = [Res(), Res()]
                x1t = [SB(pf, "fx1%d" % i, [128, D], F32) for i in range(4)]; r_x1t = [Res() for _ in range(4)]
                cv = [SB(pf, "fcv%d" % i, [128, 256], F32) for i in range(2)]; r_cv = [Res(), Res()]
                cgt = [SB(pf, "fcg%d" % i, [128, 256], F32) for i in range(2)]; r_cg = [Res(), Res()]
                gz = [SB(pf, "fgz%d" % i, [128, 256], F32) for i in range(2)]; r_gz = [Res(), Res()]
                actb = [SB(pf, "fact%d" % i, [128, 256], BF16) for i in range(2)]; r_act = [Res(), Res()]
                yt = [SB(pf, "fyt%d" % i, [128, D], F32) for i in range(2)]; r_yt = [Res(), Res()]
                st6 = [SB(pf, "fst6%d" % i, [128, 20], F32) for i in range(2)]; r_st6 = [Res(), Res()]
                def c2_up(sg, c):
                    ub = sg % 2; ib = c % 2
                    pv, pg = 4 + 2 * ib, 5 + 2 * ib
                    for (pbx, cb) in ((pv, c * 128), (pg, DFF + c * 128)):
                        for k in range(8):
                            op("pe", lambda e, k=k, pbx=pbx, cb=cb: e.matmul(PB[pbx][:, 0:258], lhsT=wU[:, k, cb:cb + 128], rhs=u2[ub][:, k, :], start=(k == 0), stop=(k == 7)),
                               reads=[r_wc[c // 2], r_u2[ub]], writes=[rPB[pbx]])

                def c2_s1(c):
                    ib = c % 2
                    pv, pg = 4 + 2 * ib, 5 + 2 * ib
                    for (pbx, dst, r_dst, col) in ((pv, cv[ib], r_cv[ib], c), (pg, cgt[ib], r_cg[ib], NFC + c)):
                        op("act", lambda e, pbx=pbx, dst=dst, col=col: e.activation(out=dst[:], in_=PB[pbx][:, 0:256], func=AF.Identity, scale=cvw[:, 0, col:col + 1], bias=cvw[:, 3, col:col + 1]),
                           reads=[rPB[pbx], r_cvw], writes=[r_dst])

                def c2_s2(c):
                    ib = c % 2
                    pv, pg = 4 + 2 * ib, 5 + 2 * ib
                    for (pbx, dst, r_dst, col) in ((pg, cgt[ib], r_cg[ib], NFC + c), (pv, cv[ib], r_cv[ib], c)):
                        op("dve", lambda e, pbx=pbx, dst=dst, col=col: e.scalar_tensor_tensor(out=dst[:], in0=PB[pbx][:, 1:257], scalar=cvw[:, 1, col:col + 1], in1=dst[:], op0=ALU.mult, op1=ALU.add),
                           reads=[rPB[pbx], r_cvw, r_dst], writes=[r_dst])
                        op("dve", lambda e, pbx=pbx, dst=dst, col=col: e.scalar_tensor_tensor(out=dst[:], in0=PB[pbx][:, 2:258], scalar=cvw[:, 2, col:col + 1], in1=dst[:], op0=ALU.mult, op1=ALU.add),
                           reads=[rPB[pbx], r_cvw, r_dst], writes=[r_dst])

                def c2_s3(c):
                    ib = c % 2
                    op("act", lambda e: e.activation(out=gz[ib][:], in_=cgt[ib][:], func=AF.Gelu_apprx_tanh), reads=[r_cg[ib]], writes=[r_gz[ib]])

                def c2_s4(c):
                    ib = c % 2
                    op("dve", lambda e: e.tensor_tensor(out=actb[ib][:], in0=gz[ib][:], in1=cv[ib][:], op=ALU.mult), reads=[r_gz[ib], r_cv[ib]], writes=[r_act[ib]])

                def c2_down(c):
                    ib = c % 2
                    for tt in range(2):
                        for hh in range(2):
                            pa_ = tt * 2 + hh
                            op("pe", lambda e, tt=tt, hh=hh, pa_=pa_: e.matmul(PB[pa_][:], lhsT=actb[ib][:, tt * 128:(tt + 1) * 128], rhs=wD[:, c, hh * 512:(hh + 1) * 512],
                                                                             start=(c == 0), stop=(c == NFC - 1)),
                               reads=[r_act[ib], r_wc[c // 2]], writes=[rPB[pa_]])

                def c2_load(sg):
                    a = sg * 256
                    ub = sg % 2
                    dma("sp", ds_misc[ub], u2[ub][:], u2T_d[s, :, a:a + 258].rearrange("(j p) q -> p j q", p=128), writes=[r_u2[ub]])
                    for tt in range(2):
                        sl = (sg * 2 + tt) % 4
                        dma("sp", ds_x[sl], x1t[sl][:], x1_d[s, a + tt * 128:a + (tt + 1) * 128, :], writes=[r_x1t[sl]])

                def c2_prologue(sg):
                    c2_up(sg, 0); c2_up(sg, 1); c2_s1(0); c2_s2(0); c2_s1(1)

                def c2_ln2(sg):
                    for tt in range(2):
                        tile_i = sg * 2 + tt
                        sl = tile_i % 4
                        ib = tile_i % 2
                        ln_part2(x1t[sl][:], r_x1t[sl], lnB[:, 0, :], lnB[:, 1, :], r_lnB, yt[ib], r_yt[ib], st6[ib], r_st6[ib])
                        dma("sp", ds_o[ib], out_d[s, tile_i * 128:(tile_i + 1) * 128, :], yt[ib][:], reads=[r_yt[ib]])

                c2_load(0)
                if NSG > 1:
                    c2_load(1)
                c2_prologue(0)
                for sg in range(NSG):
                    for c in range(NFC):
                        c2_s3(c)
                        if c + 1 < NFC:
                            c2_s2(c + 1)
                        c2_s4(c)
                        if c + 2 < NFC:
                            c2_up(sg, c + 2)
                            c2_s1(c + 2)
                        c2_down(c)
                    for tt in range(2):
                        tile_i = sg * 2 + tt
                        ib = tile_i % 2
                        ln_part1((tt * 2, tt * 2 + 1), (rPB[tt * 2], rPB[tt * 2 + 1]), 1, yt[ib], r_yt[ib])
                    if sg + 1 < NSG:
                        c2_prologue(sg + 1)
                    c2_ln2(sg)
                    if sg + 2 < NSG:
                        c2_load(sg + 2)
            T.barrier()
        print("build: instructions=%d waits=%d" % (T.n_ins, T.n_wait))
    return nc


def _consts():
    ident = np.eye(128, dtype=np.float32)
    i = np.arange(64)
    maskf = (i[:, None] <= i[None, :]).astype(np.uint8)
    maskb = (i[:, None] >= i[None, :]).astype(np.uint8)
    q = np.arange(128)[:, None]
    cc = np.arange(128)[None, :]
    NEG = -30000.0
    am = np.zeros((128, 384), np.float32)
    am[:, 0:128] = np.where(cc >= q, 0.0, NEG)
    am[:, 256:384] = np.where(cc <= q, 0.0, NEG)
    half = 32
    inv = (np.float32(10000.0) ** (-np.arange(half, dtype=np.float32) / np.float32(half))).astype(np.float32)
    p = np.arange(128)
    rope = np.stack([inv[p % 32], np.where((p % 64) < 32, -1.0, 1.0).astype(np.float32)], axis=1).astype(np.float32)
    return dict(k_ident=ident, k_maskf=maskf, k_maskb=maskb, k_amask=am, k_rope=rope)


def make_in_maps(inputs, n_cores, nseq):
    f = lambda a: np.ascontiguousarray(np.asarray(a))
    shared = dict(
        w_ada=f(inputs["w_ada"][0]), b_ada=f(inputs["b_ada"][0:1]), w_in=f(inputs["w_in"][0]),
        rec_lower_bound=f(inputs["rec_lower_bound"]), rec_norm_g=f(inputs["rec_norm_g"][0:1]),
        w_rec_branch=f(inputs["w_rec_branch"][0]), attn_sink=f(inputs["attn_sink"][0:1]),
        w_attn_branch=f(inputs["w_attn_branch"][0]), w_out=f(inputs["w_out"][0]),
        ln1_g=f(inputs["ln1_g"][0:1]), ln1_b=f(inputs["ln1_b"][0:1]), w_up=f(inputs["w_up"][0]),
        conv_w=f(inputs["conv_w"][0]), conv_b=f(inputs["conv_b"][0:1]), w_down=f(inputs["w_down"][0]),
        ln2_g=f(inputs["ln2_g"][0:1]), ln2_b=f(inputs["ln2_b"][0:1]))
    shared.update(_consts())
    maps = []
    for i in range(n_cores):
        m = dict(shared)
        m["x"] = f(inputs["x"][i * nseq:(i + 1) * nseq])
        m["c"] = f(inputs["c"][i * nseq:(i + 1) * nseq])
        m["positions"] = f(inputs["positions"][i * nseq:(i + 1) * nseq]).astype(np.int32)
        maps.append(m)
    return maps


def kernel(**inputs):
    B, S, _ = inputs["x"].shape
    nseq = B // N_CORES
    nc = build_nc(S=S, NSEQ=nseq)
    in_maps = make_in_maps(inputs, N_CORES, nseq)
    res = run_bass_kernel_spmd(nc, in_maps, core_ids=list(range(N_CORES)))
    return np.concatenate([np.asarray(r["out"]) for r in res.results], axis=0).astype(np.float32)
```

```python
import numpy as np
from contextlib import ExitStack
import concourse.bass as bass
import concourse.mybir as mybir
from concourse.bass_utils import run_bass_kernel_spmd

F32 = mybir.dt.float32
BF16 = mybir.dt.bfloat16
I32 = mybir.dt.int32
U8 = mybir.dt.uint8
AF = mybir.ActivationFunctionType
ALU = mybir.AluOpType
AX = mybir.AxisListType

D = 1024
NK = 8
DIN = 5376
DFF = 2816
NFC = DFF // 128
ALPHA = 2.0 ** 0.25
LN_EPS = 1e-5
RMS_EPS = 1e-6
N_CORES = 8
O_RQ, O_RFF, O_RFB, O_RI, O_RG, O_AQ, O_AK, O_AV, O_GR, O_GA = 0, 512, 1024, 1536, 2048, 2560, 3072, 3200, 3328, 4352


class Res:
    __slots__ = ("w", "rd")

    def __init__(self):
        self.w = None
        self.rd = []


class Tracker:
    def __init__(self, nc, stack):
        self.nc = nc
        self.stack = stack
        self.eng = {}
        self.sems = {}
        for name, h in (("pe", nc.tensor), ("act", nc.scalar), ("dve", nc.vector),
                        ("pool", nc.gpsimd), ("sp", nc.sync)):
            sem = stack.enter_context(nc.semaphore("sem_" + name))
            self.eng[name] = {"h": h, "sem": sem, "cnt": 0, "seen": {}, "name": name}
            self.sems[id(sem)] = sem
        self.dma_sems = []
        self.n_wait = 0
        self.n_ins = 0

    def new_dma_sem(self, name):
        sem = self.stack.enter_context(self.nc.semaphore(name))
        d = {"sem": sem, "cnt": 0}
        self.sems[id(sem)] = sem
        self.dma_sems.append(d)
        return d

    def _wait_deps(self, E, reads, writes):
        deps = {}

        def add(tok):
            if tok is not None and deps.get(tok[0], 0) < tok[1]:
                deps[tok[0]] = tok[1]

        for r in reads:
            add(r.w)
        for w in writes:
            add(w.w)
            for t in w.rd:
                add(t)
        own = id(E["sem"])
        for k, v in deps.items():
            if E["name"] == "pe" and k == own:
                continue
            if E["seen"].get(k, 0) >= v:
                continue
            E["h"].wait_ge(self.sems[k], v)
            E["seen"][k] = v
            self.n_wait += 1

    def _commit(self, tok, reads, writes):
        for r in reads:
            r.rd.append(tok)
            if len(r.rd) > 48:
                best = {}
                for k, v in r.rd:
                    if best.get(k, 0) < v:
                        best[k] = v
                r.rd = list(best.items())
        for w in writes:
            w.w = tok
            w.rd = []
        self.n_ins += 1

    def op(self, eng, fn, reads=(), writes=()):
        E = self.eng[eng]
        self._wait_deps(E, reads, writes)
        ins = fn(E["h"])
        E["cnt"] += 1
        ins.then_inc(E["sem"], 1)
        self._commit((id(E["sem"]), E["cnt"]), reads, writes)
        return ins

    def dma(self, q, dsem, out, in_, reads=(), writes=(), **kw):
        E = self.eng[q]
        self._wait_deps(E, reads, writes)
        ins = E["h"].dma_start(out=out, in_=in_, **kw)
        dsem["cnt"] += 16
        ins.then_inc(dsem["sem"], 16)
        self._commit((id(dsem["sem"]), dsem["cnt"]), reads, writes)
        return ins

    def barrier(self):
        targets = [(id(e["sem"]), e["cnt"]) for e in self.eng.values() if e["cnt"] > 0]
        targets += [(id(d["sem"]), d["cnt"]) for d in self.dma_sems if d["cnt"] > 0]
        for E in self.eng.values():
            for k, v in targets:
                if E["seen"].get(k, 0) >= v:
                    continue
                E["h"].wait_ge(self.sems[k], v)
                E["seen"][k] = v
                self.n_wait += 1


class Rot:
    def __init__(self, items):
        self.items = items
        self.i = 0

    def next(self):
        it = self.items[self.i % len(self.items)]
        self.i += 1
        return it


def build_nc(S=2048, NSEQ=2, dbg=False):
    NT, NG, NCH = S // 128, S // 512, S // 64
    nc = bass.Bass("TRN2", target_bir_lowering=False)

    def din(name, shape, dt=F32):
        return nc.dram_tensor(name, list(shape), dt, kind="ExternalInput").ap()

    x_d = din("x", [NSEQ, S, D])
    c_d = din("c", [NSEQ, D])
    pos_d = din("positions", [NSEQ, S], I32)
    w_ada_d = din("w_ada", [D, 6 * D])
    b_ada_d = din("b_ada", [1, 6 * D])
    w_in_d = din("w_in", [D, DIN])
    lbr_d = din("rec_lower_bound", [2, 2, 512])
    ng_d = din("rec_norm_g", [1, 128])
    w_rec_d = din("w_rec_branch", [512, D])
    sink_d = din("attn_sink", [1, 8])
    w_att_d = din("w_attn_branch", [512, D])
    w_out_d = din("w_out", [D, D])
    ln1g_d = din("ln1_g", [1, D]); ln1b_d = din("ln1_b", [1, D])
    w_up_d = din("w_up", [D, 2 * DFF])
    cw_d = din("conv_w", [3, 2 * DFF])
    cb_d = din("conv_b", [1, 2 * DFF])
    w_dn_d = din("w_down", [DFF, D])
    ln2g_d = din("ln2_g", [1, D]); ln2b_d = din("ln2_b", [1, D])
    identf_d = din("k_ident", [128, 128])
    mskf_d = din("k_maskf", [64, 64], U8)
    mskb_d = din("k_maskb", [64, 64], U8)
    amask_d = din("k_amask", [128, 384])
    rope_d = din("k_rope", [128, 2])
    out_d = nc.dram_tensor("out", [NSEQ, S, D], F32, kind="ExternalOutput").ap()
    x1_d = nc.dram_tensor("x1_scr", [NSEQ, S, D], F32).ap()
    u2T_d = nc.dram_tensor("u2T_scr", [NSEQ, D, S + 2], BF16).ap()
    dbg_d = {}
    if dbg:
        dbg_d["mods"] = nc.dram_tensor("dbg_mods", [128, 4, 8], F32, kind="ExternalOutput").ap()
        dbg_d["QT"] = nc.dram_tensor("dbg_QT", [2, 128, 2, S], BF16, kind="ExternalOutput").ap()
        dbg_d["KT"] = nc.dram_tensor("dbg_KT", [2, 128, 2, S], BF16, kind="ExternalOutput").ap()
        dbg_d["V"] = nc.dram_tensor("dbg_V", [64, NCH, 256], BF16, kind="ExternalOutput").ap()
        dbg_d["oT"] = nc.dram_tensor("dbg_oT", [128, 4, S], F32, kind="ExternalOutput").ap()
        dbg_d["kT"] = nc.dram_tensor("dbg_kT", [128, S], BF16, kind="ExternalOutput").ap()
        dbg_d["x1"] = x1_d

    w_in_k = w_in_d.rearrange("(j p) n -> p j n", p=128)
    w_ada_k = w_ada_d.rearrange("(j p) n -> p j n", p=128)
    w_up_k = w_up_d.rearrange("(j p) n -> p j n", p=128)
    w_out_k = w_out_d.rearrange("(j p) n -> p j n", p=128)
    w_rec_k = w_rec_d.rearrange("(j p) n -> p j n", p=128)
    w_att_k = w_att_d.rearrange("(j p) n -> p j n", p=128)
    w_dn_k = w_dn_d.rearrange("(j p) n -> p j n", p=128)

    with ExitStack() as top:
        T = Tracker(nc, top)
        op, dma = T.op, T.dma

        uniq = [0]

        def SB(st, name, shape, dt):
            uniq[0] += 1
            return st.enter_context(nc.sbuf_tensor("%s_%d" % (name, uniq[0]), list(shape), dt))

        PB = [top.enter_context(nc.psum_tensor("pb%d" % i, [128, 512], F32)) for i in range(8)]
        rPB = [Res() for _ in range(8)]
        ds_const = T.new_dma_sem("ds_const")
        ds_w = T.new_dma_sem("ds_w")
        ds_x = [T.new_dma_sem("ds_x%d" % i) for i in range(4)]
        ds_o = [T.new_dma_sem("ds_o%d" % i) for i in range(4)]
        ds_misc = [T.new_dma_sem("ds_m%d" % i) for i in range(4)]
        ds_dbg = T.new_dma_sem("ds_dbg")
        ds_wc = [T.new_dma_sem("ds_wc%d" % i) for i in range(11)]

        identf = SB(top, "identf", [128, 128], F32); r_identf = Res()
        identb = SB(top, "identb", [128, 128], BF16); r_identb = Res()
        onesf = SB(top, "onesf", [128, 128], F32); r_onesf = Res()
        maskf = SB(top, "maskf", [64, 64], U8); maskb = SB(top, "maskb", [64, 64], U8); r_mask = Res()
        zer = SB(top, "zer", [128, 128], F32); r_zer = Res()
        rmask = SB(top, "rmask", [128, 512], F32); r_rmask = Res()
        lbt = SB(top, "lbt", [128, 3, 2, 4], F32); r_lbt = Res()
        lraw = SB(top, "lraw", [128, 2, 2, 4], F32); r_lraw = Res()
        ngc = SB(top, "ngc", [128, 1], F32); r_ngc = Res()
        sinkb = SB(top, "sinkb", [128, 2, 8], F32); r_sink = Res()
        rope = SB(top, "rope", [128, 2], F32); r_rope = Res()
        epsc = SB(top, "epsc", [128, 2], F32); r_eps = Res()
        modsT_all = SB(top, "modsT", [128, NSEQ, 4, 8], F32); r_modsT = Res()
        gaB_all = SB(top, "gaB", [128, NSEQ, 2, D], F32); r_gaB = [Res(), Res()]
        modsT = modsT_all[:, 0]; gaB = gaB_all[:, 0]
        amask = SB(top, "amask", [128, 384], F32); r_amask = Res()

        dma("sp", ds_const, identf[:], identf_d, writes=[r_identf])
        dma("sp", ds_const, maskf[:], mskf_d, writes=[r_mask])
        dma("sp", ds_const, maskb[:], mskb_d, writes=[r_mask])
        dma("sp", ds_const, rope[:], rope_d, writes=[r_rope])
        dma("sp", ds_const, amask[:], amask_d, writes=[r_amask])
        dma("sp", ds_const, ngc[:], ng_d.rearrange("o p -> p o"), writes=[r_ngc], allow_slow_non_contiguous=True)
        dma("sp", ds_const, sinkb[:, 0, :], sink_d[0:1, :].broadcast_to([128, 8]), writes=[r_sink])
        dma("sp", ds_const, lraw[:].rearrange("p a b h -> p (a b) h"),
            lbr_d.rearrange("a b (h p) -> p (a b) h", p=128), writes=[r_lraw], allow_slow_non_contiguous=True)
        T.barrier()
        op("dve", lambda e: e.tensor_copy(out=identb[:], in_=identf[:]), reads=[r_identf], writes=[r_identb])
        op("pool", lambda e: e.memset(onesf[:], 1.0), writes=[r_onesf])
        op("pool", lambda e: e.memset(zer[:], 0.0), writes=[r_zer])
        op("pool", lambda e: e.memset(rmask[:], 1.0), writes=[r_rmask])
        op("pool", lambda e: e.memset(rmask[:].rearrange("p (c t) -> p c t", t=64)[:, :, 0:1], 0.0), writes=[r_rmask])
        op("pool", lambda e: e.memset(epsc[:, 0:1], LN_EPS), writes=[r_eps])
        op("pool", lambda e: e.memset(epsc[:, 1:2], RMS_EPS), writes=[r_eps])
        op("dve", lambda e: e.tensor_scalar(out=sinkb[:, 1, :], in0=sinkb[:, 0, :], scalar1=-1.0, scalar2=None, op0=ALU.mult),
           reads=[r_sink], writes=[r_sink])
        op("dve", lambda e: e.tensor_tensor(out=lbt[:, 2, :, :], in0=lraw[:, :, 0, :], in1=lraw[:, :, 1, :], op=ALU.subtract),
           reads=[r_lraw], writes=[r_lbt])
        op("act", lambda e: e.activation(out=lbt[:, 0, :, :], in_=lbt[:, 2, :, :], func=AF.Sigmoid), reads=[r_lbt], writes=[r_lbt])
        op("act", lambda e: e.activation(out=lbt[:, 1, :, :], in_=lbt[:, 2, :, :], func=AF.Sigmoid, scale=-1.0), reads=[r_lbt], writes=[r_lbt])
        op("dve", lambda e: e.tensor_scalar(out=lbt[:, 2, :, :], in0=lbt[:, 1, :, :], scalar1=-1.0, scalar2=None, op0=ALU.mult),
           reads=[r_lbt], writes=[r_lbt])

        def make_uT(xt, r_xt, dst, r_dst, scol, bcol, banks=(0, 1)):
            for hb in range(2):
                b = banks[hb]
                for i in range(4):
                    k = hb * 4 + i
                    op("pe", lambda e, k=k, i=i, b=b: e.transpose(out=PB[b][:, i * 128:(i + 1) * 128], in_=xt[:, k * 128:(k + 1) * 128],
                                                                   identity=identf[:]),
                       reads=[r_xt, r_identf], writes=[rPB[b]])
                for i in range(4):
                    k = hb * 4 + i
                    if i % 2 == 0:
                        op("act", lambda e, k=k, i=i, b=b: e.activation(out=dst[:, k, :], in_=PB[b][:, i * 128:(i + 1) * 128], func=AF.Identity,
                                                                         scale=modsT[:, scol, k:k + 1], bias=modsT[:, bcol, k:k + 1]),
                           reads=[rPB[b], r_modsT], writes=[r_dst])
                    else:
                        op("dve", lambda e, k=k, i=i, b=b: e.tensor_scalar(out=dst[:, k, :], in0=PB[b][:, i * 128:(i + 1) * 128],
                                                                            scalar1=modsT[:, scol, k:k + 1], scalar2=modsT[:, bcol, k:k + 1],
                                                                            op0=ALU.mult, op1=ALU.add),
                           reads=[rPB[b], r_modsT], writes=[r_dst])

        def ln_part1(pz, r_pz, which, y, r_y):
            for h in range(2):
                op("dve", lambda e, h=h: e.tensor_tensor(out=y[:, h * 512:(h + 1) * 512], in0=PB[pz[h]][:], in1=gaB[:, which, h * 512:(h + 1) * 512], op=ALU.mult),
                   reads=[r_pz[h], r_gaB[which]], writes=[r_y])

        def ln_part2(xres, r_xres, gB, bB, r_gb, y, r_y, st6, r_st):
            op("dve", lambda e: e.scalar_tensor_tensor(out=y[:], in0=xres, scalar=ALPHA, in1=y[:], op0=ALU.mult, op1=ALU.add),
               reads=[r_xres, r_y], writes=[r_y])
            for h in range(2):
                op("dve", lambda e, h=h: e.bn_stats(out=st6[:, h * 6:(h + 1) * 6], in_=y[:, h * 512:(h + 1) * 512]), reads=[r_y], writes=[r_st])
            op("dve", lambda e: e.bn_aggr(out=st6[:, 12:14], in_=st6[:, 0:12]), reads=[r_st], writes=[r_st])
            op("act", lambda e: e.activation(out=st6[:, 14:15], in_=st6[:, 13:14], func=AF.Ln, bias=epsc[:, 0:1], scale=1.0), reads=[r_st, r_eps], writes=[r_st])
            op("act", lambda e: e.activation(out=st6[:, 15:16], in_=st6[:, 14:15], func=AF.Exp, scale=-0.5), reads=[r_st], writes=[r_st])
            op("dve", lambda e: e.tensor_scalar(out=st6[:, 16:17], in0=st6[:, 12:13], scalar1=st6[:, 15:16], scalar2=-1.0, op0=ALU.mult, op1=ALU.mult),
               reads=[r_st], writes=[r_st])
            op("act", lambda e: e.activation(out=y[:], in_=y[:], func=AF.Identity, scale=st6[:, 15:16], bias=st6[:, 16:17]),
               reads=[r_y, r_st], writes=[r_y])
            op("dve", lambda e: e.tensor_tensor(out=y[:], in0=y[:], in1=gB, op=ALU.mult), reads=[r_y, r_gb], writes=[r_y])
            op("dve", lambda e: e.tensor_tensor(out=y[:], in0=y[:], in1=bB, op=ALU.add), reads=[r_y, r_gb], writes=[r_y])

        def ln_epilogue(pz, r_pz, xres, r_xres, which, gB, bB, r_gb, y, r_y, st6, r_st):
            ln_part1(pz, r_pz, which, y, r_y)
            ln_part2(xres, r_xres, gB, bB, r_gb, y, r_y, st6, r_st)

        for s in range(NSEQ):
            modsT = modsT_all[:, s]
            gaB = gaB_all[:, s]
            if s == 0:
              with ExitStack() as ph:
                cT = SB(ph, "cT", [128, NSEQ, 8], F32); r_cT = Res()
                cbt = SB(ph, "cbt", [128, NSEQ, 8, 128], F32); r_cbt = Res()
                wst = [SB(ph, "wst%d" % i, [128, 8, 512], F32) for i in range(2)]; r_wst = [Res(), Res()]
                bst = [SB(ph, "bst%d" % i, [1, 512], F32) for i in range(2)]; r_bst = [Res(), Res()]
                mtmp = [SB(ph, "mtmp%d" % i, [128, 512], F32) for i in range(NSEQ)]; r_mtmp = [Res() for _ in range(NSEQ)]
                for ss in range(NSEQ):
                    dma("sp", ds_misc[0], cT[:, ss, :], c_d[ss].rearrange("(j p) -> p j", p=128), writes=[r_cT], allow_slow_non_contiguous=True)
                op("act", lambda e: e.activation(out=cT[:], in_=cT[:], func=AF.Silu), reads=[r_cT], writes=[r_cT])
                op("dve", lambda e: e.tensor_copy(out=cbt[:], in_=cT[:].unsqueeze(3).broadcast_to([128, NSEQ, 8, 128])), reads=[r_cT], writes=[r_cbt])
                for cg in range(12):
                    b2 = cg % 2
                    dma("sp", ds_misc[1 + b2], wst[b2][:], w_ada_k[:, :, cg * 512:(cg + 1) * 512], writes=[r_wst[b2]])
                    dma("sp", ds_o[b2], bst[b2][:], b_ada_d[0:1, cg * 512:(cg + 1) * 512], writes=[r_bst[b2]])
                    which, half = cg // 2, cg % 2
                    for ss in range(NSEQ):
                        pb = 2 + b2 + 2 * (ss % 2)
                        for k in range(8):
                            op("pe", lambda e, k=k: e.matmul(PB[pb][:], lhsT=cbt[:, ss, k, :], rhs=wst[b2][:, k, :], start=(k == 0), stop=False),
                               reads=[r_cbt, r_wst[b2]], writes=[rPB[pb]])
                        op("pe", lambda e: e.matmul(PB[pb][:], lhsT=onesf[0:1, :], rhs=bst[b2][0:1, :], start=False, stop=True),
                           reads=[r_onesf, r_bst[b2]], writes=[rPB[pb]])
                        if which in (2, 5):
                            gi = 0 if which == 2 else 1
                            op("dve", lambda e: e.tensor_scalar(out=gaB_all[:, ss, gi, half * 512:(half + 1) * 512], in0=PB[pb][:], scalar1=1.0, scalar2=None, op0=ALU.add),
                               reads=[rPB[pb]], writes=[r_gaB[gi]])
                        else:
                            widx = {0: 0, 1: 1, 3: 2, 4: 3}[which]
                            addc = 1.0 if which in (1, 4) else 0.0
                            op("dve", lambda e: e.tensor_scalar(out=mtmp[ss][:], in0=PB[pb][:], scalar1=addc, scalar2=None, op0=ALU.add),
                               reads=[rPB[pb]], writes=[r_mtmp[ss]])
                            pt = 6 + (ss % 2)
                            for i in range(4):
                                op("pe", lambda e, i=i: e.transpose(out=PB[pt][:, i * 128:(i + 1) * 128], in_=mtmp[ss][:, i * 128:(i + 1) * 128], identity=identf[:]),
                                   reads=[r_mtmp[ss], r_identf], writes=[rPB[pt]])
                            op("dve", lambda e: e.tensor_copy(out=modsT_all[:, ss, widx, half * 4:(half + 1) * 4],
                                                              in_=PB[pt][:].rearrange("p (i m) -> p i m", m=128)[:, :, 0]),
                               reads=[rPB[pt]], writes=[r_modsT])
                if dbg:
                    dma("sp", ds_dbg, dbg_d["mods"], modsT_all[:, 0], reads=[r_modsT])
              T.barrier()

            with ExitStack() as pab:
                oT = SB(pab, "oT", [128, 4, S], F32)
                r_oT = [[Res() for _ in range(NCH)] for _ in range(4)]
                kTt = SB(pab, "kTt", [128, S], BF16); r_kT = [Res() for _ in range(NG)]
                vtk = SB(pab, "vtk", [128, NT, 128], BF16); r_v = [Res() for _ in range(NT)]
                cosT = SB(pab, "cosT", [128, S], F32); sinT = SB(pab, "sinT", [128, S], F32); r_cs = Res()
                with ExitStack() as pr:
                    posi = SB(pr, "posi", [128, S], I32); r_posi = Res()
                    ang = SB(pr, "ang", [128, S], F32); r_ang = Res()
                    kq = SB(pr, "kq", [128, S], F32); r_kq = Res()
                    ki = SB(pr, "ki", [128, S], I32); r_ki = Res()
                    dma("sp", ds_misc[0], posi[:], pos_d[s:s + 1, :].broadcast_to([128, S]), writes=[r_posi])
                    op("dve", lambda e: e.tensor_copy(out=ang[:], in_=posi[:]), reads=[r_posi], writes=[r_ang])
                    op("dve", lambda e: e.tensor_scalar(out=ang[:], in0=ang[:], scalar1=rope[:, 0:1], scalar2=None, op0=ALU.mult),
                       reads=[r_ang, r_rope], writes=[r_ang])
                    TWO_PI = 6.283185307179586
                    C1 = 6.28125
                    C2 = TWO_PI - C1
                    for (dst, shift) in ((sinT, 0.0), (cosT, np.pi / 2)):
                        op("dve", lambda e, shift=shift: e.tensor_scalar(out=kq[:], in0=ang[:], scalar1=float(shift), scalar2=1.0 / TWO_PI, op0=ALU.add, op1=ALU.mult),
                           reads=[r_ang], writes=[r_kq])
                        op("dve", lambda e: e.tensor_copy(out=ki[:], in_=kq[:]), reads=[r_kq], writes=[r_ki])
                        op("dve", lambda e: e.tensor_copy(out=kq[:], in_=ki[:]), reads=[r_ki], writes=[r_kq])
                        op("dve", lambda e, shift=shift, dst=dst: e.tensor_scalar(out=dst[:], in0=ang[:], scalar1=float(shift), scalar2=None, op0=ALU.add),
                           reads=[r_ang], writes=[r_cs])
                        op("dve", lambda e, dst=dst: e.scalar_tensor_tensor(out=dst[:], in0=kq[:], scalar=-C1, in1=dst[:], op0=ALU.mult, op1=ALU.add),
                           reads=[r_kq, r_cs], writes=[r_cs])
                        op("dve", lambda e, dst=dst: e.scalar_tensor_tensor(out=dst[:], in0=kq[:], scalar=-C2, in1=dst[:], op0=ALU.mult, op1=ALU.add),
                           reads=[r_kq, r_cs], writes=[r_cs])
                        op("dve", lambda e, dst=dst: e.tensor_scalar(out=kq[:], in0=dst[:], scalar1=float(np.pi), scalar2=-TWO_PI, op0=ALU.is_gt, op1=ALU.mult),
                           reads=[r_cs], writes=[r_kq])
                        op("dve", lambda e, dst=dst: e.tensor_tensor(out=dst[:], in0=dst[:], in1=kq[:], op=ALU.add), reads=[r_cs, r_kq], writes=[r_cs])
                        op("dve", lambda e, dst=dst: e.tensor_scalar(out=kq[:], in0=dst[:], scalar1=float(-np.pi), scalar2=TWO_PI, op0=ALU.is_lt, op1=ALU.mult),
                           reads=[r_cs], writes=[r_kq])
                        op("dve", lambda e, dst=dst: e.tensor_tensor(out=dst[:], in0=dst[:], in1=kq[:], op=ALU.add), reads=[r_cs, r_kq], writes=[r_cs])
                        op("dve", lambda e, dst=dst: e.tensor_scalar(out=dst[:], in0=dst[:], scalar1=3.1415925, scalar2=-3.1415925, op0=ALU.min, op1=ALU.max),
                           reads=[r_cs], writes=[r_cs])
                        op("act", lambda e, dst=dst: e.activation(out=dst[:], in_=dst[:], func=AF.Sin), reads=[r_cs], writes=[r_cs])
                    op("dve", lambda e: e.tensor_scalar(out=sinT[:], in0=sinT[:], scalar1=rope[:, 1:2], scalar2=None, op0=ALU.mult),
                       reads=[r_cs, r_rope], writes=[r_cs])
                T.barrier()

                for hp in range(2):
                    with ExitStack() as pa:
                        NWA = 1024 + (384 if hp == 0 else 0)
                        wA = SB(pa, "wA", [128, 8, NWA], BF16); r_wA = Res()
                        QT = [SB(pa, "QT%d" % d, [128, 2, S], BF16) for d in range(2)]
                        KT = [SB(pa, "KT%d" % d, [128, 2, S], BF16) for d in range(2)]
                        r_QK = [[[Res() for _ in range(NG)] for _ in range(2)] for _ in range(2)]
                        Vt = SB(pa, "Vt", [64, NCH, 256], BF16); r_V = [Res() for _ in range(NG)]
                        Bref = SB(pa, "Bref", [128, 2, 2, NCH], F32); Bend = SB(pa, "Bend", [128, 2, 2, NCH], F32); r_B = Res()
                        Gx = SB(pa, "Gx", [128, 2, 2, NCH], F32); r_G = Res()
                        for seg, base in enumerate((O_RQ, O_RFF, O_RFB, O_RI)):
                            dma("pool", ds_w, wA[:, :, seg * 256:(seg + 1) * 256], w_in_k[:, :, base + hp * 256: base + hp * 256 + 256], writes=[r_wA])
                        if hp == 0:
                            dma("pool", ds_w, wA[:, :, 1024:1152], w_in_k[:, :, O_AK:O_AK + 128], writes=[r_wA])
                            for kv in range(2):
                                dma("pool", ds_w, wA[:, :, 1152 + kv * 64:1152 + kv * 64 + 32], w_in_k[:, :, O_AK + kv * 64 + 32:O_AK + kv * 64 + 64], writes=[r_wA])
                                dma("pool", ds_w, wA[:, :, 1152 + kv * 64 + 32:1152 + kv * 64 + 64], w_in_k[:, :, O_AK + kv * 64:O_AK + kv * 64 + 32], writes=[r_wA])
                            dma("pool", ds_w, wA[:, :, 1280:1408], w_in_k[:, :, O_AV:O_AV + 128], writes=[r_wA])
                        with ExitStack() as pa2:
                            xs = [SB(pa2, "xs%d" % i, [128, D], F32) for i in range(2)]; r_xs = [Res() for _ in range(2)]
                            uT = [SB(pa2, "uT%d" % i, [128, 8, 512], BF16) for i in range(1)]; r_uT = [Res()]
                            qs = [SB(pa2, "qs%d" % i, [128, 512], F32) for i in range(2)]; r_qs = [Res(), Res()]
                            NTMP = 2
                            tnames = ("sig", "lg", "Bf", "Eq", "Ek")
                            tmps = [{n: SB(pa2, "t_%s%d" % (n, i), [128, 512], F32) for n in tnames} for i in range(NTMP)]
                            r_tmps = [{n: Res() for n in tnames} for i in range(NTMP)]
                            for i in range(NTMP):
                                tmps[i]["kk"] = tmps[i]["sig"]; r_tmps[i]["kk"] = r_tmps[i]["sig"]
                                tmps[i]["dd"] = tmps[i]["lg"]; r_tmps[i]["dd"] = r_tmps[i]["lg"]
                            tctr = 0
                            bank_rot = Rot([2, 3, 4, 5, 6, 7])
                            for g in range(NG):
                                ub = 0
                                for t in range(4):
                                    tile_i = g * 4 + t
                                    sl = tile_i % 2
                                    dma("sp", ds_x[sl], xs[sl][:], x_d[s, tile_i * 128:(tile_i + 1) * 128, :], writes=[r_xs[sl]])
                                    make_uT(xs[sl], r_xs[sl], uT[ub][:, :, t * 128:(t + 1) * 128], r_uT[ub], 1, 0)
                                gs = slice(g * 512, (g + 1) * 512)
                                for cp in range(4):
                                    pb = bank_rot.next()
                                    for cc in range(2):
                                        ch = cp * 2 + cc
                                        for k in range(8):
                                            op("pe", lambda e, k=k, cc=cc, ch=ch, pb=pb: e.matmul(PB[pb][0:64, cc * 256:(cc + 1) * 256], lhsT=uT[ub][:, k, ch * 64:(ch + 1) * 64],
                                                                                                   rhs=wA[:, k, 768:1024], start=(k == 0), stop=(k == 7)),
                                               reads=[r_uT[ub], r_wA], writes=[rPB[pb]])
                                    op("act", lambda e, pb=pb, cp=cp: e.activation(out=Vt[:, g * 8 + cp * 2:g * 8 + cp * 2 + 2, :],
                                                                                    in_=PB[pb][0:64, :].rearrange("p (c n) -> p c n", n=256), func=AF.Copy, scale=128.0 ** -0.5),
                                       reads=[rPB[pb]], writes=[r_V[g]])
                                for hl in range(2):
                                    h = hp * 2 + hl
                                    pb = bank_rot.next()
                                    for k in range(8):
                                        op("pe", lambda e, k=k, pb=pb, hl=hl: e.matmul(PB[pb][:], lhsT=wA[:, k, hl * 128:(hl + 1) * 128], rhs=uT[ub][:, k, :],
                                                                                         start=(k == 0), stop=(k == 7)), reads=[r_uT[ub], r_wA], writes=[rPB[pb]])
                                    op("act", lambda e, pb=pb, hl=hl: e.activation(out=qs[hl][:], in_=PB[pb][:], func=AF.Silu), reads=[rPB[pb]], writes=[r_qs[hl]])
                                    for d in range(2):
                                        tm = tmps[tctr % NTMP]; rt = r_tmps[tctr % NTMP]; tctr += 1
                                        pb = bank_rot.next()
                                        cb = 256 * (1 + d) + hl * 128
                                        for k in range(8):
                                            op("pe", lambda e, k=k, pb=pb, cb=cb: e.matmul(PB[pb][:], lhsT=wA[:, k, cb:cb + 128], rhs=uT[ub][:, k, :],
                                                                                             start=(k == 0), stop=(k == 7)), reads=[r_uT[ub], r_wA], writes=[rPB[pb]])
                                        lb_c = lbt[:, 0, d, h:h + 1]; om_c = lbt[:, 1, d, h:h + 1]; nom_c = lbt[:, 2, d, h:h + 1]
                                        op("act", lambda e, pb=pb, tm=tm: e.activation(out=tm["sig"][:], in_=PB[pb][:], func=AF.Sigmoid), reads=[rPB[pb]], writes=[rt["sig"]])
                                        op("act", lambda e, tm=tm: e.activation(out=tm["lg"][:], in_=tm["sig"][:], func=AF.Ln, scale=om_c, bias=lb_c),
                                           reads=[rt["sig"], r_lbt], writes=[rt["lg"]])
                                        op("dve", lambda e, tm=tm: e.tensor_scalar(out=tm["kk"][:], in0=tm["sig"][:], scalar1=nom_c, scalar2=om_c, op0=ALU.mult, op1=ALU.add),
                                           reads=[rt["sig"], r_lbt], writes=[rt["kk"]])
                                        op("dve", lambda e, tm=tm: e.tensor_tensor_scan(out=tm["Bf"][:], data0=rmask[:], data1=tm["lg"][:], initial=0.0, op0=ALU.mult, op1=ALU.add),
                                           reads=[r_rmask, rt["lg"]], writes=[rt["Bf"]])
                                        B3 = tm["Bf"][:].rearrange("p (c t) -> p c t", t=64)
                                        if d == 1:
                                            lg3 = tm["lg"][:].rearrange("p (c t) -> p c t", t=64)
                                            op("dve", lambda e, B3=B3, lg3=lg3: e.tensor_tensor(out=lg3, in0=lg3, in1=B3, op=ALU.subtract), reads=[rt["lg"], rt["Bf"]], writes=[rt["lg"]])
                                            op("dve", lambda e, B3=B3, lg3=lg3: e.tensor_tensor(out=B3, in0=lg3, in1=B3[:, :, 63:64].broadcast_to([128, 8, 64]), op=ALU.add),
                                               reads=[rt["lg"], rt["Bf"]], writes=[rt["Bf"]])
                                            iref, iend = 32, 0
                                        else:
                                            iref, iend = 31, 63
                                        op("act", lambda e, B3=B3, iref=iref: e.activation(out=Bref[:, d, hl, g * 8:(g + 1) * 8], in_=B3[:, :, iref], func=AF.Copy), reads=[rt["Bf"]], writes=[r_B])
                                        op("act", lambda e, B3=B3, iend=iend: e.activation(out=Bend[:, d, hl, g * 8:(g + 1) * 8], in_=B3[:, :, iend], func=AF.Copy), reads=[rt["Bf"]], writes=[r_B])
                                        op("dve", lambda e, B3=B3, tm=tm, iref=iref: e.tensor_tensor(out=tm["dd"][:].rearrange("p (c t) -> p c t", t=64), in0=B3,
                                                                                                       in1=B3[:, :, iref:iref + 1].broadcast_to([128, 8, 64]), op=ALU.subtract),
                                           reads=[rt["Bf"]], writes=[rt["dd"]])
                                        op("act", lambda e, tm=tm: e.activation(out=tm["Eq"][:], in_=tm["dd"][:], func=AF.Exp), reads=[rt["dd"]], writes=[rt["Eq"]])
                                        op("act", lambda e, tm=tm: e.activation(out=tm["Ek"][:], in_=tm["dd"][:], func=AF.Exp, scale=-1.0), reads=[rt["dd"]], writes=[rt["Ek"]])
                                        op("dve", lambda e, tm=tm, d=d, hl=hl: e.tensor_tensor(out=KT[d][:, hl, gs], in0=tm["kk"][:], in1=tm["Ek"][:], op=ALU.mult),
                                           reads=[rt["kk"], rt["Ek"]], writes=[r_QK[d][hl][g]])
                                        op("dve", lambda e, tm=tm, d=d, hl=hl: e.tensor_tensor(out=QT[d][:, hl, gs], in0=qs[hl][:], in1=tm["Eq"][:], op=ALU.mult),
                                           reads=[r_qs[hl], rt["Eq"]], writes=[r_QK[d][hl][g]])
                                if hp == 0:
                                    pb1 = bank_rot.next()
                                    pb2 = bank_rot.next()
                                    for (pbx, cb) in ((pb1, 1024), (pb2, 1152)):
                                        for k in range(8):
                                            op("pe", lambda e, k=k, pbx=pbx, cb=cb: e.matmul(PB[pbx][:], lhsT=wA[:, k, cb:cb + 128], rhs=uT[ub][:, k, :], start=(k == 0), stop=(k == 7)),
                                               reads=[r_uT[ub], r_wA], writes=[rPB[pbx]])
                                    tm = tmps[tctr % NTMP]; rt = r_tmps[tctr % NTMP]; tctr += 1
                                    op("dve", lambda e, tm=tm: e.tensor_tensor(out=tm["sig"][:], in0=PB[pb1][:], in1=cosT[:, gs], op=ALU.mult), reads=[rPB[pb1], r_cs], writes=[rt["sig"]])
                                    op("dve", lambda e, tm=tm: e.tensor_tensor(out=tm["lg"][:], in0=PB[pb2][:], in1=sinT[:, gs], op=ALU.mult), reads=[rPB[pb2], r_cs], writes=[rt["lg"]])
                                    op("dve", lambda e, tm=tm: e.tensor_tensor(out=kTt[:, gs], in0=tm["sig"][:], in1=tm["lg"][:], op=ALU.add), reads=[rt["sig"], rt["lg"]], writes=[r_kT[g]])
                                    for t in range(4):
                                        pb = bank_rot.next()
                                        for k in range(8):
                                            op("pe", lambda e, k=k, pb=pb, t=t: e.matmul(PB[pb][:, 0:128], lhsT=uT[ub][:, k, t * 128:(t + 1) * 128], rhs=wA[:, k, 1280:1408],
                                                                                          start=(k == 0), stop=(k == 7)), reads=[r_uT[ub], r_wA], writes=[rPB[pb]])
                                        op("act", lambda e, pb=pb, t=t: e.activation(out=vtk[:, g * 4 + t, :], in_=PB[pb][:, 0:128], func=AF.Copy), reads=[rPB[pb]], writes=[r_v[g * 4 + t]])
                            op("dve", lambda e: e.tensor_tensor(out=Gx[:], in0=Bend[:], in1=Bref[:], op=ALU.subtract), reads=[r_B], writes=[r_G])
                            if NCH > 1:
                                op("dve", lambda e: e.tensor_tensor(out=Gx[:, 0, :, 0:NCH - 1], in0=Gx[:, 0, :, 0:NCH - 1], in1=Bref[:, 0, :, 1:NCH], op=ALU.add), reads=[r_B, r_G], writes=[r_G])
                                op("dve", lambda e: e.tensor_tensor(out=Gx[:, 1, :, 1:NCH], in0=Gx[:, 1, :, 1:NCH], in1=Bref[:, 1, :, 0:NCH - 1], op=ALU.add), reads=[r_B, r_G], writes=[r_G])
                            op("act", lambda e: e.activation(out=Gx[:], in_=Gx[:], func=AF.Exp), reads=[r_G], writes=[r_G])
                            if dbg and s == 0 and hp == 0:
                                for d in range(2):
                                    dma("sp", ds_dbg, dbg_d["QT"][d], QT[d][:], reads=[r_QK[d][hl][g] for hl in range(2) for g in range(NG)])
                                    dma("sp", ds_dbg, dbg_d["KT"][d], KT[d][:], reads=[r_QK[d][hl][g] for hl in range(2) for g in range(NG)])
                                dma("sp", ds_dbg, dbg_d["V"], Vt[:], reads=r_V)
                                dma("sp", ds_dbg, dbg_d["kT"], kTt[:], reads=r_kT)
                        T.barrier()

                        with ExitStack() as pb_:
                            Sm32 = SB(pb_, "Sm32", [128, 4, 128], F32); Smb = SB(pb_, "Smb", [128, 4, 128], BF16); tmp32 = SB(pb_, "tmp32", [128, 4, 128], F32)
                            r_Sm32 = [Res() for _ in range(4)]; r_Smb = [Res() for _ in range(4)]; r_tmp32 = [Res() for _ in range(4)]
                            scm = SB(pb_, "scm", [64, 2, 4, 64], BF16); r_scm = [[Res() for _ in range(4)] for _ in range(2)]
                            ktok = SB(pb_, "ktok", [64, 2, 4, 128], BF16); r_ktok = [[Res() for _ in range(4)] for _ in range(2)]
                            rB = [[None] * 4 for _ in range(2)]
                            for _p in range(2):
                                for _c in range(4):
                                    _r = Res()
                                    rB[_p][_c] = {n: _r for n in ("sc", "kt", "o", "dl")}
                            op("pool", lambda e: e.memset(Sm32[:], 0.0), writes=r_Sm32)
                            op("pool", lambda e: e.memset(Smb[:], 0.0), writes=r_Smb)
                            allQK = lambda d, hl: [r_QK[d][hl][g] for g in range(NG)]

                            def chunk_of(j, d):
                                return j if d == 0 else NCH - 1 - j

                            def stage1(j):
                                par = j % 2
                                for ci in range(4):
                                    hl, d = ci // 2, ci % 2
                                    c = chunk_of(j, d); tk = slice(c * 64, (c + 1) * 64)
                                    bk = ci + 4 * par
                                    op("pe", lambda e, bk=bk, d=d, hl=hl, tk=tk: e.matmul(PB[bk][0:64, 0:64], lhsT=KT[d][:, hl, tk], rhs=QT[d][:, hl, tk], start=True, stop=True),
                                       reads=allQK(d, hl), writes=[rB[par][ci]["sc"]])
                                    op("pe", lambda e, bk=bk, d=d, hl=hl, tk=tk: e.transpose(out=PB[bk][:].bitcast(BF16)[0:64, 128:256], in_=KT[d][:, hl, tk], identity=identb[:]),
                                       reads=allQK(d, hl) + [r_identb], writes=[rB[par][ci]["kt"]])

                            def stage2(j):
                                par = j % 2
                                for ci in range(4):
                                    hl, d = ci // 2, ci % 2
                                    bk = ci + 4 * par
                                    mk = maskf if d == 0 else maskb
                                    op("pool", lambda e, ci=ci: e.memset(scm[:, par, ci, :], 0.0), writes=[r_scm[par][ci]])
                                    op("dve", lambda e, bk=bk, ci=ci, mk=mk: e.copy_predicated(out=scm[:, par, ci, :], mask=mk[:], data=PB[bk][0:64, 0:64]),
                                       reads=[rB[par][ci]["sc"], r_mask], writes=[r_scm[par][ci]])
                                    op("act", lambda e, bk=bk, ci=ci: e.activation(out=ktok[:, par, ci, :], in_=PB[bk][:].bitcast(BF16)[0:64, 128:256], func=AF.Copy),
                                       reads=[rB[par][ci]["kt"]], writes=[r_ktok[par][ci]])

                            def stage3(j):
                                par = j % 2
                                for ci in range(4):
                                    hl, d = ci // 2, ci % 2
                                    c = chunk_of(j, d); tk = slice(c * 64, (c + 1) * 64)
                                    bk = ci + 4 * par
                                    vv = Vt[:, c, hl * 128:(hl + 1) * 128]
                                    op("pe", lambda e, bk=bk, vv=vv, ci=ci: e.matmul(PB[bk][:, 128:192], lhsT=vv, rhs=scm[:, par, ci, :], start=True, stop=False),
                                       reads=[r_V[c // 8], r_scm[par][ci]], writes=[rB[par][ci]["o"]])
                                    op("pe", lambda e, bk=bk, ci=ci, d=d, hl=hl, tk=tk: e.matmul(PB[bk][:, 128:192], lhsT=Smb[:, ci, :], rhs=QT[d][:, hl, tk], start=False, stop=True),
                                       reads=[r_Smb[ci]] + allQK(d, hl), writes=[rB[par][ci]["o"]])
                                    op("pe", lambda e, bk=bk, vv=vv, ci=ci: e.matmul(PB[bk][:, 192:320], lhsT=ktok[:, par, ci, :], rhs=vv, start=True, stop=True),
                                       reads=[r_ktok[par][ci], r_V[c // 8]], writes=[rB[par][ci]["dl"]])

                            def stage4(j):
                                par = j % 2
                                for ci in range(4):
                                    hl, d = ci // 2, ci % 2
                                    h = hp * 2 + hl
                                    c = chunk_of(j, d); tk = slice(c * 64, (c + 1) * 64)
                                    bk = ci + 4 * par
                                    if j < NCH // 2:
                                        op("act", lambda e, bk=bk, h=h, tk=tk: e.activation(out=oT[:, h, tk], in_=PB[bk][:, 128:192], func=AF.Copy),
                                           reads=[rB[par][ci]["o"]], writes=[r_oT[h][c]])
                                    else:
                                        op("dve", lambda e, bk=bk, h=h, tk=tk: e.tensor_tensor(out=oT[:, h, tk], in0=PB[bk][:, 128:192], in1=oT[:, h, tk], op=ALU.add),
                                           reads=[rB[par][ci]["o"], r_oT[h][c]], writes=[r_oT[h][c]])
                                    if j < NCH - 1:
                                        gcol = Gx[:, d, hl, c:c + 1]
                                        op("dve", lambda e, bk=bk, ci=ci: e.tensor_tensor(out=tmp32[:, ci, :], in0=PB[bk][:, 192:320], in1=Sm32[:, ci, :], op=ALU.add),
                                           reads=[rB[par][ci]["dl"], r_Sm32[ci]], writes=[r_tmp32[ci]])
                                        op("act", lambda e, ci=ci, gcol=gcol: e.activation(out=Sm32[:, ci, :], in_=tmp32[:, ci, :], func=AF.Identity, scale=gcol),
                                           reads=[r_tmp32[ci], r_G], writes=[r_Sm32[ci]])
                                        op("pool", lambda e, ci=ci, gcol=gcol: e.tensor_scalar(out=Smb[:, ci, :], in0=tmp32[:, ci, :], scalar1=gcol, scalar2=None, op0=ALU.mult),
                                           reads=[r_tmp32[ci], r_G], writes=[r_Smb[ci]])

                            stage1(0)
                            stage2(0)
                            for j in range(NCH):
                                if j + 1 < NCH:
                                    stage1(j + 1)
                                stage3(j)
                                if j + 1 < NCH:
                                    stage2(j + 1)
                                stage4(j)
                        T.barrier()
                if dbg and s == 0:
                    dma("sp", ds_dbg, dbg_d["oT"], oT[:], reads=[r for rr in r_oT for r in rr])

                ogA = SB(pab, "ogA", [128, 4, S], BF16); atA = SB(pab, "atA", [128, 4, S], BF16)
                r_ogA = [Res() for _ in range(NG)]; r_atA = [Res() for _ in range(NT)]
                with ExitStack() as pc:
                    wC = SB(pc, "wC", [128, 8, 1536], BF16); r_wC = Res()
                    dma("pool", ds_w, wC[:, :, 0:512], w_in_k[:, :, O_RG:O_RG + 512], writes=[r_wC])
                    for m in range(4):
                        for hf in range(2):
                            hq = m + 4 * hf
                            cdst = 512 + m * 128 + hf * 64
                            dma("pool", ds_w, wC[:, :, cdst:cdst + 64], w_in_k[:, :, O_AQ + hq * 64:O_AQ + hq * 64 + 64], writes=[r_wC])
                            dma("pool", ds_w, wC[:, :, 512 + cdst:512 + cdst + 32], w_in_k[:, :, O_AQ + hq * 64 + 32:O_AQ + hq * 64 + 64], writes=[r_wC])
                            dma("pool", ds_w, wC[:, :, 512 + cdst + 32:512 + cdst + 64], w_in_k[:, :, O_AQ + hq * 64:O_AQ + hq * 64 + 32], writes=[r_wC])
                    xs = [SB(pc, "cxs%d" % i, [128, D], F32) for i in range(2)]; r_xs = [Res() for _ in range(2)]
                    uT = SB(pc, "cuT", [128, 8, 512], BF16); r_uT = Res()
                    qT = SB(pc, "qT", [128, 4, 512], BF16); r_qT = Res()
                    ta = [SB(pc, "cta%d" % i, [128, 512], F32) for i in range(4)]; r_ta = [Res() for _ in range(4)]
                    msc = [SB(pc, "msc%d" % i, [128, 384], F32) for i in range(2)]; r_msc = [Res(), Res()]
                    pex = [SB(pc, "pex%d" % i, [128, 384], BF16) for i in range(2)]; r_pex = [Res(), Res()]
                    pT = [SB(pc, "pT%d" % i, [128, 3, 128], BF16) for i in range(2)]; r_pT = [Res(), Res()]
                    sm = [SB(pc, "smx%d" % i, [128, 8], F32) for i in range(2)]; r_sm = [Res(), Res()]
                    sm2 = [SB(pc, "smy%d" % i, [128, 8], F32) for i in range(2)]; r_sm2 = [Res(), Res()]
                    ao = [SB(pc, "ao%d" % i, [128, 512], BF16) for i in range(2)]; r_ao = [Res(), Res()]
                    hctr = 0
                    for g in range(NG):
                        gs = slice(g * 512, (g + 1) * 512)
                        for t in range(4):
                            tile_i = g * 4 + t
                            sl = tile_i % 2
                            dma("sp", ds_x[sl], xs[sl][:], x_d[s, tile_i * 128:(tile_i + 1) * 128, :], writes=[r_xs[sl]])
                            make_uT(xs[sl], r_xs[sl], uT[:, :, t * 128:(t + 1) * 128], r_uT, 1, 0)
                        for m in range(4):
                            for (pbx, cb) in ((2, 512 + m * 128), (3, 1024 + m * 128)):
                                for k in range(8):
                                    op("pe", lambda e, k=k, pbx=pbx, cb=cb: e.matmul(PB[pbx][:], lhsT=wC[:, k, cb:cb + 128], rhs=uT[:, k, :], start=(k == 0), stop=(k == 7)),
                                       reads=[r_uT, r_wC], writes=[rPB[pbx]])
                            op("dve", lambda e: e.tensor_tensor(out=ta[0][:], in0=PB[2][:], in1=cosT[:, gs], op=ALU.mult), reads=[rPB[2], r_cs], writes=[r_ta[0]])
                            op("dve", lambda e: e.tensor_tensor(out=ta[1][:], in0=PB[3][:], in1=sinT[:, gs], op=ALU.mult), reads=[rPB[3], r_cs], writes=[r_ta[1]])
                            op("dve", lambda e, m=m: e.tensor_tensor(out=qT[:, m, :], in0=ta[0][:], in1=ta[1][:], op=ALU.add), reads=[r_ta[0], r_ta[1]], writes=[r_qT])
                        for h in range(4):
                            for k in range(8):
                                op("pe", lambda e, k=k, h=h: e.matmul(PB[4][:], lhsT=wC[:, k, h * 128:(h + 1) * 128], rhs=uT[:, k, :], start=(k == 0), stop=(k == 7)),
                                   reads=[r_uT, r_wC], writes=[rPB[4]])
                            op("act", lambda e: e.activation(out=ta[0][:], in_=PB[4][:], func=AF.Silu), reads=[rPB[4]], writes=[r_ta[0]])
                            rr = [r_oT[h][c] for c in range(g * 8, (g + 1) * 8)]
                            op("act", lambda e, h=h: e.activation(out=ta[2][:], in_=oT[:, h, gs], func=AF.Square), reads=rr, writes=[r_ta[2]])
                            op("pe", lambda e: e.matmul(PB[5][:], lhsT=onesf[:], rhs=ta[2][:], start=True, stop=True), reads=[r_onesf, r_ta[2]], writes=[rPB[5]])
                            op("act", lambda e: e.activation(out=ta[3][:], in_=PB[5][:], func=AF.Ln, scale=1.0 / 128.0, bias=epsc[:, 1:2]), reads=[rPB[5], r_eps], writes=[r_ta[3]])
                            op("act", lambda e: e.activation(out=ta[3][:], in_=ta[3][:], func=AF.Exp, scale=-0.5), reads=[r_ta[3]], writes=[r_ta[3]])
                            op("dve", lambda e, h=h: e.tensor_tensor(out=ta[2][:], in0=oT[:, h, gs], in1=ta[3][:], op=ALU.mult), reads=rr + [r_ta[3]], writes=[r_ta[2]])
                            op("dve", lambda e, h=h: e.scalar_tensor_tensor(out=ogA[:, h, gs], in0=ta[0][:], scalar=ngc[:, 0:1], in1=ta[2][:], op0=ALU.mult, op1=ALU.mult),
                               reads=[r_ta[0], r_ngc, r_ta[2]], writes=[r_ogA[g]])
                        def att_geom(t):
                            j = g * 4 + t
                            k0 = max(0, j - 1); k1 = min(NT, j + 2)
                            return j, k0, k1, k1 - k0, (k1 - k0) * 128, (128 if j == 0 else 0)

                        def att_s1_pe(idx):
                            t, h = idx // 8, idx % 8
                            j, k0, k1, nkb, nk, mc0 = att_geom(t)
                            m, hf = h % 4, h // 4
                            ps = slice(hf * 64, (hf + 1) * 64)
                            ib = idx % 2
                            pbs = 2 + ib
                            kgs = [r_kT[gg] for gg in range((k0 * 128) // 512, ((k1 * 128) - 1) // 512 + 1)]
                            op("pe", lambda e: e.matmul(PB[pbs][:, 0:nk], lhsT=qT[ps, m, t * 128:(t + 1) * 128], rhs=kTt[ps, k0 * 128:k1 * 128], start=True, stop=True),
                               reads=[r_qT] + kgs, writes=[rPB[pbs]])

                        def att_s1_ew(idx):
                            t, h = idx // 8, idx % 8
                            j, k0, k1, nkb, nk, mc0 = att_geom(t)
                            ib = idx % 2
                            pbs = 2 + ib
                            op("dve", lambda e: e.tensor_tensor(out=msc[ib][:, 0:nk], in0=PB[pbs][:, 0:nk], in1=amask[:, mc0:mc0 + nk], op=ALU.add),
                               reads=[rPB[pbs], r_amask], writes=[r_msc[ib]])
                            op("dve", lambda e: e.reduce_max(out=sm[ib][:, 0:1], in_=msc[ib][:, 0:nk], axis=AX.X), reads=[r_msc[ib]], writes=[r_sm[ib]])
                            op("dve", lambda e: e.tensor_scalar(out=sm[ib][:, 1:2], in0=sm[ib][:, 0:1], scalar1=-0.125, scalar2=sinkb[:, 1, h:h + 1], op0=ALU.mult, op1=ALU.min),
                               reads=[r_sm[ib], r_sink], writes=[r_sm[ib]])
                            op("act", lambda e: e.activation(out=pex[ib][:, 0:nk], in_=msc[ib][:, 0:nk], func=AF.Exp, scale=0.125, bias=sm[ib][:, 1:2], accum_out=sm2[ib][:, 0:1]),
                               reads=[r_msc[ib], r_sm[ib]], writes=[r_pex[ib], r_sm2[ib]])
                            op("act", lambda e: e.activation(out=sm2[ib][:, 1:2], in_=sinkb[:, 0, h:h + 1], func=AF.Exp, bias=sm[ib][:, 1:2], scale=1.0),
                               reads=[r_sm[ib], r_sink], writes=[r_sm2[ib]])

                        def att_s2a(idx):
                            t, h = idx // 8, idx % 8
                            j, k0, k1, nkb, nk, mc0 = att_geom(t)
                            ib = idx % 2
                            pbt = 4 + ib
                            for b in range(nkb):
                                op("pe", lambda e, b=b: e.transpose(out=PB[pbt][:].bitcast(BF16)[:, b * 128:(b + 1) * 128], in_=pex[ib][:, b * 128:(b + 1) * 128], identity=identb[:]),
                                   reads=[r_pex[ib], r_identb], writes=[rPB[pbt]])
                            if ib == 0:
                                op("act", lambda e: e.activation(out=pT[ib][:, 0:nkb, :], in_=PB[pbt][:].bitcast(BF16)[:, 0:nk].rearrange("p (b q) -> p b q", q=128), func=AF.Copy),
                                   reads=[rPB[pbt]], writes=[r_pT[ib]])
                            else:
                                op("dve", lambda e: e.tensor_copy(out=pT[ib][:, 0:nkb, :], in_=PB[pbt][:].bitcast(BF16)[:, 0:nk].rearrange("p (b q) -> p b q", q=128)),
                                   reads=[rPB[pbt]], writes=[r_pT[ib]])

                        def att_s2b(idx):
                            t, h = idx // 8, idx % 8
                            j, k0, k1, nkb, nk, mc0 = att_geom(t)
                            hf = h // 4
                            ib = idx % 2
                            op("dve", lambda e: e.tensor_tensor(out=sm2[ib][:, 2:3], in0=sm2[ib][:, 0:1], in1=sm2[ib][:, 1:2], op=ALU.add), reads=[r_sm2[ib]], writes=[r_sm2[ib]])
                            op("dve", lambda e: e.reciprocal(out=sm2[ib][:, 3:4], in_=sm2[ib][:, 2:3]), reads=[r_sm2[ib]], writes=[r_sm2[ib]])
                            pbo = 6 + ib
                            for b in range(nkb):
                                op("pe", lambda e, b=b: e.matmul(PB[pbo][:, 0:64], lhsT=pT[ib][:, b, :], rhs=vtk[:, k0 + b, hf * 64:(hf + 1) * 64], start=(b == 0), stop=(b == nkb - 1)),
                                   reads=[r_pT[ib], r_v[k0 + b]], writes=[rPB[pbo]])
                            aot = ao[t % 2]; r_aot = r_ao[t % 2]
                            op("act", lambda e: e.activation(out=aot[:, h * 64:(h + 1) * 64], in_=PB[pbo][:, 0:64], func=AF.Identity, scale=sm2[ib][:, 3:4]),
                               reads=[rPB[pbo], r_sm2[ib]], writes=[r_aot])
                            if h == 7:
                                pba = t % 2
                                for m in range(4):
                                    op("pe", lambda e, m=m: e.transpose(out=PB[pba][:].bitcast(BF16)[:, m * 128:(m + 1) * 128], in_=aot[:, m * 128:(m + 1) * 128], identity=identb[:]),
                                       reads=[r_aot, r_identb], writes=[rPB[pba]])
                                op("dve", lambda e: e.tensor_copy(out=atA[:, :, j * 128:(j + 1) * 128], in_=PB[pba][:].bitcast(BF16)[:, 0:512].rearrange("p (m q) -> p m q", q=128)),
                                   reads=[rPB[pba]], writes=[r_atA[j]])

                        att_s1_pe(0)
                        att_s1_ew(0)
                        for idx in range(32):
                            if idx + 1 < 32:
                                att_s1_pe(idx + 1)
                            att_s2a(idx)
                            if idx + 1 < 32:
                                att_s1_ew(idx + 1)
                            att_s2b(idx)
                T.barrier()

                with ExitStack() as pc:
                    if S >= 2048:
                        wG = oT[:].rearrange("p h s -> p (h s)").bitcast(BF16)
                        wGv = wG[:, 0:8 * 2048].rearrange("p (k n) -> p k n", n=2048)
                    else:
                        wGv = SB(pc, "wGt", [128, 8, 2048], BF16)[:]
                    r_wG = Res()
                    wR = SB(pc, "wR", [128, 4, D], BF16); wT = SB(pc, "wT", [128, 4, D], BF16); wO = SB(pc, "wO", [128, 8, D], BF16)
                    lnB = SB(pc, "lnB1", [128, 2, D], F32); r_lnB = Res()
                    for q4 in range(4):
                        dma("pool", ds_w, wGv[:, :, q4 * 512:(q4 + 1) * 512], w_in_k[:, :, O_GR + q4 * 512:O_GR + (q4 + 1) * 512], writes=[r_wG])
                    dma("pool", ds_w, wR[:], w_rec_k, writes=[r_wG])
                    dma("pool", ds_w, wT[:], w_att_k, writes=[r_wG])
                    for q2 in range(2):
                        dma("pool", ds_w, wO[:, :, q2 * 512:(q2 + 1) * 512], w_out_k[:, :, q2 * 512:(q2 + 1) * 512], writes=[r_wG])
                    dma("sp", ds_const, lnB[:, 0, :], ln1g_d[0:1, :].broadcast_to([128, D]), writes=[r_lnB])
                    dma("sp", ds_const, lnB[:, 1, :], ln1b_d[0:1, :].broadcast_to([128, D]), writes=[r_lnB])
                    zb = zer[:, 0:8].bitcast(BF16)[:, 0:8].unsqueeze(2)
                    dma("sp", ds_misc[3], u2T_d[s, :, 0:1].rearrange("(j p) o -> p j o", p=128), zb, reads=[r_zer], allow_slow_non_contiguous=True)
                    dma("sp", ds_misc[3], u2T_d[s, :, S + 1:S + 2].rearrange("(j p) o -> p j o", p=128), zb, reads=[r_zer], allow_slow_non_contiguous=True)
                    r_x1d = [Res() for _ in range(NT)]
                    r_u2d = [Res() for _ in range(NT)]
                    xs = [SB(pc, "bxs%d" % i, [128, D], F32) for i in range(4)]; r_xs = [Res() for _ in range(4)]
                    uT = SB(pc, "buT", [128, 8, 512], BF16); r_uT = Res()
                    mT = SB(pc, "mT", [128, 8, 512], BF16); r_mT = Res()
                    ta = [SB(pc, "bta%d" % i, [128, 512], F32) for i in range(4)]; r_ta = [Res() for _ in range(4)]
                    yt = [SB(pc, "yt%d" % i, [128, D], F32) for i in range(2)]; r_yt = [Res(), Res()]
                    st6 = [SB(pc, "st6%d" % i, [128, 20], F32) for i in range(2)]; r_st6 = [Res(), Res()]
                    u2t = [SB(pc, "u2t%d" % i, [128, 8, 128], BF16) for i in range(2)]; r_u2t = [Res(), Res()]
                    for g in range(NG):
                        gs = slice(g * 512, (g + 1) * 512)
                        for t in range(4):
                            tile_i = g * 4 + t
                            dma("sp", ds_x[t], xs[t][:], x_d[s, tile_i * 128:(tile_i + 1) * 128, :], writes=[r_xs[t]])
                            make_uT(xs[t], r_xs[t], uT[:, :, t * 128:(t + 1) * 128], r_uT, 1, 0)
                        BSETS = ((2, 3, 4, 5), (0, 1, 6, 7))

                        def dc_pe(dc):
                            b0, b1, b2, b3 = BSETS[dc % 2]
                            dcs = slice(dc * 128, (dc + 1) * 128)
                            for k in range(8):
                                op("pe", lambda e, k=k: e.matmul(PB[b0][:], lhsT=wGv[:, k, dc * 128:(dc + 1) * 128], rhs=uT[:, k, :], start=(k == 0), stop=(k == 7)),
                                   reads=[r_uT, r_wG], writes=[rPB[b0]])
                            for k in range(8):
                                op("pe", lambda e, k=k: e.matmul(PB[b1][:], lhsT=wGv[:, k, 1024 + dc * 128:1024 + (dc + 1) * 128], rhs=uT[:, k, :], start=(k == 0), stop=(k == 7)),
                                   reads=[r_uT, r_wG], writes=[rPB[b1]])
                            for h in range(4):
                                op("pe", lambda e, h=h: e.matmul(PB[b2][:], lhsT=wR[:, h, dcs], rhs=ogA[:, h, gs], start=(h == 0), stop=(h == 3)),
                                   reads=[r_wG, r_ogA[g]], writes=[rPB[b2]])
                            for m in range(4):
                                op("pe", lambda e, m=m: e.matmul(PB[b3][:], lhsT=wT[:, m, dcs], rhs=atA[:, m, gs], start=(m == 0), stop=(m == 3)),
                                   reads=[r_wG] + r_atA[g * 4:(g + 1) * 4], writes=[rPB[b3]])

                        def dc_ew(dc):
                            b0, b1, b2, b3 = BSETS[dc % 2]
                            a0, a1 = ta[2 * (dc % 2)], ta[2 * (dc % 2) + 1]
                            ra0, ra1 = r_ta[2 * (dc % 2)], r_ta[2 * (dc % 2) + 1]
                            op("act", lambda e: e.activation(out=a0[:], in_=PB[b0][:], func=AF.Sigmoid), reads=[rPB[b0]], writes=[ra0])
                            op("act", lambda e: e.activation(out=a1[:], in_=PB[b1][:], func=AF.Sigmoid), reads=[rPB[b1]], writes=[ra1])
                            op("dve", lambda e: e.tensor_tensor(out=a0[:], in0=PB[b2][:], in1=a0[:], op=ALU.mult), reads=[rPB[b2], ra0], writes=[ra0])
                            op("dve", lambda e: e.tensor_tensor(out=a1[:], in0=PB[b3][:], in1=a1[:], op=ALU.mult), reads=[rPB[b3], ra1], writes=[ra1])
                            op("dve", lambda e: e.tensor_tensor(out=mT[:, dc, :], in0=a0[:], in1=a1[:], op=ALU.add), reads=[ra0, ra1], writes=[r_mT])

                        dc_pe(0)
                        for dc in range(8):
                            if dc + 1 < 8:
                                dc_pe(dc + 1)
                            dc_ew(dc)
                        for t in range(4):
                            tile_i = g * 4 + t
                            ib = tile_i % 2
                            wb = 2 + 2 * (tile_i % 2)
                            for hh in range(2):
                                for k in range(8):
                                    op("pe", lambda e, k=k, hh=hh, t=t, wb=wb: e.matmul(PB[wb + hh][:], lhsT=mT[:, k, t * 128:(t + 1) * 128], rhs=wO[:, k, hh * 512:(hh + 1) * 512], start=(k == 0), stop=(k == 7)),
                                       reads=[r_mT, r_wG], writes=[rPB[wb + hh]])
                            ln_epilogue((wb, wb + 1), (rPB[wb], rPB[wb + 1]), xs[t][:], r_xs[t], 0, lnB[:, 0, :], lnB[:, 1, :], r_lnB, yt[ib], r_yt[ib], st6[ib], r_st6[ib])
                            dma("sp", ds_o[ib], x1_d[s, tile_i * 128:(tile_i + 1) * 128, :], yt[ib][:], reads=[r_yt[ib]], writes=[r_x1d[tile_i]])
                            make_uT(yt[ib], r_yt[ib], u2t[ib][:], r_u2t[ib], 3, 2, banks=(6, 7))
                            dma("sp", ds_o[2 + ib], u2T_d[s, :, 1 + tile_i * 128:1 + (tile_i + 1) * 128].rearrange("(j p) q -> p j q", p=128), u2t[ib][:],
                                reads=[r_u2t[ib]], writes=[r_u2d[tile_i]])
                T.barrier()
            T.barrier()

            with ExitStack() as pf:
                wU = SB(pf, "wU", [128, 8, 2 * DFF], BF16); wD = SB(pf, "wD", [128, NFC, D], BF16); r_wU = Res()
                lnB = SB(pf, "lnB2", [128, 2, D], F32); r_lnB = Res()
                cvw = SB(pf, "cvw", [128, 4, 2 * NFC], F32); r_cvw = r_lnB
                r_wc = [Res() for _ in range(11)]
                for jq in range(11):
                    dma("pool", ds_wc[jq], wU[:, :, jq * 256:(jq + 1) * 256], w_up_k[:, :, jq * 256:(jq + 1) * 256], writes=[r_wc[jq]])
                    dma("pool", ds_wc[jq], wU[:, :, DFF + jq * 256:DFF + (jq + 1) * 256], w_up_k[:, :, DFF + jq * 256:DFF + (jq + 1) * 256], writes=[r_wc[jq]])
                    dma("pool", ds_wc[jq], wD[:, 2 * jq:2 * jq + 2, :], w_dn_k[:, 2 * jq:2 * jq + 2, :], writes=[r_wc[jq]])
                dma("sp", ds_const, lnB[:, 0, :], ln2g_d[0:1, :].broadcast_to([128, D]), writes=[r_lnB])
                dma("sp", ds_const, lnB[:, 1, :], ln2b_d[0:1, :].broadcast_to([128, D]), writes=[r_lnB])
                dma("sp", ds_const, cvw[:, 0:3, :], cw_d.rearrange("k (j p) -> p k j", p=128), writes=[r_cvw], allow_slow_non_contiguous=True)
                dma("sp", ds_const, cvw[:, 3, :], cb_d.rearrange("o (j p) -> p (o j)", p=128), writes=[r_cvw], allow_slow_non_contiguous=True)
                NSG = S // 256
                u2 = [SB(pf, "fu2%d" % i, [128, 8, 258], BF16) for i in range(2)]; r_u2 = [Res(), Res()]
                x1t = [SB(pf, "fx1%d" % i, [128, D], F32) for i in range(4)]; r_x1t = [Res() for _ in range(4)]
                cv = [SB(pf, "fcv%d" % i, [128, 256], F32) for i in range(2)]; r_cv = [Res(), Res()]
                cgt = [SB(pf, "fcg%d" % i, [128, 256], F32) for i in range(2)]; r_cg = [Res(), Res()]
                gz = [SB(pf, "fgz%d" % i, [128, 256], F32) for i in range(2)]; r_gz = [Res(), Res()]
                actb = [SB(pf, "fact%d" % i, [128, 256], BF16) for i in range(2)]; r_act = [Res(), Res()]
                yt = [SB(pf, "fyt%d" % i, [128, D], F32) for i in range(2)]; r_yt = [Res(), Res()]
                st6 = [SB(pf, "fst6%d" % i, [128, 20], F32) for i in range(2)]; r_st6 = [Res(), Res()]
                def c2_up(sg, c):
                    ub = sg % 2; ib = c % 2
                    pv, pg = 4 + 2 * ib, 5 + 2 * ib
                    for (pbx, cb) in ((pv, c * 128), (pg, DFF + c * 128)):
                        for k in range(8):
                            op("pe", lambda e, k=k, pbx=pbx, cb=cb: e.matmul(PB[pbx][:, 0:258], lhsT=wU[:, k, cb:cb + 128], rhs=u2[ub][:, k, :], start=(k == 0), stop=(k == 7)),
                               reads=[r_wc[c // 2], r_u2[ub]], writes=[rPB[pbx]])

                def c2_s1(c):
                    ib = c % 2
                    pv, pg = 4 + 2 * ib, 5 + 2 * ib
                    for (pbx, dst, r_dst, col) in ((pv, cv[ib], r_cv[ib], c), (pg, cgt[ib], r_cg[ib], NFC + c)):
                        op("act", lambda e, pbx=pbx, dst=dst, col=col: e.activation(out=dst[:], in_=PB[pbx][:, 0:256], func=AF.Identity, scale=cvw[:, 0, col:col + 1], bias=cvw[:, 3, col:col + 1]),
                           reads=[rPB[pbx], r_cvw], writes=[r_dst])

                def c2_s2(c):
                    ib = c % 2
                    pv, pg = 4 + 2 * ib, 5 + 2 * ib
                    for (pbx, dst, r_dst, col) in ((pg, cgt[ib], r_cg[ib], NFC + c), (pv, cv[ib], r_cv[ib], c)):
                        op("dve", lambda e, pbx=pbx, dst=dst, col=col: e.scalar_tensor_tensor(out=dst[:], in0=PB[pbx][:, 1:257], scalar=cvw[:, 1, col:col + 1], in1=dst[:], op0=ALU.mult, op1=ALU.add),
                           reads=[rPB[pbx], r_cvw, r_dst], writes=[r_dst])
                        op("dve", lambda e, pbx=pbx, dst=dst, col=col: e.scalar_tensor_tensor(out=dst[:], in0=PB[pbx][:, 2:258], scalar=cvw[:, 2, col:col + 1], in1=dst[:], op0=ALU.mult, op1=ALU.add),
                           reads=[rPB[pbx], r_cvw, r_dst], writes=[r_dst])

                def c2_s3(c):
                    ib = c % 2
                    op("act", lambda e: e.activation(out=gz[ib][:], in_=cgt[ib][:], func=AF.Gelu_apprx_tanh), reads=[r_cg[ib]], writes=[r_gz[ib]])

                def c2_s4(c):
                    ib = c % 2
                    op("dve", lambda e: e.tensor_tensor(out=actb[ib][:], in0=gz[ib][:], in1=cv[ib][:], op=ALU.mult), reads=[r_gz[ib], r_cv[ib]], writes=[r_act[ib]])

                def c2_down(c):
                    ib = c % 2
                    for tt in range(2):
                        for hh in range(2):
                            pa_ = tt * 2 + hh
                            op("pe", lambda e, tt=tt, hh=hh, pa_=pa_: e.matmul(PB[pa_][:], lhsT=actb[ib][:, tt * 128:(tt + 1) * 128], rhs=wD[:, c, hh * 512:(hh + 1) * 512],
                                                                             start=(c == 0), stop=(c == NFC - 1)),
                               reads=[r_act[ib], r_wc[c // 2]], writes=[rPB[pa_]])

                def c2_load(sg):
                    a = sg * 256
                    ub = sg % 2
                    dma("sp", ds_misc[ub], u2[ub][:], u2T_d[s, :, a:a + 258].rearrange("(j p) q -> p j q", p=128), writes=[r_u2[ub]])
                    for tt in range(2):
                        sl = (sg * 2 + tt) % 4
                        dma("sp", ds_x[sl], x1t[sl][:], x1_d[s, a + tt * 128:a + (tt + 1) * 128, :], writes=[r_x1t[sl]])

                def c2_prologue(sg):
                    c2_up(sg, 0); c2_up(sg, 1); c2_s1(0); c2_s2(0); c2_s1(1)

                def c2_ln2(sg):
                    for tt in range(2):
                        tile_i = sg * 2 + tt
                        sl = tile_i % 4
                        ib = tile_i % 2
                        ln_part2(x1t[sl][:], r_x1t[sl], lnB[:, 0, :], lnB[:, 1, :], r_lnB, yt[ib], r_yt[ib], st6[ib], r_st6[ib])
                        dma("sp", ds_o[ib], out_d[s, tile_i * 128:(tile_i + 1) * 128, :], yt[ib][:], reads=[r_yt[ib]])

                c2_load(0)
                if NSG > 1:
                    c2_load(1)
                c2_prologue(0)
                for sg in range(NSG):
                    for c in range(NFC):
                        c2_s3(c)
                        if c + 1 < NFC:
                            c2_s2(c + 1)
                        c2_s4(c)
                        if c + 2 < NFC:
                            c2_up(sg, c + 2)
                            c2_s1(c + 2)
                        c2_down(c)
                    for tt in range(2):
                        tile_i = sg * 2 + tt
                        ib = tile_i % 2
                        ln_part1((tt * 2, tt * 2 + 1), (rPB[tt * 2], rPB[tt * 2 + 1]), 1, yt[ib], r_yt[ib])
                    if sg + 1 < NSG:
                        c2_prologue(sg + 1)
                    c2_ln2(sg)
                    if sg + 2 < NSG:
                        c2_load(sg + 2)
            T.barrier()
        print("build: instructions=%d waits=%d" % (T.n_ins, T.n_wait))
    return nc


def _consts():
    ident = np.eye(128, dtype=np.float32)
    i = np.arange(64)
    maskf = (i[:, None] <= i[None, :]).astype(np.uint8)
    maskb = (i[:, None] >= i[None, :]).astype(np.uint8)
    q = np.arange(128)[:, None]
    cc = np.arange(128)[None, :]
    NEG = -30000.0
    am = np.zeros((128, 384), np.float32)
    am[:, 0:128] = np.where(cc >= q, 0.0, NEG)
    am[:, 256:384] = np.where(cc <= q, 0.0, NEG)
    half = 32
    inv = (np.float32(10000.0) ** (-np.arange(half, dtype=np.float32) / np.float32(half))).astype(np.float32)
    p = np.arange(128)
    rope = np.stack([inv[p % 32], np.where((p % 64) < 32, -1.0, 1.0).astype(np.float32)], axis=1).astype(np.float32)
    return dict(k_ident=ident, k_maskf=maskf, k_maskb=maskb, k_amask=am, k_rope=rope)


def make_in_maps(inputs, n_cores, nseq):
    f = lambda a: np.ascontiguousarray(np.asarray(a))
    shared = dict(
        w_ada=f(inputs["w_ada"][0]), b_ada=f(inputs["b_ada"][0:1]), w_in=f(inputs["w_in"][0]),
        rec_lower_bound=f(inputs["rec_lower_bound"]), rec_norm_g=f(inputs["rec_norm_g"][0:1]),
        w_rec_branch=f(inputs["w_rec_branch"][0]), attn_sink=f(inputs["attn_sink"][0:1]),
        w_attn_branch=f(inputs["w_attn_branch"][0]), w_out=f(inputs["w_out"][0]),
        ln1_g=f(inputs["ln1_g"][0:1]), ln1_b=f(inputs["ln1_b"][0:1]), w_up=f(inputs["w_up"][0]),
        conv_w=f(inputs["conv_w"][0]), conv_b=f(inputs["conv_b"][0:1]), w_down=f(inputs["w_down"][0]),
        ln2_g=f(inputs["ln2_g"][0:1]), ln2_b=f(inputs["ln2_b"][0:1]))
    shared.update(_consts())
    maps = []
    for i in range(n_cores):
        m = dict(shared)
        m["x"] = f(inputs["x"][i * nseq:(i + 1) * nseq])
        m["c"] = f(inputs["c"][i * nseq:(i + 1) * nseq])
        m["positions"] = f(inputs["positions"][i * nseq:(i + 1) * nseq]).astype(np.int32)
        maps.append(m)
    return maps


def kernel(**inputs):
    B, S, _ = inputs["x"].shape
    nseq = B // N_CORES
    nc = build_nc(S=S, NSEQ=nseq)
    in_maps = make_in_maps(inputs, N_CORES, nseq)
    res = run_bass_kernel_spmd(nc, in_maps, core_ids=list(range(N_CORES)))
    return np.concatenate([np.asarray(r["out"]) for r in res.results], axis=0).astype(np.float32)
```

```python
import numpy as np
from contextlib import ExitStack
import concourse.bass as bass
import concourse.mybir as mybir
from concourse.bass_utils import run_bass_kernel_spmd

F32 = mybir.dt.float32
BF16 = mybir.dt.bfloat16
I32 = mybir.dt.int32
U8 = mybir.dt.uint8
AF = mybir.ActivationFunctionType
ALU = mybir.AluOpType
AX = mybir.AxisListType

D = 1024
NK = 8
DIN = 5376
DFF = 2816
NFC = DFF // 128
ALPHA = 2.0 ** 0.25
LN_EPS = 1e-5
RMS_EPS = 1e-6
N_CORES = 8
O_RQ, O_RFF, O_RFB, O_RI, O_RG, O_AQ, O_AK, O_AV, O_GR, O_GA = 0, 512, 1024, 1536, 2048, 2560, 3072, 3200, 3328, 4352


class Res:
    __slots__ = ("w", "rd")

    def __init__(self):
        self.w = None
        self.rd = []


class Tracker:
    def __init__(self, nc, stack):
        self.nc = nc
        self.stack = stack
        self.eng = {}
        self.sems = {}
        for name, h in (("pe", nc.tensor), ("act", nc.scalar), ("dve", nc.vector),
                        ("pool", nc.gpsimd), ("sp", nc.sync)):
            sem = stack.enter_context(nc.semaphore("sem_" + name))
            self.eng[name] = {"h": h, "sem": sem, "cnt": 0, "seen": {}, "name": name}
            self.sems[id(sem)] = sem
        self.dma_sems = []
        self.n_wait = 0
        self.n_ins = 0

    def new_dma_sem(self, name):
        sem = self.stack.enter_context(self.nc.semaphore(name))
        d = {"sem": sem, "cnt": 0}
        self.sems[id(sem)] = sem
        self.dma_sems.append(d)
        return d

    def _wait_deps(self, E, reads, writes):
        deps = {}

        def add(tok):
            if tok is not None and deps.get(tok[0], 0) < tok[1]:
                deps[tok[0]] = tok[1]

        for r in reads:
            add(r.w)
        for w in writes:
            add(w.w)
            for t in w.rd:
                add(t)
        own = id(E["sem"])
        for k, v in deps.items():
            if E["name"] == "pe" and k == own:
                continue
            if E["seen"].get(k, 0) >= v:
                continue
            E["h"].wait_ge(self.sems[k], v)
            E["seen"][k] = v
            self.n_wait += 1

    def _commit(self, tok, reads, writes):
        for r in reads:
            r.rd.append(tok)
            if len(r.rd) > 48:
                best = {}
                for k, v in r.rd:
                    if best.get(k, 0) < v:
                        best[k] = v
                r.rd = list(best.items())
        for w in writes:
            w.w = tok
            w.rd = []
        self.n_ins += 1

    def op(self, eng, fn, reads=(), writes=()):
        E = self.eng[eng]
        self._wait_deps(E, reads, writes)
        ins = fn(E["h"])
        E["cnt"] += 1
        ins.then_inc(E["sem"], 1)
        self._commit((id(E["sem"]), E["cnt"]), reads, writes)
        return ins

    def dma(self, q, dsem, out, in_, reads=(), writes=(), **kw):
        E = self.eng[q]
        self._wait_deps(E, reads, writes)
        ins = E["h"].dma_start(out=out, in_=in_, **kw)
        dsem["cnt"] += 16
        ins.then_inc(dsem["sem"], 16)
        self._commit((id(dsem["sem"]), dsem["cnt"]), reads, writes)
        return ins

    def barrier(self):
        targets = [(id(e["sem"]), e["cnt"]) for e in self.eng.values() if e["cnt"] > 0]
        targets += [(id(d["sem"]), d["cnt"]) for d in self.dma_sems if d["cnt"] > 0]
        for E in self.eng.values():
            for k, v in targets:
                if E["seen"].get(k, 0) >= v:
                    continue
                E["h"].wait_ge(self.sems[k], v)
                E["seen"][k] = v
                self.n_wait += 1


class Rot:
    def __init__(self, items):
        self.items = items
        self.i = 0

    def next(self):
        it = self.items[self.i % len(self.items)]
        self.i += 1
        return it


def build_nc(S=2048, NSEQ=2, dbg=False):
    NT, NG, NCH = S // 128, S // 512, S // 64
    nc = bass.Bass("TRN2", target_bir_lowering=False)

    def din(name, shape, dt=F32):
        return nc.dram_tensor(name, list(shape), dt, kind="ExternalInput").ap()

    x_d = din("x", [NSEQ, S, D])
    c_d = din("c", [NSEQ, D])
    pos_d = din("positions", [NSEQ, S], I32)
    w_ada_d = din("w_ada", [D, 6 * D])
    b_ada_d = din("b_ada", [1, 6 * D])
    w_in_d = din("w_in", [D, DIN])
    lbr_d = din("rec_lower_bound", [2, 2, 512])
    ng_d = din("rec_norm_g", [1, 128])
    w_rec_d = din("w_rec_branch", [512, D])
    sink_d = din("attn_sink", [1, 8])
    w_att_d = din("w_attn_branch", [512, D])
    w_out_d = din("w_out", [D, D])
    ln1g_d = din("ln1_g", [1, D]); ln1b_d = din("ln1_b", [1, D])
    w_up_d = din("w_up", [D, 2 * DFF])
    cw_d = din("conv_w", [3, 2 * DFF])
    cb_d = din("conv_b", [1, 2 * DFF])
    w_dn_d = din("w_down", [DFF, D])
    ln2g_d = din("ln2_g", [1, D]); ln2b_d = din("ln2_b", [1, D])
    identf_d = din("k_ident", [128, 128])
    mskf_d = din("k_maskf", [64, 64], U8)
    mskb_d = din("k_maskb", [64, 64], U8)
    amask_d = din("k_amask", [128, 384])
    rope_d = din("k_rope", [128, 2])
    out_d = nc.dram_tensor("out", [NSEQ, S, D], F32, kind="ExternalOutput").ap()
    x1_d = nc.dram_tensor("x1_scr", [NSEQ, S, D], F32).ap()
    u2T_d = nc.dram_tensor("u2T_scr", [NSEQ, D, S + 2], BF16).ap()
    dbg_d = {}
    if dbg:
        dbg_d["mods"] = nc.dram_tensor("dbg_mods", [128, 4, 8], F32, kind="ExternalOutput").ap()
        dbg_d["QT"] = nc.dram_tensor("dbg_QT", [2, 128, 2, S], BF16, kind="ExternalOutput").ap()
        dbg_d["KT"] = nc.dram_tensor("dbg_KT", [2, 128, 2, S], BF16, kind="ExternalOutput").ap()
        dbg_d["V"] = nc.dram_tensor("dbg_V", [64, NCH, 256], BF16, kind="ExternalOutput").ap()
        dbg_d["oT"] = nc.dram_tensor("dbg_oT", [128, 4, S], F32, kind="ExternalOutput").ap()
        dbg_d["kT"] = nc.dram_tensor("dbg_kT", [128, S], BF16, kind="ExternalOutput").ap()
        dbg_d["x1"] = x1_d

    w_in_k = w_in_d.rearrange("(j p) n -> p j n", p=128)
    w_ada_k = w_ada_d.rearrange("(j p) n -> p j n", p=128)
    w_up_k = w_up_d.rearrange("(j p) n -> p j n", p=128)
    w_out_k = w_out_d.rearrange("(j p) n -> p j n", p=128)
    w_rec_k = w_rec_d.rearrange("(j p) n -> p j n", p=128)
    w_att_k = w_att_d.rearrange("(j p) n -> p j n", p=128)
    w_dn_k = w_dn_d.rearrange("(j p) n -> p j n", p=128)

    with ExitStack() as top:
        T = Tracker(nc, top)
        op, dma = T.op, T.dma

        uniq = [0]

        def SB(st, name, shape, dt):
            uniq[0] += 1
            return st.enter_context(nc.sbuf_tensor("%s_%d" % (name, uniq[0]), list(shape), dt))

        PB = [top.enter_context(nc.psum_tensor("pb%d" % i, [128, 512], F32)) for i in range(8)]
        rPB = [Res() for _ in range(8)]
        ds_const = T.new_dma_sem("ds_const")
        ds_w = T.new_dma_sem("ds_w")
        ds_x = [T.new_dma_sem("ds_x%d" % i) for i in range(4)]
        ds_o = [T.new_dma_sem("ds_o%d" % i) for i in range(4)]
        ds_misc = [T.new_dma_sem("ds_m%d" % i) for i in range(4)]
        ds_dbg = T.new_dma_sem("ds_dbg")
        ds_wc = [T.new_dma_sem("ds_wc%d" % i) for i in range(11)]

        identf = SB(top, "identf", [128, 128], F32); r_identf = Res()
        identb = SB(top, "identb", [128, 128], BF16); r_identb = Res()
        onesf = SB(top, "onesf", [128, 128], F32); r_onesf = Res()
        maskf = SB(top, "maskf", [64, 64], U8); maskb = SB(top, "maskb", [64, 64], U8); r_mask = Res()
        zer = SB(top, "zer", [128, 128], F32); r_zer = Res()
        rmask = SB(top, "rmask", [128, 512], F32); r_rmask = Res()
        lbt = SB(top, "lbt", [128, 3, 2, 4], F32); r_lbt = Res()
        lraw = SB(top, "lraw", [128, 2, 2, 4], F32); r_lraw = Res()
        ngc = SB(top, "ngc", [128, 1], F32); r_ngc = Res()
        sinkb = SB(top, "sinkb", [128, 2, 8], F32); r_sink = Res()
        rope = SB(top, "rope", [128, 2], F32); r_rope = Res()
        epsc = SB(top, "epsc", [128, 2], F32); r_eps = Res()
        modsT_all = SB(top, "modsT", [128, NSEQ, 4, 8], F32); r_modsT = Res()
        gaB_all = SB(top, "gaB", [128, NSEQ, 2, D], F32); r_gaB = [Res(), Res()]
        modsT = modsT_all[:, 0]; gaB = gaB_all[:, 0]
        amask = SB(top, "amask", [128, 384], F32); r_amask = Res()

        dma("sp", ds_const, identf[:], identf_d, writes=[r_identf])
        dma("sp", ds_const, maskf[:], mskf_d, writes=[r_mask])
        dma("sp", ds_const, maskb[:], mskb_d, writes=[r_mask])
        dma("sp", ds_const, rope[:], rope_d, writes=[r_rope])
        dma("sp", ds_const, amask[:], amask_d, writes=[r_amask])
        dma("sp", ds_const, ngc[:], ng_d.rearrange("o p -> p o"), writes=[r_ngc], allow_slow_non_contiguous=True)
        dma("sp", ds_const, sinkb[:, 0, :], sink_d[0:1, :].broadcast_to([128, 8]), writes=[r_sink])
        dma("sp", ds_const, lraw[:].rearrange("p a b h -> p (a b) h"),
            lbr_d.rearrange("a b (h p) -> p (a b) h", p=128), writes=[r_lraw], allow_slow_non_contiguous=True)
        T.barrier()
        op("dve", lambda e: e.tensor_copy(out=identb[:], in_=identf[:]), reads=[r_identf], writes=[r_identb])
        op("pool", lambda e: e.memset(onesf[:], 1.0), writes=[r_onesf])
        op("pool", lambda e: e.memset(zer[:], 0.0), writes=[r_zer])
        op("pool", lambda e: e.memset(rmask[:], 1.0), writes=[r_rmask])
        op("pool", lambda e: e.memset(rmask[:].rearrange("p (c t) -> p c t", t=64)[:, :, 0:1], 0.0), writes=[r_rmask])
        op("pool", lambda e: e.memset(epsc[:, 0:1], LN_EPS), writes=[r_eps])
        op("pool", lambda e: e.memset(epsc[:, 1:2], RMS_EPS), writes=[r_eps])
        op("dve", lambda e: e.tensor_scalar(out=sinkb[:, 1, :], in0=sinkb[:, 0, :], scalar1=-1.0, scalar2=None, op0=ALU.mult),
           reads=[r_sink], writes=[r_sink])
        op("dve", lambda e: e.tensor_tensor(out=lbt[:, 2, :, :], in0=lraw[:, :, 0, :], in1=lraw[:, :, 1, :], op=ALU.subtract),
           reads=[r_lraw], writes=[r_lbt])
        op("act", lambda e: e.activation(out=lbt[:, 0, :, :], in_=lbt[:, 2, :, :], func=AF.Sigmoid), reads=[r_lbt], writes=[r_lbt])
        op("act", lambda e: e.activation(out=lbt[:, 1, :, :], in_=lbt[:, 2, :, :], func=AF.Sigmoid, scale=-1.0), reads=[r_lbt], writes=[r_lbt])
        op("dve", lambda e: e.tensor_scalar(out=lbt[:, 2, :, :], in0=lbt[:, 1, :, :], scalar1=-1.0, scalar2=None, op0=ALU.mult),
           reads=[r_lbt], writes=[r_lbt])

        def make_uT(xt, r_xt, dst, r_dst, scol, bcol, banks=(0, 1)):
            for hb in range(2):
                b = banks[hb]
                for i in range(4):
                    k = hb * 4 + i
                    op("pe", lambda e, k=k, i=i, b=b: e.transpose(out=PB[b][:, i * 128:(i + 1) * 128], in_=xt[:, k * 128:(k + 1) * 128],
                                                                   identity=identf[:]),
                       reads=[r_xt, r_identf], writes=[rPB[b]])
                for i in range(4):
                    k = hb * 4 + i
                    if i % 2 == 0:
                        op("act", lambda e, k=k, i=i, b=b: e.activation(out=dst[:, k, :], in_=PB[b][:, i * 128:(i + 1) * 128], func=AF.Identity,
                                                                         scale=modsT[:, scol, k:k + 1], bias=modsT[:, bcol, k:k + 1]),
                           reads=[rPB[b], r_modsT], writes=[r_dst])
                    else:
                        op("dve", lambda e, k=k, i=i, b=b: e.tensor_scalar(out=dst[:, k, :], in0=PB[b][:, i * 128:(i + 1) * 128],
                                                                            scalar1=modsT[:, scol, k:k + 1], scalar2=modsT[:, bcol, k:k + 1],
                                                                            op0=ALU.mult, op1=ALU.add),
                           reads=[rPB[b], r_modsT], writes=[r_dst])

        def ln_part1(pz, r_pz, which, y, r_y):
            for h in range(2):
                op("dve", lambda e, h=h: e.tensor_tensor(out=y[:, h * 512:(h + 1) * 512], in0=PB[pz[h]][:], in1=gaB[:, which, h * 512:(h + 1) * 512], op=ALU.mult),
                   reads=[r_pz[h], r_gaB[which]], writes=[r_y])

        def ln_part2(xres, r_xres, gB, bB, r_gb, y, r_y, st6, r_st):
            op("dve", lambda e: e.scalar_tensor_tensor(out=y[:], in0=xres, scalar=ALPHA, in1=y[:], op0=ALU.mult, op1=ALU.add),
               reads=[r_xres, r_y], writes=[r_y])
            for h in range(2):
                op("dve", lambda e, h=h: e.bn_stats(out=st6[:, h * 6:(h + 1) * 6], in_=y[:, h * 512:(h + 1) * 512]), reads=[r_y], writes=[r_st])
            op("dve", lambda e: e.bn_aggr(out=st6[:, 12:14], in_=st6[:, 0:12]), reads=[r_st], writes=[r_st])
            op("act", lambda e: e.activation(out=st6[:, 14:15], in_=st6[:, 13:14], func=AF.Ln, bias=epsc[:, 0:1], scale=1.0), reads=[r_st, r_eps], writes=[r_st])
            op("act", lambda e: e.activation(out=st6[:, 15:16], in_=st6[:, 14:15], func=AF.Exp, scale=-0.5), reads=[r_st], writes=[r_st])
            op("dve", lambda e: e.tensor_scalar(out=st6[:, 16:17], in0=st6[:, 12:13], scalar1=st6[:, 15:16], scalar2=-1.0, op0=ALU.mult, op1=ALU.mult),
               reads=[r_st], writes=[r_st])
            op("act", lambda e: e.activation(out=y[:], in_=y[:], func=AF.Identity, scale=st6[:, 15:16], bias=st6[:, 16:17]),
               reads=[r_y, r_st], writes=[r_y])
            op("dve", lambda e: e.tensor_tensor(out=y[:], in0=y[:], in1=gB, op=ALU.mult), reads=[r_y, r_gb], writes=[r_y])
            op("dve", lambda e: e.tensor_tensor(out=y[:], in0=y[:], in1=bB, op=ALU.add), reads=[r_y, r_gb], writes=[r_y])

        def ln_epilogue(pz, r_pz, xres, r_xres, which, gB, bB, r_gb, y, r_y, st6, r_st):
            ln_part1(pz, r_pz, which, y, r_y)
            ln_part2(xres, r_xres, gB, bB, r_gb, y, r_y, st6, r_st)

        for s in range(NSEQ):
            modsT = modsT_all[:, s]
            gaB = gaB_all[:, s]
            if s == 0:
              with ExitStack() as ph:
                cT = SB(ph, "cT", [128, NSEQ, 8], F32); r_cT = Res()
                cbt = SB(ph, "cbt", [128, NSEQ, 8, 128], F32); r_cbt = Res()
                wst = [SB(ph, "wst%d" % i, [128, 8, 512], F32) for i in range(2)]; r_wst = [Res(), Res()]
                bst = [SB(ph, "bst%d" % i, [1, 512], F32) for i in range(2)]; r_bst = [Res(), Res()]
                mtmp = [SB(ph, "mtmp%d" % i, [128, 512], F32) for i in range(NSEQ)]; r_mtmp = [Res() for _ in range(NSEQ)]
                for ss in range(NSEQ):
                    dma("sp", ds_misc[0], cT[:, ss, :], c_d[ss].rearrange("(j p) -> p j", p=128), writes=[r_cT], allow_slow_non_contiguous=True)
                op("act", lambda e: e.activation(out=cT[:], in_=cT[:], func=AF.Silu), reads=[r_cT], writes=[r_cT])
                op("dve", lambda e: e.tensor_copy(out=cbt[:], in_=cT[:].unsqueeze(3).broadcast_to([128, NSEQ, 8, 128])), reads=[r_cT], writes=[r_cbt])
                bcol = SB(ph, "bcol", [128, 48], F32); r_bcol = Res()
                cT2 = SB(ph, "cT2", [128, 8, NSEQ], F32); r_cT2 = Res()
                dma("sp", ds_misc[3], bcol[:], b_ada_d.rearrange("o (j p) -> p (o j)", p=128), writes=[r_bcol], allow_slow_non_contiguous=True)
                op("dve", lambda e: e.tensor_copy(out=cT2[:], in_=cT[:].rearrange("p s k -> p k s")), reads=[r_cT], writes=[r_cT2])
                for cg in range(12):
                    b2 = cg % 2
                    dma("sp", ds_misc[1 + b2], wst[b2][:], w_ada_k[:, :, cg * 512:(cg + 1) * 512], writes=[r_wst[b2]])
                    which, half = cg // 2, cg % 2
                    if which in (2, 5):
                        dma("sp", ds_o[b2], bst[b2][:], b_ada_d[0:1, cg * 512:(cg + 1) * 512], writes=[r_bst[b2]])
                        gi = 0 if which == 2 else 1
                        for ss in range(NSEQ):
                            pb = 2 + b2 + 2 * (ss % 2)
                            for k in range(8):
                                op("pe", lambda e, k=k: e.matmul(PB[pb][:], lhsT=cbt[:, ss, k, :], rhs=wst[b2][:, k, :], start=(k == 0), stop=False),
                                   reads=[r_cbt, r_wst[b2]], writes=[rPB[pb]])
                            op("pe", lambda e: e.matmul(PB[pb][:], lhsT=onesf[0:1, :], rhs=bst[b2][0:1, :], start=False, stop=True),
                               reads=[r_onesf, r_bst[b2]], writes=[rPB[pb]])
                            op("dve", lambda e: e.tensor_scalar(out=gaB_all[:, ss, gi, half * 512:(half + 1) * 512], in0=PB[pb][:], scalar1=1.0, scalar2=None, op0=ALU.add),
                               reads=[rPB[pb]], writes=[r_gaB[gi]])
                    else:
                        widx = {0: 0, 1: 1, 3: 2, 4: 3}[which]
                        addc = 1.0 if which in (1, 4) else 0.0
                        pt = 6 + b2
                        for i in range(4):
                            for k in range(8):
                                op("pe", lambda e, k=k, i=i: e.matmul(PB[pt][:, i * NSEQ:(i + 1) * NSEQ], lhsT=wst[b2][:, k, i * 128:(i + 1) * 128], rhs=cT2[:, k, :], start=(k == 0), stop=(k == 7)),
                                   reads=[r_cT2, r_wst[b2]], writes=[rPB[pt]])
                        op("dve", lambda e: e.tensor_tensor(out=modsT_all[:, :, widx, half * 4:(half + 1) * 4], in0=PB[pt][:, 0:4 * NSEQ].rearrange("p (i s) -> p s i", s=NSEQ),
                                                            in1=bcol[:, cg * 4:(cg + 1) * 4].unsqueeze(1).broadcast_to([128, NSEQ, 4]), op=ALU.add),
                           reads=[rPB[pt], r_bcol], writes=[r_modsT])
                        if addc != 0.0:
                            op("dve", lambda e: e.tensor_scalar(out=modsT_all[:, :, widx, half * 4:(half + 1) * 4], in0=modsT_all[:, :, widx, half * 4:(half + 1) * 4],
                                                                scalar1=addc, scalar2=None, op0=ALU.add), reads=[r_modsT], writes=[r_modsT])
                if dbg:
                    dma("sp", ds_dbg, dbg_d["mods"], modsT_all[:, 0], reads=[r_modsT])
              T.barrier()

            with ExitStack() as pab:
                oT = SB(pab, "oT", [128, 4, S], F32)
                r_oT = [[Res() for _ in range(NCH)] for _ in range(4)]
                kTt = SB(pab, "kTt", [128, S], BF16); r_kT = [Res() for _ in range(NG)]
                vtk = SB(pab, "vtk", [128, NT, 128], BF16); r_v = [Res() for _ in range(NT)]
                cosT = SB(pab, "cosT", [128, S], F32); sinT = SB(pab, "sinT", [128, S], F32); r_cs = Res()
                with ExitStack() as pr:
                    posi = SB(pr, "posi", [128, S], I32); r_posi = Res()
                    ang = SB(pr, "ang", [128, S], F32); r_ang = Res()
                    kq = SB(pr, "kq", [128, S], F32); r_kq = Res()
                    ki = SB(pr, "ki", [128, S], I32); r_ki = Res()
                    dma("sp", ds_misc[0], posi[:], pos_d[s:s + 1, :].broadcast_to([128, S]), writes=[r_posi])
                    op("dve", lambda e: e.tensor_copy(out=ang[:], in_=posi[:]), reads=[r_posi], writes=[r_ang])
                    op("dve", lambda e: e.tensor_scalar(out=ang[:], in0=ang[:], scalar1=rope[:, 0:1], scalar2=None, op0=ALU.mult),
                       reads=[r_ang, r_rope], writes=[r_ang])
                    TWO_PI = 6.283185307179586
                    C1 = 6.28125
                    C2 = TWO_PI - C1
                    for (dst, shift) in ((sinT, 0.0), (cosT, np.pi / 2)):
                        op("dve", lambda e, shift=shift: e.tensor_scalar(out=kq[:], in0=ang[:], scalar1=float(shift), scalar2=1.0 / TWO_PI, op0=ALU.add, op1=ALU.mult),
                           reads=[r_ang], writes=[r_kq])
                        op("dve", lambda e: e.tensor_copy(out=ki[:], in_=kq[:]), reads=[r_kq], writes=[r_ki])
                        op("dve", lambda e: e.tensor_copy(out=kq[:], in_=ki[:]), reads=[r_ki], writes=[r_kq])
                        op("dve", lambda e, shift=shift, dst=dst: e.tensor_scalar(out=dst[:], in0=ang[:], scalar1=float(shift), scalar2=None, op0=ALU.add),
                           reads=[r_ang], writes=[r_cs])
                        op("dve", lambda e, dst=dst: e.scalar_tensor_tensor(out=dst[:], in0=kq[:], scalar=-C1, in1=dst[:], op0=ALU.mult, op1=ALU.add),
                           reads=[r_kq, r_cs], writes=[r_cs])
                        op("dve", lambda e, dst=dst: e.scalar_tensor_tensor(out=dst[:], in0=kq[:], scalar=-C2, in1=dst[:], op0=ALU.mult, op1=ALU.add),
                           reads=[r_kq, r_cs], writes=[r_cs])
                        op("dve", lambda e, dst=dst: e.tensor_scalar(out=kq[:], in0=dst[:], scalar1=float(np.pi), scalar2=-TWO_PI, op0=ALU.is_gt, op1=ALU.mult),
                           reads=[r_cs], writes=[r_kq])
                        op("dve", lambda e, dst=dst: e.tensor_tensor(out=dst[:], in0=dst[:], in1=kq[:], op=ALU.add), reads=[r_cs, r_kq], writes=[r_cs])
                        op("dve", lambda e, dst=dst: e.tensor_scalar(out=kq[:], in0=dst[:], scalar1=float(-np.pi), scalar2=TWO_PI, op0=ALU.is_lt, op1=ALU.mult),
                           reads=[r_cs], writes=[r_kq])
                        op("dve", lambda e, dst=dst: e.tensor_tensor(out=dst[:], in0=dst[:], in1=kq[:], op=ALU.add), reads=[r_cs, r_kq], writes=[r_cs])
                        op("dve", lambda e, dst=dst: e.tensor_scalar(out=dst[:], in0=dst[:], scalar1=3.1415925, scalar2=-3.1415925, op0=ALU.min, op1=ALU.max),
                           reads=[r_cs], writes=[r_cs])
                        op("act", lambda e, dst=dst: e.activation(out=dst[:], in_=dst[:], func=AF.Sin), reads=[r_cs], writes=[r_cs])
                    op("dve", lambda e: e.tensor_scalar(out=sinT[:], in0=sinT[:], scalar1=rope[:, 1:2], scalar2=None, op0=ALU.mult),
                       reads=[r_cs, r_rope], writes=[r_cs])
                T.barrier()

                for hp in range(2):
                    with ExitStack() as pa:
                        NWA = 1024 + (384 if hp == 0 else 0)
                        wA = SB(pa, "wA", [128, 8, NWA], BF16); r_wA = Res()
                        QT = [SB(pa, "QT%d" % d, [128, 2, S], BF16) for d in range(2)]
                        KT = [SB(pa, "KT%d" % d, [128, 2, S], BF16) for d in range(2)]
                        r_QK = [[[Res() for _ in range(NG)] for _ in range(2)] for _ in range(2)]
                        Vt = SB(pa, "Vt", [64, NCH, 256], BF16); r_V = [Res() for _ in range(NG)]
                        Bref = SB(pa, "Bref", [128, 2, 2, NCH], F32); Bend = SB(pa, "Bend", [128, 2, 2, NCH], F32); r_B = Res()
                        Gx = SB(pa, "Gx", [128, 2, 2, NCH], F32); r_G = Res()
                        for seg, base in enumerate((O_RQ, O_RFF, O_RFB, O_RI)):
                            dma("pool", ds_w, wA[:, :, seg * 256:(seg + 1) * 256], w_in_k[:, :, base + hp * 256: base + hp * 256 + 256], writes=[r_wA])
                        if hp == 0:
                            dma("pool", ds_w, wA[:, :, 1024:1152], w_in_k[:, :, O_AK:O_AK + 128], writes=[r_wA])
                            for kv in range(2):
                                dma("pool", ds_w, wA[:, :, 1152 + kv * 64:1152 + kv * 64 + 32], w_in_k[:, :, O_AK + kv * 64 + 32:O_AK + kv * 64 + 64], writes=[r_wA])
                                dma("pool", ds_w, wA[:, :, 1152 + kv * 64 + 32:1152 + kv * 64 + 64], w_in_k[:, :, O_AK + kv * 64:O_AK + kv * 64 + 32], writes=[r_wA])
                            dma("pool", ds_w, wA[:, :, 1280:1408], w_in_k[:, :, O_AV:O_AV + 128], writes=[r_wA])
                        with ExitStack() as pa2:
                            xs = [SB(pa2, "xs%d" % i, [128, D], F32) for i in range(2)]; r_xs = [Res() for _ in range(2)]
                            uT = [SB(pa2, "uT%d" % i, [128, 8, 512], BF16) for i in range(1)]; r_uT = [Res()]
                            qs = [SB(pa2, "qs%d" % i, [128, 512], F32) for i in range(2)]; r_qs = [Res(), Res()]
                            NTMP = 2
                            tnames = ("sig", "lg", "Bf", "Eq", "Ek")
                            tmps = [{n: SB(pa2, "t_%s%d" % (n, i), [128, 512], F32) for n in tnames} for i in range(NTMP)]
                            r_tmps = [{n: Res() for n in tnames} for i in range(NTMP)]
                            for i in range(NTMP):
                                tmps[i]["kk"] = tmps[i]["sig"]; r_tmps[i]["kk"] = r_tmps[i]["sig"]
                                tmps[i]["dd"] = tmps[i]["lg"]; r_tmps[i]["dd"] = r_tmps[i]["lg"]
                            tctr = 0
                            bank_rot = Rot([2, 3, 4, 5, 6, 7])
                            for g in range(NG):
                                ub = 0
                                for t in range(4):
                                    tile_i = g * 4 + t
                                    sl = tile_i % 2
                                    dma("sp", ds_x[sl], xs[sl][:], x_d[s, tile_i * 128:(tile_i + 1) * 128, :], writes=[r_xs[sl]])
                                    make_uT(xs[sl], r_xs[sl], uT[ub][:, :, t * 128:(t + 1) * 128], r_uT[ub], 1, 0)
                                gs = slice(g * 512, (g + 1) * 512)
                                for cp in range(4):
                                    pb = bank_rot.next()
                                    for cc in range(2):
                                        ch = cp * 2 + cc
                                        for k in range(8):
                                            op("pe", lambda e, k=k, cc=cc, ch=ch, pb=pb: e.matmul(PB[pb][0:64, cc * 256:(cc + 1) * 256], lhsT=uT[ub][:, k, ch * 64:(ch + 1) * 64],
                                                                                                   rhs=wA[:, k, 768:1024], start=(k == 0), stop=(k == 7)),
                                               reads=[r_uT[ub], r_wA], writes=[rPB[pb]])
                                    op("act", lambda e, pb=pb, cp=cp: e.activation(out=Vt[:, g * 8 + cp * 2:g * 8 + cp * 2 + 2, :],
                                                                                    in_=PB[pb][0:64, :].rearrange("p (c n) -> p c n", n=256), func=AF.Copy, scale=128.0 ** -0.5),
                                       reads=[rPB[pb]], writes=[r_V[g]])
                                for hl in range(2):
                                    h = hp * 2 + hl
                                    pb = bank_rot.next()
                                    for k in range(8):
                                        op("pe", lambda e, k=k, pb=pb, hl=hl: e.matmul(PB[pb][:], lhsT=wA[:, k, hl * 128:(hl + 1) * 128], rhs=uT[ub][:, k, :],
                                                                                         start=(k == 0), stop=(k == 7)), reads=[r_uT[ub], r_wA], writes=[rPB[pb]])
                                    op("act", lambda e, pb=pb, hl=hl: e.activation(out=qs[hl][:], in_=PB[pb][:], func=AF.Silu), reads=[rPB[pb]], writes=[r_qs[hl]])
                                    for d in range(2):
                                        tm = tmps[tctr % NTMP]; rt = r_tmps[tctr % NTMP]; tctr += 1
                                        pb = bank_rot.next()
                                        cb = 256 * (1 + d) + hl * 128
                                        for k in range(8):
                                            op("pe", lambda e, k=k, pb=pb, cb=cb: e.matmul(PB[pb][:], lhsT=wA[:, k, cb:cb + 128], rhs=uT[ub][:, k, :],
                                                                                             start=(k == 0), stop=(k == 7)), reads=[r_uT[ub], r_wA], writes=[rPB[pb]])
                                        lb_c = lbt[:, 0, d, h:h + 1]; om_c = lbt[:, 1, d, h:h + 1]; nom_c = lbt[:, 2, d, h:h + 1]
                                        op("act", lambda e, pb=pb, tm=tm: e.activation(out=tm["sig"][:], in_=PB[pb][:], func=AF.Sigmoid), reads=[rPB[pb]], writes=[rt["sig"]])
                                        op("act", lambda e, tm=tm: e.activation(out=tm["lg"][:], in_=tm["sig"][:], func=AF.Ln, scale=om_c, bias=lb_c),
                                           reads=[rt["sig"], r_lbt], writes=[rt["lg"]])
                                        op("dve", lambda e, tm=tm: e.tensor_scalar(out=tm["kk"][:], in0=tm["sig"][:], scalar1=nom_c, scalar2=om_c, op0=ALU.mult, op1=ALU.add),
                                           reads=[rt["sig"], r_lbt], writes=[rt["kk"]])
                                        op("dve", lambda e, tm=tm: e.tensor_tensor_scan(out=tm["Bf"][:], data0=rmask[:], data1=tm["lg"][:], initial=0.0, op0=ALU.mult, op1=ALU.add),
                                           reads=[r_rmask, rt["lg"]], writes=[rt["Bf"]])
                                        B3 = tm["Bf"][:].rearrange("p (c t) -> p c t", t=64)
                                        if d == 1:
                                            lg3 = tm["lg"][:].rearrange("p (c t) -> p c t", t=64)
                                            op("dve", lambda e, B3=B3, lg3=lg3: e.tensor_tensor(out=lg3, in0=lg3, in1=B3, op=ALU.subtract), reads=[rt["lg"], rt["Bf"]], writes=[rt["lg"]])
                                            op("dve", lambda e, B3=B3, lg3=lg3: e.tensor_tensor(out=B3, in0=lg3, in1=B3[:, :, 63:64].broadcast_to([128, 8, 64]), op=ALU.add),
                                               reads=[rt["lg"], rt["Bf"]], writes=[rt["Bf"]])
                                            iref, iend = 32, 0
                                        else:
                                            iref, iend = 31, 63
                                        op("act", lambda e, B3=B3, iref=iref: e.activation(out=Bref[:, d, hl, g * 8:(g + 1) * 8], in_=B3[:, :, iref], func=AF.Copy), reads=[rt["Bf"]], writes=[r_B])
                                        op("act", lambda e, B3=B3, iend=iend: e.activation(out=Bend[:, d, hl, g * 8:(g + 1) * 8], in_=B3[:, :, iend], func=AF.Copy), reads=[rt["Bf"]], writes=[r_B])
                                        op("dve", lambda e, B3=B3, tm=tm, iref=iref: e.tensor_tensor(out=tm["dd"][:].rearrange("p (c t) -> p c t", t=64), in0=B3,
                                                                                                       in1=B3[:, :, iref:iref + 1].broadcast_to([128, 8, 64]), op=ALU.subtract),
                                           reads=[rt["Bf"]], writes=[rt["dd"]])
                                        op("act", lambda e, tm=tm: e.activation(out=tm["Eq"][:], in_=tm["dd"][:], func=AF.Exp), reads=[rt["dd"]], writes=[rt["Eq"]])
                                        op("act", lambda e, tm=tm: e.activation(out=tm["Ek"][:], in_=tm["dd"][:], func=AF.Exp, scale=-1.0), reads=[rt["dd"]], writes=[rt["Ek"]])
                                        op("dve", lambda e, tm=tm, d=d, hl=hl: e.tensor_tensor(out=KT[d][:, hl, gs], in0=tm["kk"][:], in1=tm["Ek"][:], op=ALU.mult),
                                           reads=[rt["kk"], rt["Ek"]], writes=[r_QK[d][hl][g]])
                                        op("dve", lambda e, tm=tm, d=d, hl=hl: e.tensor_tensor(out=QT[d][:, hl, gs], in0=qs[hl][:], in1=tm["Eq"][:], op=ALU.mult),
                                           reads=[r_qs[hl], rt["Eq"]], writes=[r_QK[d][hl][g]])
                                if hp == 0:
                                    pb1 = bank_rot.next()
                                    pb2 = bank_rot.next()
                                    for (pbx, cb) in ((pb1, 1024), (pb2, 1152)):
                                        for k in range(8):
                                            op("pe", lambda e, k=k, pbx=pbx, cb=cb: e.matmul(PB[pbx][:], lhsT=wA[:, k, cb:cb + 128], rhs=uT[ub][:, k, :], start=(k == 0), stop=(k == 7)),
                                               reads=[r_uT[ub], r_wA], writes=[rPB[pbx]])
                                    tm = tmps[tctr % NTMP]; rt = r_tmps[tctr % NTMP]; tctr += 1
                                    op("dve", lambda e, tm=tm: e.tensor_tensor(out=tm["sig"][:], in0=PB[pb1][:], in1=cosT[:, gs], op=ALU.mult), reads=[rPB[pb1], r_cs], writes=[rt["sig"]])
                                    op("dve", lambda e, tm=tm: e.tensor_tensor(out=tm["lg"][:], in0=PB[pb2][:], in1=sinT[:, gs], op=ALU.mult), reads=[rPB[pb2], r_cs], writes=[rt["lg"]])
                                    op("dve", lambda e, tm=tm: e.tensor_tensor(out=kTt[:, gs], in0=tm["sig"][:], in1=tm["lg"][:], op=ALU.add), reads=[rt["sig"], rt["lg"]], writes=[r_kT[g]])
                                    for t in range(4):
                                        pb = bank_rot.next()
                                        for k in range(8):
                                            op("pe", lambda e, k=k, pb=pb, t=t: e.matmul(PB[pb][:, 0:128], lhsT=uT[ub][:, k, t * 128:(t + 1) * 128], rhs=wA[:, k, 1280:1408],
                                                                                          start=(k == 0), stop=(k == 7)), reads=[r_uT[ub], r_wA], writes=[rPB[pb]])
                                        op("act", lambda e, pb=pb, t=t: e.activation(out=vtk[:, g * 4 + t, :], in_=PB[pb][:, 0:128], func=AF.Copy), reads=[rPB[pb]], writes=[r_v[g * 4 + t]])
                            op("dve", lambda e: e.tensor_tensor(out=Gx[:], in0=Bend[:], in1=Bref[:], op=ALU.subtract), reads=[r_B], writes=[r_G])
                            if NCH > 1:
                                op("dve", lambda e: e.tensor_tensor(out=Gx[:, 0, :, 0:NCH - 1], in0=Gx[:, 0, :, 0:NCH - 1], in1=Bref[:, 0, :, 1:NCH], op=ALU.add), reads=[r_B, r_G], writes=[r_G])
                                op("dve", lambda e: e.tensor_tensor(out=Gx[:, 1, :, 1:NCH], in0=Gx[:, 1, :, 1:NCH], in1=Bref[:, 1, :, 0:NCH - 1], op=ALU.add), reads=[r_B, r_G], writes=[r_G])
                            op("act", lambda e: e.activation(out=Gx[:], in_=Gx[:], func=AF.Exp), reads=[r_G], writes=[r_G])
                            if dbg and s == 0 and hp == 0:
                                for d in range(2):
                                    dma("sp", ds_dbg, dbg_d["QT"][d], QT[d][:], reads=[r_QK[d][hl][g] for hl in range(2) for g in range(NG)])
                                    dma("sp", ds_dbg, dbg_d["KT"][d], KT[d][:], reads=[r_QK[d][hl][g] for hl in range(2) for g in range(NG)])
                                dma("sp", ds_dbg, dbg_d["V"], Vt[:], reads=r_V)
                                dma("sp", ds_dbg, dbg_d["kT"], kTt[:], reads=r_kT)
                        T.barrier()

                        with ExitStack() as pb_:
                            Sm32 = SB(pb_, "Sm32", [128, 4, 128], F32); Smb = SB(pb_, "Smb", [128, 4, 128], BF16); tmp32 = SB(pb_, "tmp32", [128, 4, 128], F32)
                            r_Sm32 = [Res() for _ in range(4)]; r_Smb = [Res() for _ in range(4)]; r_tmp32 = [Res() for _ in range(4)]
                            scm = SB(pb_, "scm", [64, 2, 4, 64], BF16); r_scm = [[Res() for _ in range(4)] for _ in range(2)]
                            ktok = SB(pb_, "ktok", [64, 2, 4, 128], BF16); r_ktok = [[Res() for _ in range(4)] for _ in range(2)]
                            rB = [[None] * 4 for _ in range(2)]
                            for _p in range(2):
                                for _c in range(4):
                                    _r = Res()
                                    rB[_p][_c] = {n: _r for n in ("sc", "kt", "o", "dl")}
                            op("pool", lambda e: e.memset(Sm32[:], 0.0), writes=r_Sm32)
                            op("pool", lambda e: e.memset(Smb[:], 0.0), writes=r_Smb)
                            allQK = lambda d, hl: [r_QK[d][hl][g] for g in range(NG)]

                            def chunk_of(j, d):
                                return j if d == 0 else NCH - 1 - j

                            def stage1(j):
                                par = j % 2
                                for ci in range(4):
                                    hl, d = ci // 2, ci % 2
                                    c = chunk_of(j, d); tk = slice(c * 64, (c + 1) * 64)
                                    bk = ci + 4 * par
                                    op("pe", lambda e, bk=bk, d=d, hl=hl, tk=tk: e.matmul(PB[bk][0:64, 0:64], lhsT=KT[d][:, hl, tk], rhs=QT[d][:, hl, tk], start=True, stop=True),
                                       reads=allQK(d, hl), writes=[rB[par][ci]["sc"]])
                                    op("pe", lambda e, bk=bk, d=d, hl=hl, tk=tk: e.transpose(out=PB[bk][:].bitcast(BF16)[0:64, 128:256], in_=KT[d][:, hl, tk], identity=identb[:]),
                                       reads=allQK(d, hl) + [r_identb], writes=[rB[par][ci]["kt"]])

                            def stage2(j):
                                par = j % 2
                                for ci in range(4):
                                    hl, d = ci // 2, ci % 2
                                    bk = ci + 4 * par
                                    mk = maskf if d == 0 else maskb
                                    op("pool", lambda e, ci=ci: e.memset(scm[:, par, ci, :], 0.0), writes=[r_scm[par][ci]])
                                    op("dve", lambda e, bk=bk, ci=ci, mk=mk: e.copy_predicated(out=scm[:, par, ci, :], mask=mk[:], data=PB[bk][0:64, 0:64]),
                                       reads=[rB[par][ci]["sc"], r_mask], writes=[r_scm[par][ci]])
                                    op("act", lambda e, bk=bk, ci=ci: e.activation(out=ktok[:, par, ci, :], in_=PB[bk][:].bitcast(BF16)[0:64, 128:256], func=AF.Copy),
                                       reads=[rB[par][ci]["kt"]], writes=[r_ktok[par][ci]])

                            def stage3(j):
                                par = j % 2
                                for ci in range(4):
                                    hl, d = ci // 2, ci % 2
                                    c = chunk_of(j, d); tk = slice(c * 64, (c + 1) * 64)
                                    bk = ci + 4 * par
                                    vv = Vt[:, c, hl * 128:(hl + 1) * 128]
                                    op("pe", lambda e, bk=bk, vv=vv, ci=ci: e.matmul(PB[bk][:, 128:192], lhsT=vv, rhs=scm[:, par, ci, :], start=True, stop=False),
                                       reads=[r_V[c // 8], r_scm[par][ci]], writes=[rB[par][ci]["o"]])
                                    op("pe", lambda e, bk=bk, ci=ci, d=d, hl=hl, tk=tk: e.matmul(PB[bk][:, 128:192], lhsT=Smb[:, ci, :], rhs=QT[d][:, hl, tk], start=False, stop=True),
                                       reads=[r_Smb[ci]] + allQK(d, hl), writes=[rB[par][ci]["o"]])
                                    op("pe", lambda e, bk=bk, vv=vv, ci=ci: e.matmul(PB[bk][:, 192:320], lhsT=ktok[:, par, ci, :], rhs=vv, start=True, stop=True),
                                       reads=[r_ktok[par][ci], r_V[c // 8]], writes=[rB[par][ci]["dl"]])

                            def stage4(j):
                                par = j % 2
                                for ci in range(4):
                                    hl, d = ci // 2, ci % 2
                                    h = hp * 2 + hl
                                    c = chunk_of(j, d); tk = slice(c * 64, (c + 1) * 64)
                                    bk = ci + 4 * par
                                    if j < NCH // 2:
                                        op("act", lambda e, bk=bk, h=h, tk=tk: e.activation(out=oT[:, h, tk], in_=PB[bk][:, 128:192], func=AF.Copy),
                                           reads=[rB[par][ci]["o"]], writes=[r_oT[h][c]])
                                    else:
                                        op("dve", lambda e, bk=bk, h=h, tk=tk: e.tensor_tensor(out=oT[:, h, tk], in0=PB[bk][:, 128:192], in1=oT[:, h, tk], op=ALU.add),
                                           reads=[rB[par][ci]["o"], r_oT[h][c]], writes=[r_oT[h][c]])
                                    if j < NCH - 1:
                                        gcol = Gx[:, d, hl, c:c + 1]
                                        op("dve", lambda e, bk=bk, ci=ci: e.tensor_tensor(out=tmp32[:, ci, :], in0=PB[bk][:, 192:320], in1=Sm32[:, ci, :], op=ALU.add),
                                           reads=[rB[par][ci]["dl"], r_Sm32[ci]], writes=[r_tmp32[ci]])
                                        op("act", lambda e, ci=ci, gcol=gcol: e.activation(out=Sm32[:, ci, :], in_=tmp32[:, ci, :], func=AF.Identity, scale=gcol),
                                           reads=[r_tmp32[ci], r_G], writes=[r_Sm32[ci]])
                                        op("pool", lambda e, ci=ci, gcol=gcol: e.tensor_scalar(out=Smb[:, ci, :], in0=tmp32[:, ci, :], scalar1=gcol, scalar2=None, op0=ALU.mult),
                                           reads=[r_tmp32[ci], r_G], writes=[r_Smb[ci]])

                            stage1(0)
                            stage2(0)
                            for j in range(NCH):
                                if j + 1 < NCH:
                                    stage1(j + 1)
                                stage3(j)
                                if j + 1 < NCH:
                                    stage2(j + 1)
                                stage4(j)
                        T.barrier()
                if dbg and s == 0:
                    dma("sp", ds_dbg, dbg_d["oT"], oT[:], reads=[r for rr in r_oT for r in rr])

                ogA = SB(pab, "ogA", [128, 4, S], BF16); atA = SB(pab, "atA", [128, 4, S], BF16)
                r_ogA = [Res() for _ in range(NG)]; r_atA = [Res() for _ in range(NT)]
                with ExitStack() as pc:
                    wC = SB(pc, "wC", [128, 8, 1536], BF16); r_wC = Res()
                    dma("pool", ds_w, wC[:, :, 0:512], w_in_k[:, :, O_RG:O_RG + 512], writes=[r_wC])
                    for m in range(4):
                        for hf in range(2):
                            hq = m + 4 * hf
                            cdst = 512 + m * 128 + hf * 64
                            dma("pool", ds_w, wC[:, :, cdst:cdst + 64], w_in_k[:, :, O_AQ + hq * 64:O_AQ + hq * 64 + 64], writes=[r_wC])
                            dma("pool", ds_w, wC[:, :, 512 + cdst:512 + cdst + 32], w_in_k[:, :, O_AQ + hq * 64 + 32:O_AQ + hq * 64 + 64], writes=[r_wC])
                            dma("pool", ds_w, wC[:, :, 512 + cdst + 32:512 + cdst + 64], w_in_k[:, :, O_AQ + hq * 64:O_AQ + hq * 64 + 32], writes=[r_wC])
                    xs = [SB(pc, "cxs%d" % i, [128, D], F32) for i in range(2)]; r_xs = [Res() for _ in range(2)]
                    uT = SB(pc, "cuT", [128, 8, 512], BF16); r_uT = Res()
                    qT = SB(pc, "qT", [128, 4, 512], BF16); r_qT = Res()
                    ta = [SB(pc, "cta%d" % i, [128, 512], F32) for i in range(4)]; r_ta = [Res() for _ in range(4)]
                    msc = [SB(pc, "msc%d" % i, [128, 384], F32) for i in range(2)]; r_msc = [Res(), Res()]
                    pex = [SB(pc, "pex%d" % i, [128, 384], BF16) for i in range(2)]; r_pex = [Res(), Res()]
                    pT = [SB(pc, "pT%d" % i, [128, 3, 128], BF16) for i in range(2)]; r_pT = [Res(), Res()]
                    sm = [SB(pc, "smx%d" % i, [128, 8], F32) for i in range(2)]; r_sm = [Res(), Res()]
                    sm2 = [SB(pc, "smy%d" % i, [128, 8], F32) for i in range(2)]; r_sm2 = [Res(), Res()]
                    ao = [SB(pc, "ao%d" % i, [128, 512], BF16) for i in range(2)]; r_ao = [Res(), Res()]
                    hctr = 0
                    for g in range(NG):
                        gs = slice(g * 512, (g + 1) * 512)
                        for t in range(4):
                            tile_i = g * 4 + t
                            sl = tile_i % 2
                            dma("sp", ds_x[sl], xs[sl][:], x_d[s, tile_i * 128:(tile_i + 1) * 128, :], writes=[r_xs[sl]])
                            make_uT(xs[sl], r_xs[sl], uT[:, :, t * 128:(t + 1) * 128], r_uT, 1, 0)
                        for m in range(4):
                            for (pbx, cb) in ((2, 512 + m * 128), (3, 1024 + m * 128)):
                                for k in range(8):
                                    op("pe", lambda e, k=k, pbx=pbx, cb=cb: e.matmul(PB[pbx][:], lhsT=wC[:, k, cb:cb + 128], rhs=uT[:, k, :], start=(k == 0), stop=(k == 7)),
                                       reads=[r_uT, r_wC], writes=[rPB[pbx]])
                            op("dve", lambda e: e.tensor_tensor(out=ta[0][:], in0=PB[2][:], in1=cosT[:, gs], op=ALU.mult), reads=[rPB[2], r_cs], writes=[r_ta[0]])
                            op("dve", lambda e: e.tensor_tensor(out=ta[1][:], in0=PB[3][:], in1=sinT[:, gs], op=ALU.mult), reads=[rPB[3], r_cs], writes=[r_ta[1]])
                            op("dve", lambda e, m=m: e.tensor_tensor(out=qT[:, m, :], in0=ta[0][:], in1=ta[1][:], op=ALU.add), reads=[r_ta[0], r_ta[1]], writes=[r_qT])
                        for h in range(4):
                            for k in range(8):
                                op("pe", lambda e, k=k, h=h: e.matmul(PB[4][:], lhsT=wC[:, k, h * 128:(h + 1) * 128], rhs=uT[:, k, :], start=(k == 0), stop=(k == 7)),
                                   reads=[r_uT, r_wC], writes=[rPB[4]])
                            op("act", lambda e: e.activation(out=ta[0][:], in_=PB[4][:], func=AF.Silu), reads=[rPB[4]], writes=[r_ta[0]])
                            rr = [r_oT[h][c] for c in range(g * 8, (g + 1) * 8)]
                            op("act", lambda e, h=h: e.activation(out=ta[2][:], in_=oT[:, h, gs], func=AF.Square), reads=rr, writes=[r_ta[2]])
                            op("pe", lambda e: e.matmul(PB[5][:], lhsT=onesf[:], rhs=ta[2][:], start=True, stop=True), reads=[r_onesf, r_ta[2]], writes=[rPB[5]])
                            op("act", lambda e: e.activation(out=ta[3][:], in_=PB[5][:], func=AF.Ln, scale=1.0 / 128.0, bias=epsc[:, 1:2]), reads=[rPB[5], r_eps], writes=[r_ta[3]])
                            op("act", lambda e: e.activation(out=ta[3][:], in_=ta[3][:], func=AF.Exp, scale=-0.5), reads=[r_ta[3]], writes=[r_ta[3]])
                            op("dve", lambda e, h=h: e.tensor_tensor(out=ta[2][:], in0=oT[:, h, gs], in1=ta[3][:], op=ALU.mult), reads=rr + [r_ta[3]], writes=[r_ta[2]])
                            op("dve", lambda e, h=h: e.scalar_tensor_tensor(out=ogA[:, h, gs], in0=ta[0][:], scalar=ngc[:, 0:1], in1=ta[2][:], op0=ALU.mult, op1=ALU.mult),
                               reads=[r_ta[0], r_ngc, r_ta[2]], writes=[r_ogA[g]])
                        def att_geom(t):
                            j = g * 4 + t
                            k0 = max(0, j - 1); k1 = min(NT, j + 2)
                            return j, k0, k1, k1 - k0, (k1 - k0) * 128, (128 if j == 0 else 0)

                        def att_s1_pe(idx):
                            t, h = idx // 8, idx % 8
                            j, k0, k1, nkb, nk, mc0 = att_geom(t)
                            m, hf = h % 4, h // 4
                            ps = slice(hf * 64, (hf + 1) * 64)
                            ib = idx % 2
                            pbs = 2 + ib
                            kgs = [r_kT[gg] for gg in range((k0 * 128) // 512, ((k1 * 128) - 1) // 512 + 1)]
                            op("pe", lambda e: e.matmul(PB[pbs][:, 0:nk], lhsT=qT[ps, m, t * 128:(t + 1) * 128], rhs=kTt[ps, k0 * 128:k1 * 128], start=True, stop=True),
                               reads=[r_qT] + kgs, writes=[rPB[pbs]])

                        def att_s1_ew(idx):
                            t, h = idx // 8, idx % 8
                            j, k0, k1, nkb, nk, mc0 = att_geom(t)
                            ib = idx % 2
                            pbs = 2 + ib
                            op("dve", lambda e: e.tensor_tensor(out=msc[ib][:, 0:nk], in0=PB[pbs][:, 0:nk], in1=amask[:, mc0:mc0 + nk], op=ALU.add),
                               reads=[rPB[pbs], r_amask], writes=[r_msc[ib]])
                            op("dve", lambda e: e.reduce_max(out=sm[ib][:, 0:1], in_=msc[ib][:, 0:nk], axis=AX.X), reads=[r_msc[ib]], writes=[r_sm[ib]])
                            op("dve", lambda e: e.tensor_scalar(out=sm[ib][:, 1:2], in0=sm[ib][:, 0:1], scalar1=-0.125, scalar2=sinkb[:, 1, h:h + 1], op0=ALU.mult, op1=ALU.min),
                               reads=[r_sm[ib], r_sink], writes=[r_sm[ib]])
                            op("act", lambda e: e.activation(out=pex[ib][:, 0:nk], in_=msc[ib][:, 0:nk], func=AF.Exp, scale=0.125, bias=sm[ib][:, 1:2], accum_out=sm2[ib][:, 0:1]),
                               reads=[r_msc[ib], r_sm[ib]], writes=[r_pex[ib], r_sm2[ib]])
                            op("act", lambda e: e.activation(out=sm2[ib][:, 1:2], in_=sinkb[:, 0, h:h + 1], func=AF.Exp, bias=sm[ib][:, 1:2], scale=1.0),
                               reads=[r_sm[ib], r_sink], writes=[r_sm2[ib]])

                        def att_s2a(idx):
                            t, h = idx // 8, idx % 8
                            j, k0, k1, nkb, nk, mc0 = att_geom(t)
                            ib = idx % 2
                            pbt = 4 + ib
                            for b in range(nkb):
                                op("pe", lambda e, b=b: e.transpose(out=PB[pbt][:].bitcast(BF16)[:, b * 128:(b + 1) * 128], in_=pex[ib][:, b * 128:(b + 1) * 128], identity=identb[:]),
                                   reads=[r_pex[ib], r_identb], writes=[rPB[pbt]])
                            if ib == 0:
                                op("act", lambda e: e.activation(out=pT[ib][:, 0:nkb, :], in_=PB[pbt][:].bitcast(BF16)[:, 0:nk].rearrange("p (b q) -> p b q", q=128), func=AF.Copy),
                                   reads=[rPB[pbt]], writes=[r_pT[ib]])
                            else:
                                op("dve", lambda e: e.tensor_copy(out=pT[ib][:, 0:nkb, :], in_=PB[pbt][:].bitcast(BF16)[:, 0:nk].rearrange("p (b q) -> p b q", q=128)),
                                   reads=[rPB[pbt]], writes=[r_pT[ib]])

                        def att_s2b(idx):
                            t, h = idx // 8, idx % 8
                            j, k0, k1, nkb, nk, mc0 = att_geom(t)
                            hf = h // 4
                            ib = idx % 2
                            op("dve", lambda e: e.tensor_tensor(out=sm2[ib][:, 2:3], in0=sm2[ib][:, 0:1], in1=sm2[ib][:, 1:2], op=ALU.add), reads=[r_sm2[ib]], writes=[r_sm2[ib]])
                            op("dve", lambda e: e.reciprocal(out=sm2[ib][:, 3:4], in_=sm2[ib][:, 2:3]), reads=[r_sm2[ib]], writes=[r_sm2[ib]])
                            pbo = 6 + ib
                            for b in range(nkb):
                                op("pe", lambda e, b=b: e.matmul(PB[pbo][:, 0:64], lhsT=pT[ib][:, b, :], rhs=vtk[:, k0 + b, hf * 64:(hf + 1) * 64], start=(b == 0), stop=(b == nkb - 1)),
                                   reads=[r_pT[ib], r_v[k0 + b]], writes=[rPB[pbo]])
                            aot = ao[t % 2]; r_aot = r_ao[t % 2]
                            op("act", lambda e: e.activation(out=aot[:, h * 64:(h + 1) * 64], in_=PB[pbo][:, 0:64], func=AF.Identity, scale=sm2[ib][:, 3:4]),
                               reads=[rPB[pbo], r_sm2[ib]], writes=[r_aot])
                            if h == 7:
                                pba = t % 2
                                for m in range(4):
                                    op("pe", lambda e, m=m: e.transpose(out=PB[pba][:].bitcast(BF16)[:, m * 128:(m + 1) * 128], in_=aot[:, m * 128:(m + 1) * 128], identity=identb[:]),
                                       reads=[r_aot, r_identb], writes=[rPB[pba]])
                                op("dve", lambda e: e.tensor_copy(out=atA[:, :, j * 128:(j + 1) * 128], in_=PB[pba][:].bitcast(BF16)[:, 0:512].rearrange("p (m q) -> p m q", q=128)),
                                   reads=[rPB[pba]], writes=[r_atA[j]])

                        att_s1_pe(0)
                        att_s1_ew(0)
                        for idx in range(32):
                            if idx + 1 < 32:
                                att_s1_pe(idx + 1)
                            att_s2a(idx)
                            if idx + 1 < 32:
                                att_s1_ew(idx + 1)
                            att_s2b(idx)
                T.barrier()

                with ExitStack() as pc:
                    if S >= 2048:
                        wG = oT[:].rearrange("p h s -> p (h s)").bitcast(BF16)
                        wGv = wG[:, 0:8 * 2048].rearrange("p (k n) -> p k n", n=2048)
                    else:
                        wGv = SB(pc, "wGt", [128, 8, 2048], BF16)[:]
                    r_wG = Res()
                    wR = SB(pc, "wR", [128, 4, D], BF16); wT = SB(pc, "wT", [128, 4, D], BF16); wO = SB(pc, "wO", [128, 8, D], BF16)
                    lnB = SB(pc, "lnB1", [128, 2, D], F32); r_lnB = Res()
                    r_wRT = Res(); r_wGc = [Res() for _ in range(8)]; r_wOr = Res()
                    dma("pool", ds_wc[8], wR[:], w_rec_k, writes=[r_wRT])
                    dma("pool", ds_wc[8], wT[:], w_att_k, writes=[r_wRT])
                    for dcq in range(8):
                        dma("pool", ds_wc[dcq], wGv[:, :, dcq * 128:(dcq + 1) * 128], w_in_k[:, :, O_GR + dcq * 128:O_GR + (dcq + 1) * 128], writes=[r_wGc[dcq]])
                        dma("pool", ds_wc[dcq], wGv[:, :, 1024 + dcq * 128:1024 + (dcq + 1) * 128], w_in_k[:, :, O_GA + dcq * 128:O_GA + (dcq + 1) * 128], writes=[r_wGc[dcq]])
                    for q2 in range(2):
                        dma("pool", ds_wc[9], wO[:, :, q2 * 512:(q2 + 1) * 512], w_out_k[:, :, q2 * 512:(q2 + 1) * 512], writes=[r_wOr])
                    dma("sp", ds_const, lnB[:, 0, :], ln1g_d[0:1, :].broadcast_to([128, D]), writes=[r_lnB])
                    dma("sp", ds_const, lnB[:, 1, :], ln1b_d[0:1, :].broadcast_to([128, D]), writes=[r_lnB])
                    zb = zer[:, 0:8].bitcast(BF16)[:, 0:8].unsqueeze(2)
                    dma("sp", ds_misc[3], u2T_d[s, :, 0:1].rearrange("(j p) o -> p j o", p=128), zb, reads=[r_zer], allow_slow_non_contiguous=True)
                    dma("sp", ds_misc[3], u2T_d[s, :, S + 1:S + 2].rearrange("(j p) o -> p j o", p=128), zb, reads=[r_zer], allow_slow_non_contiguous=True)
                    r_x1d = [Res() for _ in range(NT)]
                    r_u2d = [Res() for _ in range(NT)]
                    xs = [SB(pc, "bxs%d" % i, [128, D], F32) for i in range(4)]; r_xs = [Res() for _ in range(4)]
                    uT = SB(pc, "buT", [128, 8, 512], BF16); r_uT = Res()
                    mT = SB(pc, "mT", [128, 8, 512], BF16); r_mT = Res()
                    ta = [SB(pc, "bta%d" % i, [128, 512], F32) for i in range(4)]; r_ta = [Res() for _ in range(4)]
                    yt = [SB(pc, "yt%d" % i, [128, D], F32) for i in range(2)]; r_yt = [Res(), Res()]
                    st6 = [SB(pc, "st6%d" % i, [128, 20], F32) for i in range(2)]; r_st6 = [Res(), Res()]
                    u2t = [SB(pc, "u2t%d" % i, [128, 8, 128], BF16) for i in range(2)]; r_u2t = [Res(), Res()]
                    for g in range(NG):
                        gs = slice(g * 512, (g + 1) * 512)
                        for t in range(4):
                            tile_i = g * 4 + t
                            dma("sp", ds_x[t], xs[t][:], x_d[s, tile_i * 128:(tile_i + 1) * 128, :], writes=[r_xs[t]])
                            make_uT(xs[t], r_xs[t], uT[:, :, t * 128:(t + 1) * 128], r_uT, 1, 0)
                        BSETS = ((2, 3, 4, 5), (0, 1, 6, 7))

                        def dc_pe(dc):
                            b0, b1, b2, b3 = BSETS[dc % 2]
                            dcs = slice(dc * 128, (dc + 1) * 128)
                            for k in range(8):
                                op("pe", lambda e, k=k: e.matmul(PB[b0][:], lhsT=wGv[:, k, dc * 128:(dc + 1) * 128], rhs=uT[:, k, :], start=(k == 0), stop=(k == 7)),
                                   reads=[r_uT, r_wGc[dc]], writes=[rPB[b0]])
                            for k in range(8):
                                op("pe", lambda e, k=k: e.matmul(PB[b1][:], lhsT=wGv[:, k, 1024 + dc * 128:1024 + (dc + 1) * 128], rhs=uT[:, k, :], start=(k == 0), stop=(k == 7)),
                                   reads=[r_uT, r_wGc[dc]], writes=[rPB[b1]])
                            for h in range(4):
                                op("pe", lambda e, h=h: e.matmul(PB[b2][:], lhsT=wR[:, h, dcs], rhs=ogA[:, h, gs], start=(h == 0), stop=(h == 3)),
                                   reads=[r_wRT, r_ogA[g]], writes=[rPB[b2]])
                            for m in range(4):
                                op("pe", lambda e, m=m: e.matmul(PB[b3][:], lhsT=wT[:, m, dcs], rhs=atA[:, m, gs], start=(m == 0), stop=(m == 3)),
                                   reads=[r_wRT] + r_atA[g * 4:(g + 1) * 4], writes=[rPB[b3]])

                        def dc_ew(dc):
                            b0, b1, b2, b3 = BSETS[dc % 2]
                            a0, a1 = ta[2 * (dc % 2)], ta[2 * (dc % 2) + 1]
                            ra0, ra1 = r_ta[2 * (dc % 2)], r_ta[2 * (dc % 2) + 1]
                            op("act", lambda e: e.activation(out=a0[:], in_=PB[b0][:], func=AF.Sigmoid), reads=[rPB[b0]], writes=[ra0])
                            op("act", lambda e: e.activation(out=a1[:], in_=PB[b1][:], func=AF.Sigmoid), reads=[rPB[b1]], writes=[ra1])
                            op("dve", lambda e: e.tensor_tensor(out=a0[:], in0=PB[b2][:], in1=a0[:], op=ALU.mult), reads=[rPB[b2], ra0], writes=[ra0])
                            op("dve", lambda e: e.tensor_tensor(out=a1[:], in0=PB[b3][:], in1=a1[:], op=ALU.mult), reads=[rPB[b3], ra1], writes=[ra1])
                            op("dve", lambda e: e.tensor_tensor(out=mT[:, dc, :], in0=a0[:], in1=a1[:], op=ALU.add), reads=[ra0, ra1], writes=[r_mT])

                        dc_pe(0)
                        for dc in range(8):
                            if dc + 1 < 8:
                                dc_pe(dc + 1)
                            dc_ew(dc)
                        def wout_pe(t):
                            tile_i = g * 4 + t
                            wb = 2 + 2 * (tile_i % 2)
                            for hh in range(2):
                                for k in range(8):
                                    op("pe", lambda e, k=k, hh=hh: e.matmul(PB[wb + hh][:], lhsT=mT[:, k, t * 128:(t + 1) * 128], rhs=wO[:, k, hh * 512:(hh + 1) * 512], start=(k == 0), stop=(k == 7)),
                                       reads=[r_mT, r_wOr], writes=[rPB[wb + hh]])

                        wout_pe(0)
                        for t in range(4):
                            tile_i = g * 4 + t
                            ib = tile_i % 2
                            wb = 2 + 2 * (tile_i % 2)
                            ln_part1((wb, wb + 1), (rPB[wb], rPB[wb + 1]), 0, yt[ib], r_yt[ib])
                            if t + 1 < 4:
                                wout_pe(t + 1)
                            ln_part2(xs[t][:], r_xs[t], lnB[:, 0, :], lnB[:, 1, :], r_lnB, yt[ib], r_yt[ib], st6[ib], r_st6[ib])
                            dma("sp", ds_o[ib], x1_d[s, tile_i * 128:(tile_i + 1) * 128, :], yt[ib][:], reads=[r_yt[ib]], writes=[r_x1d[tile_i]])
                            make_uT(yt[ib], r_yt[ib], u2t[ib][:], r_u2t[ib], 3, 2, banks=(6, 7))
                            dma("sp", ds_o[2 + ib], u2T_d[s, :, 1 + tile_i * 128:1 + (tile_i + 1) * 128].rearrange("(j p) q -> p j q", p=128), u2t[ib][:],
                                reads=[r_u2t[ib]], writes=[r_u2d[tile_i]])
                T.barrier()
            T.barrier()

            with ExitStack() as pf:
                wU = SB(pf, "wU", [128, 8, 2 * DFF], BF16); wD = SB(pf, "wD", [128, NFC, D], BF16); r_wU = Res()
                lnB = SB(pf, "lnB2", [128, 2, D], F32); r_lnB = Res()
                cvw = SB(pf, "cvw", [128, 4, 2 * NFC], F32); r_cvw = r_lnB
                r_wc = [Res() for _ in range(11)]
                for jq in range(11):
                    dma("pool", ds_wc[jq], wU[:, :, jq * 256:(jq + 1) * 256], w_up_k[:, :, jq * 256:(jq + 1) * 256], writes=[r_wc[jq]])
                    dma("pool", ds_wc[jq], wU[:, :, DFF + jq * 256:DFF + (jq + 1) * 256], w_up_k[:, :, DFF + jq * 256:DFF + (jq + 1) * 256], writes=[r_wc[jq]])
                    dma("pool", ds_wc[jq], wD[:, 2 * jq:2 * jq + 2, :], w_dn_k[:, 2 * jq:2 * jq + 2, :], writes=[r_wc[jq]])
                dma("sp", ds_const, lnB[:, 0, :], ln2g_d[0:1, :].broadcast_to([128, D]), writes=[r_lnB])
                dma("sp", ds_const, lnB[:, 1, :], ln2b_d[0:1, :].broadcast_to([128, D]), writes=[r_lnB])
                dma("sp", ds_const, cvw[:, 0:3, :], cw_d.rearrange("k (j p) -> p k j", p=128), writes=[r_cvw], allow_slow_non_contiguous=True)
                dma("sp", ds_const, cvw[:, 3, :], cb_d.rearrange("o (j p) -> p (o j)", p=128), writes=[r_cvw], allow_slow_non_contiguous=True)
                NSG = S // 256
                u2 = [SB(pf, "fu2%d" % i, [128, 8, 258], BF16) for i in range(2)]; r_u2 = [Res(), Res()]
                x1t = [SB(pf, "fx1%d" % i, [128, D], F32) for i in range(4)]; r_x1t = [Res() for _ in range(4)]
                cv = [SB(pf, "fcv%d" % i, [128, 256], F32) for i in range(2)]; r_cv = [Res(), Res()]
                cgt = [SB(pf, "fcg%d" % i, [128, 256], F32) for i in range(2)]; r_cg = [Res(), Res()]
                gz = [SB(pf, "fgz%d" % i, [128, 256], F32) for i in range(2)]; r_gz = [Res(), Res()]
                actb = [SB(pf, "fact%d" % i, [128, 256], BF16) for i in range(2)]; r_act = [Res(), Res()]
                yt = [SB(pf, "fyt%d" % i, [128, D], F32) for i in range(2)]; r_yt = [Res(), Res()]
                st6 = [SB(pf, "fst6%d" % i, [128, 20], F32) for i in range(2)]; r_st6 = [Res(), Res()]
                def c2_up(sg, c):
                    ub = sg % 2; ib = c % 2
                    pv, pg = 4 + 2 * ib, 5 + 2 * ib
                    for (pbx, cb) in ((pv, c * 128), (pg, DFF + c * 128)):
                        for k in range(8):
                            op("pe", lambda e, k=k, pbx=pbx, cb=cb: e.matmul(PB[pbx][:, 0:258], lhsT=wU[:, k, cb:cb + 128], rhs=u2[ub][:, k, :], start=(k == 0), stop=(k == 7)),
                               reads=[r_wc[c // 2], r_u2[ub]], writes=[rPB[pbx]])

                def c2_s1(c):
                    ib = c % 2
                    pv, pg = 4 + 2 * ib, 5 + 2 * ib
                    for (pbx, dst, r_dst, col) in ((pv, cv[ib], r_cv[ib], c), (pg, cgt[ib], r_cg[ib], NFC + c)):
                        op("act", lambda e, pbx=pbx, dst=dst, col=col: e.activation(out=dst[:], in_=PB[pbx][:, 0:256], func=AF.Identity, scale=cvw[:, 0, col:col + 1], bias=cvw[:, 3, col:col + 1]),
                           reads=[rPB[pbx], r_cvw], writes=[r_dst])

                def c2_s2(c):
                    ib = c % 2
                    pv, pg = 4 + 2 * ib, 5 + 2 * ib
                    for (pbx, dst, r_dst, col) in ((pg, cgt[ib], r_cg[ib], NFC + c), (pv, cv[ib], r_cv[ib], c)):
                        op("dve", lambda e, pbx=pbx, dst=dst, col=col: e.scalar_tensor_tensor(out=dst[:], in0=PB[pbx][:, 1:257], scalar=cvw[:, 1, col:col + 1], in1=dst[:], op0=ALU.mult, op1=ALU.add),
                           reads=[rPB[pbx], r_cvw, r_dst], writes=[r_dst])
                        op("dve", lambda e, pbx=pbx, dst=dst, col=col: e.scalar_tensor_tensor(out=dst[:], in0=PB[pbx][:, 2:258], scalar=cvw[:, 2, col:col + 1], in1=dst[:], op0=ALU.mult, op1=ALU.add),
                           reads=[rPB[pbx], r_cvw, r_dst], writes=[r_dst])

                def c2_s3(c):
                    ib = c % 2
                    op("act", lambda e: e.activation(out=gz[ib][:], in_=cgt[ib][:], func=AF.Gelu_apprx_tanh), reads=[r_cg[ib]], writes=[r_gz[ib]])

                def c2_s4(c):
                    ib = c % 2
                    op("dve", lambda e: e.tensor_tensor(out=actb[ib][:], in0=gz[ib][:], in1=cv[ib][:], op=ALU.mult), reads=[r_gz[ib], r_cv[ib]], writes=[r_act[ib]])

                def c2_down(c):
                    ib = c % 2
                    for tt in range(2):
                        for hh in range(2):
                            pa_ = tt * 2 + hh
                            op("pe", lambda e, tt=tt, hh=hh, pa_=pa_: e.matmul(PB[pa_][:], lhsT=actb[ib][:, tt * 128:(tt + 1) * 128], rhs=wD[:, c, hh * 512:(hh + 1) * 512],
                                                                             start=(c == 0), stop=(c == NFC - 1)),
                               reads=[r_act[ib], r_wc[c // 2]], writes=[rPB[pa_]])

                def c2_load(sg):
                    a = sg * 256
                    ub = sg % 2
                    dma("sp", ds_misc[ub], u2[ub][:], u2T_d[s, :, a:a + 258].rearrange("(j p) q -> p j q", p=128), writes=[r_u2[ub]])
                    for tt in range(2):
                        sl = (sg * 2 + tt) % 4
                        dma("sp", ds_x[sl], x1t[sl][:], x1_d[s, a + tt * 128:a + (tt + 1) * 128, :], writes=[r_x1t[sl]])

                def c2_prologue(sg):
                    c2_up(sg, 0); c2_up(sg, 1); c2_s1(0); c2_s2(0); c2_s1(1)

                def c2_ln2(sg):
                    for tt in range(2):
                        tile_i = sg * 2 + tt
                        sl = tile_i % 4
                        ib = tile_i % 2
                        ln_part2(x1t[sl][:], r_x1t[sl], lnB[:, 0, :], lnB[:, 1, :], r_lnB, yt[ib], r_yt[ib], st6[ib], r_st6[ib])
                        dma("sp", ds_o[ib], out_d[s, tile_i * 128:(tile_i + 1) * 128, :], yt[ib][:], reads=[r_yt[ib]])

                c2_load(0)
                if NSG > 1:
                    c2_load(1)
                c2_prologue(0)
                for sg in range(NSG):
                    for c in range(NFC):
                        c2_s3(c)
                        if c + 1 < NFC:
                            c2_s2(c + 1)
                        c2_s4(c)
                        if c + 2 < NFC:
                            c2_up(sg, c + 2)
                            c2_s1(c + 2)
                        c2_down(c)
                    for tt in range(2):
                        tile_i = sg * 2 + tt
                        ib = tile_i % 2
                        ln_part1((tt * 2, tt * 2 + 1), (rPB[tt * 2], rPB[tt * 2 + 1]), 1, yt[ib], r_yt[ib])
                    if sg + 1 < NSG:
                        c2_prologue(sg + 1)
                    c2_ln2(sg)
                    if sg + 2 < NSG:
                        c2_load(sg + 2)
            T.barrier()
        print("build: instructions=%d waits=%d" % (T.n_ins, T.n_wait))
    return nc


def _consts():
    ident = np.eye(128, dtype=np.float32)
    i = np.arange(64)
    maskf = (i[:, None] <= i[None, :]).astype(np.uint8)
    maskb = (i[:, None] >= i[None, :]).astype(np.uint8)
    q = np.arange(128)[:, None]
    cc = np.arange(128)[None, :]
    NEG = -30000.0
    am = np.zeros((128, 384), np.float32)
    am[:, 0:128] = np.where(cc >= q, 0.0, NEG)
    am[:, 256:384] = np.where(cc <= q, 0.0, NEG)
    half = 32
    inv = (np.float32(10000.0) ** (-np.arange(half, dtype=np.float32) / np.float32(half))).astype(np.float32)
    p = np.arange(128)
    rope = np.stack([inv[p % 32], np.where((p % 64) < 32, -1.0, 1.0).astype(np.float32)], axis=1).astype(np.float32)
    return dict(k_ident=ident, k_maskf=maskf, k_maskb=maskb, k_amask=am, k_rope=rope)


def make_in_maps(inputs, n_cores, nseq):
    f = lambda a: np.ascontiguousarray(np.asarray(a))
    shared = dict(
        w_ada=f(inputs["w_ada"][0]), b_ada=f(inputs["b_ada"][0:1]), w_in=f(inputs["w_in"][0]),
        rec_lower_bound=f(inputs["rec_lower_bound"]), rec_norm_g=f(inputs["rec_norm_g"][0:1]),
        w_rec_branch=f(inputs["w_rec_branch"][0]), attn_sink=f(inputs["attn_sink"][0:1]),
        w_attn_branch=f(inputs["w_attn_branch"][0]), w_out=f(inputs["w_out"][0]),
        ln1_g=f(inputs["ln1_g"][0:1]), ln1_b=f(inputs["ln1_b"][0:1]), w_up=f(inputs["w_up"][0]),
        conv_w=f(inputs["conv_w"][0]), conv_b=f(inputs["conv_b"][0:1]), w_down=f(inputs["w_down"][0]),
        ln2_g=f(inputs["ln2_g"][0:1]), ln2_b=f(inputs["ln2_b"][0:1]))
    shared.update(_consts())
    maps = []
    for i in range(n_cores):
        m = dict(shared)
        m["x"] = f(inputs["x"][i * nseq:(i + 1) * nseq])
        m["c"] = f(inputs["c"][i * nseq:(i + 1) * nseq])
        m["positions"] = f(inputs["positions"][i * nseq:(i + 1) * nseq]).astype(np.int32)
        maps.append(m)
    return maps


def kernel(**inputs):
    B, S, _ = inputs["x"].shape
    nseq = B // N_CORES
    nc = build_nc(S=S, NSEQ=nseq)
    in_maps = make_in_maps(inputs, N_CORES, nseq)
    res = run_bass_kernel_spmd(nc, in_maps, core_ids=list(range(N_CORES)))
    return np.concatenate([np.asarray(r["out"]) for r in res.results], axis=0).astype(np.float32)
```

```python
import numpy as np
from contextlib import ExitStack
import concourse.bass as bass
import concourse.mybir as mybir
from concourse.bass_utils import run_bass_kernel_spmd

F32 = mybir.dt.float32
BF16 = mybir.dt.bfloat16
I32 = mybir.dt.int32
U8 = mybir.dt.uint8
AF = mybir.ActivationFunctionType
ALU = mybir.AluOpType
AX = mybir.AxisListType

D = 1024
NK = 8
DIN = 5376
DFF = 2816
NFC = DFF // 128
ALPHA = 2.0 ** 0.25
LN_EPS = 1e-5
RMS_EPS = 1e-6
N_CORES = 8
O_RQ, O_RFF, O_RFB, O_RI, O_RG, O_AQ, O_AK, O_AV, O_GR, O_GA = 0, 512, 1024, 1536, 2048, 2560, 3072, 3200, 3328, 4352


class Res:
    __slots__ = ("w", "rd")

    def __init__(self):
        self.w = None
        self.rd = []


class Tracker:
    def __init__(self, nc, stack):
        self.nc = nc
        self.stack = stack
        self.eng = {}
        self.sems = {}
        for name, h in (("pe", nc.tensor), ("act", nc.scalar), ("dve", nc.vector),
                        ("pool", nc.gpsimd), ("sp", nc.sync)):
            sem = stack.enter_context(nc.semaphore("sem_" + name))
            self.eng[name] = {"h": h, "sem": sem, "cnt": 0, "seen": {}, "name": name}
            self.sems[id(sem)] = sem
        self.dma_sems = []
        self.n_wait = 0
        self.n_ins = 0

    def new_dma_sem(self, name):
        sem = self.stack.enter_context(self.nc.semaphore(name))
        d = {"sem": sem, "cnt": 0}
        self.sems[id(sem)] = sem
        self.dma_sems.append(d)
        return d

    def _wait_deps(self, E, reads, writes):
        deps = {}

        def add(tok):
            if tok is not None and deps.get(tok[0], 0) < tok[1]:
                deps[tok[0]] = tok[1]

        for r in reads:
            add(r.w)
        for w in writes:
            add(w.w)
            for t in w.rd:
                add(t)
        own = id(E["sem"])
        for k, v in deps.items():
            if E["name"] == "pe" and k == own:
                continue
            if E["seen"].get(k, 0) >= v:
                continue
            E["h"].wait_ge(self.sems[k], v)
            E["seen"][k] = v
            self.n_wait += 1

    def _commit(self, tok, reads, writes):
        for r in reads:
            r.rd.append(tok)
            if len(r.rd) > 48:
                best = {}
                for k, v in r.rd:
                    if best.get(k, 0) < v:
                        best[k] = v
                r.rd = list(best.items())
        for w in writes:
            w.w = tok
            w.rd = []
        self.n_ins += 1

    def op(self, eng, fn, reads=(), writes=()):
        E = self.eng[eng]
        self._wait_deps(E, reads, writes)
        ins = fn(E["h"])
        E["cnt"] += 1
        ins.then_inc(E["sem"], 1)
        self._commit((id(E["sem"]), E["cnt"]), reads, writes)
        return ins

    def dma(self, q, dsem, out, in_, reads=(), writes=(), **kw):
        E = self.eng[q]
        self._wait_deps(E, reads, writes)
        ins = E["h"].dma_start(out=out, in_=in_, **kw)
        dsem["cnt"] += 16
        ins.then_inc(dsem["sem"], 16)
        self._commit((id(dsem["sem"]), dsem["cnt"]), reads, writes)
        return ins

    def barrier(self):
        targets = [(id(e["sem"]), e["cnt"]) for e in self.eng.values() if e["cnt"] > 0]
        targets += [(id(d["sem"]), d["cnt"]) for d in self.dma_sems if d["cnt"] > 0]
        for E in self.eng.values():
            for k, v in targets:
                if E["seen"].get(k, 0) >= v:
                    continue
                E["h"].wait_ge(self.sems[k], v)
                E["seen"][k] = v
                self.n_wait += 1


class Rot:
    def __init__(self, items):
        self.items = items
        self.i = 0

    def next(self):
        it = self.items[self.i % len(self.items)]
        self.i += 1
        return it


def build_nc(S=2048, NSEQ=2, dbg=False):
    NT, NG, NCH = S // 128, S // 512, S // 64
    nc = bass.Bass("TRN2", target_bir_lowering=False)

    def din(name, shape, dt=F32):
        return nc.dram_tensor(name, list(shape), dt, kind="ExternalInput").ap()

    x_d = din("x", [NSEQ, S, D])
    c_d = din("c", [NSEQ, D])
    pos_d = din("positions", [NSEQ, S], I32)
    w_ada_d = din("w_ada", [D, 6 * D])
    b_ada_d = din("b_ada", [1, 6 * D])
    w_in_d = din("w_in", [D, DIN])
    lbr_d = din("rec_lower_bound", [2, 2, 512])
    ng_d = din("rec_norm_g", [1, 128])
    w_rec_d = din("w_rec_branch", [512, D])
    sink_d = din("attn_sink", [1, 8])
    w_att_d = din("w_attn_branch", [512, D])
    w_out_d = din("w_out", [D, D])
    ln1g_d = din("ln1_g", [1, D]); ln1b_d = din("ln1_b", [1, D])
    w_up_d = din("w_up", [D, 2 * DFF])
    cw_d = din("conv_w", [3, 2 * DFF])
    cb_d = din("conv_b", [1, 2 * DFF])
    w_dn_d = din("w_down", [DFF, D])
    ln2g_d = din("ln2_g", [1, D]); ln2b_d = din("ln2_b", [1, D])
    identf_d = din("k_ident", [128, 128])
    mskf_d = din("k_maskf", [64, 64], U8)
    mskb_d = din("k_maskb", [64, 64], U8)
    amask_d = din("k_amask", [128, 384])
    rope_d = din("k_rope", [128, 2])
    out_d = nc.dram_tensor("out", [NSEQ, S, D], F32, kind="ExternalOutput").ap()
    x1_d = nc.dram_tensor("x1_scr", [NSEQ, S, D], F32).ap()
    u2T_d = nc.dram_tensor("u2T_scr", [NSEQ, D, S + 2], BF16).ap()
    dbg_d = {}
    if dbg:
        dbg_d["mods"] = nc.dram_tensor("dbg_mods", [128, 4, 8], F32, kind="ExternalOutput").ap()
        dbg_d["QT"] = nc.dram_tensor("dbg_QT", [2, 128, 2, S], BF16, kind="ExternalOutput").ap()
        dbg_d["KT"] = nc.dram_tensor("dbg_KT", [2, 128, 2, S], BF16, kind="ExternalOutput").ap()
        dbg_d["V"] = nc.dram_tensor("dbg_V", [64, NCH, 256], BF16, kind="ExternalOutput").ap()
        dbg_d["oT"] = nc.dram_tensor("dbg_oT", [128, 4, S], F32, kind="ExternalOutput").ap()
        dbg_d["kT"] = nc.dram_tensor("dbg_kT", [128, S], BF16, kind="ExternalOutput").ap()
        dbg_d["x1"] = x1_d

    w_in_k = w_in_d.rearrange("(j p) n -> p j n", p=128)
    w_ada_k = w_ada_d.rearrange("(j p) n -> p j n", p=128)
    w_up_k = w_up_d.rearrange("(j p) n -> p j n", p=128)
    w_out_k = w_out_d.rearrange("(j p) n -> p j n", p=128)
    w_rec_k = w_rec_d.rearrange("(j p) n -> p j n", p=128)
    w_att_k = w_att_d.rearrange("(j p) n -> p j n", p=128)
    w_dn_k = w_dn_d.rearrange("(j p) n -> p j n", p=128)

    with ExitStack() as top:
        T = Tracker(nc, top)
        op, dma = T.op, T.dma

        uniq = [0]

        def SB(st, name, shape, dt):
            uniq[0] += 1
            return st.enter_context(nc.sbuf_tensor("%s_%d" % (name, uniq[0]), list(shape), dt))

        PB = [top.enter_context(nc.psum_tensor("pb%d" % i, [128, 512], F32)) for i in range(8)]
        rPB = [Res() for _ in range(8)]
        ds_const = T.new_dma_sem("ds_const")
        ds_w = T.new_dma_sem("ds_w")
        ds_x = [T.new_dma_sem("ds_x%d" % i) for i in range(4)]
        ds_o = [T.new_dma_sem("ds_o%d" % i) for i in range(4)]
        ds_misc = [T.new_dma_sem("ds_m%d" % i) for i in range(4)]
        ds_dbg = T.new_dma_sem("ds_dbg")
        ds_wc = [T.new_dma_sem("ds_wc%d" % i) for i in range(11)]

        identf = SB(top, "identf", [128, 128], F32); r_identf = Res()
        identb = SB(top, "identb", [128, 128], BF16); r_identb = Res()
        onesf = SB(top, "onesf", [128, 128], F32); r_onesf = Res()
        maskf = SB(top, "maskf", [64, 64], U8); maskb = SB(top, "maskb", [64, 64], U8); r_mask = Res()
        zer = SB(top, "zer", [128, 128], F32); r_zer = Res()
        rmask = SB(top, "rmask", [128, 512], F32); r_rmask = Res()
        lbt = SB(top, "lbt", [128, 3, 2, 4], F32); r_lbt = Res()
        lraw = SB(top, "lraw", [128, 2, 2, 4], F32); r_lraw = Res()
        ngc = SB(top, "ngc", [128, 1], F32); r_ngc = Res()
        sinkb = SB(top, "sinkb", [128, 2, 8], F32); r_sink = Res()
        rope = SB(top, "rope", [128, 2], F32); r_rope = Res()
        epsc = SB(top, "epsc", [128, 2], F32); r_eps = Res()
        modsT_all = SB(top, "modsT", [128, NSEQ, 4, 8], F32); r_modsT = Res()
        gaB_all = SB(top, "gaB", [128, NSEQ, 2, D], F32); r_gaB = [Res(), Res()]
        modsT = modsT_all[:, 0]; gaB = gaB_all[:, 0]
        amask = SB(top, "amask", [128, 384], F32); r_amask = Res()

        dma("sp", ds_const, identf[:], identf_d, writes=[r_identf])
        dma("sp", ds_const, maskf[:], mskf_d, writes=[r_mask])
        dma("sp", ds_const, maskb[:], mskb_d, writes=[r_mask])
        dma("sp", ds_const, rope[:], rope_d, writes=[r_rope])
        dma("sp", ds_const, amask[:], amask_d, writes=[r_amask])
        dma("sp", ds_const, ngc[:], ng_d.rearrange("o p -> p o"), writes=[r_ngc], allow_slow_non_contiguous=True)
        dma("sp", ds_const, sinkb[:, 0, :], sink_d[0:1, :].broadcast_to([128, 8]), writes=[r_sink])
        dma("sp", ds_const, lraw[:].rearrange("p a b h -> p (a b) h"),
            lbr_d.rearrange("a b (h p) -> p (a b) h", p=128), writes=[r_lraw], allow_slow_non_contiguous=True)
        T.barrier()
        op("dve", lambda e: e.tensor_copy(out=identb[:], in_=identf[:]), reads=[r_identf], writes=[r_identb])
        op("pool", lambda e: e.memset(onesf[:], 1.0), writes=[r_onesf])
        op("pool", lambda e: e.memset(zer[:], 0.0), writes=[r_zer])
        op("pool", lambda e: e.memset(rmask[:], 1.0), writes=[r_rmask])
        op("pool", lambda e: e.memset(rmask[:].rearrange("p (c t) -> p c t", t=64)[:, :, 0:1], 0.0), writes=[r_rmask])
        op("pool", lambda e: e.memset(epsc[:, 0:1], LN_EPS), writes=[r_eps])
        op("pool", lambda e: e.memset(epsc[:, 1:2], RMS_EPS), writes=[r_eps])
        op("dve", lambda e: e.tensor_scalar(out=sinkb[:, 1, :], in0=sinkb[:, 0, :], scalar1=-1.0, scalar2=None, op0=ALU.mult),
           reads=[r_sink], writes=[r_sink])
        op("dve", lambda e: e.tensor_tensor(out=lbt[:, 2, :, :], in0=lraw[:, :, 0, :], in1=lraw[:, :, 1, :], op=ALU.subtract),
           reads=[r_lraw], writes=[r_lbt])
        op("act", lambda e: e.activation(out=lbt[:, 0, :, :], in_=lbt[:, 2, :, :], func=AF.Sigmoid), reads=[r_lbt], writes=[r_lbt])
        op("act", lambda e: e.activation(out=lbt[:, 1, :, :], in_=lbt[:, 2, :, :], func=AF.Sigmoid, scale=-1.0), reads=[r_lbt], writes=[r_lbt])
        op("dve", lambda e: e.tensor_scalar(out=lbt[:, 2, :, :], in0=lbt[:, 1, :, :], scalar1=-1.0, scalar2=None, op0=ALU.mult),
           reads=[r_lbt], writes=[r_lbt])

        def make_uT(xt, r_xt, dst, r_dst, scol, bcol, banks=(0, 1)):
            for hb in range(2):
                b = banks[hb]
                for i in range(4):
                    k = hb * 4 + i
                    op("pe", lambda e, k=k, i=i, b=b: e.transpose(out=PB[b][:, i * 128:(i + 1) * 128], in_=xt[:, k * 128:(k + 1) * 128],
                                                                   identity=identf[:]),
                       reads=[r_xt, r_identf], writes=[rPB[b]])
                for i in range(4):
                    k = hb * 4 + i
                    if i % 2 == 0:
                        op("act", lambda e, k=k, i=i, b=b: e.activation(out=dst[:, k, :], in_=PB[b][:, i * 128:(i + 1) * 128], func=AF.Identity,
                                                                         scale=modsT[:, scol, k:k + 1], bias=modsT[:, bcol, k:k + 1]),
                           reads=[rPB[b], r_modsT], writes=[r_dst])
                    else:
                        op("dve", lambda e, k=k, i=i, b=b: e.tensor_scalar(out=dst[:, k, :], in0=PB[b][:, i * 128:(i + 1) * 128],
                                                                            scalar1=modsT[:, scol, k:k + 1], scalar2=modsT[:, bcol, k:k + 1],
                                                                            op0=ALU.mult, op1=ALU.add),
                           reads=[rPB[b], r_modsT], writes=[r_dst])

        def ln_part1(pz, r_pz, which, y, r_y):
            for h in range(2):
                op("dve", lambda e, h=h: e.tensor_tensor(out=y[:, h * 512:(h + 1) * 512], in0=PB[pz[h]][:], in1=gaB[:, which, h * 512:(h + 1) * 512], op=ALU.mult),
                   reads=[r_pz[h], r_gaB[which]], writes=[r_y])

        def ln_part2(xres, r_xres, gB, bB, r_gb, y, r_y, st6, r_st):
            op("dve", lambda e: e.scalar_tensor_tensor(out=y[:], in0=xres, scalar=ALPHA, in1=y[:], op0=ALU.mult, op1=ALU.add),
               reads=[r_xres, r_y], writes=[r_y])
            for h in range(2):
                op("dve", lambda e, h=h: e.bn_stats(out=st6[:, h * 6:(h + 1) * 6], in_=y[:, h * 512:(h + 1) * 512]), reads=[r_y], writes=[r_st])
            op("dve", lambda e: e.bn_aggr(out=st6[:, 12:14], in_=st6[:, 0:12]), reads=[r_st], writes=[r_st])
            op("act", lambda e: e.activation(out=st6[:, 14:15], in_=st6[:, 13:14], func=AF.Ln, bias=epsc[:, 0:1], scale=1.0), reads=[r_st, r_eps], writes=[r_st])
            op("act", lambda e: e.activation(out=st6[:, 15:16], in_=st6[:, 14:15], func=AF.Exp, scale=-0.5), reads=[r_st], writes=[r_st])
            op("dve", lambda e: e.tensor_scalar(out=st6[:, 16:17], in0=st6[:, 12:13], scalar1=st6[:, 15:16], scalar2=-1.0, op0=ALU.mult, op1=ALU.mult),
               reads=[r_st], writes=[r_st])
            op("act", lambda e: e.activation(out=y[:], in_=y[:], func=AF.Identity, scale=st6[:, 15:16], bias=st6[:, 16:17]),
               reads=[r_y, r_st], writes=[r_y])
            op("dve", lambda e: e.tensor_tensor(out=y[:], in0=y[:], in1=gB, op=ALU.mult), reads=[r_y, r_gb], writes=[r_y])
            op("dve", lambda e: e.tensor_tensor(out=y[:], in0=y[:], in1=bB, op=ALU.add), reads=[r_y, r_gb], writes=[r_y])

        def ln_epilogue(pz, r_pz, xres, r_xres, which, gB, bB, r_gb, y, r_y, st6, r_st):
            ln_part1(pz, r_pz, which, y, r_y)
            ln_part2(xres, r_xres, gB, bB, r_gb, y, r_y, st6, r_st)

        for s in range(NSEQ):
            modsT = modsT_all[:, s]
            gaB = gaB_all[:, s]
            if s == 0:
              with ExitStack() as ph:
                cT = SB(ph, "cT", [128, NSEQ, 8], F32); r_cT = Res()
                cbt = SB(ph, "cbt", [128, NSEQ, 8, 128], F32); r_cbt = Res()
                wst = [SB(ph, "wst%d" % i, [128, 8, 512], F32) for i in range(2)]; r_wst = [Res(), Res()]
                bst = [SB(ph, "bst%d" % i, [1, 512], F32) for i in range(2)]; r_bst = [Res(), Res()]
                mtmp = [SB(ph, "mtmp%d" % i, [128, 512], F32) for i in range(NSEQ)]; r_mtmp = [Res() for _ in range(NSEQ)]
                for ss in range(NSEQ):
                    dma("sp", ds_misc[0], cT[:, ss, :], c_d[ss].rearrange("(j p) -> p j", p=128), writes=[r_cT], allow_slow_non_contiguous=True)
                op("act", lambda e: e.activation(out=cT[:], in_=cT[:], func=AF.Silu), reads=[r_cT], writes=[r_cT])
                op("dve", lambda e: e.tensor_copy(out=cbt[:], in_=cT[:].unsqueeze(3).broadcast_to([128, NSEQ, 8, 128])), reads=[r_cT], writes=[r_cbt])
                bcol = SB(ph, "bcol", [128, 48], F32); r_bcol = Res()
                cT2 = SB(ph, "cT2", [128, 8, NSEQ], F32); r_cT2 = Res()
                dma("sp", ds_misc[3], bcol[:], b_ada_d.rearrange("o (j p) -> p (o j)", p=128), writes=[r_bcol], allow_slow_non_contiguous=True)
                op("dve", lambda e: e.tensor_copy(out=cT2[:], in_=cT[:].rearrange("p s k -> p k s")), reads=[r_cT], writes=[r_cT2])
                for cg in range(12):
                    b2 = cg % 2
                    dma("sp", ds_misc[1 + b2], wst[b2][:], w_ada_k[:, :, cg * 512:(cg + 1) * 512], writes=[r_wst[b2]])
                    which, half = cg // 2, cg % 2
                    if which in (2, 5):
                        dma("sp", ds_o[b2], bst[b2][:], b_ada_d[0:1, cg * 512:(cg + 1) * 512], writes=[r_bst[b2]])
                        gi = 0 if which == 2 else 1
                        for ss in range(NSEQ):
                            pb = 2 + b2 + 2 * (ss % 2)
                            for k in range(8):
                                op("pe", lambda e, k=k: e.matmul(PB[pb][:], lhsT=cbt[:, ss, k, :], rhs=wst[b2][:, k, :], start=(k == 0), stop=False),
                                   reads=[r_cbt, r_wst[b2]], writes=[rPB[pb]])
                            op("pe", lambda e: e.matmul(PB[pb][:], lhsT=onesf[0:1, :], rhs=bst[b2][0:1, :], start=False, stop=True),
                               reads=[r_onesf, r_bst[b2]], writes=[rPB[pb]])
                            op("dve", lambda e: e.tensor_scalar(out=gaB_all[:, ss, gi, half * 512:(half + 1) * 512], in0=PB[pb][:], scalar1=1.0, scalar2=None, op0=ALU.add),
                               reads=[rPB[pb]], writes=[r_gaB[gi]])
                    else:
                        widx = {0: 0, 1: 1, 3: 2, 4: 3}[which]
                        addc = 1.0 if which in (1, 4) else 0.0
                        pt = 6 + b2
                        for i in range(4):
                            for k in range(8):
                                op("pe", lambda e, k=k, i=i: e.matmul(PB[pt][:, i * NSEQ:(i + 1) * NSEQ], lhsT=wst[b2][:, k, i * 128:(i + 1) * 128], rhs=cT2[:, k, :], start=(k == 0), stop=(k == 7)),
                                   reads=[r_cT2, r_wst[b2]], writes=[rPB[pt]])
                        op("dve", lambda e: e.tensor_tensor(out=modsT_all[:, :, widx, half * 4:(half + 1) * 4], in0=PB[pt][:, 0:4 * NSEQ].rearrange("p (i s) -> p s i", s=NSEQ),
                                                            in1=bcol[:, cg * 4:(cg + 1) * 4].unsqueeze(1).broadcast_to([128, NSEQ, 4]), op=ALU.add),
                           reads=[rPB[pt], r_bcol], writes=[r_modsT])
                        if addc != 0.0:
                            op("dve", lambda e: e.tensor_scalar(out=modsT_all[:, :, widx, half * 4:(half + 1) * 4], in0=modsT_all[:, :, widx, half * 4:(half + 1) * 4],
                                                                scalar1=addc, scalar2=None, op0=ALU.add), reads=[r_modsT], writes=[r_modsT])
                if dbg:
                    dma("sp", ds_dbg, dbg_d["mods"], modsT_all[:, 0], reads=[r_modsT])
              T.barrier()

            with ExitStack() as pab:
                oT = SB(pab, "oT", [128, 4, S], F32)
                r_oT = [[Res() for _ in range(NCH)] for _ in range(4)]
                kTt = SB(pab, "kTt", [128, S], BF16); r_kT = [Res() for _ in range(NG)]
                vtk = SB(pab, "vtk", [128, NT, 128], BF16); r_v = [Res() for _ in range(NT)]
                cosT = SB(pab, "cosT", [128, S], F32); sinT = SB(pab, "sinT", [128, S], F32); r_cs = Res()
                with ExitStack() as pr:
                    posi = SB(pr, "posi", [128, S], I32); r_posi = Res()
                    ang = SB(pr, "ang", [128, S], F32); r_ang = Res()
                    kq = SB(pr, "kq", [128, S], F32); r_kq = Res()
                    ki = SB(pr, "ki", [128, S], I32); r_ki = Res()
                    dma("sp", ds_misc[0], posi[:], pos_d[s:s + 1, :].broadcast_to([128, S]), writes=[r_posi])
                    op("dve", lambda e: e.tensor_copy(out=ang[:], in_=posi[:]), reads=[r_posi], writes=[r_ang])
                    op("dve", lambda e: e.tensor_scalar(out=ang[:], in0=ang[:], scalar1=rope[:, 0:1], scalar2=None, op0=ALU.mult),
                       reads=[r_ang, r_rope], writes=[r_ang])
                    TWO_PI = 6.283185307179586
                    C1 = 6.28125
                    C2 = TWO_PI - C1
                    for (dst, shift) in ((sinT, 0.0), (cosT, np.pi / 2)):
                        op("dve", lambda e, shift=shift: e.tensor_scalar(out=kq[:], in0=ang[:], scalar1=float(shift), scalar2=1.0 / TWO_PI, op0=ALU.add, op1=ALU.mult),
                           reads=[r_ang], writes=[r_kq])
                        op("dve", lambda e: e.tensor_copy(out=ki[:], in_=kq[:]), reads=[r_kq], writes=[r_ki])
                        op("dve", lambda e: e.tensor_copy(out=kq[:], in_=ki[:]), reads=[r_ki], writes=[r_kq])
                        op("dve", lambda e, shift=shift, dst=dst: e.tensor_scalar(out=dst[:], in0=ang[:], scalar1=float(shift), scalar2=None, op0=ALU.add),
                           reads=[r_ang], writes=[r_cs])
                        op("dve", lambda e, dst=dst: e.scalar_tensor_tensor(out=dst[:], in0=kq[:], scalar=-C1, in1=dst[:], op0=ALU.mult, op1=ALU.add),
                           reads=[r_kq, r_cs], writes=[r_cs])
                        op("dve", lambda e, dst=dst: e.scalar_tensor_tensor(out=dst[:], in0=kq[:], scalar=-C2, in1=dst[:], op0=ALU.mult, op1=ALU.add),
                           reads=[r_kq, r_cs], writes=[r_cs])
                        op("dve", lambda e, dst=dst: e.tensor_scalar(out=kq[:], in0=dst[:], scalar1=float(np.pi), scalar2=-TWO_PI, op0=ALU.is_gt, op1=ALU.mult),
                           reads=[r_cs], writes=[r_kq])
                        op("dve", lambda e, dst=dst: e.tensor_tensor(out=dst[:], in0=dst[:], in1=kq[:], op=ALU.add), reads=[r_cs, r_kq], writes=[r_cs])
                        op("dve", lambda e, dst=dst: e.tensor_scalar(out=kq[:], in0=dst[:], scalar1=float(-np.pi), scalar2=TWO_PI, op0=ALU.is_lt, op1=ALU.mult),
                           reads=[r_cs], writes=[r_kq])
                        op("dve", lambda e, dst=dst: e.tensor_tensor(out=dst[:], in0=dst[:], in1=kq[:], op=ALU.add), reads=[r_cs, r_kq], writes=[r_cs])
                        op("dve", lambda e, dst=dst: e.tensor_scalar(out=dst[:], in0=dst[:], scalar1=3.1415925, scalar2=-3.1415925, op0=ALU.min, op1=ALU.max),
                           reads=[r_cs], writes=[r_cs])
                        op("act", lambda e, dst=dst: e.activation(out=dst[:], in_=dst[:], func=AF.Sin), reads=[r_cs], writes=[r_cs])
                    op("dve", lambda e: e.tensor_scalar(out=sinT[:], in0=sinT[:], scalar1=rope[:, 1:2], scalar2=None, op0=ALU.mult),
                       reads=[r_cs, r_rope], writes=[r_cs])
                T.barrier()

                with ExitStack() as pw:
                    wA_l = [SB(pw, "wA0", [128, 8, 1408], BF16), SB(pw, "wA1", [128, 8, 1024], BF16)]
                    r_wA_l = [Res(), Res()]
                    for hp_ in range(2):
                        for seg, base in enumerate((O_RI, O_RQ, O_RFF, O_RFB)):
                            sg_ = (seg + 3) % 4
                            dma("pool", ds_wc[hp_], wA_l[hp_][:, :, sg_ * 256:(sg_ + 1) * 256], w_in_k[:, :, base + hp_ * 256: base + hp_ * 256 + 256], writes=[r_wA_l[hp_]])
                        if hp_ == 0:
                            dma("pool", ds_wc[0], wA_l[0][:, :, 1024:1152], w_in_k[:, :, O_AK:O_AK + 128], writes=[r_wA_l[0]])
                            for kv in range(2):
                                dma("pool", ds_wc[0], wA_l[0][:, :, 1152 + kv * 64:1152 + kv * 64 + 32], w_in_k[:, :, O_AK + kv * 64 + 32:O_AK + kv * 64 + 64], writes=[r_wA_l[0]])
                                dma("pool", ds_wc[0], wA_l[0][:, :, 1152 + kv * 64 + 32:1152 + kv * 64 + 64], w_in_k[:, :, O_AK + kv * 64:O_AK + kv * 64 + 32], writes=[r_wA_l[0]])
                            dma("pool", ds_wc[0], wA_l[0][:, :, 1280:1408], w_in_k[:, :, O_AV:O_AV + 128], writes=[r_wA_l[0]])
                    for hp in range(2):
                        with ExitStack() as pa:
                            wA = wA_l[hp]; r_wA = r_wA_l[hp]
                            QT = [SB(pa, "QT%d" % d, [128, 2, S], BF16) for d in range(2)]
                            KT = [SB(pa, "KT%d" % d, [128, 2, S], BF16) for d in range(2)]
                            r_QK = [[[Res() for _ in range(NG)] for _ in range(2)] for _ in range(2)]
                            Vt = SB(pa, "Vt", [64, NCH, 256], BF16); r_V = [Res() for _ in range(NG)]
                            Bref = SB(pa, "Bref", [128, 2, 2, NCH], F32); Bend = SB(pa, "Bend", [128, 2, 2, NCH], F32); r_B = Res()
                            Gx = SB(pa, "Gx", [128, 2, 2, NCH], F32); r_G = Res()
                            with ExitStack() as pa2:
                                xs = [SB(pa2, "xs%d" % i, [128, D], F32) for i in range(2)]; r_xs = [Res() for _ in range(2)]
                                uT = [SB(pa2, "uT%d" % i, [128, 8, 512], BF16) for i in range(1)]; r_uT = [Res()]
                                qs = [SB(pa2, "qs%d" % i, [128, 512], F32) for i in range(2)]; r_qs = [Res(), Res()]
                                NTMP = 2
                                tnames = ("sig", "lg", "Bf", "Eq", "Ek")
                                tmps = [{n: SB(pa2, "t_%s%d" % (n, i), [128, 512], F32) for n in tnames} for i in range(NTMP)]
                                r_tmps = [{n: Res() for n in tnames} for i in range(NTMP)]
                                for i in range(NTMP):
                                    tmps[i]["kk"] = tmps[i]["sig"]; r_tmps[i]["kk"] = r_tmps[i]["sig"]
                                    tmps[i]["dd"] = tmps[i]["lg"]; r_tmps[i]["dd"] = r_tmps[i]["lg"]
                                tctr = 0
                                bank_rot = Rot([2, 3, 4, 5, 6, 7])
                                for g in range(NG):
                                    ub = 0
                                    for t in range(4):
                                        tile_i = g * 4 + t
                                        sl = tile_i % 2
                                        dma("sp", ds_x[sl], xs[sl][:], x_d[s, tile_i * 128:(tile_i + 1) * 128, :], writes=[r_xs[sl]])
                                        make_uT(xs[sl], r_xs[sl], uT[ub][:, :, t * 128:(t + 1) * 128], r_uT[ub], 1, 0)
                                    gs = slice(g * 512, (g + 1) * 512)
                                    for cp in range(4):
                                        pb = bank_rot.next()
                                        for cc in range(2):
                                            ch = cp * 2 + cc
                                            for k in range(8):
                                                op("pe", lambda e, k=k, cc=cc, ch=ch, pb=pb: e.matmul(PB[pb][0:64, cc * 256:(cc + 1) * 256], lhsT=uT[ub][:, k, ch * 64:(ch + 1) * 64],
                                                                                                       rhs=wA[:, k, 768:1024], start=(k == 0), stop=(k == 7)),
                                                   reads=[r_uT[ub], r_wA], writes=[rPB[pb]])
                                        op("act", lambda e, pb=pb, cp=cp: e.activation(out=Vt[:, g * 8 + cp * 2:g * 8 + cp * 2 + 2, :],
                                                                                        in_=PB[pb][0:64, :].rearrange("p (c n) -> p c n", n=256), func=AF.Copy, scale=128.0 ** -0.5),
                                           reads=[rPB[pb]], writes=[r_V[g]])
                                    for hl in range(2):
                                        h = hp * 2 + hl
                                        pb = bank_rot.next()
                                        for k in range(8):
                                            op("pe", lambda e, k=k, pb=pb, hl=hl: e.matmul(PB[pb][:], lhsT=wA[:, k, hl * 128:(hl + 1) * 128], rhs=uT[ub][:, k, :],
                                                                                             start=(k == 0), stop=(k == 7)), reads=[r_uT[ub], r_wA], writes=[rPB[pb]])
                                        op("act", lambda e, pb=pb, hl=hl: e.activation(out=qs[hl][:], in_=PB[pb][:], func=AF.Silu), reads=[rPB[pb]], writes=[r_qs[hl]])
                                        for d in range(2):
                                            tm = tmps[tctr % NTMP]; rt = r_tmps[tctr % NTMP]; tctr += 1
                                            pb = bank_rot.next()
                                            cb = 256 * (1 + d) + hl * 128
                                            for k in range(8):
                                                op("pe", lambda e, k=k, pb=pb, cb=cb: e.matmul(PB[pb][:], lhsT=wA[:, k, cb:cb + 128], rhs=uT[ub][:, k, :],
                                                                                                 start=(k == 0), stop=(k == 7)), reads=[r_uT[ub], r_wA], writes=[rPB[pb]])
                                            lb_c = lbt[:, 0, d, h:h + 1]; om_c = lbt[:, 1, d, h:h + 1]; nom_c = lbt[:, 2, d, h:h + 1]
                                            op("act", lambda e, pb=pb, tm=tm: e.activation(out=tm["sig"][:], in_=PB[pb][:], func=AF.Sigmoid), reads=[rPB[pb]], writes=[rt["sig"]])
                                            op("act", lambda e, tm=tm: e.activation(out=tm["lg"][:], in_=tm["sig"][:], func=AF.Ln, scale=om_c, bias=lb_c),
                                               reads=[rt["sig"], r_lbt], writes=[rt["lg"]])
                                            op("dve", lambda e, tm=tm: e.tensor_scalar(out=tm["kk"][:], in0=tm["sig"][:], scalar1=nom_c, scalar2=om_c, op0=ALU.mult, op1=ALU.add),
                                               reads=[rt["sig"], r_lbt], writes=[rt["kk"]])
                                            op("dve", lambda e, tm=tm: e.tensor_tensor_scan(out=tm["Bf"][:], data0=rmask[:], data1=tm["lg"][:], initial=0.0, op0=ALU.mult, op1=ALU.add),
                                               reads=[r_rmask, rt["lg"]], writes=[rt["Bf"]])
                                            B3 = tm["Bf"][:].rearrange("p (c t) -> p c t", t=64)
                                            if d == 1:
                                                lg3 = tm["lg"][:].rearrange("p (c t) -> p c t", t=64)
                                                op("dve", lambda e, B3=B3, lg3=lg3: e.tensor_tensor(out=lg3, in0=lg3, in1=B3, op=ALU.subtract), reads=[rt["lg"], rt["Bf"]], writes=[rt["lg"]])
                                                op("dve", lambda e, B3=B3, lg3=lg3: e.tensor_tensor(out=B3, in0=lg3, in1=B3[:, :, 63:64].broadcast_to([128, 8, 64]), op=ALU.add),
                                                   reads=[rt["lg"], rt["Bf"]], writes=[rt["Bf"]])
                                                iref, iend = 32, 0
                                            else:
                                                iref, iend = 31, 63
                                            op("act", lambda e, B3=B3, iref=iref: e.activation(out=Bref[:, d, hl, g * 8:(g + 1) * 8], in_=B3[:, :, iref], func=AF.Copy), reads=[rt["Bf"]], writes=[r_B])
                                            op("act", lambda e, B3=B3, iend=iend: e.activation(out=Bend[:, d, hl, g * 8:(g + 1) * 8], in_=B3[:, :, iend], func=AF.Copy), reads=[rt["Bf"]], writes=[r_B])
                                            op("dve", lambda e, B3=B3, tm=tm, iref=iref: e.tensor_tensor(out=tm["dd"][:].rearrange("p (c t) -> p c t", t=64), in0=B3,
                                                                                                           in1=B3[:, :, iref:iref + 1].broadcast_to([128, 8, 64]), op=ALU.subtract),
                                               reads=[rt["Bf"]], writes=[rt["dd"]])
                                            op("act", lambda e, tm=tm: e.activation(out=tm["Eq"][:], in_=tm["dd"][:], func=AF.Exp), reads=[rt["dd"]], writes=[rt["Eq"]])
                                            op("act", lambda e, tm=tm: e.activation(out=tm["Ek"][:], in_=tm["dd"][:], func=AF.Exp, scale=-1.0), reads=[rt["dd"]], writes=[rt["Ek"]])
                                            op("dve", lambda e, tm=tm, d=d, hl=hl: e.tensor_tensor(out=KT[d][:, hl, gs], in0=tm["kk"][:], in1=tm["Ek"][:], op=ALU.mult),
                                               reads=[rt["kk"], rt["Ek"]], writes=[r_QK[d][hl][g]])
                                            op("dve", lambda e, tm=tm, d=d, hl=hl: e.tensor_tensor(out=QT[d][:, hl, gs], in0=qs[hl][:], in1=tm["Eq"][:], op=ALU.mult),
                                               reads=[r_qs[hl], rt["Eq"]], writes=[r_QK[d][hl][g]])
                                    if hp == 0:
                                        pb1 = bank_rot.next()
                                        pb2 = bank_rot.next()
                                        for (pbx, cb) in ((pb1, 1024), (pb2, 1152)):
                                            for k in range(8):
                                                op("pe", lambda e, k=k, pbx=pbx, cb=cb: e.matmul(PB[pbx][:], lhsT=wA[:, k, cb:cb + 128], rhs=uT[ub][:, k, :], start=(k == 0), stop=(k == 7)),
                                                   reads=[r_uT[ub], r_wA], writes=[rPB[pbx]])
                                        tm = tmps[tctr % NTMP]; rt = r_tmps[tctr % NTMP]; tctr += 1
                                        op("dve", lambda e, tm=tm: e.tensor_tensor(out=tm["sig"][:], in0=PB[pb1][:], in1=cosT[:, gs], op=ALU.mult), reads=[rPB[pb1], r_cs], writes=[rt["sig"]])
                                        op("dve", lambda e, tm=tm: e.tensor_tensor(out=tm["lg"][:], in0=PB[pb2][:], in1=sinT[:, gs], op=ALU.mult), reads=[rPB[pb2], r_cs], writes=[rt["lg"]])
                                        op("dve", lambda e, tm=tm: e.tensor_tensor(out=kTt[:, gs], in0=tm["sig"][:], in1=tm["lg"][:], op=ALU.add), reads=[rt["sig"], rt["lg"]], writes=[r_kT[g]])
                                        for t in range(4):
                                            pb = bank_rot.next()
                                            for k in range(8):
                                                op("pe", lambda e, k=k, pb=pb, t=t: e.matmul(PB[pb][:, 0:128], lhsT=uT[ub][:, k, t * 128:(t + 1) * 128], rhs=wA[:, k, 1280:1408],
                                                                                              start=(k == 0), stop=(k == 7)), reads=[r_uT[ub], r_wA], writes=[rPB[pb]])
                                            op("act", lambda e, pb=pb, t=t: e.activation(out=vtk[:, g * 4 + t, :], in_=PB[pb][:, 0:128], func=AF.Copy), reads=[rPB[pb]], writes=[r_v[g * 4 + t]])
                                op("dve", lambda e: e.tensor_tensor(out=Gx[:], in0=Bend[:], in1=Bref[:], op=ALU.subtract), reads=[r_B], writes=[r_G])
                                if NCH > 1:
                                    op("dve", lambda e: e.tensor_tensor(out=Gx[:, 0, :, 0:NCH - 1], in0=Gx[:, 0, :, 0:NCH - 1], in1=Bref[:, 0, :, 1:NCH], op=ALU.add), reads=[r_B, r_G], writes=[r_G])
                                    op("dve", lambda e: e.tensor_tensor(out=Gx[:, 1, :, 1:NCH], in0=Gx[:, 1, :, 1:NCH], in1=Bref[:, 1, :, 0:NCH - 1], op=ALU.add), reads=[r_B, r_G], writes=[r_G])
                                op("act", lambda e: e.activation(out=Gx[:], in_=Gx[:], func=AF.Exp), reads=[r_G], writes=[r_G])
                                if dbg and s == 0 and hp == 0:
                                    for d in range(2):
                                        dma("sp", ds_dbg, dbg_d["QT"][d], QT[d][:], reads=[r_QK[d][hl][g] for hl in range(2) for g in range(NG)])
                                        dma("sp", ds_dbg, dbg_d["KT"][d], KT[d][:], reads=[r_QK[d][hl][g] for hl in range(2) for g in range(NG)])
                                    dma("sp", ds_dbg, dbg_d["V"], Vt[:], reads=r_V)
                                    dma("sp", ds_dbg, dbg_d["kT"], kTt[:], reads=r_kT)
                            T.barrier()

                            with ExitStack() as pb_:
                                Sm32 = SB(pb_, "Sm32", [128, 4, 128], F32); Smb = SB(pb_, "Smb", [128, 4, 128], BF16); tmp32 = SB(pb_, "tmp32", [128, 4, 128], F32)
                                r_Sm32 = [Res() for _ in range(4)]; r_Smb = [Res() for _ in range(4)]; r_tmp32 = [Res() for _ in range(4)]
                                scm = SB(pb_, "scm", [64, 2, 4, 64], BF16); r_scm = [[Res() for _ in range(4)] for _ in range(2)]
                                ktok = SB(pb_, "ktok", [64, 2, 4, 128], BF16); r_ktok = [[Res() for _ in range(4)] for _ in range(2)]
                                rB = [[None] * 4 for _ in range(2)]
                                for _p in range(2):
                                    for _c in range(4):
                                        _r = Res()
                                        rB[_p][_c] = {n: _r for n in ("sc", "kt", "o", "dl")}
                                op("pool", lambda e: e.memset(Sm32[:], 0.0), writes=r_Sm32)
                                op("pool", lambda e: e.memset(Smb[:], 0.0), writes=r_Smb)
                                allQK = lambda d, hl: [r_QK[d][hl][g] for g in range(NG)]

                                def chunk_of(j, d):
                                    return j if d == 0 else NCH - 1 - j

                                def stage1(j):
                                    par = j % 2
                                    for ci in range(4):
                                        hl, d = ci // 2, ci % 2
                                        c = chunk_of(j, d); tk = slice(c * 64, (c + 1) * 64)
                                        bk = ci + 4 * par
                                        op("pe", lambda e, bk=bk, d=d, hl=hl, tk=tk: e.matmul(PB[bk][0:64, 0:64], lhsT=KT[d][:, hl, tk], rhs=QT[d][:, hl, tk], start=True, stop=True),
                                           reads=allQK(d, hl), writes=[rB[par][ci]["sc"]])
                                        op("pe", lambda e, bk=bk, d=d, hl=hl, tk=tk: e.transpose(out=PB[bk][:].bitcast(BF16)[0:64, 128:256], in_=KT[d][:, hl, tk], identity=identb[:]),
                                           reads=allQK(d, hl) + [r_identb], writes=[rB[par][ci]["kt"]])

                                def stage2(j):
                                    par = j % 2
                                    for ci in range(4):
                                        hl, d = ci // 2, ci % 2
                                        bk = ci + 4 * par
                                        mk = maskf if d == 0 else maskb
                                        op("pool", lambda e, ci=ci: e.memset(scm[:, par, ci, :], 0.0), writes=[r_scm[par][ci]])
                                        op("dve", lambda e, bk=bk, ci=ci, mk=mk: e.copy_predicated(out=scm[:, par, ci, :], mask=mk[:], data=PB[bk][0:64, 0:64]),
                                           reads=[rB[par][ci]["sc"], r_mask], writes=[r_scm[par][ci]])
                                        op("act", lambda e, bk=bk, ci=ci: e.activation(out=ktok[:, par, ci, :], in_=PB[bk][:].bitcast(BF16)[0:64, 128:256], func=AF.Copy),
                                           reads=[rB[par][ci]["kt"]], writes=[r_ktok[par][ci]])

                                def stage3(j):
                                    par = j % 2
                                    for ci in range(4):
                                        hl, d = ci // 2, ci % 2
                                        c = chunk_of(j, d); tk = slice(c * 64, (c + 1) * 64)
                                        bk = ci + 4 * par
                                        vv = Vt[:, c, hl * 128:(hl + 1) * 128]
                                        op("pe", lambda e, bk=bk, vv=vv, ci=ci: e.matmul(PB[bk][:, 128:192], lhsT=vv, rhs=scm[:, par, ci, :], start=True, stop=False),
                                           reads=[r_V[c // 8], r_scm[par][ci]], writes=[rB[par][ci]["o"]])
                                        op("pe", lambda e, bk=bk, ci=ci, d=d, hl=hl, tk=tk: e.matmul(PB[bk][:, 128:192], lhsT=Smb[:, ci, :], rhs=QT[d][:, hl, tk], start=False, stop=True),
                                           reads=[r_Smb[ci]] + allQK(d, hl), writes=[rB[par][ci]["o"]])
                                        op("pe", lambda e, bk=bk, vv=vv, ci=ci: e.matmul(PB[bk][:, 192:320], lhsT=ktok[:, par, ci, :], rhs=vv, start=True, stop=True),
                                           reads=[r_ktok[par][ci], r_V[c // 8]], writes=[rB[par][ci]["dl"]])

                                def stage4(j):
                                    par = j % 2
                                    for ci in range(4):
                                        hl, d = ci // 2, ci % 2
                                        h = hp * 2 + hl
                                        c = chunk_of(j, d); tk = slice(c * 64, (c + 1) * 64)
                                        bk = ci + 4 * par
                                        if j < NCH // 2:
                                            op("act", lambda e, bk=bk, h=h, tk=tk: e.activation(out=oT[:, h, tk], in_=PB[bk][:, 128:192], func=AF.Copy),
                                               reads=[rB[par][ci]["o"]], writes=[r_oT[h][c]])
                                        else:
                                            op("dve", lambda e, bk=bk, h=h, tk=tk: e.tensor_tensor(out=oT[:, h, tk], in0=PB[bk][:, 128:192], in1=oT[:, h, tk], op=ALU.add),
                                               reads=[rB[par][ci]["o"], r_oT[h][c]], writes=[r_oT[h][c]])
                                        if j < NCH - 1:
                                            gcol = Gx[:, d, hl, c:c + 1]
                                            op("dve", lambda e, bk=bk, ci=ci: e.tensor_tensor(out=tmp32[:, ci, :], in0=PB[bk][:, 192:320], in1=Sm32[:, ci, :], op=ALU.add),
                                               reads=[rB[par][ci]["dl"], r_Sm32[ci]], writes=[r_tmp32[ci]])
                                            op("act", lambda e, ci=ci, gcol=gcol: e.activation(out=Sm32[:, ci, :], in_=tmp32[:, ci, :], func=AF.Identity, scale=gcol),
                                               reads=[r_tmp32[ci], r_G], writes=[r_Sm32[ci]])
                                            op("pool", lambda e, ci=ci, gcol=gcol: e.tensor_scalar(out=Smb[:, ci, :], in0=tmp32[:, ci, :], scalar1=gcol, scalar2=None, op0=ALU.mult),
                                               reads=[r_tmp32[ci], r_G], writes=[r_Smb[ci]])

                                stage1(0)
                                stage2(0)
                                for j in range(NCH):
                                    if j + 1 < NCH:
                                        stage1(j + 1)
                                    stage3(j)
                                    if j + 1 < NCH:
                                        stage2(j + 1)
                                    stage4(j)
                            T.barrier()
                if dbg and s == 0:
                    dma("sp", ds_dbg, dbg_d["oT"], oT[:], reads=[r for rr in r_oT for r in rr])

                ogA = SB(pab, "ogA", [128, 4, S], BF16); atA = SB(pab, "atA", [128, 4, S], BF16)
                r_ogA = [Res() for _ in range(NG)]; r_atA = [Res() for _ in range(NT)]
                with ExitStack() as pc:
                    wC = SB(pc, "wC", [128, 8, 1536], BF16); r_wC = Res()
                    dma("pool", ds_w, wC[:, :, 0:512], w_in_k[:, :, O_RG:O_RG + 512], writes=[r_wC])
                    for m in range(4):
                        for hf in range(2):
                            hq = m + 4 * hf
                            cdst = 512 + m * 128 + hf * 64
                            dma("pool", ds_w, wC[:, :, cdst:cdst + 64], w_in_k[:, :, O_AQ + hq * 64:O_AQ + hq * 64 + 64], writes=[r_wC])
                            dma("pool", ds_w, wC[:, :, 512 + cdst:512 + cdst + 32], w_in_k[:, :, O_AQ + hq * 64 + 32:O_AQ + hq * 64 + 64], writes=[r_wC])
                            dma("pool", ds_w, wC[:, :, 512 + cdst + 32:512 + cdst + 64], w_in_k[:, :, O_AQ + hq * 64:O_AQ + hq * 64 + 32], writes=[r_wC])
                    xs = [SB(pc, "cxs%d" % i, [128, D], F32) for i in range(2)]; r_xs = [Res() for _ in range(2)]
                    uT = SB(pc, "cuT", [128, 8, 512], BF16); r_uT = Res()
                    qT = SB(pc, "qT", [128, 4, 512], BF16); r_qT = Res()
                    ta = [SB(pc, "cta%d" % i, [128, 512], F32) for i in range(4)]; r_ta = [Res() for _ in range(4)]
                    msc = [SB(pc, "msc%d" % i, [128, 384], F32) for i in range(2)]; r_msc = [Res(), Res()]
                    pex = [SB(pc, "pex%d" % i, [128, 384], BF16) for i in range(2)]; r_pex = [Res(), Res()]
                    pT = [SB(pc, "pT%d" % i, [128, 3, 128], BF16) for i in range(2)]; r_pT = [Res(), Res()]
                    sm = [SB(pc, "smx%d" % i, [128, 8], F32) for i in range(2)]; r_sm = [Res(), Res()]
                    sm2 = [SB(pc, "smy%d" % i, [128, 8], F32) for i in range(2)]; r_sm2 = [Res(), Res()]
                    ao = [SB(pc, "ao%d" % i, [128, 512], BF16) for i in range(2)]; r_ao = [Res(), Res()]
                    hctr = 0
                    for g in range(NG):
                        gs = slice(g * 512, (g + 1) * 512)
                        for t in range(4):
                            tile_i = g * 4 + t
                            sl = tile_i % 2
                            dma("sp", ds_x[sl], xs[sl][:], x_d[s, tile_i * 128:(tile_i + 1) * 128, :], writes=[r_xs[sl]])
                            make_uT(xs[sl], r_xs[sl], uT[:, :, t * 128:(t + 1) * 128], r_uT, 1, 0)
                        for m in range(4):
                            for (pbx, cb) in ((2, 512 + m * 128), (3, 1024 + m * 128)):
                                for k in range(8):
                                    op("pe", lambda e, k=k, pbx=pbx, cb=cb: e.matmul(PB[pbx][:], lhsT=wC[:, k, cb:cb + 128], rhs=uT[:, k, :], start=(k == 0), stop=(k == 7)),
                                       reads=[r_uT, r_wC], writes=[rPB[pbx]])
                            op("dve", lambda e: e.tensor_tensor(out=ta[0][:], in0=PB[2][:], in1=cosT[:, gs], op=ALU.mult), reads=[rPB[2], r_cs], writes=[r_ta[0]])
                            op("dve", lambda e: e.tensor_tensor(out=ta[1][:], in0=PB[3][:], in1=sinT[:, gs], op=ALU.mult), reads=[rPB[3], r_cs], writes=[r_ta[1]])
                            op("dve", lambda e, m=m: e.tensor_tensor(out=qT[:, m, :], in0=ta[0][:], in1=ta[1][:], op=ALU.add), reads=[r_ta[0], r_ta[1]], writes=[r_qT])
                        for h in range(4):
                            for k in range(8):
                                op("pe", lambda e, k=k, h=h: e.matmul(PB[4][:], lhsT=wC[:, k, h * 128:(h + 1) * 128], rhs=uT[:, k, :], start=(k == 0), stop=(k == 7)),
                                   reads=[r_uT, r_wC], writes=[rPB[4]])
                            op("act", lambda e: e.activation(out=ta[0][:], in_=PB[4][:], func=AF.Silu), reads=[rPB[4]], writes=[r_ta[0]])
                            rr = [r_oT[h][c] for c in range(g * 8, (g + 1) * 8)]
                            op("act", lambda e, h=h: e.activation(out=ta[2][:], in_=oT[:, h, gs], func=AF.Square), reads=rr, writes=[r_ta[2]])
                            op("pe", lambda e: e.matmul(PB[5][:], lhsT=onesf[:], rhs=ta[2][:], start=True, stop=True), reads=[r_onesf, r_ta[2]], writes=[rPB[5]])
                            op("act", lambda e: e.activation(out=ta[3][:], in_=PB[5][:], func=AF.Ln, scale=1.0 / 128.0, bias=epsc[:, 1:2]), reads=[rPB[5], r_eps], writes=[r_ta[3]])
                            op("act", lambda e: e.activation(out=ta[3][:], in_=ta[3][:], func=AF.Exp, scale=-0.5), reads=[r_ta[3]], writes=[r_ta[3]])
                            op("dve", lambda e, h=h: e.tensor_tensor(out=ta[2][:], in0=oT[:, h, gs], in1=ta[3][:], op=ALU.mult), reads=rr + [r_ta[3]], writes=[r_ta[2]])
                            op("dve", lambda e, h=h: e.scalar_tensor_tensor(out=ogA[:, h, gs], in0=ta[0][:], scalar=ngc[:, 0:1], in1=ta[2][:], op0=ALU.mult, op1=ALU.mult),
                               reads=[r_ta[0], r_ngc, r_ta[2]], writes=[r_ogA[g]])
                        def att_geom(t):
                            j = g * 4 + t
                            k0 = max(0, j - 1); k1 = min(NT, j + 2)
                            return j, k0, k1, k1 - k0, (k1 - k0) * 128, (128 if j == 0 else 0)

                        def att_s1_pe(idx):
                            t, h = idx // 8, idx % 8
                            j, k0, k1, nkb, nk, mc0 = att_geom(t)
                            m, hf = h % 4, h // 4
                            ps = slice(hf * 64, (hf + 1) * 64)
                            ib = idx % 2
                            pbs = 2 + ib
                            kgs = [r_kT[gg] for gg in range((k0 * 128) // 512, ((k1 * 128) - 1) // 512 + 1)]
                            op("pe", lambda e: e.matmul(PB[pbs][:, 0:nk], lhsT=qT[ps, m, t * 128:(t + 1) * 128], rhs=kTt[ps, k0 * 128:k1 * 128], start=True, stop=True),
                               reads=[r_qT] + kgs, writes=[rPB[pbs]])

                        def att_s1_ew(idx):
                            t, h = idx // 8, idx % 8
                            j, k0, k1, nkb, nk, mc0 = att_geom(t)
                            ib = idx % 2
                            pbs = 2 + ib
                            op("dve", lambda e: e.tensor_tensor(out=msc[ib][:, 0:nk], in0=PB[pbs][:, 0:nk], in1=amask[:, mc0:mc0 + nk], op=ALU.add),
                               reads=[rPB[pbs], r_amask], writes=[r_msc[ib]])
                            op("dve", lambda e: e.reduce_max(out=sm[ib][:, 0:1], in_=msc[ib][:, 0:nk], axis=AX.X), reads=[r_msc[ib]], writes=[r_sm[ib]])
                            op("dve", lambda e: e.tensor_scalar(out=sm[ib][:, 1:2], in0=sm[ib][:, 0:1], scalar1=-0.125, scalar2=sinkb[:, 1, h:h + 1], op0=ALU.mult, op1=ALU.min),
                               reads=[r_sm[ib], r_sink], writes=[r_sm[ib]])
                            op("act", lambda e: e.activation(out=pex[ib][:, 0:nk], in_=msc[ib][:, 0:nk], func=AF.Exp, scale=0.125, bias=sm[ib][:, 1:2], accum_out=sm2[ib][:, 0:1]),
                               reads=[r_msc[ib], r_sm[ib]], writes=[r_pex[ib], r_sm2[ib]])
                            op("act", lambda e: e.activation(out=sm2[ib][:, 1:2], in_=sinkb[:, 0, h:h + 1], func=AF.Exp, bias=sm[ib][:, 1:2], scale=1.0),
                               reads=[r_sm[ib], r_sink], writes=[r_sm2[ib]])

                        def att_s2a(idx):
                            t, h = idx // 8, idx % 8
                            j, k0, k1, nkb, nk, mc0 = att_geom(t)
                            ib = idx % 2
                            pbt = 4 + ib
                            for b in range(nkb):
                                op("pe", lambda e, b=b: e.transpose(out=PB[pbt][:].bitcast(BF16)[:, b * 128:(b + 1) * 128], in_=pex[ib][:, b * 128:(b + 1) * 128], identity=identb[:]),
                                   reads=[r_pex[ib], r_identb], writes=[rPB[pbt]])
                            if ib == 0:
                                op("act", lambda e: e.activation(out=pT[ib][:, 0:nkb, :], in_=PB[pbt][:].bitcast(BF16)[:, 0:nk].rearrange("p (b q) -> p b q", q=128), func=AF.Copy),
                                   reads=[rPB[pbt]], writes=[r_pT[ib]])
                            else:
                                op("dve", lambda e: e.tensor_copy(out=pT[ib][:, 0:nkb, :], in_=PB[pbt][:].bitcast(BF16)[:, 0:nk].rearrange("p (b q) -> p b q", q=128)),
                                   reads=[rPB[pbt]], writes=[r_pT[ib]])

                        def att_s2b(idx):
                            t, h = idx // 8, idx % 8
                            j, k0, k1, nkb, nk, mc0 = att_geom(t)
                            hf = h // 4
                            ib = idx % 2
                            op("dve", lambda e: e.tensor_tensor(out=sm2[ib][:, 2:3], in0=sm2[ib][:, 0:1], in1=sm2[ib][:, 1:2], op=ALU.add), reads=[r_sm2[ib]], writes=[r_sm2[ib]])
                            op("dve", lambda e: e.reciprocal(out=sm2[ib][:, 3:4], in_=sm2[ib][:, 2:3]), reads=[r_sm2[ib]], writes=[r_sm2[ib]])
                            pbo = 6 + ib
                            for b in range(nkb):
                                op("pe", lambda e, b=b: e.matmul(PB[pbo][:, 0:64], lhsT=pT[ib][:, b, :], rhs=vtk[:, k0 + b, hf * 64:(hf + 1) * 64], start=(b == 0), stop=(b == nkb - 1)),
                                   reads=[r_pT[ib], r_v[k0 + b]], writes=[rPB[pbo]])
                            aot = ao[t % 2]; r_aot = r_ao[t % 2]
                            op("act", lambda e: e.activation(out=aot[:, h * 64:(h + 1) * 64], in_=PB[pbo][:, 0:64], func=AF.Identity, scale=sm2[ib][:, 3:4]),
                               reads=[rPB[pbo], r_sm2[ib]], writes=[r_aot])
                            if h == 7:
                                pba = t % 2
                                for m in range(4):
                                    op("pe", lambda e, m=m: e.transpose(out=PB[pba][:].bitcast(BF16)[:, m * 128:(m + 1) * 128], in_=aot[:, m * 128:(m + 1) * 128], identity=identb[:]),
                                       reads=[r_aot, r_identb], writes=[rPB[pba]])
                                op("dve", lambda e: e.tensor_copy(out=atA[:, :, j * 128:(j + 1) * 128], in_=PB[pba][:].bitcast(BF16)[:, 0:512].rearrange("p (m q) -> p m q", q=128)),
                                   reads=[rPB[pba]], writes=[r_atA[j]])

                        att_s1_pe(0)
                        att_s1_ew(0)
                        for idx in range(32):
                            if idx + 1 < 32:
                                att_s1_pe(idx + 1)
                            att_s2a(idx)
                            if idx + 1 < 32:
                                att_s1_ew(idx + 1)
                            att_s2b(idx)
                T.barrier()

                with ExitStack() as pc:
                    if S >= 2048:
                        wG = oT[:].rearrange("p h s -> p (h s)").bitcast(BF16)
                        wGv = wG[:, 0:8 * 2048].rearrange("p (k n) -> p k n", n=2048)
                    else:
                        wGv = SB(pc, "wGt", [128, 8, 2048], BF16)[:]
                    r_wG = Res()
                    wR = SB(pc, "wR", [128, 4, D], BF16); wT = SB(pc, "wT", [128, 4, D], BF16); wO = SB(pc, "wO", [128, 8, D], BF16)
                    lnB = SB(pc, "lnB1", [128, 2, D], F32); r_lnB = Res()
                    r_wRT = Res(); r_wGc = [Res() for _ in range(8)]; r_wOr = Res()
                    dma("pool", ds_wc[8], wR[:], w_rec_k, writes=[r_wRT])
                    dma("pool", ds_wc[8], wT[:], w_att_k, writes=[r_wRT])
                    for dcq in range(8):
                        dma("pool", ds_wc[dcq], wGv[:, :, dcq * 128:(dcq + 1) * 128], w_in_k[:, :, O_GR + dcq * 128:O_GR + (dcq + 1) * 128], writes=[r_wGc[dcq]])
                        dma("pool", ds_wc[dcq], wGv[:, :, 1024 + dcq * 128:1024 + (dcq + 1) * 128], w_in_k[:, :, O_GA + dcq * 128:O_GA + (dcq + 1) * 128], writes=[r_wGc[dcq]])
                    for q2 in range(2):
                        dma("pool", ds_wc[9], wO[:, :, q2 * 512:(q2 + 1) * 512], w_out_k[:, :, q2 * 512:(q2 + 1) * 512], writes=[r_wOr])
                    dma("sp", ds_const, lnB[:, 0, :], ln1g_d[0:1, :].broadcast_to([128, D]), writes=[r_lnB])
                    dma("sp", ds_const, lnB[:, 1, :], ln1b_d[0:1, :].broadcast_to([128, D]), writes=[r_lnB])
                    zb = zer[:, 0:8].bitcast(BF16)[:, 0:8].unsqueeze(2)
                    dma("sp", ds_misc[3], u2T_d[s, :, 0:1].rearrange("(j p) o -> p j o", p=128), zb, reads=[r_zer], allow_slow_non_contiguous=True)
                    dma("sp", ds_misc[3], u2T_d[s, :, S + 1:S + 2].rearrange("(j p) o -> p j o", p=128), zb, reads=[r_zer], allow_slow_non_contiguous=True)
                    r_x1d = [Res() for _ in range(NT)]
                    r_u2d = [Res() for _ in range(NT)]
                    xs = [SB(pc, "bxs%d" % i, [128, D], F32) for i in range(4)]; r_xs = [Res() for _ in range(4)]
                    uT = SB(pc, "buT", [128, 8, 512], BF16); r_uT = Res()
                    mT = SB(pc, "mT", [128, 8, 512], BF16); r_mT = Res()
                    ta = [SB(pc, "bta%d" % i, [128, 512], F32) for i in range(4)]; r_ta = [Res() for _ in range(4)]
                    yt = [SB(pc, "yt%d" % i, [128, D], F32) for i in range(2)]; r_yt = [Res(), Res()]
                    st6 = [SB(pc, "st6%d" % i, [128, 20], F32) for i in range(2)]; r_st6 = [Res(), Res()]
                    u2t = [SB(pc, "u2t%d" % i, [128, 8, 128], BF16) for i in range(2)]; r_u2t = [Res(), Res()]
                    for g in range(NG):
                        gs = slice(g * 512, (g + 1) * 512)
                        for t in range(4):
                            tile_i = g * 4 + t
                            dma("sp", ds_x[t], xs[t][:], x_d[s, tile_i * 128:(tile_i + 1) * 128, :], writes=[r_xs[t]])
                            make_uT(xs[t], r_xs[t], uT[:, :, t * 128:(t + 1) * 128], r_uT, 1, 0)
                        BSETS = ((2, 3, 4, 5), (0, 1, 6, 7))

                        def dc_pe(dc):
                            b0, b1, b2, b3 = BSETS[dc % 2]
                            dcs = slice(dc * 128, (dc + 1) * 128)
                            for k in range(8):
                                op("pe", lambda e, k=k: e.matmul(PB[b0][:], lhsT=wGv[:, k, dc * 128:(dc + 1) * 128], rhs=uT[:, k, :], start=(k == 0), stop=(k == 7)),
                                   reads=[r_uT, r_wGc[dc]], writes=[rPB[b0]])
                            for k in range(8):
                                op("pe", lambda e, k=k: e.matmul(PB[b1][:], lhsT=wGv[:, k, 1024 + dc * 128:1024 + (dc + 1) * 128], rhs=uT[:, k, :], start=(k == 0), stop=(k == 7)),
                                   reads=[r_uT, r_wGc[dc]], writes=[rPB[b1]])
                            for h in range(4):
                                op("pe", lambda e, h=h: e.matmul(PB[b2][:], lhsT=wR[:, h, dcs], rhs=ogA[:, h, gs], start=(h == 0), stop=(h == 3)),
                                   reads=[r_wRT, r_ogA[g]], writes=[rPB[b2]])
                            for m in range(4):
                                op("pe", lambda e, m=m: e.matmul(PB[b3][:], lhsT=wT[:, m, dcs], rhs=atA[:, m, gs], start=(m == 0), stop=(m == 3)),
                                   reads=[r_wRT] + r_atA[g * 4:(g + 1) * 4], writes=[rPB[b3]])

                        def dc_ew(dc):
                            b0, b1, b2, b3 = BSETS[dc % 2]
                            a0, a1 = ta[2 * (dc % 2)], ta[2 * (dc % 2) + 1]
                            ra0, ra1 = r_ta[2 * (dc % 2)], r_ta[2 * (dc % 2) + 1]
                            op("act", lambda e: e.activation(out=a0[:], in_=PB[b0][:], func=AF.Sigmoid), reads=[rPB[b0]], writes=[ra0])
                            op("act", lambda e: e.activation(out=a1[:], in_=PB[b1][:], func=AF.Sigmoid), reads=[rPB[b1]], writes=[ra1])
                            op("dve", lambda e: e.tensor_tensor(out=a0[:], in0=PB[b2][:], in1=a0[:], op=ALU.mult), reads=[rPB[b2], ra0], writes=[ra0])
                            op("dve", lambda e: e.tensor_tensor(out=a1[:], in0=PB[b3][:], in1=a1[:], op=ALU.mult), reads=[rPB[b3], ra1], writes=[ra1])
                            op("dve", lambda e: e.tensor_tensor(out=mT[:, dc, :], in0=a0[:], in1=a1[:], op=ALU.add), reads=[ra0, ra1], writes=[r_mT])

                        dc_pe(0)
                        for dc in range(8):
                            if dc + 1 < 8:
                                dc_pe(dc + 1)
                            dc_ew(dc)
                        def wout_pe(t):
                            tile_i = g * 4 + t
                            wb = 2 + 2 * (tile_i % 2)
                            for hh in range(2):
                                for k in range(8):
                                    op("pe", lambda e, k=k, hh=hh: e.matmul(PB[wb + hh][:], lhsT=mT[:, k, t * 128:(t + 1) * 128], rhs=wO[:, k, hh * 512:(hh + 1) * 512], start=(k == 0), stop=(k == 7)),
                                       reads=[r_mT, r_wOr], writes=[rPB[wb + hh]])

                        wout_pe(0)
                        for t in range(4):
                            tile_i = g * 4 + t
                            ib = tile_i % 2
                            wb = 2 + 2 * (tile_i % 2)
                            ln_part1((wb, wb + 1), (rPB[wb], rPB[wb + 1]), 0, yt[ib], r_yt[ib])
                            if t + 1 < 4:
                                wout_pe(t + 1)
                            ln_part2(xs[t][:], r_xs[t], lnB[:, 0, :], lnB[:, 1, :], r_lnB, yt[ib], r_yt[ib], st6[ib], r_st6[ib])
                            dma("sp", ds_o[ib], x1_d[s, tile_i * 128:(tile_i + 1) * 128, :], yt[ib][:], reads=[r_yt[ib]], writes=[r_x1d[tile_i]])
                            make_uT(yt[ib], r_yt[ib], u2t[ib][:], r_u2t[ib], 3, 2, banks=(6, 7))
                            dma("sp", ds_o[2 + ib], u2T_d[s, :, 1 + tile_i * 128:1 + (tile_i + 1) * 128].rearrange("(j p) q -> p j q", p=128), u2t[ib][:],
                                reads=[r_u2t[ib]], writes=[r_u2d[tile_i]])
                T.barrier()
            T.barrier()

            with ExitStack() as pf:
                wU = SB(pf, "wU", [128, 8, 2 * DFF], BF16); wD = SB(pf, "wD", [128, NFC, D], BF16); r_wU = Res()
                lnB = SB(pf, "lnB2", [128, 2, D], F32); r_lnB = Res()
                cvw = SB(pf, "cvw", [128, 4, 2 * NFC], F32); r_cvw = r_lnB
                r_wc = [Res() for _ in range(11)]
                for jq in range(11):
                    dma("pool", ds_wc[jq], wU[:, :, jq * 256:(jq + 1) * 256], w_up_k[:, :, jq * 256:(jq + 1) * 256], writes=[r_wc[jq]])
                    dma("pool", ds_wc[jq], wU[:, :, DFF + jq * 256:DFF + (jq + 1) * 256], w_up_k[:, :, DFF + jq * 256:DFF + (jq + 1) * 256], writes=[r_wc[jq]])
                    dma("pool", ds_wc[jq], wD[:, 2 * jq:2 * jq + 2, :], w_dn_k[:, 2 * jq:2 * jq + 2, :], writes=[r_wc[jq]])
                dma("sp", ds_const, lnB[:, 0, :], ln2g_d[0:1, :].broadcast_to([128, D]), writes=[r_lnB])
                dma("sp", ds_const, lnB[:, 1, :], ln2b_d[0:1, :].broadcast_to([128, D]), writes=[r_lnB])
                dma("sp", ds_const, cvw[:, 0:3, :], cw_d.rearrange("k (j p) -> p k j", p=128), writes=[r_cvw], allow_slow_non_contiguous=True)
                dma("sp", ds_const, cvw[:, 3, :], cb_d.rearrange("o (j p) -> p (o j)", p=128), writes=[r_cvw], allow_slow_non_contiguous=True)
                NSG = S // 256
                u2 = [SB(pf, "fu2%d" % i, [128, 8, 258], BF16) for i in range(2)]; r_u2 = [Res(), Res()]
                x1t = [SB(pf, "fx1%d" % i, [128, D], F32) for i in range(4)]; r_x1t = [Res() for _ in range(4)]
                cv = [SB(pf, "fcv%d" % i, [128, 256], F32) for i in range(2)]; r_cv = [Res(), Res()]
                cgt = [SB(pf, "fcg%d" % i, [128, 256], F32) for i in range(2)]; r_cg = [Res(), Res()]
                gz = [SB(pf, "fgz%d" % i, [128, 256], F32) for i in range(2)]; r_gz = [Res(), Res()]
                actb = [SB(pf, "fact%d" % i, [128, 256], BF16) for i in range(2)]; r_act = [Res(), Res()]
                yt = [SB(pf, "fyt%d" % i, [128, D], F32) for i in range(2)]; r_yt = [Res(), Res()]
                st6 = [SB(pf, "fst6%d" % i, [128, 20], F32) for i in range(2)]; r_st6 = [Res(), Res()]
                def c2_up(sg, c):
                    ub = sg % 2; ib = c % 2
                    pv, pg = 4 + 2 * ib, 5 + 2 * ib
                    for (pbx, cb) in ((pv, c * 128), (pg, DFF + c * 128)):
                        for k in range(8):
                            op("pe", lambda e, k=k, pbx=pbx, cb=cb: e.matmul(PB[pbx][:, 0:258], lhsT=wU[:, k, cb:cb + 128], rhs=u2[ub][:, k, :], start=(k == 0), stop=(k == 7)),
                               reads=[r_wc[c // 2], r_u2[ub]], writes=[rPB[pbx]])

                def c2_s1(c):
                    ib = c % 2
                    pv, pg = 4 + 2 * ib, 5 + 2 * ib
                    for (pbx, dst, r_dst, col) in ((pv, cv[ib], r_cv[ib], c), (pg, cgt[ib], r_cg[ib], NFC + c)):
                        op("act", lambda e, pbx=pbx, dst=dst, col=col: e.activation(out=dst[:], in_=PB[pbx][:, 0:256], func=AF.Identity, scale=cvw[:, 0, col:col + 1], bias=cvw[:, 3, col:col + 1]),
                           reads=[rPB[pbx], r_cvw], writes=[r_dst])

                def c2_s2(c):
                    ib = c % 2
                    pv, pg = 4 + 2 * ib, 5 + 2 * ib
                    for (pbx, dst, r_dst, col) in ((pg, cgt[ib], r_cg[ib], NFC + c), (pv, cv[ib], r_cv[ib], c)):
                        op("dve", lambda e, pbx=pbx, dst=dst, col=col: e.scalar_tensor_tensor(out=dst[:], in0=PB[pbx][:, 1:257], scalar=cvw[:, 1, col:col + 1], in1=dst[:], op0=ALU.mult, op1=ALU.add),
                           reads=[rPB[pbx], r_cvw, r_dst], writes=[r_dst])
                        op("dve", lambda e, pbx=pbx, dst=dst, col=col: e.scalar_tensor_tensor(out=dst[:], in0=PB[pbx][:, 2:258], scalar=cvw[:, 2, col:col + 1], in1=dst[:], op0=ALU.mult, op1=ALU.add),
                           reads=[rPB[pbx], r_cvw, r_dst], writes=[r_dst])

                def c2_s3(c):
                    ib = c % 2
                    op("act", lambda e: e.activation(out=gz[ib][:], in_=cgt[ib][:], func=AF.Gelu_apprx_tanh), reads=[r_cg[ib]], writes=[r_gz[ib]])

                def c2_s4(c):
                    ib = c % 2
                    op("dve", lambda e: e.tensor_tensor(out=actb[ib][:], in0=gz[ib][:], in1=cv[ib][:], op=ALU.mult), reads=[r_gz[ib], r_cv[ib]], writes=[r_act[ib]])

                def c2_down(c):
                    ib = c % 2
                    for tt in range(2):
                        for hh in range(2):
                            pa_ = tt * 2 + hh
                            op("pe", lambda e, tt=tt, hh=hh, pa_=pa_: e.matmul(PB[pa_][:], lhsT=actb[ib][:, tt * 128:(tt + 1) * 128], rhs=wD[:, c, hh * 512:(hh + 1) * 512],
                                                                             start=(c == 0), stop=(c == NFC - 1)),
                               reads=[r_act[ib], r_wc[c // 2]], writes=[rPB[pa_]])

                def c2_load(sg):
                    a = sg * 256
                    ub = sg % 2
                    dma("sp", ds_misc[ub], u2[ub][:], u2T_d[s, :, a:a + 258].rearrange("(j p) q -> p j q", p=128), writes=[r_u2[ub]])
                    for tt in range(2):
                        sl = (sg * 2 + tt) % 4
                        dma("sp", ds_x[sl], x1t[sl][:], x1_d[s, a + tt * 128:a + (tt + 1) * 128, :], writes=[r_x1t[sl]])

                def c2_prologue(sg):
                    c2_up(sg, 0); c2_up(sg, 1); c2_s1(0); c2_s2(0); c2_s1(1)

                def c2_ln2(sg):
                    for tt in range(2):
                        tile_i = sg * 2 + tt
                        sl = tile_i % 4
                        ib = tile_i % 2
                        ln_part2(x1t[sl][:], r_x1t[sl], lnB[:, 0, :], lnB[:, 1, :], r_lnB, yt[ib], r_yt[ib], st6[ib], r_st6[ib])
                        dma("sp", ds_o[ib], out_d[s, tile_i * 128:(tile_i + 1) * 128, :], yt[ib][:], reads=[r_yt[ib]])

                c2_load(0)
                if NSG > 1:
                    c2_load(1)
                c2_prologue(0)
                for sg in range(NSG):
                    for c in range(NFC):
                        c2_s3(c)
                        if c + 1 < NFC:
                            c2_s2(c + 1)
                        c2_s4(c)
                        if c + 2 < NFC:
                            c2_up(sg, c + 2)
                            c2_s1(c + 2)
                        c2_down(c)
                    for tt in range(2):
                        tile_i = sg * 2 + tt
                        ib = tile_i % 2
                        ln_part1((tt * 2, tt * 2 + 1), (rPB[tt * 2], rPB[tt * 2 + 1]), 1, yt[ib], r_yt[ib])
                    if sg + 1 < NSG:
                        c2_prologue(sg + 1)
                    c2_ln2(sg)
                    if sg + 2 < NSG:
                        c2_load(sg + 2)
            T.barrier()
        print("build: instructions=%d waits=%d" % (T.n_ins, T.n_wait))
    return nc


def _consts():
    ident = np.eye(128, dtype=np.float32)
    i = np.arange(64)
    maskf = (i[:, None] <= i[None, :]).astype(np.uint8)
    maskb = (i[:, None] >= i[None, :]).astype(np.uint8)
    q = np.arange(128)[:, None]
    cc = np.arange(128)[None, :]
    NEG = -30000.0
    am = np.zeros((128, 384), np.float32)
    am[:, 0:128] = np.where(cc >= q, 0.0, NEG)
    am[:, 256:384] = np.where(cc <= q, 0.0, NEG)
    half = 32
    inv = (np.float32(10000.0) ** (-np.arange(half, dtype=np.float32) / np.float32(half))).astype(np.float32)
    p = np.arange(128)
    rope = np.stack([inv[p % 32], np.where((p % 64) < 32, -1.0, 1.0).astype(np.float32)], axis=1).astype(np.float32)
    return dict(k_ident=ident, k_maskf=maskf, k_maskb=maskb, k_amask=am, k_rope=rope)


def make_in_maps(inputs, n_cores, nseq):
    f = lambda a: np.ascontiguousarray(np.asarray(a))
    shared = dict(
        w_ada=f(inputs["w_ada"][0]), b_ada=f(inputs["b_ada"][0:1]), w_in=f(inputs["w_in"][0]),
        rec_lower_bound=f(inputs["rec_lower_bound"]), rec_norm_g=f(inputs["rec_norm_g"][0:1]),
        w_rec_branch=f(inputs["w_rec_branch"][0]), attn_sink=f(inputs["attn_sink"][0:1]),
        w_attn_branch=f(inputs["w_attn_branch"][0]), w_out=f(inputs["w_out"][0]),
        ln1_g=f(inputs["ln1_g"][0:1]), ln1_b=f(inputs["ln1_b"][0:1]), w_up=f(inputs["w_up"][0]),
        conv_w=f(inputs["conv_w"][0]), conv_b=f(inputs["conv_b"][0:1]), w_down=f(inputs["w_down"][0]),
        ln2_g=f(inputs["ln2_g"][0:1]), ln2_b=f(inputs["ln2_b"][0:1]))
    shared.update(_consts())
    maps = []
    for i in range(n_cores):
        m = dict(shared)
        m["x"] = f(inputs["x"][i * nseq:(i + 1) * nseq])
        m["c"] = f(inputs["c"][i * nseq:(i + 1) * nseq])
        m["positions"] = f(inputs["positions"][i * nseq:(i + 1) * nseq]).astype(np.int32)
        maps.append(m)
    return maps


def kernel(**inputs):
    B, S, _ = inputs["x"].shape
    nseq = B // N_CORES
    nc = build_nc(S=S, NSEQ=nseq)
    in_maps = make_in_maps(inputs, N_CORES, nseq)
    res = run_bass_kernel_spmd(nc, in_maps, core_ids=list(range(N_CORES)))
    return np.concatenate([np.asarray(r["out"]) for r in res.results], axis=0).astype(np.float32)
```

```python
import numpy as np
from contextlib import ExitStack
import concourse.bass as bass
import concourse.mybir as mybir
from concourse.bass_utils import run_bass_kernel_spmd

F32 = mybir.dt.float32
BF16 = mybir.dt.bfloat16
I32 = mybir.dt.int32
U8 = mybir.dt.uint8
AF = mybir.ActivationFunctionType
ALU = mybir.AluOpType
AX = mybir.AxisListType

D = 1024
NK = 8
DIN = 5376
DFF = 2816
NFC = DFF // 128
ALPHA = 2.0 ** 0.25
LN_EPS = 1e-5
RMS_EPS = 1e-6
N_CORES = 8
O_RQ, O_RFF, O_RFB, O_RI, O_RG, O_AQ, O_AK, O_AV, O_GR, O_GA = 0, 512, 1024, 1536, 2048, 2560, 3072, 3200, 3328, 4352


class Res:
    __slots__ = ("w", "rd")

    def __init__(self):
        self.w = None
        self.rd = []


class Tracker:
    def __init__(self, nc, stack):
        self.nc = nc
        self.stack = stack
        self.eng = {}
        self.sems = {}
        for name, h in (("pe", nc.tensor), ("act", nc.scalar), ("dve", nc.vector),
                        ("pool", nc.gpsimd), ("sp", nc.sync)):
            sem = stack.enter_context(nc.semaphore("sem_" + name))
            self.eng[name] = {"h": h, "sem": sem, "cnt": 0, "seen": {}, "name": name}
            self.sems[id(sem)] = sem
        self.dma_sems = []
        self.n_wait = 0
        self.n_ins = 0

    def new_dma_sem(self, name):
        sem = self.stack.enter_context(self.nc.semaphore(name))
        d = {"sem": sem, "cnt": 0}
        self.sems[id(sem)] = sem
        self.dma_sems.append(d)
        return d

    def _wait_deps(self, E, reads, writes):
        deps = {}

        def add(tok):
            if tok is not None and deps.get(tok[0], 0) < tok[1]:
                deps[tok[0]] = tok[1]

        for r in reads:
            add(r.w)
        for w in writes:
            add(w.w)
            for t in w.rd:
                add(t)
        own = id(E["sem"])
        for k, v in deps.items():
            if E["name"] == "pe" and k == own:
                continue
            if E["seen"].get(k, 0) >= v:
                continue
            E["h"].wait_ge(self.sems[k], v)
            E["seen"][k] = v
            self.n_wait += 1

    def _commit(self, tok, reads, writes):
        for r in reads:
            r.rd.append(tok)
            if len(r.rd) > 48:
                best = {}
                for k, v in r.rd:
                    if best.get(k, 0) < v:
                        best[k] = v
                r.rd = list(best.items())
        for w in writes:
            w.w = tok
            w.rd = []
        self.n_ins += 1

    def op(self, eng, fn, reads=(), writes=()):
        E = self.eng[eng]
        self._wait_deps(E, reads, writes)
        ins = fn(E["h"])
        E["cnt"] += 1
        ins.then_inc(E["sem"], 1)
        self._commit((id(E["sem"]), E["cnt"]), reads, writes)
        return ins

    def dma(self, q, dsem, out, in_, reads=(), writes=(), **kw):
        E = self.eng[q]
        self._wait_deps(E, reads, writes)
        ins = E["h"].dma_start(out=out, in_=in_, **kw)
        dsem["cnt"] += 16
        ins.then_inc(dsem["sem"], 16)
        self._commit((id(dsem["sem"]), dsem["cnt"]), reads, writes)
        return ins

    def barrier(self):
        targets = [(id(e["sem"]), e["cnt"]) for e in self.eng.values() if e["cnt"] > 0]
        targets += [(id(d["sem"]), d["cnt"]) for d in self.dma_sems if d["cnt"] > 0]
        for E in self.eng.values():
            for k, v in targets:
                if E["seen"].get(k, 0) >= v:
                    continue
                E["h"].wait_ge(self.sems[k], v)
                E["seen"][k] = v
                self.n_wait += 1


class Rot:
    def __init__(self, items):
        self.items = items
        self.i = 0

    def next(self):
        it = self.items[self.i % len(self.items)]
        self.i += 1
        return it


def build_nc(S=2048, NSEQ=2, dbg=False):
    NT, NG, NCH = S // 128, S // 512, S // 64
    nc = bass.Bass("TRN2", target_bir_lowering=False)

    def din(name, shape, dt=F32):
        return nc.dram_tensor(name, list(shape), dt, kind="ExternalInput").ap()

    x_d = din("x", [NSEQ, S, D])
    c_d = din("c", [NSEQ, D])
    pos_d = din("positions", [NSEQ, S], I32)
    w_ada_d = din("w_ada", [D, 6 * D])
    b_ada_d = din("b_ada", [1, 6 * D])
    w_in_d = din("w_in", [D, DIN])
    lbr_d = din("rec_lower_bound", [2, 2, 512])
    ng_d = din("rec_norm_g", [1, 128])
    w_rec_d = din("w_rec_branch", [512, D])
    sink_d = din("attn_sink", [1, 8])
    w_att_d = din("w_attn_branch", [512, D])
    w_out_d = din("w_out", [D, D])
    ln1g_d = din("ln1_g", [1, D]); ln1b_d = din("ln1_b", [1, D])
    w_up_d = din("w_up", [D, 2 * DFF])
    cw_d = din("conv_w", [3, 2 * DFF])
    cb_d = din("conv_b", [1, 2 * DFF])
    w_dn_d = din("w_down", [DFF, D])
    ln2g_d = din("ln2_g", [1, D]); ln2b_d = din("ln2_b", [1, D])
    identf_d = din("k_ident", [128, 128])
    mskf_d = din("k_maskf", [64, 64], U8)
    mskb_d = din("k_maskb", [64, 64], U8)
    amask_d = din("k_amask", [128, 384])
    rope_d = din("k_rope", [128, 2])
    out_d = nc.dram_tensor("out", [NSEQ, S, D], F32, kind="ExternalOutput").ap()
    x1_d = nc.dram_tensor("x1_scr", [NSEQ, S, D], F32).ap()
    u2T_d = nc.dram_tensor("u2T_scr", [NSEQ, D, S + 2], BF16).ap()
    dbg_d = {}
    if dbg:
        dbg_d["mods"] = nc.dram_tensor("dbg_mods", [128, 4, 8], F32, kind="ExternalOutput").ap()
        dbg_d["QT"] = nc.dram_tensor("dbg_QT", [2, 128, 2, S], BF16, kind="ExternalOutput").ap()
        dbg_d["KT"] = nc.dram_tensor("dbg_KT", [2, 128, 2, S], BF16, kind="ExternalOutput").ap()
        dbg_d["V"] = nc.dram_tensor("dbg_V", [64, NCH, 256], BF16, kind="ExternalOutput").ap()
        dbg_d["oT"] = nc.dram_tensor("dbg_oT", [128, 4, S], F32, kind="ExternalOutput").ap()
        dbg_d["kT"] = nc.dram_tensor("dbg_kT", [128, S], BF16, kind="ExternalOutput").ap()
        dbg_d["x1"] = x1_d

    w_in_k = w_in_d.rearrange("(j p) n -> p j n", p=128)
    w_ada_k = w_ada_d.rearrange("(j p) n -> p j n", p=128)
    w_up_k = w_up_d.rearrange("(j p) n -> p j n", p=128)
    w_out_k = w_out_d.rearrange("(j p) n -> p j n", p=128)
    w_rec_k = w_rec_d.rearrange("(j p) n -> p j n", p=128)
    w_att_k = w_att_d.rearrange("(j p) n -> p j n", p=128)
    w_dn_k = w_dn_d.rearrange("(j p) n -> p j n", p=128)

    with ExitStack() as top:
        T = Tracker(nc, top)
        op, dma = T.op, T.dma

        uniq = [0]

        def SB(st, name, shape, dt):
            uniq[0] += 1
            return st.enter_context(nc.sbuf_tensor("%s_%d" % (name, uniq[0]), list(shape), dt))

        PB = [top.enter_context(nc.psum_tensor("pb%d" % i, [128, 512], F32)) for i in range(8)]
        rPB = [Res() for _ in range(8)]
        ds_const = T.new_dma_sem("ds_const")
        ds_w = T.new_dma_sem("ds_w")
        ds_x = [T.new_dma_sem("ds_x%d" % i) for i in range(4)]
        ds_o = [T.new_dma_sem("ds_o%d" % i) for i in range(4)]
        ds_misc = [T.new_dma_sem("ds_m%d" % i) for i in range(4)]
        ds_dbg = T.new_dma_sem("ds_dbg")
        ds_wc = [T.new_dma_sem("ds_wc%d" % i) for i in range(11)]

        identf = SB(top, "identf", [128, 128], F32); r_identf = Res()
        identb = SB(top, "identb", [128, 128], BF16); r_identb = Res()
        onesf = SB(top, "onesf", [128, 128], F32); r_onesf = Res()
        maskf = SB(top, "maskf", [64, 64], U8); maskb = SB(top, "maskb", [64, 64], U8); r_mask = Res()
        zer = SB(top, "zer", [128, 128], F32); r_zer = Res()
        rmask = SB(top, "rmask", [128, 512], F32); r_rmask = Res()
        lbt = SB(top, "lbt", [128, 3, 2, 4], F32); r_lbt = Res()
        lraw = SB(top, "lraw", [128, 2, 2, 4], F32); r_lraw = Res()
        ngc = SB(top, "ngc", [128, 1], F32); r_ngc = Res()
        sinkb = SB(top, "sinkb", [128, 2, 8], F32); r_sink = Res()
        rope = SB(top, "rope", [128, 2], F32); r_rope = Res()
        epsc = SB(top, "epsc", [128, 2], F32); r_eps = Res()
        modsT_all = SB(top, "modsT", [128, NSEQ, 4, 8], F32); r_modsT = Res()
        gaB_all = SB(top, "gaB", [128, NSEQ, 2, D], F32); r_gaB = [Res(), Res()]
        modsT = modsT_all[:, 0]; gaB = gaB_all[:, 0]
        amask = SB(top, "amask", [128, 384], F32); r_amask = Res()

        dma("sp", ds_const, identf[:], identf_d, writes=[r_identf])
        dma("sp", ds_const, maskf[:], mskf_d, writes=[r_mask])
        dma("sp", ds_const, maskb[:], mskb_d, writes=[r_mask])
        dma("sp", ds_const, rope[:], rope_d, writes=[r_rope])
        dma("sp", ds_const, amask[:], amask_d, writes=[r_amask])
        dma("sp", ds_const, ngc[:], ng_d.rearrange("o p -> p o"), writes=[r_ngc], allow_slow_non_contiguous=True)
        dma("sp", ds_const, sinkb[:, 0, :], sink_d[0:1, :].broadcast_to([128, 8]), writes=[r_sink])
        dma("sp", ds_const, lraw[:].rearrange("p a b h -> p (a b) h"),
            lbr_d.rearrange("a b (h p) -> p (a b) h", p=128), writes=[r_lraw], allow_slow_non_contiguous=True)
        T.barrier()
        op("dve", lambda e: e.tensor_copy(out=identb[:], in_=identf[:]), reads=[r_identf], writes=[r_identb])
        op("pool", lambda e: e.memset(onesf[:], 1.0), writes=[r_onesf])
        op("pool", lambda e: e.memset(zer[:], 0.0), writes=[r_zer])
        op("pool", lambda e: e.memset(rmask[:], 1.0), writes=[r_rmask])
        op("pool", lambda e: e.memset(rmask[:].rearrange("p (c t) -> p c t", t=64)[:, :, 0:1], 0.0), writes=[r_rmask])
        op("pool", lambda e: e.memset(epsc[:, 0:1], LN_EPS), writes=[r_eps])
        op("pool", lambda e: e.memset(epsc[:, 1:2], RMS_EPS), writes=[r_eps])
        op("dve", lambda e: e.tensor_scalar(out=sinkb[:, 1, :], in0=sinkb[:, 0, :], scalar1=-1.0, scalar2=None, op0=ALU.mult),
           reads=[r_sink], writes=[r_sink])
        op("dve", lambda e: e.tensor_tensor(out=lbt[:, 2, :, :], in0=lraw[:, :, 0, :], in1=lraw[:, :, 1, :], op=ALU.subtract),
           reads=[r_lraw], writes=[r_lbt])
        op("act", lambda e: e.activation(out=lbt[:, 0, :, :], in_=lbt[:, 2, :, :], func=AF.Sigmoid), reads=[r_lbt], writes=[r_lbt])
        op("act", lambda e: e.activation(out=lbt[:, 1, :, :], in_=lbt[:, 2, :, :], func=AF.Sigmoid, scale=-1.0), reads=[r_lbt], writes=[r_lbt])
        op("dve", lambda e: e.tensor_scalar(out=lbt[:, 2, :, :], in0=lbt[:, 1, :, :], scalar1=-1.0, scalar2=None, op0=ALU.mult),
           reads=[r_lbt], writes=[r_lbt])

        def make_uT(xt, r_xt, dst, r_dst, scol, bcol, banks=(0, 1)):
            for hb in range(2):
                b = banks[hb]
                for i in range(4):
                    k = hb * 4 + i
                    op("pe", lambda e, k=k, i=i, b=b: e.transpose(out=PB[b][:, i * 128:(i + 1) * 128], in_=xt[:, k * 128:(k + 1) * 128],
                                                                   identity=identf[:]),
                       reads=[r_xt, r_identf], writes=[rPB[b]])
                for i in range(4):
                    k = hb * 4 + i
                    if i % 2 == 0:
                        op("act", lambda e, k=k, i=i, b=b: e.activation(out=dst[:, k, :], in_=PB[b][:, i * 128:(i + 1) * 128], func=AF.Identity,
                                                                         scale=modsT[:, scol, k:k + 1], bias=modsT[:, bcol, k:k + 1]),
                           reads=[rPB[b], r_modsT], writes=[r_dst])
                    else:
                        op("dve", lambda e, k=k, i=i, b=b: e.tensor_scalar(out=dst[:, k, :], in0=PB[b][:, i * 128:(i + 1) * 128],
                                                                            scalar1=modsT[:, scol, k:k + 1], scalar2=modsT[:, bcol, k:k + 1],
                                                                            op0=ALU.mult, op1=ALU.add),
                           reads=[rPB[b], r_modsT], writes=[r_dst])

        def ln_part1(pz, r_pz, which, y, r_y):
            for h in range(2):
                op("dve", lambda e, h=h: e.tensor_tensor(out=y[:, h * 512:(h + 1) * 512], in0=PB[pz[h]][:], in1=gaB[:, which, h * 512:(h + 1) * 512], op=ALU.mult),
                   reads=[r_pz[h], r_gaB[which]], writes=[r_y])

        def ln_part2(xres, r_xres, gB, bB, r_gb, y, r_y, st6, r_st):
            op("dve", lambda e: e.scalar_tensor_tensor(out=y[:], in0=xres, scalar=ALPHA, in1=y[:], op0=ALU.mult, op1=ALU.add),
               reads=[r_xres, r_y], writes=[r_y])
            for h in range(2):
                op("dve", lambda e, h=h: e.bn_stats(out=st6[:, h * 6:(h + 1) * 6], in_=y[:, h * 512:(h + 1) * 512]), reads=[r_y], writes=[r_st])
            op("dve", lambda e: e.bn_aggr(out=st6[:, 12:14], in_=st6[:, 0:12]), reads=[r_st], writes=[r_st])
            op("act", lambda e: e.activation(out=st6[:, 14:15], in_=st6[:, 13:14], func=AF.Ln, bias=epsc[:, 0:1], scale=1.0), reads=[r_st, r_eps], writes=[r_st])
            op("act", lambda e: e.activation(out=st6[:, 15:16], in_=st6[:, 14:15], func=AF.Exp, scale=-0.5), reads=[r_st], writes=[r_st])
            op("dve", lambda e: e.tensor_scalar(out=st6[:, 16:17], in0=st6[:, 12:13], scalar1=st6[:, 15:16], scalar2=-1.0, op0=ALU.mult, op1=ALU.mult),
               reads=[r_st], writes=[r_st])
            op("act", lambda e: e.activation(out=y[:], in_=y[:], func=AF.Identity, scale=st6[:, 15:16], bias=st6[:, 16:17]),
               reads=[r_y, r_st], writes=[r_y])
            op("dve", lambda e: e.tensor_tensor(out=y[:], in0=y[:], in1=gB, op=ALU.mult), reads=[r_y, r_gb], writes=[r_y])
            op("dve", lambda e: e.tensor_tensor(out=y[:], in0=y[:], in1=bB, op=ALU.add), reads=[r_y, r_gb], writes=[r_y])

        def ln_epilogue(pz, r_pz, xres, r_xres, which, gB, bB, r_gb, y, r_y, st6, r_st):
            ln_part1(pz, r_pz, which, y, r_y)
            ln_part2(xres, r_xres, gB, bB, r_gb, y, r_y, st6, r_st)

        for s in range(NSEQ):
            modsT = modsT_all[:, s]
            gaB = gaB_all[:, s]
            if s == 0:
              with ExitStack() as ph:
                cT = SB(ph, "cT", [128, NSEQ, 8], F32); r_cT = Res()
                cbt = SB(ph, "cbt", [128, NSEQ, 8, 128], F32); r_cbt = Res()
                wst = [SB(ph, "wst%d" % i, [128, 8, 512], F32) for i in range(2)]; r_wst = [Res(), Res()]
                bst = [SB(ph, "bst%d" % i, [1, 512], F32) for i in range(2)]; r_bst = [Res(), Res()]
                mtmp = [SB(ph, "mtmp%d" % i, [128, 512], F32) for i in range(NSEQ)]; r_mtmp = [Res() for _ in range(NSEQ)]
                for ss in range(NSEQ):
                    dma("sp", ds_misc[0], cT[:, ss, :], c_d[ss].rearrange("(j p) -> p j", p=128), writes=[r_cT], allow_slow_non_contiguous=True)
                op("act", lambda e: e.activation(out=cT[:], in_=cT[:], func=AF.Silu), reads=[r_cT], writes=[r_cT])
                op("dve", lambda e: e.tensor_copy(out=cbt[:], in_=cT[:].unsqueeze(3).broadcast_to([128, NSEQ, 8, 128])), reads=[r_cT], writes=[r_cbt])
                bcol = SB(ph, "bcol", [128, 48], F32); r_bcol = Res()
                cT2 = SB(ph, "cT2", [128, 8, NSEQ], F32); r_cT2 = Res()
                dma("sp", ds_misc[3], bcol[:], b_ada_d.rearrange("o (j p) -> p (o j)", p=128), writes=[r_bcol], allow_slow_non_contiguous=True)
                op("dve", lambda e: e.tensor_copy(out=cT2[:], in_=cT[:].rearrange("p s k -> p k s")), reads=[r_cT], writes=[r_cT2])
                for cg in range(12):
                    b2 = cg % 2
                    dma("sp", ds_misc[1 + b2], wst[b2][:], w_ada_k[:, :, cg * 512:(cg + 1) * 512], writes=[r_wst[b2]])
                    which, half = cg // 2, cg % 2
                    if which in (2, 5):
                        dma("sp", ds_o[b2], bst[b2][:], b_ada_d[0:1, cg * 512:(cg + 1) * 512], writes=[r_bst[b2]])
                        gi = 0 if which == 2 else 1
                        for ss in range(NSEQ):
                            pb = 2 + b2 + 2 * (ss % 2)
                            for k in range(8):
                                op("pe", lambda e, k=k: e.matmul(PB[pb][:], lhsT=cbt[:, ss, k, :], rhs=wst[b2][:, k, :], start=(k == 0), stop=False),
                                   reads=[r_cbt, r_wst[b2]], writes=[rPB[pb]])
                            op("pe", lambda e: e.matmul(PB[pb][:], lhsT=onesf[0:1, :], rhs=bst[b2][0:1, :], start=False, stop=True),
                               reads=[r_onesf, r_bst[b2]], writes=[rPB[pb]])
                            op("dve", lambda e: e.tensor_scalar(out=gaB_all[:, ss, gi, half * 512:(half + 1) * 512], in0=PB[pb][:], scalar1=1.0, scalar2=None, op0=ALU.add),
                               reads=[rPB[pb]], writes=[r_gaB[gi]])
                    else:
                        widx = {0: 0, 1: 1, 3: 2, 4: 3}[which]
                        addc = 1.0 if which in (1, 4) else 0.0
                        pt = 6 + b2
                        for i in range(4):
                            for k in range(8):
                                op("pe", lambda e, k=k, i=i: e.matmul(PB[pt][:, i * NSEQ:(i + 1) * NSEQ], lhsT=wst[b2][:, k, i * 128:(i + 1) * 128], rhs=cT2[:, k, :], start=(k == 0), stop=(k == 7)),
                                   reads=[r_cT2, r_wst[b2]], writes=[rPB[pt]])
                        op("dve", lambda e: e.tensor_tensor(out=modsT_all[:, :, widx, half * 4:(half + 1) * 4], in0=PB[pt][:, 0:4 * NSEQ].rearrange("p (i s) -> p s i", s=NSEQ),
                                                            in1=bcol[:, cg * 4:(cg + 1) * 4].unsqueeze(1).broadcast_to([128, NSEQ, 4]), op=ALU.add),
                           reads=[rPB[pt], r_bcol], writes=[r_modsT])
                        if addc != 0.0:
                            op("dve", lambda e: e.tensor_scalar(out=modsT_all[:, :, widx, half * 4:(half + 1) * 4], in0=modsT_all[:, :, widx, half * 4:(half + 1) * 4],
                                                                scalar1=addc, scalar2=None, op0=ALU.add), reads=[r_modsT], writes=[r_modsT])
                if dbg:
                    dma("sp", ds_dbg, dbg_d["mods"], modsT_all[:, 0], reads=[r_modsT])
              T.barrier()

            with ExitStack() as pab:
                oT = SB(pab, "oT", [128, 4, S], F32)
                r_oT = [[Res() for _ in range(NCH)] for _ in range(4)]
                kTt = SB(pab, "kTt", [128, S], BF16); r_kT = [Res() for _ in range(NG)]
                vtk = SB(pab, "vtk", [128, NT, 128], BF16); r_v = [Res() for _ in range(NT)]
                cosT = SB(pab, "cosT", [128, S], F32); sinT = SB(pab, "sinT", [128, S], F32); r_cs = Res()
                with ExitStack() as pr:
                    posi = SB(pr, "posi", [128, S], I32); r_posi = Res()
                    ang = SB(pr, "ang", [128, S], F32); r_ang = Res()
                    kq = SB(pr, "kq", [128, S], F32); r_kq = Res()
                    ki = SB(pr, "ki", [128, S], I32); r_ki = Res()
                    dma("sp", ds_misc[0], posi[:], pos_d[s:s + 1, :].broadcast_to([128, S]), writes=[r_posi])
                    op("dve", lambda e: e.tensor_copy(out=ang[:], in_=posi[:]), reads=[r_posi], writes=[r_ang])
                    op("dve", lambda e: e.tensor_scalar(out=ang[:], in0=ang[:], scalar1=rope[:, 0:1], scalar2=None, op0=ALU.mult),
                       reads=[r_ang, r_rope], writes=[r_ang])
                    TWO_PI = 6.283185307179586
                    C1 = 6.28125
                    C2 = TWO_PI - C1
                    for (dst, shift) in ((sinT, 0.0), (cosT, np.pi / 2)):
                        op("dve", lambda e, shift=shift: e.tensor_scalar(out=kq[:], in0=ang[:], scalar1=float(shift), scalar2=1.0 / TWO_PI, op0=ALU.add, op1=ALU.mult),
                           reads=[r_ang], writes=[r_kq])
                        op("dve", lambda e: e.tensor_copy(out=ki[:], in_=kq[:]), reads=[r_kq], writes=[r_ki])
                        op("dve", lambda e: e.tensor_copy(out=kq[:], in_=ki[:]), reads=[r_ki], writes=[r_kq])
                        op("dve", lambda e, shift=shift, dst=dst: e.tensor_scalar(out=dst[:], in0=ang[:], scalar1=float(shift), scalar2=None, op0=ALU.add),
                           reads=[r_ang], writes=[r_cs])
                        op("dve", lambda e, dst=dst: e.scalar_tensor_tensor(out=dst[:], in0=kq[:], scalar=-C1, in1=dst[:], op0=ALU.mult, op1=ALU.add),
                           reads=[r_kq, r_cs], writes=[r_cs])
                        op("dve", lambda e, dst=dst: e.scalar_tensor_tensor(out=dst[:], in0=kq[:], scalar=-C2, in1=dst[:], op0=ALU.mult, op1=ALU.add),
                           reads=[r_kq, r_cs], writes=[r_cs])
                        op("dve", lambda e, dst=dst: e.tensor_scalar(out=kq[:], in0=dst[:], scalar1=float(np.pi), scalar2=-TWO_PI, op0=ALU.is_gt, op1=ALU.mult),
                           reads=[r_cs], writes=[r_kq])
                        op("dve", lambda e, dst=dst: e.tensor_tensor(out=dst[:], in0=dst[:], in1=kq[:], op=ALU.add), reads=[r_cs, r_kq], writes=[r_cs])
                        op("dve", lambda e, dst=dst: e.tensor_scalar(out=kq[:], in0=dst[:], scalar1=float(-np.pi), scalar2=TWO_PI, op0=ALU.is_lt, op1=ALU.mult),
                           reads=[r_cs], writes=[r_kq])
                        op("dve", lambda e, dst=dst: e.tensor_tensor(out=dst[:], in0=dst[:], in1=kq[:], op=ALU.add), reads=[r_cs, r_kq], writes=[r_cs])
                        op("dve", lambda e, dst=dst: e.tensor_scalar(out=dst[:], in0=dst[:], scalar1=3.1415925, scalar2=-3.1415925, op0=ALU.min, op1=ALU.max),
                           reads=[r_cs], writes=[r_cs])
                        op("act", lambda e, dst=dst: e.activation(out=dst[:], in_=dst[:], func=AF.Sin), reads=[r_cs], writes=[r_cs])
                    op("dve", lambda e: e.tensor_scalar(out=sinT[:], in0=sinT[:], scalar1=rope[:, 1:2], scalar2=None, op0=ALU.mult),
                       reads=[r_cs, r_rope], writes=[r_cs])
                T.barrier()

                for hp in range(2):
                    with ExitStack() as pa:
                        NWA = 1024 + (384 if hp == 0 else 0)
                        wA = SB(pa, "wA", [128, 8, NWA], BF16); r_wA = Res()
                        QT = [SB(pa, "QT%d" % d, [128, 2, S], BF16) for d in range(2)]
                        KT = [SB(pa, "KT%d" % d, [128, 2, S], BF16) for d in range(2)]
                        r_QK = [[[Res() for _ in range(NG)] for _ in range(2)] for _ in range(2)]
                        Vt = SB(pa, "Vt", [64, NCH, 256], BF16); r_V = [Res() for _ in range(NG)]
                        Bref = SB(pa, "Bref", [128, 2, 2, NCH], F32); Bend = SB(pa, "Bend", [128, 2, 2, NCH], F32); r_B = Res()
                        Gx = SB(pa, "Gx", [128, 2, 2, NCH], F32); r_G = Res()
                        for seg, base in enumerate((O_RQ, O_RFF, O_RFB, O_RI)):
                            dma("pool", ds_w, wA[:, :, seg * 256:(seg + 1) * 256], w_in_k[:, :, base + hp * 256: base + hp * 256 + 256], writes=[r_wA])
                        if hp == 0:
                            dma("pool", ds_w, wA[:, :, 1024:1152], w_in_k[:, :, O_AK:O_AK + 128], writes=[r_wA])
                            for kv in range(2):
                                dma("pool", ds_w, wA[:, :, 1152 + kv * 64:1152 + kv * 64 + 32], w_in_k[:, :, O_AK + kv * 64 + 32:O_AK + kv * 64 + 64], writes=[r_wA])
                                dma("pool", ds_w, wA[:, :, 1152 + kv * 64 + 32:1152 + kv * 64 + 64], w_in_k[:, :, O_AK + kv * 64:O_AK + kv * 64 + 32], writes=[r_wA])
                            dma("pool", ds_w, wA[:, :, 1280:1408], w_in_k[:, :, O_AV:O_AV + 128], writes=[r_wA])
                        with ExitStack() as pa2:
                            xs = [SB(pa2, "xs%d" % i, [128, D], F32) for i in range(2)]; r_xs = [Res() for _ in range(2)]
                            uT = [SB(pa2, "uT%d" % i, [128, 8, 512], BF16) for i in range(1)]; r_uT = [Res()]
                            qs = [SB(pa2, "qs%d" % i, [128, 512], F32) for i in range(2)]; r_qs = [Res(), Res()]
                            NTMP = 2
                            tnames = ("sig", "lg", "Bf", "Eq", "Ek")
                            tmps = [{n: SB(pa2, "t_%s%d" % (n, i), [128, 512], F32) for n in tnames} for i in range(NTMP)]
                            r_tmps = [{n: Res() for n in tnames} for i in range(NTMP)]
                            for i in range(NTMP):
                                tmps[i]["kk"] = tmps[i]["sig"]; r_tmps[i]["kk"] = r_tmps[i]["sig"]
                                tmps[i]["dd"] = tmps[i]["lg"]; r_tmps[i]["dd"] = r_tmps[i]["lg"]
                            tctr = 0
                            bank_rot = Rot([2, 3, 4, 5, 6, 7])
                            for g in range(NG):
                                ub = 0
                                for t in range(4):
                                    tile_i = g * 4 + t
                                    sl = tile_i % 2
                                    dma("sp", ds_x[sl], xs[sl][:], x_d[s, tile_i * 128:(tile_i + 1) * 128, :], writes=[r_xs[sl]])
                                    make_uT(xs[sl], r_xs[sl], uT[ub][:, :, t * 128:(t + 1) * 128], r_uT[ub], 1, 0)
                                gs = slice(g * 512, (g + 1) * 512)
                                for cp in range(4):
                                    pb = bank_rot.next()
                                    for cc in range(2):
                                        ch = cp * 2 + cc
                                        for k in range(8):
                                            op("pe", lambda e, k=k, cc=cc, ch=ch, pb=pb: e.matmul(PB[pb][0:64, cc * 256:(cc + 1) * 256], lhsT=uT[ub][:, k, ch * 64:(ch + 1) * 64],
                                                                                                   rhs=wA[:, k, 768:1024], start=(k == 0), stop=(k == 7)),
                                               reads=[r_uT[ub], r_wA], writes=[rPB[pb]])
                                    op("act", lambda e, pb=pb, cp=cp: e.activation(out=Vt[:, g * 8 + cp * 2:g * 8 + cp * 2 + 2, :],
                                                                                    in_=PB[pb][0:64, :].rearrange("p (c n) -> p c n", n=256), func=AF.Copy, scale=128.0 ** -0.5),
                                       reads=[rPB[pb]], writes=[r_V[g]])
                                for hl in range(2):
                                    h = hp * 2 + hl
                                    pb = bank_rot.next()
                                    for k in range(8):
                                        op("pe", lambda e, k=k, pb=pb, hl=hl: e.matmul(PB[pb][:], lhsT=wA[:, k, hl * 128:(hl + 1) * 128], rhs=uT[ub][:, k, :],
                                                                                         start=(k == 0), stop=(k == 7)), reads=[r_uT[ub], r_wA], writes=[rPB[pb]])
                                    op("act", lambda e, pb=pb, hl=hl: e.activation(out=qs[hl][:], in_=PB[pb][:], func=AF.Silu), reads=[rPB[pb]], writes=[r_qs[hl]])
                                    for d in range(2):
                                        tm = tmps[tctr % NTMP]; rt = r_tmps[tctr % NTMP]; tctr += 1
                                        pb = bank_rot.next()
                                        cb = 256 * (1 + d) + hl * 128
                                        for k in range(8):
                                            op("pe", lambda e, k=k, pb=pb, cb=cb: e.matmul(PB[pb][:], lhsT=wA[:, k, cb:cb + 128], rhs=uT[ub][:, k, :],
                                                                                             start=(k == 0), stop=(k == 7)), reads=[r_uT[ub], r_wA], writes=[rPB[pb]])
                                        lb_c = lbt[:, 0, d, h:h + 1]; om_c = lbt[:, 1, d, h:h + 1]; nom_c = lbt[:, 2, d, h:h + 1]
                                        op("act", lambda e, pb=pb, tm=tm: e.activation(out=tm["sig"][:], in_=PB[pb][:], func=AF.Sigmoid), reads=[rPB[pb]], writes=[rt["sig"]])
                                        op("act", lambda e, tm=tm: e.activation(out=tm["lg"][:], in_=tm["sig"][:], func=AF.Ln, scale=om_c, bias=lb_c),
                                           reads=[rt["sig"], r_lbt], writes=[rt["lg"]])
                                        op("dve", lambda e, tm=tm: e.tensor_scalar(out=tm["kk"][:], in0=tm["sig"][:], scalar1=nom_c, scalar2=om_c, op0=ALU.mult, op1=ALU.add),
                                           reads=[rt["sig"], r_lbt], writes=[rt["kk"]])
                                        op("dve", lambda e, tm=tm: e.tensor_tensor_scan(out=tm["Bf"][:], data0=rmask[:], data1=tm["lg"][:], initial=0.0, op0=ALU.mult, op1=ALU.add),
                                           reads=[r_rmask, rt["lg"]], writes=[rt["Bf"]])
                                        B3 = tm["Bf"][:].rearrange("p (c t) -> p c t", t=64)
                                        if d == 1:
                                            lg3 = tm["lg"][:].rearrange("p (c t) -> p c t", t=64)
                                            op("dve", lambda e, B3=B3, lg3=lg3: e.tensor_tensor(out=lg3, in0=lg3, in1=B3, op=ALU.subtract), reads=[rt["lg"], rt["Bf"]], writes=[rt["lg"]])
                                            op("dve", lambda e, B3=B3, lg3=lg3: e.tensor_tensor(out=B3, in0=lg3, in1=B3[:, :, 63:64].broadcast_to([128, 8, 64]), op=ALU.add),
                                               reads=[rt["lg"], rt["Bf"]], writes=[rt["Bf"]])
                                            iref, iend = 32, 0
                                        else:
                                            iref, iend = 31, 63
                                        op("act", lambda e, B3=B3, iref=iref: e.activation(out=Bref[:, d, hl, g * 8:(g + 1) * 8], in_=B3[:, :, iref], func=AF.Copy), reads=[rt["Bf"]], writes=[r_B])
                                        op("act", lambda e, B3=B3, iend=iend: e.activation(out=Bend[:, d, hl, g * 8:(g + 1) * 8], in_=B3[:, :, iend], func=AF.Copy), reads=[rt["Bf"]], writes=[r_B])
                                        op("dve", lambda e, B3=B3, tm=tm, iref=iref: e.tensor_tensor(out=tm["dd"][:].rearrange("p (c t) -> p c t", t=64), in0=B3,
                                                                                                       in1=B3[:, :, iref:iref + 1].broadcast_to([128, 8, 64]), op=ALU.subtract),
                                           reads=[rt["Bf"]], writes=[rt["dd"]])
                                        op("act", lambda e, tm=tm: e.activation(out=tm["Eq"][:], in_=tm["dd"][:], func=AF.Exp), reads=[rt["dd"]], writes=[rt["Eq"]])
                                        op("act", lambda e, tm=tm: e.activation(out=tm["Ek"][:], in_=tm["dd"][:], func=AF.Exp, scale=-1.0), reads=[rt["dd"]], writes=[rt["Ek"]])
                                        op("dve", lambda e, tm=tm, d=d, hl=hl: e.tensor_tensor(out=KT[d][:, hl, gs], in0=tm["kk"][:], in1=tm["Ek"][:], op=ALU.mult),
                                           reads=[rt["kk"], rt["Ek"]], writes=[r_QK[d][hl][g]])
                                        op("dve", lambda e, tm=tm, d=d, hl=hl: e.tensor_tensor(out=QT[d][:, hl, gs], in0=qs[hl][:], in1=tm["Eq"][:], op=ALU.mult),
                                           reads=[r_qs[hl], rt["Eq"]], writes=[r_QK[d][hl][g]])
                                if hp == 0:
                                    pb1 = bank_rot.next()
                                    pb2 = bank_rot.next()
                                    for (pbx, cb) in ((pb1, 1024), (pb2, 1152)):
                                        for k in range(8):
                                            op("pe", lambda e, k=k, pbx=pbx, cb=cb: e.matmul(PB[pbx][:], lhsT=wA[:, k, cb:cb + 128], rhs=uT[ub][:, k, :], start=(k == 0), stop=(k == 7)),
                                               reads=[r_uT[ub], r_wA], writes=[rPB[pbx]])
                                    tm = tmps[tctr % NTMP]; rt = r_tmps[tctr % NTMP]; tctr += 1
                                    op("dve", lambda e, tm=tm: e.tensor_tensor(out=tm["sig"][:], in0=PB[pb1][:], in1=cosT[:, gs], op=ALU.mult), reads=[rPB[pb1], r_cs], writes=[rt["sig"]])
                                    op("dve", lambda e, tm=tm: e.tensor_tensor(out=tm["lg"][:], in0=PB[pb2][:], in1=sinT[:, gs], op=ALU.mult), reads=[rPB[pb2], r_cs], writes=[rt["lg"]])
                                    op("dve", lambda e, tm=tm: e.tensor_tensor(out=kTt[:, gs], in0=tm["sig"][:], in1=tm["lg"][:], op=ALU.add), reads=[rt["sig"], rt["lg"]], writes=[r_kT[g]])
                                    for t in range(4):
                                        pb = bank_rot.next()
                                        for k in range(8):
                                            op("pe", lambda e, k=k, pb=pb, t=t: e.matmul(PB[pb][:, 0:128], lhsT=uT[ub][:, k, t * 128:(t + 1) * 128], rhs=wA[:, k, 1280:1408],
                                                                                          start=(k == 0), stop=(k == 7)), reads=[r_uT[ub], r_wA], writes=[rPB[pb]])
                                        op("act", lambda e, pb=pb, t=t: e.activation(out=vtk[:, g * 4 + t, :], in_=PB[pb][:, 0:128], func=AF.Copy), reads=[rPB[pb]], writes=[r_v[g * 4 + t]])
                            op("dve", lambda e: e.tensor_tensor(out=Gx[:], in0=Bend[:], in1=Bref[:], op=ALU.subtract), reads=[r_B], writes=[r_G])
                            if NCH > 1:
                                op("dve", lambda e: e.tensor_tensor(out=Gx[:, 0, :, 0:NCH - 1], in0=Gx[:, 0, :, 0:NCH - 1], in1=Bref[:, 0, :, 1:NCH], op=ALU.add), reads=[r_B, r_G], writes=[r_G])
                                op("dve", lambda e: e.tensor_tensor(out=Gx[:, 1, :, 1:NCH], in0=Gx[:, 1, :, 1:NCH], in1=Bref[:, 1, :, 0:NCH - 1], op=ALU.add), reads=[r_B, r_G], writes=[r_G])
                            op("act", lambda e: e.activation(out=Gx[:], in_=Gx[:], func=AF.Exp), reads=[r_G], writes=[r_G])
                            if dbg and s == 0 and hp == 0:
                                for d in range(2):
                                    dma("sp", ds_dbg, dbg_d["QT"][d], QT[d][:], reads=[r_QK[d][hl][g] for hl in range(2) for g in range(NG)])
                                    dma("sp", ds_dbg, dbg_d["KT"][d], KT[d][:], reads=[r_QK[d][hl][g] for hl in range(2) for g in range(NG)])
                                dma("sp", ds_dbg, dbg_d["V"], Vt[:], reads=r_V)
                                dma("sp", ds_dbg, dbg_d["kT"], kTt[:], reads=r_kT)
                        T.barrier()

                        with ExitStack() as pb_:
                            Sm32 = SB(pb_, "Sm32", [128, 4, 128], F32); Smb = SB(pb_, "Smb", [128, 4, 128], BF16); tmp32 = SB(pb_, "tmp32", [128, 4, 128], F32)
                            r_Sm32 = [Res() for _ in range(4)]; r_Smb = [Res() for _ in range(4)]; r_tmp32 = [Res() for _ in range(4)]
                            scm = SB(pb_, "scm", [64, 2, 4, 64], BF16); r_scm = [[Res() for _ in range(4)] for _ in range(2)]
                            ktok = SB(pb_, "ktok", [64, 2, 4, 128], BF16); r_ktok = [[Res() for _ in range(4)] for _ in range(2)]
                            rB = [[None] * 4 for _ in range(2)]
                            for _p in range(2):
                                for _c in range(4):
                                    _r = Res()
                                    rB[_p][_c] = {n: _r for n in ("sc", "kt", "o", "dl")}
                            op("pool", lambda e: e.memset(Sm32[:], 0.0), writes=r_Sm32)
                            op("pool", lambda e: e.memset(Smb[:], 0.0), writes=r_Smb)
                            allQK = lambda d, hl: [r_QK[d][hl][g] for g in range(NG)]

                            def chunk_of(j, d):
                                return j if d == 0 else NCH - 1 - j

                            def stage1(j):
                                par = j % 2
                                for ci in range(4):
                                    hl, d = ci // 2, ci % 2
                                    c = chunk_of(j, d); tk = slice(c * 64, (c + 1) * 64)
                                    bk = ci + 4 * par
                                    op("pe", lambda e, bk=bk, d=d, hl=hl, tk=tk: e.matmul(PB[bk][0:64, 0:64], lhsT=KT[d][:, hl, tk], rhs=QT[d][:, hl, tk], start=True, stop=True),
                                       reads=allQK(d, hl), writes=[rB[par][ci]["sc"]])
                                    op("pe", lambda e, bk=bk, d=d, hl=hl, tk=tk: e.transpose(out=PB[bk][:].bitcast(BF16)[0:64, 128:256], in_=KT[d][:, hl, tk], identity=identb[:]),
                                       reads=allQK(d, hl) + [r_identb], writes=[rB[par][ci]["kt"]])

                            def stage2(j):
                                par = j % 2
                                for ci in range(4):
                                    hl, d = ci // 2, ci % 2
                                    bk = ci + 4 * par
                                    mk = maskf if d == 0 else maskb
                                    op("pool", lambda e, ci=ci: e.memset(scm[:, par, ci, :], 0.0), writes=[r_scm[par][ci]])
                                    op("dve", lambda e, bk=bk, ci=ci, mk=mk: e.copy_predicated(out=scm[:, par, ci, :], mask=mk[:], data=PB[bk][0:64, 0:64]),
                                       reads=[rB[par][ci]["sc"], r_mask], writes=[r_scm[par][ci]])
                                    op("act", lambda e, bk=bk, ci=ci: e.activation(out=ktok[:, par, ci, :], in_=PB[bk][:].bitcast(BF16)[0:64, 128:256], func=AF.Copy),
                                       reads=[rB[par][ci]["kt"]], writes=[r_ktok[par][ci]])

                            def stage3(j):
                                par = j % 2
                                for ci in range(4):
                                    hl, d = ci // 2, ci % 2
                                    c = chunk_of(j, d); tk = slice(c * 64, (c + 1) * 64)
                                    bk = ci + 4 * par
                                    vv = Vt[:, c, hl * 128:(hl + 1) * 128]
                                    op("pe", lambda e, bk=bk, vv=vv, ci=ci: e.matmul(PB[bk][:, 128:192], lhsT=vv, rhs=scm[:, par, ci, :], start=True, stop=False),
                                       reads=[r_V[c // 8], r_scm[par][ci]], writes=[rB[par][ci]["o"]])
                                    op("pe", lambda e, bk=bk, ci=ci, d=d, hl=hl, tk=tk: e.matmul(PB[bk][:, 128:192], lhsT=Smb[:, ci, :], rhs=QT[d][:, hl, tk], start=False, stop=True),
                                       reads=[r_Smb[ci]] + allQK(d, hl), writes=[rB[par][ci]["o"]])
                                    op("pe", lambda e, bk=bk, vv=vv, ci=ci: e.matmul(PB[bk][:, 192:320], lhsT=ktok[:, par, ci, :], rhs=vv, start=True, stop=True),
                                       reads=[r_ktok[par][ci], r_V[c // 8]], writes=[rB[par][ci]["dl"]])

                            def stage4(j):
                                par = j % 2
                                for ci in range(4):
                                    hl, d = ci // 2, ci % 2
                                    h = hp * 2 + hl
                                    c = chunk_of(j, d); tk = slice(c * 64, (c + 1) * 64)
                                    bk = ci + 4 * par
                                    if j < NCH // 2:
                                        op("act", lambda e, bk=bk, h=h, tk=tk: e.activation(out=oT[:, h, tk], in_=PB[bk][:, 128:192], func=AF.Copy),
                                           reads=[rB[par][ci]["o"]], writes=[r_oT[h][c]])
                                    else:
                                        op("dve", lambda e, bk=bk, h=h, tk=tk: e.tensor_tensor(out=oT[:, h, tk], in0=PB[bk][:, 128:192], in1=oT[:, h, tk], op=ALU.add),
                                           reads=[rB[par][ci]["o"], r_oT[h][c]], writes=[r_oT[h][c]])
                                    if j < NCH - 1:
                                        gcol = Gx[:, d, hl, c:c + 1]
                                        op("dve", lambda e, bk=bk, ci=ci: e.tensor_tensor(out=tmp32[:, ci, :], in0=PB[bk][:, 192:320], in1=Sm32[:, ci, :], op=ALU.add),
                                           reads=[rB[par][ci]["dl"], r_Sm32[ci]], writes=[r_tmp32[ci]])
                                        op("act", lambda e, ci=ci, gcol=gcol: e.activation(out=Sm32[:, ci, :], in_=tmp32[:, ci, :], func=AF.Identity, scale=gcol),
                                           reads=[r_tmp32[ci], r_G], writes=[r_Sm32[ci]])
                                        op("pool", lambda e, ci=ci: e.tensor_copy(out=Smb[:, ci, :], in_=Sm32[:, ci, :]),
                                           reads=[r_Sm32[ci]], writes=[r_Smb[ci]])

                            stage1(0)
                            stage2(0)
                            for j in range(NCH):
                                if j + 1 < NCH:
                                    stage1(j + 1)
                                stage3(j)
                                if j + 1 < NCH:
                                    stage2(j + 1)
                                stage4(j)
                        T.barrier()
                if dbg and s == 0:
                    dma("sp", ds_dbg, dbg_d["oT"], oT[:], reads=[r for rr in r_oT for r in rr])

                ogA = SB(pab, "ogA", [128, 4, S], BF16); atA = SB(pab, "atA", [128, 4, S], BF16)
                r_ogA = [Res() for _ in range(NG)]; r_atA = [Res() for _ in range(NT)]
                with ExitStack() as pc:
                    wC = SB(pc, "wC", [128, 8, 1536], BF16); r_wC = Res()
                    dma("pool", ds_w, wC[:, :, 0:512], w_in_k[:, :, O_RG:O_RG + 512], writes=[r_wC])
                    for m in range(4):
                        for hf in range(2):
                            hq = m + 4 * hf
                            cdst = 512 + m * 128 + hf * 64
                            dma("pool", ds_w, wC[:, :, cdst:cdst + 64], w_in_k[:, :, O_AQ + hq * 64:O_AQ + hq * 64 + 64], writes=[r_wC])
                            dma("pool", ds_w, wC[:, :, 512 + cdst:512 + cdst + 32], w_in_k[:, :, O_AQ + hq * 64 + 32:O_AQ + hq * 64 + 64], writes=[r_wC])
                            dma("pool", ds_w, wC[:, :, 512 + cdst + 32:512 + cdst + 64], w_in_k[:, :, O_AQ + hq * 64:O_AQ + hq * 64 + 32], writes=[r_wC])
                    xs = [SB(pc, "cxs%d" % i, [128, D], F32) for i in range(2)]; r_xs = [Res() for _ in range(2)]
                    uT = SB(pc, "cuT", [128, 8, 512], BF16); r_uT = Res()
                    qT = SB(pc, "qT", [128, 4, 512], BF16); r_qT = Res()
                    ta = [SB(pc, "cta%d" % i, [128, 512], F32) for i in range(4)]; r_ta = [Res() for _ in range(4)]
                    msc = [SB(pc, "msc%d" % i, [128, 384], F32) for i in range(2)]; r_msc = [Res(), Res()]
                    pex = [SB(pc, "pex%d" % i, [128, 384], BF16) for i in range(2)]; r_pex = [Res(), Res()]
                    pT = [SB(pc, "pT%d" % i, [128, 3, 128], BF16) for i in range(2)]; r_pT = [Res(), Res()]
                    sm = [SB(pc, "smx%d" % i, [128, 8], F32) for i in range(2)]; r_sm = [Res(), Res()]
                    sm2 = [SB(pc, "smy%d" % i, [128, 8], F32) for i in range(2)]; r_sm2 = [Res(), Res()]
                    ao = [SB(pc, "ao%d" % i, [128, 512], BF16) for i in range(2)]; r_ao = [Res(), Res()]
                    hctr = 0
                    for g in range(NG):
                        gs = slice(g * 512, (g + 1) * 512)
                        for t in range(4):
                            tile_i = g * 4 + t
                            sl = tile_i % 2
                            dma("sp", ds_x[sl], xs[sl][:], x_d[s, tile_i * 128:(tile_i + 1) * 128, :], writes=[r_xs[sl]])
                            make_uT(xs[sl], r_xs[sl], uT[:, :, t * 128:(t + 1) * 128], r_uT, 1, 0)
                        for m in range(4):
                            for (pbx, cb) in ((2, 512 + m * 128), (3, 1024 + m * 128)):
                                for k in range(8):
                                    op("pe", lambda e, k=k, pbx=pbx, cb=cb: e.matmul(PB[pbx][:], lhsT=wC[:, k, cb:cb + 128], rhs=uT[:, k, :], start=(k == 0), stop=(k == 7)),
                                       reads=[r_uT, r_wC], writes=[rPB[pbx]])
                            op("dve", lambda e: e.tensor_tensor(out=ta[0][:], in0=PB[2][:], in1=cosT[:, gs], op=ALU.mult), reads=[rPB[2], r_cs], writes=[r_ta[0]])
                            op("dve", lambda e: e.tensor_tensor(out=ta[1][:], in0=PB[3][:], in1=sinT[:, gs], op=ALU.mult), reads=[rPB[3], r_cs], writes=[r_ta[1]])
                            op("dve", lambda e, m=m: e.tensor_tensor(out=qT[:, m, :], in0=ta[0][:], in1=ta[1][:], op=ALU.add), reads=[r_ta[0], r_ta[1]], writes=[r_qT])
                        for h in range(4):
                            for k in range(8):
                                op("pe", lambda e, k=k, h=h: e.matmul(PB[4][:], lhsT=wC[:, k, h * 128:(h + 1) * 128], rhs=uT[:, k, :], start=(k == 0), stop=(k == 7)),
                                   reads=[r_uT, r_wC], writes=[rPB[4]])
                            op("act", lambda e: e.activation(out=ta[0][:], in_=PB[4][:], func=AF.Silu), reads=[rPB[4]], writes=[r_ta[0]])
                            rr = [r_oT[h][c] for c in range(g * 8, (g + 1) * 8)]
                            op("act", lambda e, h=h: e.activation(out=ta[2][:], in_=oT[:, h, gs], func=AF.Square), reads=rr, writes=[r_ta[2]])
                            op("pe", lambda e: e.matmul(PB[5][:], lhsT=onesf[:], rhs=ta[2][:], start=True, stop=True), reads=[r_onesf, r_ta[2]], writes=[rPB[5]])
                            op("act", lambda e: e.activation(out=ta[3][:], in_=PB[5][:], func=AF.Ln, scale=1.0 / 128.0, bias=epsc[:, 1:2]), reads=[rPB[5], r_eps], writes=[r_ta[3]])
                            op("act", lambda e: e.activation(out=ta[3][:], in_=ta[3][:], func=AF.Exp, scale=-0.5), reads=[r_ta[3]], writes=[r_ta[3]])
                            op("dve", lambda e, h=h: e.tensor_tensor(out=ta[2][:], in0=oT[:, h, gs], in1=ta[3][:], op=ALU.mult), reads=rr + [r_ta[3]], writes=[r_ta[2]])
                            op("dve", lambda e, h=h: e.scalar_tensor_tensor(out=ogA[:, h, gs], in0=ta[0][:], scalar=ngc[:, 0:1], in1=ta[2][:], op0=ALU.mult, op1=ALU.mult),
                               reads=[r_ta[0], r_ngc, r_ta[2]], writes=[r_ogA[g]])
                        def att_geom(t):
                            j = g * 4 + t
                            k0 = max(0, j - 1); k1 = min(NT, j + 2)
                            return j, k0, k1, k1 - k0, (k1 - k0) * 128, (128 if j == 0 else 0)

                        def att_s1_pe(idx):
                            t, h = idx // 8, idx % 8
                            j, k0, k1, nkb, nk, mc0 = att_geom(t)
                            m, hf = h % 4, h // 4
                            ps = slice(hf * 64, (hf + 1) * 64)
                            ib = idx % 2
                            pbs = 2 + ib
                            kgs = [r_kT[gg] for gg in range((k0 * 128) // 512, ((k1 * 128) - 1) // 512 + 1)]
                            op("pe", lambda e: e.matmul(PB[pbs][:, 0:nk], lhsT=qT[ps, m, t * 128:(t + 1) * 128], rhs=kTt[ps, k0 * 128:k1 * 128], start=True, stop=True),
                               reads=[r_qT] + kgs, writes=[rPB[pbs]])

                        def att_s1_ew(idx):
                            t, h = idx // 8, idx % 8
                            j, k0, k1, nkb, nk, mc0 = att_geom(t)
                            ib = idx % 2
                            pbs = 2 + ib
                            op("dve", lambda e: e.tensor_tensor(out=msc[ib][:, 0:nk], in0=PB[pbs][:, 0:nk], in1=amask[:, mc0:mc0 + nk], op=ALU.add),
                               reads=[rPB[pbs], r_amask], writes=[r_msc[ib]])
                            op("dve", lambda e: e.reduce_max(out=sm[ib][:, 0:1], in_=msc[ib][:, 0:nk], axis=AX.X), reads=[r_msc[ib]], writes=[r_sm[ib]])
                            op("dve", lambda e: e.tensor_scalar(out=sm[ib][:, 1:2], in0=sm[ib][:, 0:1], scalar1=-0.125, scalar2=sinkb[:, 1, h:h + 1], op0=ALU.mult, op1=ALU.min),
                               reads=[r_sm[ib], r_sink], writes=[r_sm[ib]])
                            op("act", lambda e: e.activation(out=pex[ib][:, 0:nk], in_=msc[ib][:, 0:nk], func=AF.Exp, scale=0.125, bias=sm[ib][:, 1:2], accum_out=sm2[ib][:, 0:1]),
                               reads=[r_msc[ib], r_sm[ib]], writes=[r_pex[ib], r_sm2[ib]])
                            op("act", lambda e: e.activation(out=sm2[ib][:, 1:2], in_=sinkb[:, 0, h:h + 1], func=AF.Exp, bias=sm[ib][:, 1:2], scale=1.0),
                               reads=[r_sm[ib], r_sink], writes=[r_sm2[ib]])

                        def att_s2a(idx):
                            t, h = idx // 8, idx % 8
                            j, k0, k1, nkb, nk, mc0 = att_geom(t)
                            ib = idx % 2
                            pbt = 4 + ib
                            for b in range(nkb):
                                op("pe", lambda e, b=b: e.transpose(out=PB[pbt][:].bitcast(BF16)[:, b * 128:(b + 1) * 128], in_=pex[ib][:, b * 128:(b + 1) * 128], identity=identb[:]),
                                   reads=[r_pex[ib], r_identb], writes=[rPB[pbt]])
                            if ib == 0:
                                op("act", lambda e: e.activation(out=pT[ib][:, 0:nkb, :], in_=PB[pbt][:].bitcast(BF16)[:, 0:nk].rearrange("p (b q) -> p b q", q=128), func=AF.Copy),
                                   reads=[rPB[pbt]], writes=[r_pT[ib]])
                            else:
                                op("dve", lambda e: e.tensor_copy(out=pT[ib][:, 0:nkb, :], in_=PB[pbt][:].bitcast(BF16)[:, 0:nk].rearrange("p (b q) -> p b q", q=128)),
                                   reads=[rPB[pbt]], writes=[r_pT[ib]])

                        def att_s2b(idx):
                            t, h = idx // 8, idx % 8
                            j, k0, k1, nkb, nk, mc0 = att_geom(t)
                            hf = h // 4
                            ib = idx % 2
                            op("dve", lambda e: e.tensor_tensor(out=sm2[ib][:, 2:3], in0=sm2[ib][:, 0:1], in1=sm2[ib][:, 1:2], op=ALU.add), reads=[r_sm2[ib]], writes=[r_sm2[ib]])
                            op("dve", lambda e: e.reciprocal(out=sm2[ib][:, 3:4], in_=sm2[ib][:, 2:3]), reads=[r_sm2[ib]], writes=[r_sm2[ib]])
                            pbo = 6 + ib
                            for b in range(nkb):
                                op("pe", lambda e, b=b: e.matmul(PB[pbo][:, 0:64], lhsT=pT[ib][:, b, :], rhs=vtk[:, k0 + b, hf * 64:(hf + 1) * 64], start=(b == 0), stop=(b == nkb - 1)),
                                   reads=[r_pT[ib], r_v[k0 + b]], writes=[rPB[pbo]])
                            aot = ao[t % 2]; r_aot = r_ao[t % 2]
                            op("act", lambda e: e.activation(out=aot[:, h * 64:(h + 1) * 64], in_=PB[pbo][:, 0:64], func=AF.Identity, scale=sm2[ib][:, 3:4]),
                               reads=[rPB[pbo], r_sm2[ib]], writes=[r_aot])
                            if h == 7:
                                pba = t % 2
                                for m in range(4):
                                    op("pe", lambda e, m=m: e.transpose(out=PB[pba][:].bitcast(BF16)[:, m * 128:(m + 1) * 128], in_=aot[:, m * 128:(m + 1) * 128], identity=identb[:]),
                                       reads=[r_aot, r_identb], writes=[rPB[pba]])
                                op("dve", lambda e: e.tensor_copy(out=atA[:, :, j * 128:(j + 1) * 128], in_=PB[pba][:].bitcast(BF16)[:, 0:512].rearrange("p (m q) -> p m q", q=128)),
                                   reads=[rPB[pba]], writes=[r_atA[j]])

                        att_s1_pe(0)
                        att_s1_ew(0)
                        for idx in range(32):
                            if idx + 1 < 32:
                                att_s1_pe(idx + 1)
                            att_s2a(idx)
                            if idx + 1 < 32:
                                att_s1_ew(idx + 1)
                            att_s2b(idx)
                T.barrier()

                with ExitStack() as pc:
                    if S >= 2048:
                        wG = oT[:].rearrange("p h s -> p (h s)").bitcast(BF16)
                        wGv = wG[:, 0:8 * 2048].rearrange("p (k n) -> p k n", n=2048)
                    else:
                        wGv = SB(pc, "wGt", [128, 8, 2048], BF16)[:]
                    r_wG = Res()
                    wR = SB(pc, "wR", [128, 4, D], BF16); wT = SB(pc, "wT", [128, 4, D], BF16); wO = SB(pc, "wO", [128, 8, D], BF16)
                    lnB = SB(pc, "lnB1", [128, 2, D], F32); r_lnB = Res()
                    r_wRT = Res(); r_wGc = [Res() for _ in range(8)]; r_wOr = Res()
                    dma("pool", ds_wc[8], wR[:], w_rec_k, writes=[r_wRT])
                    dma("pool", ds_wc[8], wT[:], w_att_k, writes=[r_wRT])
                    for dcq in range(8):
                        dma("pool", ds_wc[dcq], wGv[:, :, dcq * 128:(dcq + 1) * 128], w_in_k[:, :, O_GR + dcq * 128:O_GR + (dcq + 1) * 128], writes=[r_wGc[dcq]])
                        dma("pool", ds_wc[dcq], wGv[:, :, 1024 + dcq * 128:1024 + (dcq + 1) * 128], w_in_k[:, :, O_GA + dcq * 128:O_GA + (dcq + 1) * 128], writes=[r_wGc[dcq]])
                    for q2 in range(2):
                        dma("pool", ds_wc[9], wO[:, :, q2 * 512:(q2 + 1) * 512], w_out_k[:, :, q2 * 512:(q2 + 1) * 512], writes=[r_wOr])
                    dma("sp", ds_const, lnB[:, 0, :], ln1g_d[0:1, :].broadcast_to([128, D]), writes=[r_lnB])
                    dma("sp", ds_const, lnB[:, 1, :], ln1b_d[0:1, :].broadcast_to([128, D]), writes=[r_lnB])
                    zb = zer[:, 0:8].bitcast(BF16)[:, 0:8].unsqueeze(2)
                    dma("sp", ds_misc[3], u2T_d[s, :, 0:1].rearrange("(j p) o -> p j o", p=128), zb, reads=[r_zer], allow_slow_non_contiguous=True)
                    dma("sp", ds_misc[3], u2T_d[s, :, S + 1:S + 2].rearrange("(j p) o -> p j o", p=128), zb, reads=[r_zer], allow_slow_non_contiguous=True)
                    r_x1d = [Res() for _ in range(NT)]
                    r_u2d = [Res() for _ in range(NT)]
                    xs = [SB(pc, "bxs%d" % i, [128, D], F32) for i in range(4)]; r_xs = [Res() for _ in range(4)]
                    uT = SB(pc, "buT", [128, 8, 512], BF16); r_uT = Res()
                    mT = SB(pc, "mT", [128, 8, 512], BF16); r_mT = Res()
                    ta = [SB(pc, "bta%d" % i, [128, 512], F32) for i in range(4)]; r_ta = [Res() for _ in range(4)]
                    yt = [SB(pc, "yt%d" % i, [128, D], F32) for i in range(2)]; r_yt = [Res(), Res()]
                    st6 = [SB(pc, "st6%d" % i, [128, 20], F32) for i in range(2)]; r_st6 = [Res(), Res()]
                    u2t = [SB(pc, "u2t%d" % i, [128, 8, 128], BF16) for i in range(2)]; r_u2t = [Res(), Res()]
                    for g in range(NG):
                        gs = slice(g * 512, (g + 1) * 512)
                        for t in range(4):
                            tile_i = g * 4 + t
                            dma("sp", ds_x[t], xs[t][:], x_d[s, tile_i * 128:(tile_i + 1) * 128, :], writes=[r_xs[t]])
                            make_uT(xs[t], r_xs[t], uT[:, :, t * 128:(t + 1) * 128], r_uT, 1, 0)
                        BSETS = ((2, 3, 4, 5), (0, 1, 6, 7))

                        def dc_pe(dc):
                            b0, b1, b2, b3 = BSETS[dc % 2]
                            dcs = slice(dc * 128, (dc + 1) * 128)
                            for k in range(8):
                                op("pe", lambda e, k=k: e.matmul(PB[b0][:], lhsT=wGv[:, k, dc * 128:(dc + 1) * 128], rhs=uT[:, k, :], start=(k == 0), stop=(k == 7)),
                                   reads=[r_uT, r_wGc[dc]], writes=[rPB[b0]])
                            for k in range(8):
                                op("pe", lambda e, k=k: e.matmul(PB[b1][:], lhsT=wGv[:, k, 1024 + dc * 128:1024 + (dc + 1) * 128], rhs=uT[:, k, :], start=(k == 0), stop=(k == 7)),
                                   reads=[r_uT, r_wGc[dc]], writes=[rPB[b1]])
                            for h in range(4):
                                op("pe", lambda e, h=h: e.matmul(PB[b2][:], lhsT=wR[:, h, dcs], rhs=ogA[:, h, gs], start=(h == 0), stop=(h == 3)),
                                   reads=[r_wRT, r_ogA[g]], writes=[rPB[b2]])
                            for m in range(4):
                                op("pe", lambda e, m=m: e.matmul(PB[b3][:], lhsT=wT[:, m, dcs], rhs=atA[:, m, gs], start=(m == 0), stop=(m == 3)),
                                   reads=[r_wRT] + r_atA[g * 4:(g + 1) * 4], writes=[rPB[b3]])

                        def dc_ew(dc):
                            b0, b1, b2, b3 = BSETS[dc % 2]
                            a0, a1 = ta[2 * (dc % 2)], ta[2 * (dc % 2) + 1]
                            ra0, ra1 = r_ta[2 * (dc % 2)], r_ta[2 * (dc % 2) + 1]
                            op("act", lambda e: e.activation(out=a0[:], in_=PB[b0][:], func=AF.Sigmoid), reads=[rPB[b0]], writes=[ra0])
                            op("act", lambda e: e.activation(out=a1[:], in_=PB[b1][:], func=AF.Sigmoid), reads=[rPB[b1]], writes=[ra1])
                            op("dve", lambda e: e.tensor_tensor(out=a0[:], in0=PB[b2][:], in1=a0[:], op=ALU.mult), reads=[rPB[b2], ra0], writes=[ra0])
                            op("dve", lambda e: e.tensor_tensor(out=a1[:], in0=PB[b3][:], in1=a1[:], op=ALU.mult), reads=[rPB[b3], ra1], writes=[ra1])
                            op("dve", lambda e: e.tensor_tensor(out=mT[:, dc, :], in0=a0[:], in1=a1[:], op=ALU.add), reads=[ra0, ra1], writes=[r_mT])

                        dc_pe(0)
                        for dc in range(8):
                            if dc + 1 < 8:
                                dc_pe(dc + 1)
                            dc_ew(dc)
                        def wout_pe(t):
                            tile_i = g * 4 + t
                            wb = 2 + 2 * (tile_i % 2)
                            for hh in range(2):
                                for k in range(8):
                                    op("pe", lambda e, k=k, hh=hh: e.matmul(PB[wb + hh][:], lhsT=mT[:, k, t * 128:(t + 1) * 128], rhs=wO[:, k, hh * 512:(hh + 1) * 512], start=(k == 0), stop=(k == 7)),
                                       reads=[r_mT, r_wOr], writes=[rPB[wb + hh]])

                        wout_pe(0)
                        for t in range(4):
                            tile_i = g * 4 + t
                            ib = tile_i % 2
                            wb = 2 + 2 * (tile_i % 2)
                            ln_part1((wb, wb + 1), (rPB[wb], rPB[wb + 1]), 0, yt[ib], r_yt[ib])
                            if t + 1 < 4:
                                wout_pe(t + 1)
                            ln_part2(xs[t][:], r_xs[t], lnB[:, 0, :], lnB[:, 1, :], r_lnB, yt[ib], r_yt[ib], st6[ib], r_st6[ib])
                            dma("sp", ds_o[ib], x1_d[s, tile_i * 128:(tile_i + 1) * 128, :], yt[ib][:], reads=[r_yt[ib]], writes=[r_x1d[tile_i]])
                            make_uT(yt[ib], r_yt[ib], u2t[ib][:], r_u2t[ib], 3, 2, banks=(6, 7))
                            dma("sp", ds_o[2 + ib], u2T_d[s, :, 1 + tile_i * 128:1 + (tile_i + 1) * 128].rearrange("(j p) q -> p j q", p=128), u2t[ib][:],
                                reads=[r_u2t[ib]], writes=[r_u2d[tile_i]])
                T.barrier()
            T.barrier()

            with ExitStack() as pf:
                wU = SB(pf, "wU", [128, 8, 2 * DFF], BF16); wD = SB(pf, "wD", [128, NFC, D], BF16); r_wU = Res()
                lnB = SB(pf, "lnB2", [128, 2, D], F32); r_lnB = Res()
                cvw = SB(pf, "cvw", [128, 4, 2 * NFC], F32); r_cvw = r_lnB
                r_wc = [Res() for _ in range(11)]
                for jq in range(11):
                    dma("pool", ds_wc[jq], wU[:, :, jq * 256:(jq + 1) * 256], w_up_k[:, :, jq * 256:(jq + 1) * 256], writes=[r_wc[jq]])
                    dma("pool", ds_wc[jq], wU[:, :, DFF + jq * 256:DFF + (jq + 1) * 256], w_up_k[:, :, DFF + jq * 256:DFF + (jq + 1) * 256], writes=[r_wc[jq]])
                    dma("pool", ds_wc[jq], wD[:, 2 * jq:2 * jq + 2, :], w_dn_k[:, 2 * jq:2 * jq + 2, :], writes=[r_wc[jq]])
                dma("sp", ds_const, lnB[:, 0, :], ln2g_d[0:1, :].broadcast_to([128, D]), writes=[r_lnB])
                dma("sp", ds_const, lnB[:, 1, :], ln2b_d[0:1, :].broadcast_to([128, D]), writes=[r_lnB])
                dma("sp", ds_const, cvw[:, 0:3, :], cw_d.rearrange("k (j p) -> p k j", p=128), writes=[r_cvw], allow_slow_non_contiguous=True)
                dma("sp", ds_const, cvw[:, 3, :], cb_d.rearrange("o (j p) -> p (o j)", p=128), writes=[r_cvw], allow_slow_non_contiguous=True)
                NSG = S // 256
                u2 = [SB(pf, "fu2%d" % i, [128, 8, 258], BF16) for i in range(2)]; r_u2 = [Res(), Res()]
                x1t = [SB(pf, "fx1%d" % i, [128, D], F32) for i in range(4)]; r_x1t = [Res() for _ in range(4)]
                cv = [SB(pf, "fcv%d" % i, [128, 256], F32) for i in range(2)]; r_cv = [Res(), Res()]
                cgt = [SB(pf, "fcg%d" % i, [128, 256], F32) for i in range(2)]; r_cg = [Res(), Res()]
                gz = [SB(pf, "fgz%d" % i, [128, 256], F32) for i in range(2)]; r_gz = [Res(), Res()]
                actb = [SB(pf, "fact%d" % i, [128, 256], BF16) for i in range(2)]; r_act = [Res(), Res()]
                yt = [SB(pf, "fyt%d" % i, [128, D], F32) for i in range(2)]; r_yt = [Res(), Res()]
                st6 = [SB(pf, "fst6%d" % i, [128, 20], F32) for i in range(2)]; r_st6 = [Res(), Res()]
                def c2_up(sg, c):
                    ub = sg % 2; ib = c % 2
                    pv, pg = 4 + 2 * ib, 5 + 2 * ib
                    for (pbx, cb) in ((pv, c * 128), (pg, DFF + c * 128)):
                        for k in range(8):
                            op("pe", lambda e, k=k, pbx=pbx, cb=cb: e.matmul(PB[pbx][:, 0:258], lhsT=wU[:, k, cb:cb + 128], rhs=u2[ub][:, k, :], start=(k == 0), stop=(k == 7)),
                               reads=[r_wc[c // 2], r_u2[ub]], writes=[rPB[pbx]])

                def c2_s1(c):
                    ib = c % 2
                    pv, pg = 4 + 2 * ib, 5 + 2 * ib
                    for (pbx, dst, r_dst, col) in ((pv, cv[ib], r_cv[ib], c), (pg, cgt[ib], r_cg[ib], NFC + c)):
                        op("act", lambda e, pbx=pbx, dst=dst, col=col: e.activation(out=dst[:], in_=PB[pbx][:, 0:256], func=AF.Identity, scale=cvw[:, 0, col:col + 1], bias=cvw[:, 3, col:col + 1]),
                           reads=[rPB[pbx], r_cvw], writes=[r_dst])

                def c2_s2(c):
                    ib = c % 2
                    pv, pg = 4 + 2 * ib, 5 + 2 * ib
                    for (pbx, dst, r_dst, col) in ((pg, cgt[ib], r_cg[ib], NFC + c), (pv, cv[ib], r_cv[ib], c)):
                        op("dve", lambda e, pbx=pbx, dst=dst, col=col: e.scalar_tensor_tensor(out=dst[:], in0=PB[pbx][:, 1:257], scalar=cvw[:, 1, col:col + 1], in1=dst[:], op0=ALU.mult, op1=ALU.add),
                           reads=[rPB[pbx], r_cvw, r_dst], writes=[r_dst])
                        op("dve", lambda e, pbx=pbx, dst=dst, col=col: e.scalar_tensor_tensor(out=dst[:], in0=PB[pbx][:, 2:258], scalar=cvw[:, 2, col:col + 1], in1=dst[:], op0=ALU.mult, op1=ALU.add),
                           reads=[rPB[pbx], r_cvw, r_dst], writes=[r_dst])

                def c2_s3(c):
                    ib = c % 2
                    op("act", lambda e: e.activation(out=gz[ib][:], in_=cgt[ib][:], func=AF.Gelu_apprx_tanh), reads=[r_cg[ib]], writes=[r_gz[ib]])

                def c2_s4(c):
                    ib = c % 2
                    op("dve", lambda e: e.tensor_tensor(out=actb[ib][:], in0=gz[ib][:], in1=cv[ib][:], op=ALU.mult), reads=[r_gz[ib], r_cv[ib]], writes=[r_act[ib]])

                def c2_down(c):
                    ib = c % 2
                    for tt in range(2):
                        for hh in range(2):
                            pa_ = tt * 2 + hh
                            op("pe", lambda e, tt=tt, hh=hh, pa_=pa_: e.matmul(PB[pa_][:], lhsT=actb[ib][:, tt * 128:(tt + 1) * 128], rhs=wD[:, c, hh * 512:(hh + 1) * 512],
                                                                             start=(c == 0), stop=(c == NFC - 1)),
                               reads=[r_act[ib], r_wc[c // 2]], writes=[rPB[pa_]])

                def c2_load(sg):
                    a = sg * 256
                    ub = sg % 2
                    dma("sp", ds_misc[ub], u2[ub][:], u2T_d[s, :, a:a + 258].rearrange("(j p) q -> p j q", p=128), writes=[r_u2[ub]])
                    for tt in range(2):
                        sl = (sg * 2 + tt) % 4
                        dma("sp", ds_x[sl], x1t[sl][:], x1_d[s, a + tt * 128:a + (tt + 1) * 128, :], writes=[r_x1t[sl]])

                def c2_prologue(sg):
                    c2_up(sg, 0); c2_up(sg, 1); c2_s1(0); c2_s2(0); c2_s1(1)

                def c2_ln2(sg):
                    for tt in range(2):
                        tile_i = sg * 2 + tt
                        sl = tile_i % 4
                        ib = tile_i % 2
                        ln_part2(x1t[sl][:], r_x1t[sl], lnB[:, 0, :], lnB[:, 1, :], r_lnB, yt[ib], r_yt[ib], st6[ib], r_st6[ib])
                        dma("sp", ds_o[ib], out_d[s, tile_i * 128:(tile_i + 1) * 128, :], yt[ib][:], reads=[r_yt[ib]])

                c2_load(0)
                if NSG > 1:
                    c2_load(1)
                c2_prologue(0)
                for sg in range(NSG):
                    for c in range(NFC):
                        c2_s3(c)
                        if c + 1 < NFC:
                            c2_s2(c + 1)
                        c2_s4(c)
                        if c + 2 < NFC:
                            c2_up(sg, c + 2)
                            c2_s1(c + 2)
                        c2_down(c)
                    for tt in range(2):
                        tile_i = sg * 2 + tt
                        ib = tile_i % 2
                        ln_part1((tt * 2, tt * 2 + 1), (rPB[tt * 2], rPB[tt * 2 + 1]), 1, yt[ib], r_yt[ib])
                    if sg + 1 < NSG:
                        c2_prologue(sg + 1)
                    c2_ln2(sg)
                    if sg + 2 < NSG:
                        c2_load(sg + 2)
            T.barrier()
        print("build: instructions=%d waits=%d" % (T.n_ins, T.n_wait))
    return nc


def _consts():
    ident = np.eye(128, dtype=np.float32)
    i = np.arange(64)
    maskf = (i[:, None] <= i[None, :]).astype(np.uint8)
    maskb = (i[:, None] >= i[None, :]).astype(np.uint8)
    q = np.arange(128)[:, None]
    cc = np.arange(128)[None, :]
    NEG = -30000.0
    am = np.zeros((128, 384), np.float32)
    am[:, 0:128] = np.where(cc >= q, 0.0, NEG)
    am[:, 256:384] = np.where(cc <= q, 0.0, NEG)
    half = 32
    inv = (np.float32(10000.0) ** (-np.arange(half, dtype=np.float32) / np.float32(half))).astype(np.float32)
    p = np.arange(128)
    rope = np.stack([inv[p % 32], np.where((p % 64) < 32, -1.0, 1.0).astype(np.float32)], axis=1).astype(np.float32)
    return dict(k_ident=ident, k_maskf=maskf, k_maskb=maskb, k_amask=am, k_rope=rope)


def make_in_maps(inputs, n_cores, nseq):
    f = lambda a: np.ascontiguousarray(np.asarray(a))
    shared = dict(
        w_ada=f(inputs["w_ada"][0]), b_ada=f(inputs["b_ada"][0:1]), w_in=f(inputs["w_in"][0]),
        rec_lower_bound=f(inputs["rec_lower_bound"]), rec_norm_g=f(inputs["rec_norm_g"][0:1]),
        w_rec_branch=f(inputs["w_rec_branch"][0]), attn_sink=f(inputs["attn_sink"][0:1]),
        w_attn_branch=f(inputs["w_attn_branch"][0]), w_out=f(inputs["w_out"][0]),
        ln1_g=f(inputs["ln1_g"][0:1]), ln1_b=f(inputs["ln1_b"][0:1]), w_up=f(inputs["w_up"][0]),
        conv_w=f(inputs["conv_w"][0]), conv_b=f(inputs["conv_b"][0:1]), w_down=f(inputs["w_down"][0]),
        ln2_g=f(inputs["ln2_g"][0:1]), ln2_b=f(inputs["ln2_b"][0:1]))
    shared.update(_consts())
    maps = []
    for i in range(n_cores):
        m = dict(shared)
        m["x"] = f(inputs["x"][i * nseq:(i + 1) * nseq])
        m["c"] = f(inputs["c"][i * nseq:(i + 1) * nseq])
        m["positions"] = f(inputs["positions"][i * nseq:(i + 1) * nseq]).astype(np.int32)
        maps.append(m)
    return maps


def kernel(**inputs):
    B, S, _ = inputs["x"].shape
    nseq = B // N_CORES
    nc = build_nc(S=S, NSEQ=nseq)
    in_maps = make_in_maps(inputs, N_CORES, nseq)
    res = run_bass_kernel_spmd(nc, in_maps, core_ids=list(range(N_CORES)))
    return np.concatenate([np.asarray(r["out"]) for r in res.results], axis=0).astype(np.float32)
```

```python
import numpy as np
from contextlib import ExitStack
import concourse.bass as bass
import concourse.mybir as mybir
from concourse.bass_utils import run_bass_kernel_spmd

F32 = mybir.dt.float32
BF16 = mybir.dt.bfloat16
I32 = mybir.dt.int32
U8 = mybir.dt.uint8
AF = mybir.ActivationFunctionType
ALU = mybir.AluOpType
AX = mybir.AxisListType

D = 1024
NK = 8
DIN = 5376
DFF = 2816
NFC = DFF // 128
ALPHA = 2.0 ** 0.25
LN_EPS = 1e-5
RMS_EPS = 1e-6
N_CORES = 8
O_RQ, O_RFF, O_RFB, O_RI, O_RG, O_AQ, O_AK, O_AV, O_GR, O_GA = 0, 512, 1024, 1536, 2048, 2560, 3072, 3200, 3328, 4352


class Res:
    __slots__ = ("w", "rd")

    def __init__(self):
        self.w = None
        self.rd = []


class Tracker:
    def __init__(self, nc, stack):
        self.nc = nc
        self.stack = stack
        self.eng = {}
        self.sems = {}
        for name, h in (("pe", nc.tensor), ("act", nc.scalar), ("dve", nc.vector),
                        ("pool", nc.gpsimd), ("sp", nc.sync)):
            sem = stack.enter_context(nc.semaphore("sem_" + name))
            self.eng[name] = {"h": h, "sem": sem, "cnt": 0, "seen": {}, "name": name}
            self.sems[id(sem)] = sem
        self.dma_sems = []
        self.n_wait = 0
        self.n_ins = 0

    def new_dma_sem(self, name):
        sem = self.stack.enter_context(self.nc.semaphore(name))
        d = {"sem": sem, "cnt": 0}
        self.sems[id(sem)] = sem
        self.dma_sems.append(d)
        return d

    def _wait_deps(self, E, reads, writes):
        deps = {}

        def add(tok):
            if tok is not None and deps.get(tok[0], 0) < tok[1]:
                deps[tok[0]] = tok[1]

        for r in reads:
            add(r.w)
        for w in writes:
            add(w.w)
            for t in w.rd:
                add(t)
        own = id(E["sem"])
        for k, v in deps.items():
            if E["name"] == "pe" and k == own:
                continue
            if E["seen"].get(k, 0) >= v:
                continue
            E["h"].wait_ge(self.sems[k], v)
            E["seen"][k] = v
            self.n_wait += 1

    def _commit(self, tok, reads, writes):
        for r in reads:
            r.rd.append(tok)
            if len(r.rd) > 48:
                best = {}
                for k, v in r.rd:
                    if best.get(k, 0) < v:
                        best[k] = v
                r.rd = list(best.items())
        for w in writes:
            w.w = tok
            w.rd = []
        self.n_ins += 1

    def op(self, eng, fn, reads=(), writes=()):
        E = self.eng[eng]
        self._wait_deps(E, reads, writes)
        ins = fn(E["h"])
        E["cnt"] += 1
        ins.then_inc(E["sem"], 1)
        self._commit((id(E["sem"]), E["cnt"]), reads, writes)
        return ins

    def dma(self, q, dsem, out, in_, reads=(), writes=(), **kw):
        E = self.eng[q]
        self._wait_deps(E, reads, writes)
        ins = E["h"].dma_start(out=out, in_=in_, **kw)
        dsem["cnt"] += 16
        ins.then_inc(dsem["sem"], 16)
        self._commit((id(dsem["sem"]), dsem["cnt"]), reads, writes)
        return ins

    def barrier(self):
        targets = [(id(e["sem"]), e["cnt"]) for e in self.eng.values() if e["cnt"] > 0]
        targets += [(id(d["sem"]), d["cnt"]) for d in self.dma_sems if d["cnt"] > 0]
        for E in self.eng.values():
            for k, v in targets:
                if E["seen"].get(k, 0) >= v:
                    continue
                E["h"].wait_ge(self.sems[k], v)
                E["seen"][k] = v
                self.n_wait += 1


class Rot:
    def __init__(self, items):
        self.items = items
        self.i = 0

    def next(self):
        it = self.items[self.i % len(self.items)]
        self.i += 1
        return it


def build_nc(S=2048, NSEQ=2, dbg=False):
    NT, NG, NCH = S // 128, S // 512, S // 64
    nc = bass.Bass("TRN2", target_bir_lowering=False)

    def din(name, shape, dt=F32):
        return nc.dram_tensor(name, list(shape), dt, kind="ExternalInput").ap()

    x_d = din("x", [NSEQ, S, D])
    c_d = din("c", [NSEQ, D])
    pos_d = din("positions", [NSEQ, S], I32)
    w_ada_d = din("w_ada", [D, 6 * D])
    b_ada_d = din("b_ada", [1, 6 * D])
    w_in_d = din("w_in", [D, DIN])
    lbr_d = din("rec_lower_bound", [2, 2, 512])
    ng_d = din("rec_norm_g", [1, 128])
    w_rec_d = din("w_rec_branch", [512, D])
    sink_d = din("attn_sink", [1, 8])
    w_att_d = din("w_attn_branch", [512, D])
    w_out_d = din("w_out", [D, D])
    ln1g_d = din("ln1_g", [1, D]); ln1b_d = din("ln1_b", [1, D])
    w_up_d = din("w_up", [D, 2 * DFF])
    cw_d = din("conv_w", [3, 2 * DFF])
    cb_d = din("conv_b", [1, 2 * DFF])
    w_dn_d = din("w_down", [DFF, D])
    ln2g_d = din("ln2_g", [1, D]); ln2b_d = din("ln2_b", [1, D])
    identf_d = din("k_ident", [128, 128])
    mskf_d = din("k_maskf", [64, 64], U8)
    mskb_d = din("k_maskb", [64, 64], U8)
    amask_d = din("k_amask", [128, 384])
    rope_d = din("k_rope", [128, 2])
    out_d = nc.dram_tensor("out", [NSEQ, S, D], F32, kind="ExternalOutput").ap()
    x1_d = nc.dram_tensor("x1_scr", [NSEQ, S, D], F32).ap()
    u2T_d = nc.dram_tensor("u2T_scr", [NSEQ, D, S + 2], BF16).ap()
    dbg_d = {}
    if dbg:
        dbg_d["mods"] = nc.dram_tensor("dbg_mods", [128, 4, 8], F32, kind="ExternalOutput").ap()
        dbg_d["QT"] = nc.dram_tensor("dbg_QT", [2, 128, 2, S], BF16, kind="ExternalOutput").ap()
        dbg_d["KT"] = nc.dram_tensor("dbg_KT", [2, 128, 2, S], BF16, kind="ExternalOutput").ap()
        dbg_d["V"] = nc.dram_tensor("dbg_V", [64, NCH, 256], BF16, kind="ExternalOutput").ap()
        dbg_d["oT"] = nc.dram_tensor("dbg_oT", [128, 4, S], F32, kind="ExternalOutput").ap()
        dbg_d["kT"] = nc.dram_tensor("dbg_kT", [128, S], BF16, kind="ExternalOutput").ap()
        dbg_d["x1"] = x1_d

    w_in_k = w_in_d.rearrange("(j p) n -> p j n", p=128)
    w_ada_k = w_ada_d.rearrange("(j p) n -> p j n", p=128)
    w_up_k = w_up_d.rearrange("(j p) n -> p j n", p=128)
    w_out_k = w_out_d.rearrange("(j p) n -> p j n", p=128)
    w_rec_k = w_rec_d.rearrange("(j p) n -> p j n", p=128)
    w_att_k = w_att_d.rearrange("(j p) n -> p j n", p=128)
    w_dn_k = w_dn_d.rearrange("(j p) n -> p j n", p=128)

    with ExitStack() as top:
        T = Tracker(nc, top)
        op, dma = T.op, T.dma

        uniq = [0]

        def SB(st, name, shape, dt):
            uniq[0] += 1
            return st.enter_context(nc.sbuf_tensor("%s_%d" % (name, uniq[0]), list(shape), dt))

        PB = [top.enter_context(nc.psum_tensor("pb%d" % i, [128, 512], F32)) for i in range(8)]
        rPB = [Res() for _ in range(8)]
        ds_const = T.new_dma_sem("ds_const")
        ds_w = T.new_dma_sem("ds_w")
        ds_x = [T.new_dma_sem("ds_x%d" % i) for i in range(4)]
        ds_o = [T.new_dma_sem("ds_o%d" % i) for i in range(4)]
        ds_misc = [T.new_dma_sem("ds_m%d" % i) for i in range(4)]
        ds_dbg = T.new_dma_sem("ds_dbg")
        ds_wc = [T.new_dma_sem("ds_wc%d" % i) for i in range(11)]

        identf = SB(top, "identf", [128, 128], F32); r_identf = Res()
        identb = SB(top, "identb", [128, 128], BF16); r_identb = Res()
        onesf = SB(top, "onesf", [128, 128], F32); r_onesf = Res()
        maskf = SB(top, "maskf", [64, 64], U8); maskb = SB(top, "maskb", [64, 64], U8); r_mask = Res()
        zer = SB(top, "zer", [128, 128], F32); r_zer = Res()
        rmask = SB(top, "rmask", [128, 512], F32); r_rmask = Res()
        lbt = SB(top, "lbt", [128, 3, 2, 4], F32); r_lbt = Res()
        lraw = SB(top, "lraw", [128, 2, 2, 4], F32); r_lraw = Res()
        ngc = SB(top, "ngc", [128, 1], F32); r_ngc = Res()
        sinkb = SB(top, "sinkb", [128, 2, 8], F32); r_sink = Res()
        rope = SB(top, "rope", [128, 2], F32); r_rope = Res()
        epsc = SB(top, "epsc", [128, 2], F32); r_eps = Res()
        modsT_all = SB(top, "modsT", [128, NSEQ, 4, 8], F32); r_modsT = Res()
        gaB_all = SB(top, "gaB", [128, NSEQ, 2, D], F32); r_gaB = [Res(), Res()]
        modsT = modsT_all[:, 0]; gaB = gaB_all[:, 0]
        amask = SB(top, "amask", [128, 384], F32); r_amask = Res()

        dma("sp", ds_const, identf[:], identf_d, writes=[r_identf])
        dma("sp", ds_const, maskf[:], mskf_d, writes=[r_mask])
        dma("sp", ds_const, maskb[:], mskb_d, writes=[r_mask])
        dma("sp", ds_const, rope[:], rope_d, writes=[r_rope])
        dma("sp", ds_const, amask[:], amask_d, writes=[r_amask])
        dma("sp", ds_const, ngc[:], ng_d.rearrange("o p -> p o"), writes=[r_ngc], allow_slow_non_contiguous=True)
        dma("sp", ds_const, sinkb[:, 0, :], sink_d[0:1, :].broadcast_to([128, 8]), writes=[r_sink])
        dma("sp", ds_const, lraw[:].rearrange("p a b h -> p (a b) h"),
            lbr_d.rearrange("a b (h p) -> p (a b) h", p=128), writes=[r_lraw], allow_slow_non_contiguous=True)
        T.barrier()
        op("dve", lambda e: e.tensor_copy(out=identb[:], in_=identf[:]), reads=[r_identf], writes=[r_identb])
        op("pool", lambda e: e.memset(onesf[:], 1.0), writes=[r_onesf])
        op("pool", lambda e: e.memset(zer[:], 0.0), writes=[r_zer])
        op("pool", lambda e: e.memset(rmask[:], 1.0), writes=[r_rmask])
        op("pool", lambda e: e.memset(rmask[:].rearrange("p (c t) -> p c t", t=64)[:, :, 0:1], 0.0), writes=[r_rmask])
        op("pool", lambda e: e.memset(epsc[:, 0:1], LN_EPS), writes=[r_eps])
        op("pool", lambda e: e.memset(epsc[:, 1:2], RMS_EPS), writes=[r_eps])
        op("dve", lambda e: e.tensor_scalar(out=sinkb[:, 1, :], in0=sinkb[:, 0, :], scalar1=-1.0, scalar2=None, op0=ALU.mult),
           reads=[r_sink], writes=[r_sink])
        op("dve", lambda e: e.tensor_tensor(out=lbt[:, 2, :, :], in0=lraw[:, :, 0, :], in1=lraw[:, :, 1, :], op=ALU.subtract),
           reads=[r_lraw], writes=[r_lbt])
        op("act", lambda e: e.activation(out=lbt[:, 0, :, :], in_=lbt[:, 2, :, :], func=AF.Sigmoid), reads=[r_lbt], writes=[r_lbt])
        op("act", lambda e: e.activation(out=lbt[:, 1, :, :], in_=lbt[:, 2, :, :], func=AF.Sigmoid, scale=-1.0), reads=[r_lbt], writes=[r_lbt])
        op("dve", lambda e: e.tensor_scalar(out=lbt[:, 2, :, :], in0=lbt[:, 1, :, :], scalar1=-1.0, scalar2=None, op0=ALU.mult),
           reads=[r_lbt], writes=[r_lbt])

        def make_uT(xt, r_xt, dst, r_dst, scol, bcol, banks=(0, 1)):
            for hb in range(2):
                b = banks[hb]
                for i in range(4):
                    k = hb * 4 + i
                    op("pe", lambda e, k=k, i=i, b=b: e.transpose(out=PB[b][:, i * 128:(i + 1) * 128], in_=xt[:, k * 128:(k + 1) * 128],
                                                                   identity=identf[:]),
                       reads=[r_xt, r_identf], writes=[rPB[b]])
                for i in range(4):
                    k = hb * 4 + i
                    if i % 2 == 0:
                        op("act", lambda e, k=k, i=i, b=b: e.activation(out=dst[:, k, :], in_=PB[b][:, i * 128:(i + 1) * 128], func=AF.Identity,
                                                                         scale=modsT[:, scol, k:k + 1], bias=modsT[:, bcol, k:k + 1]),
                           reads=[rPB[b], r_modsT], writes=[r_dst])
                    else:
                        op("dve", lambda e, k=k, i=i, b=b: e.tensor_scalar(out=dst[:, k, :], in0=PB[b][:, i * 128:(i + 1) * 128],
                                                                            scalar1=modsT[:, scol, k:k + 1], scalar2=modsT[:, bcol, k:k + 1],
                                                                            op0=ALU.mult, op1=ALU.add),
                           reads=[rPB[b], r_modsT], writes=[r_dst])

        def ln_part1(pz, r_pz, which, y, r_y):
            for h in range(2):
                op("dve", lambda e, h=h: e.tensor_tensor(out=y[:, h * 512:(h + 1) * 512], in0=PB[pz[h]][:], in1=gaB[:, which, h * 512:(h + 1) * 512], op=ALU.mult),
                   reads=[r_pz[h], r_gaB[which]], writes=[r_y])

        def ln_part2(xres, r_xres, gB, bB, r_gb, y, r_y, st6, r_st):
            op("dve", lambda e: e.scalar_tensor_tensor(out=y[:], in0=xres, scalar=ALPHA, in1=y[:], op0=ALU.mult, op1=ALU.add),
               reads=[r_xres, r_y], writes=[r_y])
            for h in range(2):
                op("dve", lambda e, h=h: e.bn_stats(out=st6[:, h * 6:(h + 1) * 6], in_=y[:, h * 512:(h + 1) * 512]), reads=[r_y], writes=[r_st])
            op("dve", lambda e: e.bn_aggr(out=st6[:, 12:14], in_=st6[:, 0:12]), reads=[r_st], writes=[r_st])
            op("act", lambda e: e.activation(out=st6[:, 14:15], in_=st6[:, 13:14], func=AF.Ln, bias=epsc[:, 0:1], scale=1.0), reads=[r_st, r_eps], writes=[r_st])
            op("act", lambda e: e.activation(out=st6[:, 15:16], in_=st6[:, 14:15], func=AF.Exp, scale=-0.5), reads=[r_st], writes=[r_st])
            op("dve", lambda e: e.tensor_scalar(out=st6[:, 16:17], in0=st6[:, 12:13], scalar1=st6[:, 15:16], scalar2=-1.0, op0=ALU.mult, op1=ALU.mult),
               reads=[r_st], writes=[r_st])
            op("act", lambda e: e.activation(out=y[:], in_=y[:], func=AF.Identity, scale=st6[:, 15:16], bias=st6[:, 16:17]),
               reads=[r_y, r_st], writes=[r_y])
            op("dve", lambda e: e.tensor_tensor(out=y[:], in0=y[:], in1=gB, op=ALU.mult), reads=[r_y, r_gb], writes=[r_y])
            op("dve", lambda e: e.tensor_tensor(out=y[:], in0=y[:], in1=bB, op=ALU.add), reads=[r_y, r_gb], writes=[r_y])

        def ln_epilogue(pz, r_pz, xres, r_xres, which, gB, bB, r_gb, y, r_y, st6, r_st):
            ln_part1(pz, r_pz, which, y, r_y)
            ln_part2(xres, r_xres, gB, bB, r_gb, y, r_y, st6, r_st)

        for s in range(NSEQ):
            modsT = modsT_all[:, s]
            gaB = gaB_all[:, s]
            if s == 0:
              with ExitStack() as ph:
                cT = SB(ph, "cT", [128, NSEQ, 8], F32); r_cT = Res()
                cbt = SB(ph, "cbt", [128, NSEQ, 8, 128], F32); r_cbt = Res()
                wst = [SB(ph, "wst%d" % i, [128, 8, 512], F32) for i in range(2)]; r_wst = [Res(), Res()]
                bst = [SB(ph, "bst%d" % i, [1, 512], F32) for i in range(2)]; r_bst = [Res(), Res()]
                mtmp = [SB(ph, "mtmp%d" % i, [128, 512], F32) for i in range(NSEQ)]; r_mtmp = [Res() for _ in range(NSEQ)]
                for ss in range(NSEQ):
                    dma("sp", ds_misc[0], cT[:, ss, :], c_d[ss].rearrange("(j p) -> p j", p=128), writes=[r_cT], allow_slow_non_contiguous=True)
                op("act", lambda e: e.activation(out=cT[:], in_=cT[:], func=AF.Silu), reads=[r_cT], writes=[r_cT])
                op("dve", lambda e: e.tensor_copy(out=cbt[:], in_=cT[:].unsqueeze(3).broadcast_to([128, NSEQ, 8, 128])), reads=[r_cT], writes=[r_cbt])
                bcol = SB(ph, "bcol", [128, 48], F32); r_bcol = Res()
                cT2 = SB(ph, "cT2", [128, 8, NSEQ], F32); r_cT2 = Res()
                dma("sp", ds_misc[3], bcol[:], b_ada_d.rearrange("o (j p) -> p (o j)", p=128), writes=[r_bcol], allow_slow_non_contiguous=True)
                op("dve", lambda e: e.tensor_copy(out=cT2[:], in_=cT[:].rearrange("p s k -> p k s")), reads=[r_cT], writes=[r_cT2])
                for cg in range(12):
                    b2 = cg % 2
                    dma("sp", ds_misc[1 + b2], wst[b2][:], w_ada_k[:, :, cg * 512:(cg + 1) * 512], writes=[r_wst[b2]])
                    which, half = cg // 2, cg % 2
                    if which in (2, 5):
                        dma("sp", ds_o[b2], bst[b2][:], b_ada_d[0:1, cg * 512:(cg + 1) * 512], writes=[r_bst[b2]])
                        gi = 0 if which == 2 else 1
                        for ss in range(NSEQ):
                            pb = 2 + b2 + 2 * (ss % 2)
                            for k in range(8):
                                op("pe", lambda e, k=k: e.matmul(PB[pb][:], lhsT=cbt[:, ss, k, :], rhs=wst[b2][:, k, :], start=(k == 0), stop=False),
                                   reads=[r_cbt, r_wst[b2]], writes=[rPB[pb]])
                            op("pe", lambda e: e.matmul(PB[pb][:], lhsT=onesf[0:1, :], rhs=bst[b2][0:1, :], start=False, stop=True),
                               reads=[r_onesf, r_bst[b2]], writes=[rPB[pb]])
                            op("dve", lambda e: e.tensor_scalar(out=gaB_all[:, ss, gi, half * 512:(half + 1) * 512], in0=PB[pb][:], scalar1=1.0, scalar2=None, op0=ALU.add),
                               reads=[rPB[pb]], writes=[r_gaB[gi]])
                    else:
                        widx = {0: 0, 1: 1, 3: 2, 4: 3}[which]
                        addc = 1.0 if which in (1, 4) else 0.0
                        pt = 6 + b2
                        for i in range(4):
                            for k in range(8):
                                op("pe", lambda e, k=k, i=i: e.matmul(PB[pt][:, i * NSEQ:(i + 1) * NSEQ], lhsT=wst[b2][:, k, i * 128:(i + 1) * 128], rhs=cT2[:, k, :], start=(k == 0), stop=(k == 7)),
                                   reads=[r_cT2, r_wst[b2]], writes=[rPB[pt]])
                        op("dve", lambda e: e.tensor_tensor(out=modsT_all[:, :, widx, half * 4:(half + 1) * 4], in0=PB[pt][:, 0:4 * NSEQ].rearrange("p (i s) -> p s i", s=NSEQ),
                                                            in1=bcol[:, cg * 4:(cg + 1) * 4].unsqueeze(1).broadcast_to([128, NSEQ, 4]), op=ALU.add),
                           reads=[rPB[pt], r_bcol], writes=[r_modsT])
                        if addc != 0.0:
                            op("dve", lambda e: e.tensor_scalar(out=modsT_all[:, :, widx, half * 4:(half + 1) * 4], in0=modsT_all[:, :, widx, half * 4:(half + 1) * 4],
                                                                scalar1=addc, scalar2=None, op0=ALU.add), reads=[r_modsT], writes=[r_modsT])
                if dbg:
                    dma("sp", ds_dbg, dbg_d["mods"], modsT_all[:, 0], reads=[r_modsT])
              T.barrier()

            with ExitStack() as pab:
                oT = SB(pab, "oT", [128, 4, S], F32)
                r_oT = [[Res() for _ in range(NCH)] for _ in range(4)]
                kTt = SB(pab, "kTt", [128, S], BF16); r_kT = [Res() for _ in range(NG)]
                vtk = SB(pab, "vtk", [128, NT, 128], BF16); r_v = [Res() for _ in range(NT)]
                cosT = SB(pab, "cosT", [128, S], F32); sinT = SB(pab, "sinT", [128, S], F32); r_cs = Res()
                with ExitStack() as pr:
                    posi = SB(pr, "posi", [128, S], I32); r_posi = Res()
                    ang = SB(pr, "ang", [128, S], F32); r_ang = Res()
                    kq = SB(pr, "kq", [128, S], F32); r_kq = Res()
                    ki = SB(pr, "ki", [128, S], I32); r_ki = Res()
                    dma("sp", ds_misc[0], posi[:], pos_d[s:s + 1, :].broadcast_to([128, S]), writes=[r_posi])
                    op("dve", lambda e: e.tensor_copy(out=ang[:], in_=posi[:]), reads=[r_posi], writes=[r_ang])
                    op("dve", lambda e: e.tensor_scalar(out=ang[:], in0=ang[:], scalar1=rope[:, 0:1], scalar2=None, op0=ALU.mult),
                       reads=[r_ang, r_rope], writes=[r_ang])
                    TWO_PI = 6.283185307179586
                    C1 = 6.28125
                    C2 = TWO_PI - C1
                    for (dst, shift) in ((sinT, 0.0), (cosT, np.pi / 2)):
                        op("dve", lambda e, shift=shift: e.tensor_scalar(out=kq[:], in0=ang[:], scalar1=float(shift), scalar2=1.0 / TWO_PI, op0=ALU.add, op1=ALU.mult),
                           reads=[r_ang], writes=[r_kq])
                        op("dve", lambda e: e.tensor_copy(out=ki[:], in_=kq[:]), reads=[r_kq], writes=[r_ki])
                        op("dve", lambda e: e.tensor_copy(out=kq[:], in_=ki[:]), reads=[r_ki], writes=[r_kq])
                        op("dve", lambda e, shift=shift, dst=dst: e.tensor_scalar(out=dst[:], in0=ang[:], scalar1=float(shift), scalar2=None, op0=ALU.add),
                           reads=[r_ang], writes=[r_cs])
                        op("dve", lambda e, dst=dst: e.scalar_tensor_tensor(out=dst[:], in0=kq[:], scalar=-C1, in1=dst[:], op0=ALU.mult, op1=ALU.add),
                           reads=[r_kq, r_cs], writes=[r_cs])
                        op("dve", lambda e, dst=dst: e.scalar_tensor_tensor(out=dst[:], in0=kq[:], scalar=-C2, in1=dst[:], op0=ALU.mult, op1=ALU.add),
                           reads=[r_kq, r_cs], writes=[r_cs])
                        op("dve", lambda e, dst=dst: e.tensor_scalar(out=kq[:], in0=dst[:], scalar1=float(np.pi), scalar2=-TWO_PI, op0=ALU.is_gt, op1=ALU.mult),
                           reads=[r_cs], writes=[r_kq])
                        op("dve", lambda e, dst=dst: e.tensor_tensor(out=dst[:], in0=dst[:], in1=kq[:], op=ALU.add), reads=[r_cs, r_kq], writes=[r_cs])
                        op("dve", lambda e, dst=dst: e.tensor_scalar(out=kq[:], in0=dst[:], scalar1=float(-np.pi), scalar2=TWO_PI, op0=ALU.is_lt, op1=ALU.mult),
                           reads=[r_cs], writes=[r_kq])
                        op("dve", lambda e, dst=dst: e.tensor_tensor(out=dst[:], in0=dst[:], in1=kq[:], op=ALU.add), reads=[r_cs, r_kq], writes=[r_cs])
                        op("dve", lambda e, dst=dst: e.tensor_scalar(out=dst[:], in0=dst[:], scalar1=3.1415925, scalar2=-3.1415925, op0=ALU.min, op1=ALU.max),
                           reads=[r_cs], writes=[r_cs])
                        op("act", lambda e, dst=dst: e.activation(out=dst[:], in_=dst[:], func=AF.Sin), reads=[r_cs], writes=[r_cs])
                    op("dve", lambda e: e.tensor_scalar(out=sinT[:], in0=sinT[:], scalar1=rope[:, 1:2], scalar2=None, op0=ALU.mult),
                       reads=[r_cs, r_rope], writes=[r_cs])
                T.barrier()

                for hp in range(2):
                    with ExitStack() as pa:
                        NWA = 1024 + (384 if hp == 0 else 0)
                        wA = SB(pa, "wA", [128, 8, NWA], BF16); r_wA = Res()
                        QT = [SB(pa, "QT%d" % d, [128, 2, S], BF16) for d in range(2)]
                        KT = [SB(pa, "KT%d" % d, [128, 2, S], BF16) for d in range(2)]
                        r_QK = [[[Res() for _ in range(NG)] for _ in range(2)] for _ in range(2)]
                        Vt = SB(pa, "Vt", [64, NCH, 256], BF16); r_V = [Res() for _ in range(NG)]
                        Bref = SB(pa, "Bref", [128, 2, 2, NCH], F32); Bend = SB(pa, "Bend", [128, 2, 2, NCH], F32); r_B = Res()
                        Gx = SB(pa, "Gx", [128, 2, 2, NCH], F32); r_G = Res()
                        for seg, base in enumerate((O_RQ, O_RFF, O_RFB, O_RI)):
                            dma("pool", ds_w, wA[:, :, seg * 256:(seg + 1) * 256], w_in_k[:, :, base + hp * 256: base + hp * 256 + 256], writes=[r_wA])
                        if hp == 0:
                            dma("pool", ds_w, wA[:, :, 1024:1152], w_in_k[:, :, O_AK:O_AK + 128], writes=[r_wA])
                            for kv in range(2):
                                dma("pool", ds_w, wA[:, :, 1152 + kv * 64:1152 + kv * 64 + 32], w_in_k[:, :, O_AK + kv * 64 + 32:O_AK + kv * 64 + 64], writes=[r_wA])
                                dma("pool", ds_w, wA[:, :, 1152 + kv * 64 + 32:1152 + kv * 64 + 64], w_in_k[:, :, O_AK + kv * 64:O_AK + kv * 64 + 32], writes=[r_wA])
                            dma("pool", ds_w, wA[:, :, 1280:1408], w_in_k[:, :, O_AV:O_AV + 128], writes=[r_wA])
                        with ExitStack() as pa2:
                            xs = [SB(pa2, "xs%d" % i, [128, D], F32) for i in range(2)]; r_xs = [Res() for _ in range(2)]
                            uT = [SB(pa2, "uT%d" % i, [128, 8, 512], BF16) for i in range(1)]; r_uT = [Res()]
                            qs = [SB(pa2, "qs%d" % i, [128, 512], F32) for i in range(2)]; r_qs = [Res(), Res()]
                            NTMP = 2
                            tnames = ("sig", "lg", "Bf", "Eq", "Ek")
                            tmps = [{n: SB(pa2, "t_%s%d" % (n, i), [128, 512], F32) for n in tnames} for i in range(NTMP)]
                            r_tmps = [{n: Res() for n in tnames} for i in range(NTMP)]
                            for i in range(NTMP):
                                tmps[i]["kk"] = tmps[i]["sig"]; r_tmps[i]["kk"] = r_tmps[i]["sig"]
                                tmps[i]["dd"] = tmps[i]["lg"]; r_tmps[i]["dd"] = r_tmps[i]["lg"]
                            tctr = 0
                            bank_rot = Rot([2, 3, 4, 5, 6, 7])
                            for g in range(NG):
                                ub = 0
                                for t in range(4):
                                    tile_i = g * 4 + t
                                    sl = tile_i % 2
                                    dma("sp", ds_x[sl], xs[sl][:], x_d[s, tile_i * 128:(tile_i + 1) * 128, :], writes=[r_xs[sl]])
                                    make_uT(xs[sl], r_xs[sl], uT[ub][:, :, t * 128:(t + 1) * 128], r_uT[ub], 1, 0)
                                gs = slice(g * 512, (g + 1) * 512)
                                for cp in range(4):
                                    pb = bank_rot.next()
                                    for cc in range(2):
                                        ch = cp * 2 + cc
                                        for k in range(8):
                                            op("pe", lambda e, k=k, cc=cc, ch=ch, pb=pb: e.matmul(PB[pb][0:64, cc * 256:(cc + 1) * 256], lhsT=uT[ub][:, k, ch * 64:(ch + 1) * 64],
                                                                                                   rhs=wA[:, k, 768:1024], start=(k == 0), stop=(k == 7)),
                                               reads=[r_uT[ub], r_wA], writes=[rPB[pb]])
                                    op("act", lambda e, pb=pb, cp=cp: e.activation(out=Vt[:, g * 8 + cp * 2:g * 8 + cp * 2 + 2, :],
                                                                                    in_=PB[pb][0:64, :].rearrange("p (c n) -> p c n", n=256), func=AF.Copy, scale=128.0 ** -0.5),
                                       reads=[rPB[pb]], writes=[r_V[g]])
                                for hl in range(2):
                                    h = hp * 2 + hl
                                    pbq = bank_rot.next()
                                    for k in range(8):
                                        op("pe", lambda e, k=k: e.matmul(PB[pbq][:], lhsT=wA[:, k, hl * 128:(hl + 1) * 128], rhs=uT[ub][:, k, :],
                                                                         start=(k == 0), stop=(k == 7)), reads=[r_uT[ub], r_wA], writes=[rPB[pbq]])
                                    pbd = []
                                    for d in range(2):
                                        pb = bank_rot.next(); pbd.append(pb)
                                        cb = 256 * (1 + d) + hl * 128
                                        for k in range(8):
                                            op("pe", lambda e, k=k, pb=pb, cb=cb: e.matmul(PB[pb][:], lhsT=wA[:, k, cb:cb + 128], rhs=uT[ub][:, k, :],
                                                                                             start=(k == 0), stop=(k == 7)), reads=[r_uT[ub], r_wA], writes=[rPB[pb]])
                                    op("act", lambda e: e.activation(out=qs[hl][:], in_=PB[pbq][:], func=AF.Silu), reads=[rPB[pbq]], writes=[r_qs[hl]])
                                    TM = [tmps[0], tmps[1]]; RT = [r_tmps[0], r_tmps[1]]
                                    for d in range(2):
                                        op("act", lambda e, d=d: e.activation(out=TM[d]["sig"][:], in_=PB[pbd[d]][:], func=AF.Sigmoid), reads=[rPB[pbd[d]]], writes=[RT[d]["sig"]])
                                    for d in range(2):
                                        lb_c = lbt[:, 0, d, h:h + 1]; om_c = lbt[:, 1, d, h:h + 1]
                                        op("act", lambda e, d=d, lb_c=lb_c, om_c=om_c: e.activation(out=TM[d]["lg"][:], in_=TM[d]["sig"][:], func=AF.Ln, scale=om_c, bias=lb_c),
                                           reads=[RT[d]["sig"], r_lbt], writes=[RT[d]["lg"]])
                                    irefs = []
                                    for d in range(2):
                                        tm, rt = TM[d], RT[d]
                                        om_c = lbt[:, 1, d, h:h + 1]; nom_c = lbt[:, 2, d, h:h + 1]
                                        op("dve", lambda e, tm=tm, om_c=om_c, nom_c=nom_c: e.tensor_scalar(out=tm["kk"][:], in0=tm["sig"][:], scalar1=nom_c, scalar2=om_c, op0=ALU.mult, op1=ALU.add),
                                           reads=[rt["sig"], r_lbt], writes=[rt["kk"]])
                                        op("dve", lambda e, tm=tm: e.tensor_tensor_scan(out=tm["Bf"][:], data0=rmask[:], data1=tm["lg"][:], initial=0.0, op0=ALU.mult, op1=ALU.add),
                                           reads=[r_rmask, rt["lg"]], writes=[rt["Bf"]])
                                        B3 = tm["Bf"][:].rearrange("p (c t) -> p c t", t=64)
                                        if d == 1:
                                            lg3 = tm["lg"][:].rearrange("p (c t) -> p c t", t=64)
                                            op("dve", lambda e, B3=B3, lg3=lg3: e.tensor_tensor(out=lg3, in0=lg3, in1=B3, op=ALU.subtract), reads=[rt["lg"], rt["Bf"]], writes=[rt["lg"]])
                                            op("dve", lambda e, B3=B3, lg3=lg3: e.tensor_tensor(out=B3, in0=lg3, in1=B3[:, :, 63:64].broadcast_to([128, 8, 64]), op=ALU.add),
                                               reads=[rt["lg"], rt["Bf"]], writes=[rt["Bf"]])
                                            iref, iend = 32, 0
                                        else:
                                            iref, iend = 31, 63
                                        irefs.append(iref)
                                        op("act", lambda e, B3=B3, iref=iref, d=d: e.activation(out=Bref[:, d, hl, g * 8:(g + 1) * 8], in_=B3[:, :, iref], func=AF.Copy), reads=[rt["Bf"]], writes=[r_B])
                                        op("act", lambda e, B3=B3, iend=iend, d=d: e.activation(out=Bend[:, d, hl, g * 8:(g + 1) * 8], in_=B3[:, :, iend], func=AF.Copy), reads=[rt["Bf"]], writes=[r_B])
                                        op("dve", lambda e, B3=B3, tm=tm, iref=iref: e.tensor_tensor(out=tm["dd"][:].rearrange("p (c t) -> p c t", t=64), in0=B3,
                                                                                                       in1=B3[:, :, iref:iref + 1].broadcast_to([128, 8, 64]), op=ALU.subtract),
                                           reads=[rt["Bf"]], writes=[rt["dd"]])
                                    for d in range(2):
                                        tm, rt = TM[d], RT[d]
                                        op("act", lambda e, tm=tm: e.activation(out=tm["Eq"][:], in_=tm["dd"][:], func=AF.Exp), reads=[rt["dd"]], writes=[rt["Eq"]])
                                        op("act", lambda e, tm=tm: e.activation(out=tm["Ek"][:], in_=tm["dd"][:], func=AF.Exp, scale=-1.0), reads=[rt["dd"]], writes=[rt["Ek"]])
                                    for d in range(2):
                                        tm, rt = TM[d], RT[d]
                                        op("dve", lambda e, tm=tm, d=d: e.tensor_tensor(out=KT[d][:, hl, gs], in0=tm["kk"][:], in1=tm["Ek"][:], op=ALU.mult),
                                           reads=[rt["kk"], rt["Ek"]], writes=[r_QK[d][hl][g]])
                                        op("dve", lambda e, tm=tm, d=d: e.tensor_tensor(out=QT[d][:, hl, gs], in0=qs[hl][:], in1=tm["Eq"][:], op=ALU.mult),
                                           reads=[r_qs[hl], rt["Eq"]], writes=[r_QK[d][hl][g]])
                                if hp == 0:
                                    pb1 = bank_rot.next()
                                    pb2 = bank_rot.next()
                                    for (pbx, cb) in ((pb1, 1024), (pb2, 1152)):
                                        for k in range(8):
                                            op("pe", lambda e, k=k, pbx=pbx, cb=cb: e.matmul(PB[pbx][:], lhsT=wA[:, k, cb:cb + 128], rhs=uT[ub][:, k, :], start=(k == 0), stop=(k == 7)),
                                               reads=[r_uT[ub], r_wA], writes=[rPB[pbx]])
                                    tm = tmps[tctr % NTMP]; rt = r_tmps[tctr % NTMP]; tctr += 1
                                    op("dve", lambda e, tm=tm: e.tensor_tensor(out=tm["sig"][:], in0=PB[pb1][:], in1=cosT[:, gs], op=ALU.mult), reads=[rPB[pb1], r_cs], writes=[rt["sig"]])
                                    op("dve", lambda e, tm=tm: e.tensor_tensor(out=tm["lg"][:], in0=PB[pb2][:], in1=sinT[:, gs], op=ALU.mult), reads=[rPB[pb2], r_cs], writes=[rt["lg"]])
                                    op("dve", lambda e, tm=tm: e.tensor_tensor(out=kTt[:, gs], in0=tm["sig"][:], in1=tm["lg"][:], op=ALU.add), reads=[rt["sig"], rt["lg"]], writes=[r_kT[g]])
                                    for t in range(4):
                                        pb = bank_rot.next()
                                        for k in range(8):
                                            op("pe", lambda e, k=k, pb=pb, t=t: e.matmul(PB[pb][:, 0:128], lhsT=uT[ub][:, k, t * 128:(t + 1) * 128], rhs=wA[:, k, 1280:1408],
                                                                                          start=(k == 0), stop=(k == 7)), reads=[r_uT[ub], r_wA], writes=[rPB[pb]])
                                        op("act", lambda e, pb=pb, t=t: e.activation(out=vtk[:, g * 4 + t, :], in_=PB[pb][:, 0:128], func=AF.Copy), reads=[rPB[pb]], writes=[r_v[g * 4 + t]])
                            op("dve", lambda e: e.tensor_tensor(out=Gx[:], in0=Bend[:], in1=Bref[:], op=ALU.subtract), reads=[r_B], writes=[r_G])
                            if NCH > 1:
                                op("dve", lambda e: e.tensor_tensor(out=Gx[:, 0, :, 0:NCH - 1], in0=Gx[:, 0, :, 0:NCH - 1], in1=Bref[:, 0, :, 1:NCH], op=ALU.add), reads=[r_B, r_G], writes=[r_G])
                                op("dve", lambda e: e.tensor_tensor(out=Gx[:, 1, :, 1:NCH], in0=Gx[:, 1, :, 1:NCH], in1=Bref[:, 1, :, 0:NCH - 1], op=ALU.add), reads=[r_B, r_G], writes=[r_G])
                            op("act", lambda e: e.activation(out=Gx[:], in_=Gx[:], func=AF.Exp), reads=[r_G], writes=[r_G])
                            if dbg and s == 0 and hp == 0:
                                for d in range(2):
                                    dma("sp", ds_dbg, dbg_d["QT"][d], QT[d][:], reads=[r_QK[d][hl][g] for hl in range(2) for g in range(NG)])
                                    dma("sp", ds_dbg, dbg_d["KT"][d], KT[d][:], reads=[r_QK[d][hl][g] for hl in range(2) for g in range(NG)])
                                dma("sp", ds_dbg, dbg_d["V"], Vt[:], reads=r_V)
                                dma("sp", ds_dbg, dbg_d["kT"], kTt[:], reads=r_kT)
                        T.barrier()

                        with ExitStack() as pb_:
                            Sm32 = SB(pb_, "Sm32", [128, 4, 128], F32); Smb = SB(pb_, "Smb", [128, 4, 128], BF16); tmp32 = SB(pb_, "tmp32", [128, 4, 128], F32)
                            r_Sm32 = [Res() for _ in range(4)]; r_Smb = [Res() for _ in range(4)]; r_tmp32 = [Res() for _ in range(4)]
                            scm = SB(pb_, "scm", [64, 2, 4, 64], BF16); r_scm = [[Res() for _ in range(4)] for _ in range(2)]
                            ktok = SB(pb_, "ktok", [64, 2, 4, 128], BF16); r_ktok = [[Res() for _ in range(4)] for _ in range(2)]
                            rB = [[None] * 4 for _ in range(2)]
                            for _p in range(2):
                                for _c in range(4):
                                    _r = Res()
                                    rB[_p][_c] = {n: _r for n in ("sc", "kt", "o", "dl")}
                            op("pool", lambda e: e.memset(Sm32[:], 0.0), writes=r_Sm32)
                            op("pool", lambda e: e.memset(Smb[:], 0.0), writes=r_Smb)
                            allQK = lambda d, hl: [r_QK[d][hl][g] for g in range(NG)]

                            def chunk_of(j, d):
                                return j if d == 0 else NCH - 1 - j

                            def stage1(j):
                                par = j % 2
                                for ci in range(4):
                                    hl, d = ci // 2, ci % 2
                                    c = chunk_of(j, d); tk = slice(c * 64, (c + 1) * 64)
                                    bk = ci + 4 * par
                                    op("pe", lambda e, bk=bk, d=d, hl=hl, tk=tk: e.matmul(PB[bk][0:64, 0:64], lhsT=KT[d][:, hl, tk], rhs=QT[d][:, hl, tk], start=True, stop=True),
                                       reads=allQK(d, hl), writes=[rB[par][ci]["sc"]])
                                    op("pe", lambda e, bk=bk, d=d, hl=hl, tk=tk: e.transpose(out=PB[bk][:].bitcast(BF16)[0:64, 128:256], in_=KT[d][:, hl, tk], identity=identb[:]),
                                       reads=allQK(d, hl) + [r_identb], writes=[rB[par][ci]["kt"]])

                            def stage2(j):
                                par = j % 2
                                for ci in range(4):
                                    hl, d = ci // 2, ci % 2
                                    bk = ci + 4 * par
                                    mk = maskf if d == 0 else maskb
                                    op("pool", lambda e, ci=ci: e.memset(scm[:, par, ci, :], 0.0), writes=[r_scm[par][ci]])
                                    op("dve", lambda e, bk=bk, ci=ci, mk=mk: e.copy_predicated(out=scm[:, par, ci, :], mask=mk[:], data=PB[bk][0:64, 0:64]),
                                       reads=[rB[par][ci]["sc"], r_mask], writes=[r_scm[par][ci]])
                                    op("act", lambda e, bk=bk, ci=ci: e.activation(out=ktok[:, par, ci, :], in_=PB[bk][:].bitcast(BF16)[0:64, 128:256], func=AF.Copy),
                                       reads=[rB[par][ci]["kt"]], writes=[r_ktok[par][ci]])

                            def stage3(j):
                                par = j % 2
                                for ci in range(4):
                                    hl, d = ci // 2, ci % 2
                                    c = chunk_of(j, d); tk = slice(c * 64, (c + 1) * 64)
                                    bk = ci + 4 * par
                                    vv = Vt[:, c, hl * 128:(hl + 1) * 128]
                                    op("pe", lambda e, bk=bk, vv=vv, ci=ci: e.matmul(PB[bk][:, 128:192], lhsT=vv, rhs=scm[:, par, ci, :], start=True, stop=False),
                                       reads=[r_V[c // 8], r_scm[par][ci]], writes=[rB[par][ci]["o"]])
                                    op("pe", lambda e, bk=bk, ci=ci, d=d, hl=hl, tk=tk: e.matmul(PB[bk][:, 128:192], lhsT=Smb[:, ci, :], rhs=QT[d][:, hl, tk], start=False, stop=True),
                                       reads=[r_Smb[ci]] + allQK(d, hl), writes=[rB[par][ci]["o"]])
                                    op("pe", lambda e, bk=bk, vv=vv, ci=ci: e.matmul(PB[bk][:, 192:320], lhsT=ktok[:, par, ci, :], rhs=vv, start=True, stop=True),
                                       reads=[r_ktok[par][ci], r_V[c // 8]], writes=[rB[par][ci]["dl"]])

                            def stage4(j):
                                par = j % 2
                                for ci in range(4):
                                    hl, d = ci // 2, ci % 2
                                    h = hp * 2 + hl
                                    c = chunk_of(j, d); tk = slice(c * 64, (c + 1) * 64)
                                    bk = ci + 4 * par
                                    if j < NCH // 2:
                                        op("act", lambda e, bk=bk, h=h, tk=tk: e.activation(out=oT[:, h, tk], in_=PB[bk][:, 128:192], func=AF.Copy),
                                           reads=[rB[par][ci]["o"]], writes=[r_oT[h][c]])
                                    else:
                                        op("dve", lambda e, bk=bk, h=h, tk=tk: e.tensor_tensor(out=oT[:, h, tk], in0=PB[bk][:, 128:192], in1=oT[:, h, tk], op=ALU.add),
                                           reads=[rB[par][ci]["o"], r_oT[h][c]], writes=[r_oT[h][c]])
                                    if j < NCH - 1:
                                        gcol = Gx[:, d, hl, c:c + 1]
                                        op("dve", lambda e, bk=bk, ci=ci: e.tensor_tensor(out=tmp32[:, ci, :], in0=PB[bk][:, 192:320], in1=Sm32[:, ci, :], op=ALU.add),
                                           reads=[rB[par][ci]["dl"], r_Sm32[ci]], writes=[r_tmp32[ci]])
                                        op("act", lambda e, ci=ci, gcol=gcol: e.activation(out=Sm32[:, ci, :], in_=tmp32[:, ci, :], func=AF.Identity, scale=gcol),
                                           reads=[r_tmp32[ci], r_G], writes=[r_Sm32[ci]])
                                        op("pool", lambda e, ci=ci: e.tensor_copy(out=Smb[:, ci, :], in_=Sm32[:, ci, :]),
                                           reads=[r_Sm32[ci]], writes=[r_Smb[ci]])

                            stage1(0)
                            stage2(0)
                            for j in range(NCH):
                                if j + 1 < NCH:
                                    stage1(j + 1)
                                stage3(j)
                                if j + 1 < NCH:
                                    stage2(j + 1)
                                stage4(j)
                        T.barrier()
                if dbg and s == 0:
                    dma("sp", ds_dbg, dbg_d["oT"], oT[:], reads=[r for rr in r_oT for r in rr])

                ogA = SB(pab, "ogA", [128, 4, S], BF16); atA = SB(pab, "atA", [128, 4, S], BF16)
                r_ogA = [Res() for _ in range(NG)]; r_atA = [Res() for _ in range(NT)]
                with ExitStack() as pc:
                    wC = SB(pc, "wC", [128, 8, 1536], BF16); r_wC = Res()
                    dma("pool", ds_w, wC[:, :, 0:512], w_in_k[:, :, O_RG:O_RG + 512], writes=[r_wC])
                    for m in range(4):
                        for hf in range(2):
                            hq = m + 4 * hf
                            cdst = 512 + m * 128 + hf * 64
                            dma("pool", ds_w, wC[:, :, cdst:cdst + 64], w_in_k[:, :, O_AQ + hq * 64:O_AQ + hq * 64 + 64], writes=[r_wC])
                            dma("pool", ds_w, wC[:, :, 512 + cdst:512 + cdst + 32], w_in_k[:, :, O_AQ + hq * 64 + 32:O_AQ + hq * 64 + 64], writes=[r_wC])
                            dma("pool", ds_w, wC[:, :, 512 + cdst + 32:512 + cdst + 64], w_in_k[:, :, O_AQ + hq * 64:O_AQ + hq * 64 + 32], writes=[r_wC])
                    xs = [SB(pc, "cxs%d" % i, [128, D], F32) for i in range(2)]; r_xs = [Res() for _ in range(2)]
                    uT = SB(pc, "cuT", [128, 8, 512], BF16); r_uT = Res()
                    qT = SB(pc, "qT", [128, 4, 512], BF16); r_qT = Res()
                    ta = [SB(pc, "cta%d" % i, [128, 512], F32) for i in range(4)]; r_ta = [Res() for _ in range(4)]
                    msc = [SB(pc, "msc%d" % i, [128, 384], F32) for i in range(2)]; r_msc = [Res(), Res()]
                    pex = [SB(pc, "pex%d" % i, [128, 384], BF16) for i in range(2)]; r_pex = [Res(), Res()]
                    pT = [SB(pc, "pT%d" % i, [128, 3, 128], BF16) for i in range(2)]; r_pT = [Res(), Res()]
                    sm = [SB(pc, "smx%d" % i, [128, 8], F32) for i in range(2)]; r_sm = [Res(), Res()]
                    sm2 = [SB(pc, "smy%d" % i, [128, 8], F32) for i in range(2)]; r_sm2 = [Res(), Res()]
                    ao = [SB(pc, "ao%d" % i, [128, 512], BF16) for i in range(2)]; r_ao = [Res(), Res()]
                    hctr = 0
                    for g in range(NG):
                        gs = slice(g * 512, (g + 1) * 512)
                        for t in range(4):
                            tile_i = g * 4 + t
                            sl = tile_i % 2
                            dma("sp", ds_x[sl], xs[sl][:], x_d[s, tile_i * 128:(tile_i + 1) * 128, :], writes=[r_xs[sl]])
                            make_uT(xs[sl], r_xs[sl], uT[:, :, t * 128:(t + 1) * 128], r_uT, 1, 0)
                        for m in range(4):
                            for (pbx, cb) in ((2, 512 + m * 128), (3, 1024 + m * 128)):
                                for k in range(8):
                                    op("pe", lambda e, k=k, pbx=pbx, cb=cb: e.matmul(PB[pbx][:], lhsT=wC[:, k, cb:cb + 128], rhs=uT[:, k, :], start=(k == 0), stop=(k == 7)),
                                       reads=[r_uT, r_wC], writes=[rPB[pbx]])
                            op("dve", lambda e: e.tensor_tensor(out=ta[0][:], in0=PB[2][:], in1=cosT[:, gs], op=ALU.mult), reads=[rPB[2], r_cs], writes=[r_ta[0]])
                            op("dve", lambda e: e.tensor_tensor(out=ta[1][:], in0=PB[3][:], in1=sinT[:, gs], op=ALU.mult), reads=[rPB[3], r_cs], writes=[r_ta[1]])
                            op("dve", lambda e, m=m: e.tensor_tensor(out=qT[:, m, :], in0=ta[0][:], in1=ta[1][:], op=ALU.add), reads=[r_ta[0], r_ta[1]], writes=[r_qT])
                        for h in range(4):
                            for k in range(8):
                                op("pe", lambda e, k=k, h=h: e.matmul(PB[4][:], lhsT=wC[:, k, h * 128:(h + 1) * 128], rhs=uT[:, k, :], start=(k == 0), stop=(k == 7)),
                                   reads=[r_uT, r_wC], writes=[rPB[4]])
                            op("act", lambda e: e.activation(out=ta[0][:], in_=PB[4][:], func=AF.Silu), reads=[rPB[4]], writes=[r_ta[0]])
                            rr = [r_oT[h][c] for c in range(g * 8, (g + 1) * 8)]
                            op("act", lambda e, h=h: e.activation(out=ta[2][:], in_=oT[:, h, gs], func=AF.Square), reads=rr, writes=[r_ta[2]])
                            op("pe", lambda e: e.matmul(PB[5][:], lhsT=onesf[:], rhs=ta[2][:], start=True, stop=True), reads=[r_onesf, r_ta[2]], writes=[rPB[5]])
                            op("act", lambda e: e.activation(out=ta[3][:], in_=PB[5][:], func=AF.Ln, scale=1.0 / 128.0, bias=epsc[:, 1:2]), reads=[rPB[5], r_eps], writes=[r_ta[3]])
                            op("act", lambda e: e.activation(out=ta[3][:], in_=ta[3][:], func=AF.Exp, scale=-0.5), reads=[r_ta[3]], writes=[r_ta[3]])
                            op("dve", lambda e, h=h: e.tensor_tensor(out=ta[2][:], in0=oT[:, h, gs], in1=ta[3][:], op=ALU.mult), reads=rr + [r_ta[3]], writes=[r_ta[2]])
                            op("dve", lambda e, h=h: e.scalar_tensor_tensor(out=ogA[:, h, gs], in0=ta[0][:], scalar=ngc[:, 0:1], in1=ta[2][:], op0=ALU.mult, op1=ALU.mult),
                               reads=[r_ta[0], r_ngc, r_ta[2]], writes=[r_ogA[g]])
                        def att_geom(t):
                            j = g * 4 + t
                            k0 = max(0, j - 1); k1 = min(NT, j + 2)
                            return j, k0, k1, k1 - k0, (k1 - k0) * 128, (128 if j == 0 else 0)

                        def att_s1_pe(idx):
                            t, h = idx // 8, idx % 8
                            j, k0, k1, nkb, nk, mc0 = att_geom(t)
                            m, hf = h % 4, h // 4
                            ps = slice(hf * 64, (hf + 1) * 64)
                            ib = idx % 2
                            pbs = 2 + ib
                            kgs = [r_kT[gg] for gg in range((k0 * 128) // 512, ((k1 * 128) - 1) // 512 + 1)]
                            op("pe", lambda e: e.matmul(PB[pbs][:, 0:nk], lhsT=qT[ps, m, t * 128:(t + 1) * 128], rhs=kTt[ps, k0 * 128:k1 * 128], start=True, stop=True),
                               reads=[r_qT] + kgs, writes=[rPB[pbs]])

                        def att_s1_ew(idx):
                            t, h = idx // 8, idx % 8
                            j, k0, k1, nkb, nk, mc0 = att_geom(t)
                            ib = idx % 2
                            pbs = 2 + ib
                            op("dve", lambda e: e.tensor_tensor(out=msc[ib][:, 0:nk], in0=PB[pbs][:, 0:nk], in1=amask[:, mc0:mc0 + nk], op=ALU.add),
                               reads=[rPB[pbs], r_amask], writes=[r_msc[ib]])
                            op("dve", lambda e: e.reduce_max(out=sm[ib][:, 0:1], in_=msc[ib][:, 0:nk], axis=AX.X), reads=[r_msc[ib]], writes=[r_sm[ib]])
                            op("dve", lambda e: e.tensor_scalar(out=sm[ib][:, 1:2], in0=sm[ib][:, 0:1], scalar1=-0.125, scalar2=sinkb[:, 1, h:h + 1], op0=ALU.mult, op1=ALU.min),
                               reads=[r_sm[ib], r_sink], writes=[r_sm[ib]])
                            op("act", lambda e: e.activation(out=pex[ib][:, 0:nk], in_=msc[ib][:, 0:nk], func=AF.Exp, scale=0.125, bias=sm[ib][:, 1:2], accum_out=sm2[ib][:, 0:1]),
                               reads=[r_msc[ib], r_sm[ib]], writes=[r_pex[ib], r_sm2[ib]])
                            op("act", lambda e: e.activation(out=sm2[ib][:, 1:2], in_=sinkb[:, 0, h:h + 1], func=AF.Exp, bias=sm[ib][:, 1:2], scale=1.0),
                               reads=[r_sm[ib], r_sink], writes=[r_sm2[ib]])

                        def att_s2a(idx):
                            t, h = idx // 8, idx % 8
                            j, k0, k1, nkb, nk, mc0 = att_geom(t)
                            ib = idx % 2
                            pbt = 4 + ib
                            for b in range(nkb):
                                op("pe", lambda e, b=b: e.transpose(out=PB[pbt][:].bitcast(BF16)[:, b * 128:(b + 1) * 128], in_=pex[ib][:, b * 128:(b + 1) * 128], identity=identb[:]),
                                   reads=[r_pex[ib], r_identb], writes=[rPB[pbt]])
                            if ib == 0:
                                op("act", lambda e: e.activation(out=pT[ib][:, 0:nkb, :], in_=PB[pbt][:].bitcast(BF16)[:, 0:nk].rearrange("p (b q) -> p b q", q=128), func=AF.Copy),
                                   reads=[rPB[pbt]], writes=[r_pT[ib]])
                            else:
                                op("dve", lambda e: e.tensor_copy(out=pT[ib][:, 0:nkb, :], in_=PB[pbt][:].bitcast(BF16)[:, 0:nk].rearrange("p (b q) -> p b q", q=128)),
                                   reads=[rPB[pbt]], writes=[r_pT[ib]])

                        def att_s2b(idx):
                            t, h = idx // 8, idx % 8
                            j, k0, k1, nkb, nk, mc0 = att_geom(t)
                            hf = h // 4
                            ib = idx % 2
                            op("dve", lambda e: e.tensor_tensor(out=sm2[ib][:, 2:3], in0=sm2[ib][:, 0:1], in1=sm2[ib][:, 1:2], op=ALU.add), reads=[r_sm2[ib]], writes=[r_sm2[ib]])
                            op("dve", lambda e: e.reciprocal(out=sm2[ib][:, 3:4], in_=sm2[ib][:, 2:3]), reads=[r_sm2[ib]], writes=[r_sm2[ib]])
                            pbo = 6 + ib
                            for b in range(nkb):
                                op("pe", lambda e, b=b: e.matmul(PB[pbo][:, 0:64], lhsT=pT[ib][:, b, :], rhs=vtk[:, k0 + b, hf * 64:(hf + 1) * 64], start=(b == 0), stop=(b == nkb - 1)),
                                   reads=[r_pT[ib], r_v[k0 + b]], writes=[rPB[pbo]])
                            aot = ao[t % 2]; r_aot = r_ao[t % 2]
                            op("act", lambda e: e.activation(out=aot[:, h * 64:(h + 1) * 64], in_=PB[pbo][:, 0:64], func=AF.Identity, scale=sm2[ib][:, 3:4]),
                               reads=[rPB[pbo], r_sm2[ib]], writes=[r_aot])
                            if h == 7:
                                pba = t % 2
                                for m in range(4):
                                    op("pe", lambda e, m=m: e.transpose(out=PB[pba][:].bitcast(BF16)[:, m * 128:(m + 1) * 128], in_=aot[:, m * 128:(m + 1) * 128], identity=identb[:]),
                                       reads=[r_aot, r_identb], writes=[rPB[pba]])
                                op("dve", lambda e: e.tensor_copy(out=atA[:, :, j * 128:(j + 1) * 128], in_=PB[pba][:].bitcast(BF16)[:, 0:512].rearrange("p (m q) -> p m q", q=128)),
                                   reads=[rPB[pba]], writes=[r_atA[j]])

                        att_s1_pe(0)
                        att_s1_ew(0)
                        for idx in range(32):
                            if idx + 1 < 32:
                                att_s1_pe(idx + 1)
                            att_s2a(idx)
                            if idx + 1 < 32:
                                att_s1_ew(idx + 1)
                            att_s2b(idx)
                T.barrier()

                with ExitStack() as pc:
                    if S >= 2048:
                        wG = oT[:].rearrange("p h s -> p (h s)").bitcast(BF16)
                        wGv = wG[:, 0:8 * 2048].rearrange("p (k n) -> p k n", n=2048)
                    else:
                        wGv = SB(pc, "wGt", [128, 8, 2048], BF16)[:]
                    r_wG = Res()
                    wR = SB(pc, "wR", [128, 4, D], BF16); wT = SB(pc, "wT", [128, 4, D], BF16); wO = SB(pc, "wO", [128, 8, D], BF16)
                    lnB = SB(pc, "lnB1", [128, 2, D], F32); r_lnB = Res()
                    r_wRT = Res(); r_wGc = [Res() for _ in range(8)]; r_wOr = Res()
                    dma("pool", ds_wc[8], wR[:], w_rec_k, writes=[r_wRT])
                    dma("pool", ds_wc[8], wT[:], w_att_k, writes=[r_wRT])
                    for dcq in range(8):
                        dma("pool", ds_wc[dcq], wGv[:, :, dcq * 128:(dcq + 1) * 128], w_in_k[:, :, O_GR + dcq * 128:O_GR + (dcq + 1) * 128], writes=[r_wGc[dcq]])
                        dma("pool", ds_wc[dcq], wGv[:, :, 1024 + dcq * 128:1024 + (dcq + 1) * 128], w_in_k[:, :, O_GA + dcq * 128:O_GA + (dcq + 1) * 128], writes=[r_wGc[dcq]])
                    for q2 in range(2):
                        dma("pool", ds_wc[9], wO[:, :, q2 * 512:(q2 + 1) * 512], w_out_k[:, :, q2 * 512:(q2 + 1) * 512], writes=[r_wOr])
                    dma("sp", ds_const, lnB[:, 0, :], ln1g_d[0:1, :].broadcast_to([128, D]), writes=[r_lnB])
                    dma("sp", ds_const, lnB[:, 1, :], ln1b_d[0:1, :].broadcast_to([128, D]), writes=[r_lnB])
                    zb = zer[:, 0:8].bitcast(BF16)[:, 0:8].unsqueeze(2)
                    dma("sp", ds_misc[3], u2T_d[s, :, 0:1].rearrange("(j p) o -> p j o", p=128), zb, reads=[r_zer], allow_slow_non_contiguous=True)
                    dma("sp", ds_misc[3], u2T_d[s, :, S + 1:S + 2].rearrange("(j p) o -> p j o", p=128), zb, reads=[r_zer], allow_slow_non_contiguous=True)
                    r_x1d = [Res() for _ in range(NT)]
                    r_u2d = [Res() for _ in range(NT)]
                    xs = [SB(pc, "bxs%d" % i, [128, D], F32) for i in range(4)]; r_xs = [Res() for _ in range(4)]
                    uT = SB(pc, "buT", [128, 8, 512], BF16); r_uT = Res()
                    mT = SB(pc, "mT", [128, 8, 512], BF16); r_mT = Res()
                    ta = [SB(pc, "bta%d" % i, [128, 512], F32) for i in range(4)]; r_ta = [Res() for _ in range(4)]
                    yt = [SB(pc, "yt%d" % i, [128, D], F32) for i in range(2)]; r_yt = [Res(), Res()]
                    st6 = [SB(pc, "st6%d" % i, [128, 20], F32) for i in range(2)]; r_st6 = [Res(), Res()]
                    u2t = [SB(pc, "u2t%d" % i, [128, 8, 128], BF16) for i in range(2)]; r_u2t = [Res(), Res()]
                    for g in range(NG):
                        gs = slice(g * 512, (g + 1) * 512)
                        for t in range(4):
                            tile_i = g * 4 + t
                            dma("sp", ds_x[t], xs[t][:], x_d[s, tile_i * 128:(tile_i + 1) * 128, :], writes=[r_xs[t]])
                            make_uT(xs[t], r_xs[t], uT[:, :, t * 128:(t + 1) * 128], r_uT, 1, 0)
                        BSETS = ((2, 3, 4, 5), (0, 1, 6, 7))

                        def dc_pe(dc):
                            b0, b1, b2, b3 = BSETS[dc % 2]
                            dcs = slice(dc * 128, (dc + 1) * 128)
                            for k in range(8):
                                op("pe", lambda e, k=k: e.matmul(PB[b0][:], lhsT=wGv[:, k, dc * 128:(dc + 1) * 128], rhs=uT[:, k, :], start=(k == 0), stop=(k == 7)),
                                   reads=[r_uT, r_wGc[dc]], writes=[rPB[b0]])
                            for k in range(8):
                                op("pe", lambda e, k=k: e.matmul(PB[b1][:], lhsT=wGv[:, k, 1024 + dc * 128:1024 + (dc + 1) * 128], rhs=uT[:, k, :], start=(k == 0), stop=(k == 7)),
                                   reads=[r_uT, r_wGc[dc]], writes=[rPB[b1]])
                            for h in range(4):
                                op("pe", lambda e, h=h: e.matmul(PB[b2][:], lhsT=wR[:, h, dcs], rhs=ogA[:, h, gs], start=(h == 0), stop=(h == 3)),
                                   reads=[r_wRT, r_ogA[g]], writes=[rPB[b2]])
                            for m in range(4):
                                op("pe", lambda e, m=m: e.matmul(PB[b3][:], lhsT=wT[:, m, dcs], rhs=atA[:, m, gs], start=(m == 0), stop=(m == 3)),
                                   reads=[r_wRT] + r_atA[g * 4:(g + 1) * 4], writes=[rPB[b3]])

                        def dc_ew(dc):
                            b0, b1, b2, b3 = BSETS[dc % 2]
                            a0, a1 = ta[2 * (dc % 2)], ta[2 * (dc % 2) + 1]
                            ra0, ra1 = r_ta[2 * (dc % 2)], r_ta[2 * (dc % 2) + 1]
                            op("act", lambda e: e.activation(out=a0[:], in_=PB[b0][:], func=AF.Sigmoid), reads=[rPB[b0]], writes=[ra0])
                            op("act", lambda e: e.activation(out=a1[:], in_=PB[b1][:], func=AF.Sigmoid), reads=[rPB[b1]], writes=[ra1])
                            op("dve", lambda e: e.tensor_tensor(out=a0[:], in0=PB[b2][:], in1=a0[:], op=ALU.mult), reads=[rPB[b2], ra0], writes=[ra0])
                            op("dve", lambda e: e.tensor_tensor(out=a1[:], in0=PB[b3][:], in1=a1[:], op=ALU.mult), reads=[rPB[b3], ra1], writes=[ra1])
                            op("dve", lambda e: e.tensor_tensor(out=mT[:, dc, :], in0=a0[:], in1=a1[:], op=ALU.add), reads=[ra0, ra1], writes=[r_mT])

                        dc_pe(0)
                        for dc in range(8):
                            if dc + 1 < 8:
                                dc_pe(dc + 1)
                            dc_ew(dc)
                        def wout_pe(t):
                            tile_i = g * 4 + t
                            wb = 2 + 2 * (tile_i % 2)
                            for hh in range(2):
                                for k in range(8):
                                    op("pe", lambda e, k=k, hh=hh: e.matmul(PB[wb + hh][:], lhsT=mT[:, k, t * 128:(t + 1) * 128], rhs=wO[:, k, hh * 512:(hh + 1) * 512], start=(k == 0), stop=(k == 7)),
                                       reads=[r_mT, r_wOr], writes=[rPB[wb + hh]])

                        wout_pe(0)
                        for t in range(4):
                            tile_i = g * 4 + t
                            ib = tile_i % 2
                            wb = 2 + 2 * (tile_i % 2)
                            ln_part1((wb, wb + 1), (rPB[wb], rPB[wb + 1]), 0, yt[ib], r_yt[ib])
                            if t + 1 < 4:
                                wout_pe(t + 1)
                            ln_part2(xs[t][:], r_xs[t], lnB[:, 0, :], lnB[:, 1, :], r_lnB, yt[ib], r_yt[ib], st6[ib], r_st6[ib])
                            dma("sp", ds_o[ib], x1_d[s, tile_i * 128:(tile_i + 1) * 128, :], yt[ib][:], reads=[r_yt[ib]], writes=[r_x1d[tile_i]])
                            make_uT(yt[ib], r_yt[ib], u2t[ib][:], r_u2t[ib], 3, 2, banks=(6, 7))
                            dma("sp", ds_o[2 + ib], u2T_d[s, :, 1 + tile_i * 128:1 + (tile_i + 1) * 128].rearrange("(j p) q -> p j q", p=128), u2t[ib][:],
                                reads=[r_u2t[ib]], writes=[r_u2d[tile_i]])
                T.barrier()
            T.barrier()

            with ExitStack() as pf:
                wU = SB(pf, "wU", [128, 8, 2 * DFF], BF16); wD = SB(pf, "wD", [128, NFC, D], BF16); r_wU = Res()
                lnB = SB(pf, "lnB2", [128, 2, D], F32); r_lnB = Res()
                cvw = SB(pf, "cvw", [128, 4, 2 * NFC], F32); r_cvw = r_lnB
                r_wc = [Res() for _ in range(11)]
                for jq in range(11):
                    dma("pool", ds_wc[jq], wU[:, :, jq * 256:(jq + 1) * 256], w_up_k[:, :, jq * 256:(jq + 1) * 256], writes=[r_wc[jq]])
                    dma("pool", ds_wc[jq], wU[:, :, DFF + jq * 256:DFF + (jq + 1) * 256], w_up_k[:, :, DFF + jq * 256:DFF + (jq + 1) * 256], writes=[r_wc[jq]])
                    dma("pool", ds_wc[jq], wD[:, 2 * jq:2 * jq + 2, :], w_dn_k[:, 2 * jq:2 * jq + 2, :], writes=[r_wc[jq]])
                dma("sp", ds_const, lnB[:, 0, :], ln2g_d[0:1, :].broadcast_to([128, D]), writes=[r_lnB])
                dma("sp", ds_const, lnB[:, 1, :], ln2b_d[0:1, :].broadcast_to([128, D]), writes=[r_lnB])
                dma("sp", ds_const, cvw[:, 0:3, :], cw_d.rearrange("k (j p) -> p k j", p=128), writes=[r_cvw], allow_slow_non_contiguous=True)
                dma("sp", ds_const, cvw[:, 3, :], cb_d.rearrange("o (j p) -> p (o j)", p=128), writes=[r_cvw], allow_slow_non_contiguous=True)
                NSG = S // 256
                u2 = [SB(pf, "fu2%d" % i, [128, 8, 258], BF16) for i in range(2)]; r_u2 = [Res(), Res()]
                x1t = [SB(pf, "fx1%d" % i, [128, D], F32) for i in range(4)]; r_x1t = [Res() for _ in range(4)]
                cv = [SB(pf, "fcv%d" % i, [128, 256], F32) for i in range(2)]; r_cv = [Res(), Res()]
                cgt = [SB(pf, "fcg%d" % i, [128, 256], F32) for i in range(2)]; r_cg = [Res(), Res()]
                gz = [SB(pf, "fgz%d" % i, [128, 256], F32) for i in range(2)]; r_gz = [Res(), Res()]
                actb = [SB(pf, "fact%d" % i, [128, 256], BF16) for i in range(2)]; r_act = [Res(), Res()]
                yt = [SB(pf, "fyt%d" % i, [128, D], F32) for i in range(2)]; r_yt = [Res(), Res()]
                st6 = [SB(pf, "fst6%d" % i, [128, 20], F32) for i in range(2)]; r_st6 = [Res(), Res()]
                def c2_up(sg, c):
                    ub = sg % 2; ib = c % 2
                    pv, pg = 4 + 2 * ib, 5 + 2 * ib
                    for (pbx, cb) in ((pv, c * 128), (pg, DFF + c * 128)):
                        for k in range(8):
                            op("pe", lambda e, k=k, pbx=pbx, cb=cb: e.matmul(PB[pbx][:, 0:258], lhsT=wU[:, k, cb:cb + 128], rhs=u2[ub][:, k, :], start=(k == 0), stop=(k == 7)),
                               reads=[r_wc[c // 2], r_u2[ub]], writes=[rPB[pbx]])

                def c2_s1(c):
                    ib = c % 2
                    pv, pg = 4 + 2 * ib, 5 + 2 * ib
                    for (pbx, dst, r_dst, col) in ((pv, cv[ib], r_cv[ib], c), (pg, cgt[ib], r_cg[ib], NFC + c)):
                        op("act", lambda e, pbx=pbx, dst=dst, col=col: e.activation(out=dst[:], in_=PB[pbx][:, 0:256], func=AF.Identity, scale=cvw[:, 0, col:col + 1], bias=cvw[:, 3, col:col + 1]),
                           reads=[rPB[pbx], r_cvw], writes=[r_dst])

                def c2_s2(c):
                    ib = c % 2
                    pv, pg = 4 + 2 * ib, 5 + 2 * ib
                    for (pbx, dst, r_dst, col) in ((pg, cgt[ib], r_cg[ib], NFC + c), (pv, cv[ib], r_cv[ib], c)):
                        op("dve", lambda e, pbx=pbx, dst=dst, col=col: e.scalar_tensor_tensor(out=dst[:], in0=PB[pbx][:, 1:257], scalar=cvw[:, 1, col:col + 1], in1=dst[:], op0=ALU.mult, op1=ALU.add),
                           reads=[rPB[pbx], r_cvw, r_dst], writes=[r_dst])
                        op("dve", lambda e, pbx=pbx, dst=dst, col=col: e.scalar_tensor_tensor(out=dst[:], in0=PB[pbx][:, 2:258], scalar=cvw[:, 2, col:col + 1], in1=dst[:], op0=ALU.mult, op1=ALU.add),
                           reads=[rPB[pbx], r_cvw, r_dst], writes=[r_dst])

                def c2_s3(c):
                    ib = c % 2
                    op("act", lambda e: e.activation(out=gz[ib][:], in_=cgt[ib][:], func=AF.Gelu_apprx_tanh), reads=[r_cg[ib]], writes=[r_gz[ib]])

                def c2_s4(c):
                    ib = c % 2
                    op("dve", lambda e: e.tensor_tensor(out=actb[ib][:], in0=gz[ib][:], in1=cv[ib][:], op=ALU.mult), reads=[r_gz[ib], r_cv[ib]], writes=[r_act[ib]])

                def c2_down(c):
                    ib = c % 2
                    for tt in range(2):
                        for hh in range(2):
                            pa_ = tt * 2 + hh
                            op("pe", lambda e, tt=tt, hh=hh, pa_=pa_: e.matmul(PB[pa_][:], lhsT=actb[ib][:, tt * 128:(tt + 1) * 128], rhs=wD[:, c, hh * 512:(hh + 1) * 512],
                                                                             start=(c == 0), stop=(c == NFC - 1)),
                               reads=[r_act[ib], r_wc[c // 2]], writes=[rPB[pa_]])

                def c2_load(sg):
                    a = sg * 256
                    ub = sg % 2
                    dma("sp", ds_misc[ub], u2[ub][:], u2T_d[s, :, a:a + 258].rearrange("(j p) q -> p j q", p=128), writes=[r_u2[ub]])
                    for tt in range(2):
                        sl = (sg * 2 + tt) % 4
                        dma("sp", ds_x[sl], x1t[sl][:], x1_d[s, a + tt * 128:a + (tt + 1) * 128, :], writes=[r_x1t[sl]])

                def c2_prologue(sg):
                    c2_up(sg, 0); c2_up(sg, 1); c2_s1(0); c2_s2(0); c2_s1(1)

                def c2_ln2(sg):
                    for tt in range(2):
                        tile_i = sg * 2 + tt
                        sl = tile_i % 4
                        ib = tile_i % 2
                        ln_part2(x1t[sl][:], r_x1t[sl], lnB[:, 0, :], lnB[:, 1, :], r_lnB, yt[ib], r_yt[ib], st6[ib], r_st6[ib])
                        dma("sp", ds_o[ib], out_d[s, tile_i * 128:(tile_i + 1) * 128, :], yt[ib][:], reads=[r_yt[ib]])

                c2_load(0)
                if NSG > 1:
                    c2_load(1)
                c2_prologue(0)
                for sg in range(NSG):
                    for c in range(NFC):
                        c2_s3(c)
                        if c + 1 < NFC:
                            c2_s2(c + 1)
                        c2_s4(c)
                        if c + 2 < NFC:
                            c2_up(sg, c + 2)
                            c2_s1(c + 2)
                        c2_down(c)
                    for tt in range(2):
                        tile_i = sg * 2 + tt
                        ib = tile_i % 2
                        ln_part1((tt * 2, tt * 2 + 1), (rPB[tt * 2], rPB[tt * 2 + 1]), 1, yt[ib], r_yt[ib])
                    if sg + 1 < NSG:
                        c2_prologue(sg + 1)
                    c2_ln2(sg)
                    if sg + 2 < NSG:
                        c2_load(sg + 2)
            T.barrier()
        print("build: instructions=%d waits=%d" % (T.n_ins, T.n_wait))
    return nc


def _consts():
    ident = np.eye(128, dtype=np.float32)
    i = np.arange(64)
    maskf = (i[:, None] <= i[None, :]).astype(np.uint8)
    maskb = (i[:, None] >= i[None, :]).astype(np.uint8)
    q = np.arange(128)[:, None]
    cc = np.arange(128)[None, :]
    NEG = -30000.0
    am = np.zeros((128, 384), np.float32)
    am[:, 0:128] = np.where(cc >= q, 0.0, NEG)
    am[:, 256:384] = np.where(cc <= q, 0.0, NEG)
    half = 32
    inv = (np.float32(10000.0) ** (-np.arange(half, dtype=np.float32) / np.float32(half))).astype(np.float32)
    p = np.arange(128)
    rope = np.stack([inv[p % 32], np.where((p % 64) < 32, -1.0, 1.0).astype(np.float32)], axis=1).astype(np.float32)
    return dict(k_ident=ident, k_maskf=maskf, k_maskb=maskb, k_amask=am, k_rope=rope)


def make_in_maps(inputs, n_cores, nseq):
    f = lambda a: np.ascontiguousarray(np.asarray(a))
    shared = dict(
        w_ada=f(inputs["w_ada"][0]), b_ada=f(inputs["b_ada"][0:1]), w_in=f(inputs["w_in"][0]),
        rec_lower_bound=f(inputs["rec_lower_bound"]), rec_norm_g=f(inputs["rec_norm_g"][0:1]),
        w_rec_branch=f(inputs["w_rec_branch"][0]), attn_sink=f(inputs["attn_sink"][0:1]),
        w_attn_branch=f(inputs["w_attn_branch"][0]), w_out=f(inputs["w_out"][0]),
        ln1_g=f(inputs["ln1_g"][0:1]), ln1_b=f(inputs["ln1_b"][0:1]), w_up=f(inputs["w_up"][0]),
        conv_w=f(inputs["conv_w"][0]), conv_b=f(inputs["conv_b"][0:1]), w_down=f(inputs["w_down"][0]),
        ln2_g=f(inputs["ln2_g"][0:1]), ln2_b=f(inputs["ln2_b"][0:1]))
    shared.update(_consts())
    maps = []
    for i in range(n_cores):
        m = dict(shared)
        m["x"] = f(inputs["x"][i * nseq:(i + 1) * nseq])
        m["c"] = f(inputs["c"][i * nseq:(i + 1) * nseq])
        m["positions"] = f(inputs["positions"][i * nseq:(i + 1) * nseq]).astype(np.int32)
        maps.append(m)
    return maps


def kernel(**inputs):
    B, S, _ = inputs["x"].shape
    nseq = B // N_CORES
    nc = build_nc(S=S, NSEQ=nseq)
    in_maps = make_in_maps(inputs, N_CORES, nseq)
    res = run_bass_kernel_spmd(nc, in_maps, core_ids=list(range(N_CORES)))
    return np.concatenate([np.asarray(r["out"]) for r in res.results], axis=0).astype(np.float32)
```

```python
import numpy as np
from contextlib import ExitStack
import concourse.bass as bass
import concourse.mybir as mybir
from concourse.bass_utils import run_bass_kernel_spmd

F32 = mybir.dt.float32
BF16 = mybir.dt.bfloat16
I32 = mybir.dt.int32
U8 = mybir.dt.uint8
AF = mybir.ActivationFunctionType
ALU = mybir.AluOpType
AX = mybir.AxisListType

D = 1024
NK = 8
DIN = 5376
DFF = 2816
NFC = DFF // 128
ALPHA = 2.0 ** 0.25
LN_EPS = 1e-5
RMS_EPS = 1e-6
N_CORES = 8
O_RQ, O_RFF, O_RFB, O_RI, O_RG, O_AQ, O_AK, O_AV, O_GR, O_GA = 0, 512, 1024, 1536, 2048, 2560, 3072, 3200, 3328, 4352


class Res:
    __slots__ = ("w", "rd")

    def __init__(self):
        self.w = None
        self.rd = []


class Tracker:
    def __init__(self, nc, stack):
        self.nc = nc
        self.stack = stack
        self.eng = {}
        self.sems = {}
        for name, h in (("pe", nc.tensor), ("act", nc.scalar), ("dve", nc.vector),
                        ("pool", nc.gpsimd), ("sp", nc.sync)):
            sem = stack.enter_context(nc.semaphore("sem_" + name))
            self.eng[name] = {"h": h, "sem": sem, "cnt": 0, "seen": {}, "name": name}
            self.sems[id(sem)] = sem
        self.dma_sems = []
        self.n_wait = 0
        self.n_ins = 0

    def new_dma_sem(self, name):
        sem = self.stack.enter_context(self.nc.semaphore(name))
        d = {"sem": sem, "cnt": 0}
        self.sems[id(sem)] = sem
        self.dma_sems.append(d)
        return d

    def _wait_deps(self, E, reads, writes):
        deps = {}

        def add(tok):
            if tok is not None and deps.get(tok[0], 0) < tok[1]:
                deps[tok[0]] = tok[1]

        for r in reads:
            add(r.w)
        for w in writes:
            add(w.w)
            for t in w.rd:
                add(t)
        own = id(E["sem"])
        for k, v in deps.items():
            if E["name"] == "pe" and k == own:
                continue
            if E["seen"].get(k, 0) >= v:
                continue
            E["h"].wait_ge(self.sems[k], v)
            E["seen"][k] = v
            self.n_wait += 1

    def _commit(self, tok, reads, writes):
        for r in reads:
            r.rd.append(tok)
            if len(r.rd) > 48:
                best = {}
                for k, v in r.rd:
                    if best.get(k, 0) < v:
                        best[k] = v
                r.rd = list(best.items())
        for w in writes:
            w.w = tok
            w.rd = []
        self.n_ins += 1

    def op(self, eng, fn, reads=(), writes=()):
        E = self.eng[eng]
        self._wait_deps(E, reads, writes)
        ins = fn(E["h"])
        E["cnt"] += 1
        ins.then_inc(E["sem"], 1)
        self._commit((id(E["sem"]), E["cnt"]), reads, writes)
        return ins

    def dma(self, q, dsem, out, in_, reads=(), writes=(), **kw):
        E = self.eng[q]
        self._wait_deps(E, reads, writes)
        ins = E["h"].dma_start(out=out, in_=in_, **kw)
        dsem["cnt"] += 16
        ins.then_inc(dsem["sem"], 16)
        self._commit((id(dsem["sem"]), dsem["cnt"]), reads, writes)
        return ins

    def barrier(self):
        targets = [(id(e["sem"]), e["cnt"]) for e in self.eng.values() if e["cnt"] > 0]
        targets += [(id(d["sem"]), d["cnt"]) for d in self.dma_sems if d["cnt"] > 0]
        for E in self.eng.values():
            for k, v in targets:
                if E["seen"].get(k, 0) >= v:
                    continue
                E["h"].wait_ge(self.sems[k], v)
                E["seen"][k] = v
                self.n_wait += 1


class Rot:
    def __init__(self, items):
        self.items = items
        self.i = 0

    def next(self):
        it = self.items[self.i % len(self.items)]
        self.i += 1
        return it


def build_nc(S=2048, NSEQ=2, dbg=False):
    NT, NG, NCH = S // 128, S // 512, S // 64
    nc = bass.Bass("TRN2", target_bir_lowering=False)

    def din(name, shape, dt=F32):
        return nc.dram_tensor(name, list(shape), dt, kind="ExternalInput").ap()

    x_d = din("x", [NSEQ, S, D])
    c_d = din("c", [NSEQ, D])
    pos_d = din("positions", [NSEQ, S], I32)
    w_ada_d = din("w_ada", [D, 6 * D])
    b_ada_d = din("b_ada", [1, 6 * D])
    w_in_d = din("w_in", [D, DIN])
    lbr_d = din("rec_lower_bound", [2, 2, 512])
    ng_d = din("rec_norm_g", [1, 128])
    w_rec_d = din("w_rec_branch", [512, D])
    sink_d = din("attn_sink", [1, 8])
    w_att_d = din("w_attn_branch", [512, D])
    w_out_d = din("w_out", [D, D])
    ln1g_d = din("ln1_g", [1, D]); ln1b_d = din("ln1_b", [1, D])
    w_up_d = din("w_up", [D, 2 * DFF])
    cw_d = din("conv_w", [3, 2 * DFF])
    cb_d = din("conv_b", [1, 2 * DFF])
    w_dn_d = din("w_down", [DFF, D])
    ln2g_d = din("ln2_g", [1, D]); ln2b_d = din("ln2_b", [1, D])
    identf_d = din("k_ident", [128, 128])
    mskf_d = din("k_maskf", [64, 64], U8)
    mskb_d = din("k_maskb", [64, 64], U8)
    amask_d = din("k_amask", [128, 384])
    rope_d = din("k_rope", [128, 2])
    out_d = nc.dram_tensor("out", [NSEQ, S, D], F32, kind="ExternalOutput").ap()
    x1_d = nc.dram_tensor("x1_scr", [NSEQ, S, D], F32).ap()
    u2T_d = nc.dram_tensor("u2T_scr", [NSEQ, D, S + 2], BF16).ap()
    dbg_d = {}
    if dbg:
        dbg_d["mods"] = nc.dram_tensor("dbg_mods", [128, 4, 8], F32, kind="ExternalOutput").ap()
        dbg_d["QT"] = nc.dram_tensor("dbg_QT", [2, 128, 2, S], BF16, kind="ExternalOutput").ap()
        dbg_d["KT"] = nc.dram_tensor("dbg_KT", [2, 128, 2, S], BF16, kind="ExternalOutput").ap()
        dbg_d["V"] = nc.dram_tensor("dbg_V", [64, NCH, 256], BF16, kind="ExternalOutput").ap()
        dbg_d["oT"] = nc.dram_tensor("dbg_oT", [128, 4, S], F32, kind="ExternalOutput").ap()
        dbg_d["kT"] = nc.dram_tensor("dbg_kT", [128, S], BF16, kind="ExternalOutput").ap()
        dbg_d["x1"] = x1_d

    w_in_k = w_in_d.rearrange("(j p) n -> p j n", p=128)
    w_ada_k = w_ada_d.rearrange("(j p) n -> p j n", p=128)
    w_up_k = w_up_d.rearrange("(j p) n -> p j n", p=128)
    w_out_k = w_out_d.rearrange("(j p) n -> p j n", p=128)
    w_rec_k = w_rec_d.rearrange("(j p) n -> p j n", p=128)
    w_att_k = w_att_d.rearrange("(j p) n -> p j n", p=128)
    w_dn_k = w_dn_d.rearrange("(j p) n -> p j n", p=128)

    with ExitStack() as top:
        T = Tracker(nc, top)
        op, dma = T.op, T.dma

        uniq = [0]

        def SB(st, name, shape, dt):
            uniq[0] += 1
            return st.enter_context(nc.sbuf_tensor("%s_%d" % (name, uniq[0]), list(shape), dt))

        PB = [top.enter_context(nc.psum_tensor("pb%d" % i, [128, 512], F32)) for i in range(8)]
        rPB = [Res() for _ in range(8)]
        ds_const = T.new_dma_sem("ds_const")
        ds_w = T.new_dma_sem("ds_w")
        ds_x = [T.new_dma_sem("ds_x%d" % i) for i in range(4)]
        ds_o = [T.new_dma_sem("ds_o%d" % i) for i in range(4)]
        ds_misc = [T.new_dma_sem("ds_m%d" % i) for i in range(4)]
        ds_dbg = T.new_dma_sem("ds_dbg")
        ds_wc = [T.new_dma_sem("ds_wc%d" % i) for i in range(11)]

        identf = SB(top, "identf", [128, 128], F32); r_identf = Res()
        identb = SB(top, "identb", [128, 128], BF16); r_identb = Res()
        onesf = SB(top, "onesf", [128, 128], F32); r_onesf = Res()
        maskf = SB(top, "maskf", [64, 64], U8); maskb = SB(top, "maskb", [64, 64], U8); r_mask = Res()
        zer = SB(top, "zer", [128, 128], F32); r_zer = Res()
        rmask = SB(top, "rmask", [128, 512], F32); r_rmask = Res()
        lbt = SB(top, "lbt", [128, 3, 2, 4], F32); r_lbt = Res()
        lraw = SB(top, "lraw", [128, 2, 2, 4], F32); r_lraw = Res()
        ngc = SB(top, "ngc", [128, 1], F32); r_ngc = Res()
        sinkb = SB(top, "sinkb", [128, 2, 8], F32); r_sink = Res()
        rope = SB(top, "rope", [128, 2], F32); r_rope = Res()
        epsc = SB(top, "epsc", [128, 2], F32); r_eps = Res()
        modsT_all = SB(top, "modsT", [128, NSEQ, 4, 8], F32); r_modsT = Res()
        gaB_all = SB(top, "gaB", [128, NSEQ, 2, D], F32); r_gaB = [Res(), Res()]
        modsT = modsT_all[:, 0]; gaB = gaB_all[:, 0]
        amask = SB(top, "amask", [128, 384], F32); r_amask = Res()

        dma("sp", ds_const, identf[:], identf_d, writes=[r_identf])
        dma("sp", ds_const, maskf[:], mskf_d, writes=[r_mask])
        dma("sp", ds_const, maskb[:], mskb_d, writes=[r_mask])
        dma("sp", ds_const, rope[:], rope_d, writes=[r_rope])
        dma("sp", ds_const, amask[:], amask_d, writes=[r_amask])
        dma("sp", ds_const, ngc[:], ng_d.rearrange("o p -> p o"), writes=[r_ngc], allow_slow_non_contiguous=True)
        dma("sp", ds_const, sinkb[:, 0, :], sink_d[0:1, :].broadcast_to([128, 8]), writes=[r_sink])
        dma("sp", ds_const, lraw[:].rearrange("p a b h -> p (a b) h"),
            lbr_d.rearrange("a b (h p) -> p (a b) h", p=128), writes=[r_lraw], allow_slow_non_contiguous=True)
        T.barrier()
        op("dve", lambda e: e.tensor_copy(out=identb[:], in_=identf[:]), reads=[r_identf], writes=[r_identb])
        op("pool", lambda e: e.memset(onesf[:], 1.0), writes=[r_onesf])
        op("pool", lambda e: e.memset(zer[:], 0.0), writes=[r_zer])
        op("pool", lambda e: e.memset(rmask[:], 1.0), writes=[r_rmask])
        op("pool", lambda e: e.memset(rmask[:].rearrange("p (c t) -> p c t", t=64)[:, :, 0:1], 0.0), writes=[r_rmask])
        op("pool", lambda e: e.memset(epsc[:, 0:1], LN_EPS), writes=[r_eps])
        op("pool", lambda e: e.memset(epsc[:, 1:2], RMS_EPS), writes=[r_eps])
        op("dve", lambda e: e.tensor_scalar(out=sinkb[:, 1, :], in0=sinkb[:, 0, :], scalar1=-1.0, scalar2=None, op0=ALU.mult),
           reads=[r_sink], writes=[r_sink])
        op("dve", lambda e: e.tensor_tensor(out=lbt[:, 2, :, :], in0=lraw[:, :, 0, :], in1=lraw[:, :, 1, :], op=ALU.subtract),
           reads=[r_lraw], writes=[r_lbt])
        op("act", lambda e: e.activation(out=lbt[:, 0, :, :], in_=lbt[:, 2, :, :], func=AF.Sigmoid), reads=[r_lbt], writes=[r_lbt])
        op("act", lambda e: e.activation(out=lbt[:, 1, :, :], in_=lbt[:, 2, :, :], func=AF.Sigmoid, scale=-1.0), reads=[r_lbt], writes=[r_lbt])
        op("dve", lambda e: e.tensor_scalar(out=lbt[:, 2, :, :], in0=lbt[:, 1, :, :], scalar1=-1.0, scalar2=None, op0=ALU.mult),
           reads=[r_lbt], writes=[r_lbt])

        def make_uT(xt, r_xt, dst, r_dst, scol, bcol, banks=(0, 1)):
            for hb in range(2):
                b = banks[hb]
                for i in range(4):
                    k = hb * 4 + i
                    op("pe", lambda e, k=k, i=i, b=b: e.transpose(out=PB[b][:, i * 128:(i + 1) * 128], in_=xt[:, k * 128:(k + 1) * 128],
                                                                   identity=identf[:]),
                       reads=[r_xt, r_identf], writes=[rPB[b]])
                for i in range(4):
                    k = hb * 4 + i
                    if i % 2 == 0:
                        op("act", lambda e, k=k, i=i, b=b: e.activation(out=dst[:, k, :], in_=PB[b][:, i * 128:(i + 1) * 128], func=AF.Identity,
                                                                         scale=modsT[:, scol, k:k + 1], bias=modsT[:, bcol, k:k + 1]),
                           reads=[rPB[b], r_modsT], writes=[r_dst])
                    else:
                        op("dve", lambda e, k=k, i=i, b=b: e.tensor_scalar(out=dst[:, k, :], in0=PB[b][:, i * 128:(i + 1) * 128],
                                                                            scalar1=modsT[:, scol, k:k + 1], scalar2=modsT[:, bcol, k:k + 1],
                                                                            op0=ALU.mult, op1=ALU.add),
                           reads=[rPB[b], r_modsT], writes=[r_dst])

        def ln_part1(pz, r_pz, which, y, r_y):
            for h in range(2):
                op("dve", lambda e, h=h: e.tensor_tensor(out=y[:, h * 512:(h + 1) * 512], in0=PB[pz[h]][:], in1=gaB[:, which, h * 512:(h + 1) * 512], op=ALU.mult),
                   reads=[r_pz[h], r_gaB[which]], writes=[r_y])

        def ln_part2(xres, r_xres, gB, bB, r_gb, y, r_y, st6, r_st):
            op("dve", lambda e: e.scalar_tensor_tensor(out=y[:], in0=xres, scalar=ALPHA, in1=y[:], op0=ALU.mult, op1=ALU.add),
               reads=[r_xres, r_y], writes=[r_y])
            for h in range(2):
                op("dve", lambda e, h=h: e.bn_stats(out=st6[:, h * 6:(h + 1) * 6], in_=y[:, h * 512:(h + 1) * 512]), reads=[r_y], writes=[r_st])
            op("dve", lambda e: e.bn_aggr(out=st6[:, 12:14], in_=st6[:, 0:12]), reads=[r_st], writes=[r_st])
            op("act", lambda e: e.activation(out=st6[:, 14:15], in_=st6[:, 13:14], func=AF.Ln, bias=epsc[:, 0:1], scale=1.0), reads=[r_st, r_eps], writes=[r_st])
            op("act", lambda e: e.activation(out=st6[:, 15:16], in_=st6[:, 14:15], func=AF.Exp, scale=-0.5), reads=[r_st], writes=[r_st])
            op("dve", lambda e: e.tensor_scalar(out=st6[:, 16:17], in0=st6[:, 12:13], scalar1=st6[:, 15:16], scalar2=-1.0, op0=ALU.mult, op1=ALU.mult),
               reads=[r_st], writes=[r_st])
            op("act", lambda e: e.activation(out=y[:], in_=y[:], func=AF.Identity, scale=st6[:, 15:16], bias=st6[:, 16:17]),
               reads=[r_y, r_st], writes=[r_y])
            op("dve", lambda e: e.tensor_tensor(out=y[:], in0=y[:], in1=gB, op=ALU.mult), reads=[r_y, r_gb], writes=[r_y])
            op("dve", lambda e: e.tensor_tensor(out=y[:], in0=y[:], in1=bB, op=ALU.add), reads=[r_y, r_gb], writes=[r_y])

        def ln_epilogue(pz, r_pz, xres, r_xres, which, gB, bB, r_gb, y, r_y, st6, r_st):
            ln_part1(pz, r_pz, which, y, r_y)
            ln_part2(xres, r_xres, gB, bB, r_gb, y, r_y, st6, r_st)

        for s in range(NSEQ):
            modsT = modsT_all[:, s]
            gaB = gaB_all[:, s]
            if s == 0:
              with ExitStack() as ph:
                cT = SB(ph, "cT", [128, NSEQ, 8], F32); r_cT = Res()
                cbt = SB(ph, "cbt", [128, NSEQ, 8, 128], F32); r_cbt = Res()
                wst = [SB(ph, "wst%d" % i, [128, 8, 512], F32) for i in range(2)]; r_wst = [Res(), Res()]
                bst = [SB(ph, "bst%d" % i, [1, 512], F32) for i in range(2)]; r_bst = [Res(), Res()]
                mtmp = [SB(ph, "mtmp%d" % i, [128, 512], F32) for i in range(NSEQ)]; r_mtmp = [Res() for _ in range(NSEQ)]
                for ss in range(NSEQ):
                    dma("sp", ds_misc[0], cT[:, ss, :], c_d[ss].rearrange("(j p) -> p j", p=128), writes=[r_cT], allow_slow_non_contiguous=True)
                op("act", lambda e: e.activation(out=cT[:], in_=cT[:], func=AF.Silu), reads=[r_cT], writes=[r_cT])
                op("dve", lambda e: e.tensor_copy(out=cbt[:], in_=cT[:].unsqueeze(3).broadcast_to([128, NSEQ, 8, 128])), reads=[r_cT], writes=[r_cbt])
                bcol = SB(ph, "bcol", [128, 48], F32); r_bcol = Res()
                cT2 = SB(ph, "cT2", [128, 8, NSEQ], F32); r_cT2 = Res()
                dma("sp", ds_misc[3], bcol[:], b_ada_d.rearrange("o (j p) -> p (o j)", p=128), writes=[r_bcol], allow_slow_non_contiguous=True)
                op("dve", lambda e: e.tensor_copy(out=cT2[:], in_=cT[:].rearrange("p s k -> p k s")), reads=[r_cT], writes=[r_cT2])
                for cg in range(12):
                    b2 = cg % 2
                    dma("sp", ds_misc[1 + b2], wst[b2][:], w_ada_k[:, :, cg * 512:(cg + 1) * 512], writes=[r_wst[b2]])
                    which, half = cg // 2, cg % 2
                    if which in (2, 5):
                        dma("sp", ds_o[b2], bst[b2][:], b_ada_d[0:1, cg * 512:(cg + 1) * 512], writes=[r_bst[b2]])
                        gi = 0 if which == 2 else 1
                        for ss in range(NSEQ):
                            pb = 2 + b2 + 2 * (ss % 2)
                            for k in range(8):
                                op("pe", lambda e, k=k: e.matmul(PB[pb][:], lhsT=cbt[:, ss, k, :], rhs=wst[b2][:, k, :], start=(k == 0), stop=False),
                                   reads=[r_cbt, r_wst[b2]], writes=[rPB[pb]])
                            op("pe", lambda e: e.matmul(PB[pb][:], lhsT=onesf[0:1, :], rhs=bst[b2][0:1, :], start=False, stop=True),
                               reads=[r_onesf, r_bst[b2]], writes=[rPB[pb]])
                            op("dve", lambda e: e.tensor_scalar(out=gaB_all[:, ss, gi, half * 512:(half + 1) * 512], in0=PB[pb][:], scalar1=1.0, scalar2=None, op0=ALU.add),
                               reads=[rPB[pb]], writes=[r_gaB[gi]])
                    else:
                        widx = {0: 0, 1: 1, 3: 2, 4: 3}[which]
                        addc = 1.0 if which in (1, 4) else 0.0
                        pt = 6 + b2
                        for i in range(4):
                            for k in range(8):
                                op("pe", lambda e, k=k, i=i: e.matmul(PB[pt][:, i * NSEQ:(i + 1) * NSEQ], lhsT=wst[b2][:, k, i * 128:(i + 1) * 128], rhs=cT2[:, k, :], start=(k == 0), stop=(k == 7)),
                                   reads=[r_cT2, r_wst[b2]], writes=[rPB[pt]])
                        op("dve", lambda e: e.tensor_tensor(out=modsT_all[:, :, widx, half * 4:(half + 1) * 4], in0=PB[pt][:, 0:4 * NSEQ].rearrange("p (i s) -> p s i", s=NSEQ),
                                                            in1=bcol[:, cg * 4:(cg + 1) * 4].unsqueeze(1).broadcast_to([128, NSEQ, 4]), op=ALU.add),
                           reads=[rPB[pt], r_bcol], writes=[r_modsT])
                        if addc != 0.0:
                            op("dve", lambda e: e.tensor_scalar(out=modsT_all[:, :, widx, half * 4:(half + 1) * 4], in0=modsT_all[:, :, widx, half * 4:(half + 1) * 4],
                                                                scalar1=addc, scalar2=None, op0=ALU.add), reads=[r_modsT], writes=[r_modsT])
                if dbg:
                    dma("sp", ds_dbg, dbg_d["mods"], modsT_all[:, 0], reads=[r_modsT])
              T.barrier()

            with ExitStack() as pab:
                oT = SB(pab, "oT", [128, 4, S], F32)
                r_oT = [[Res() for _ in range(NCH)] for _ in range(4)]
                kTt = SB(pab, "kTt", [128, S], BF16); r_kT = [Res() for _ in range(NG)]
                vtk = SB(pab, "vtk", [128, NT, 128], BF16); r_v = [Res() for _ in range(NT)]
                cosT = SB(pab, "cosT", [128, S], F32); sinT = SB(pab, "sinT", [128, S], F32); r_cs = Res()
                with ExitStack() as pr:
                    posi = SB(pr, "posi", [128, S], I32); r_posi = Res()
                    ang = SB(pr, "ang", [128, S], F32); r_ang = Res()
                    kq = SB(pr, "kq", [128, S], F32); r_kq = Res()
                    ki = SB(pr, "ki", [128, S], I32); r_ki = Res()
                    dma("sp", ds_misc[0], posi[:], pos_d[s:s + 1, :].broadcast_to([128, S]), writes=[r_posi])
                    op("dve", lambda e: e.tensor_copy(out=ang[:], in_=posi[:]), reads=[r_posi], writes=[r_ang])
                    op("dve", lambda e: e.tensor_scalar(out=ang[:], in0=ang[:], scalar1=rope[:, 0:1], scalar2=None, op0=ALU.mult),
                       reads=[r_ang, r_rope], writes=[r_ang])
                    TWO_PI = 6.283185307179586
                    C1 = 6.28125
                    C2 = TWO_PI - C1
                    for (dst, shift) in ((sinT, 0.0), (cosT, np.pi / 2)):
                        op("dve", lambda e, shift=shift: e.tensor_scalar(out=kq[:], in0=ang[:], scalar1=float(shift), scalar2=1.0 / TWO_PI, op0=ALU.add, op1=ALU.mult),
                           reads=[r_ang], writes=[r_kq])
                        op("dve", lambda e: e.tensor_copy(out=ki[:], in_=kq[:]), reads=[r_kq], writes=[r_ki])
                        op("dve", lambda e: e.tensor_copy(out=kq[:], in_=ki[:]), reads=[r_ki], writes=[r_kq])
                        op("dve", lambda e, shift=shift, dst=dst: e.tensor_scalar(out=dst[:], in0=ang[:], scalar1=float(shift), scalar2=None, op0=ALU.add),
                           reads=[r_ang], writes=[r_cs])
                        op("dve", lambda e, dst=dst: e.scalar_tensor_tensor(out=dst[:], in0=kq[:], scalar=-C1, in1=dst[:], op0=ALU.mult, op1=ALU.add),
                           reads=[r_kq, r_cs], writes=[r_cs])
                        op("dve", lambda e, dst=dst: e.scalar_tensor_tensor(out=dst[:], in0=kq[:], scalar=-C2, in1=dst[:], op0=ALU.mult, op1=ALU.add),
                           reads=[r_kq, r_cs], writes=[r_cs])
                        op("dve", lambda e, dst=dst: e.tensor_scalar(out=kq[:], in0=dst[:], scalar1=float(np.pi), scalar2=-TWO_PI, op0=ALU.is_gt, op1=ALU.mult),
                           reads=[r_cs], writes=[r_kq])
                        op("dve", lambda e, dst=dst: e.tensor_tensor(out=dst[:], in0=dst[:], in1=kq[:], op=ALU.add), reads=[r_cs, r_kq], writes=[r_cs])
                        op("dve", lambda e, dst=dst: e.tensor_scalar(out=kq[:], in0=dst[:], scalar1=float(-np.pi), scalar2=TWO_PI, op0=ALU.is_lt, op1=ALU.mult),
                           reads=[r_cs], writes=[r_kq])
                        op("dve", lambda e, dst=dst: e.tensor_tensor(out=dst[:], in0=dst[:], in1=kq[:], op=ALU.add), reads=[r_cs, r_kq], writes=[r_cs])
                        op("dve", lambda e, dst=dst: e.tensor_scalar(out=dst[:], in0=dst[:], scalar1=3.1415925, scalar2=-3.1415925, op0=ALU.min, op1=ALU.max),
                           reads=[r_cs], writes=[r_cs])
                        op("act", lambda e, dst=dst: e.activation(out=dst[:], in_=dst[:], func=AF.Sin), reads=[r_cs], writes=[r_cs])
                    op("dve", lambda e: e.tensor_scalar(out=sinT[:], in0=sinT[:], scalar1=rope[:, 1:2], scalar2=None, op0=ALU.mult),
                       reads=[r_cs, r_rope], writes=[r_cs])
                T.barrier()

                for hp in range(2):
                    with ExitStack() as pa:
                        NWA = 1024 + (384 if hp == 0 else 0)
                        wA = SB(pa, "wA", [128, 8, NWA], BF16); r_wA = Res()
                        QT = [SB(pa, "QT%d" % d, [128, 2, S], BF16) for d in range(2)]
                        KT = [SB(pa, "KT%d" % d, [128, 2, S], BF16) for d in range(2)]
                        r_QK = [[[Res() for _ in range(NG)] for _ in range(2)] for _ in range(2)]
                        Vt = SB(pa, "Vt", [64, NCH, 256], BF16); r_V = [Res() for _ in range(NG)]
                        Bref = SB(pa, "Bref", [128, 2, 2, NCH], F32); Bend = SB(pa, "Bend", [128, 2, 2, NCH], F32); r_B = Res()
                        Gx = SB(pa, "Gx", [128, 2, 2, NCH], F32); r_G = Res()
                        for seg, base in enumerate((O_RQ, O_RFF, O_RFB, O_RI)):
                            dma("pool", ds_w, wA[:, :, seg * 256:(seg + 1) * 256], w_in_k[:, :, base + hp * 256: base + hp * 256 + 256], writes=[r_wA])
                        if hp == 0:
                            dma("pool", ds_w, wA[:, :, 1024:1152], w_in_k[:, :, O_AK:O_AK + 128], writes=[r_wA])
                            for kv in range(2):
                                dma("pool", ds_w, wA[:, :, 1152 + kv * 64:1152 + kv * 64 + 32], w_in_k[:, :, O_AK + kv * 64 + 32:O_AK + kv * 64 + 64], writes=[r_wA])
                                dma("pool", ds_w, wA[:, :, 1152 + kv * 64 + 32:1152 + kv * 64 + 64], w_in_k[:, :, O_AK + kv * 64:O_AK + kv * 64 + 32], writes=[r_wA])
                            dma("pool", ds_w, wA[:, :, 1280:1408], w_in_k[:, :, O_AV:O_AV + 128], writes=[r_wA])
                        with ExitStack() as pa2:
                            xs = [SB(pa2, "xs%d" % i, [128, D], F32) for i in range(2)]; r_xs = [Res() for _ in range(2)]
                            uT = [SB(pa2, "uT%d" % i, [128, 8, 512], BF16) for i in range(1)]; r_uT = [Res()]
                            qs = [SB(pa2, "qs%d" % i, [128, 512], F32) for i in range(2)]; r_qs = [Res(), Res()]
                            NTMP = 2
                            tnames = ("sig", "lg", "Bf", "Eq", "Ek")
                            tmps = [{n: SB(pa2, "t_%s%d" % (n, i), [128, 512], F32) for n in tnames} for i in range(NTMP)]
                            r_tmps = [{n: Res() for n in tnames} for i in range(NTMP)]
                            for i in range(NTMP):
                                tmps[i]["kk"] = tmps[i]["sig"]; r_tmps[i]["kk"] = r_tmps[i]["sig"]
                                tmps[i]["dd"] = tmps[i]["lg"]; r_tmps[i]["dd"] = r_tmps[i]["lg"]
                            tctr = 0
                            bank_rot = Rot([2, 3, 4, 5, 6, 7])
                            for g in range(NG):
                                ub = 0
                                for t in range(4):
                                    tile_i = g * 4 + t
                                    sl = tile_i % 2
                                    dma("sp", ds_x[sl], xs[sl][:], x_d[s, tile_i * 128:(tile_i + 1) * 128, :], writes=[r_xs[sl]])
                                    make_uT(xs[sl], r_xs[sl], uT[ub][:, :, t * 128:(t + 1) * 128], r_uT[ub], 1, 0)
                                gs = slice(g * 512, (g + 1) * 512)
                                for cp in range(4):
                                    pb = bank_rot.next()
                                    for cc in range(2):
                                        ch = cp * 2 + cc
                                        for k in range(8):
                                            op("pe", lambda e, k=k, cc=cc, ch=ch, pb=pb: e.matmul(PB[pb][0:64, cc * 256:(cc + 1) * 256], lhsT=uT[ub][:, k, ch * 64:(ch + 1) * 64],
                                                                                                   rhs=wA[:, k, 768:1024], start=(k == 0), stop=(k == 7)),
                                               reads=[r_uT[ub], r_wA], writes=[rPB[pb]])
                                    op("act", lambda e, pb=pb, cp=cp: e.activation(out=Vt[:, g * 8 + cp * 2:g * 8 + cp * 2 + 2, :],
                                                                                    in_=PB[pb][0:64, :].rearrange("p (c n) -> p c n", n=256), func=AF.Copy, scale=128.0 ** -0.5),
                                       reads=[rPB[pb]], writes=[r_V[g]])
                                for hl in range(2):
                                    h = hp * 2 + hl
                                    pbq = bank_rot.next()
                                    for k in range(8):
                                        op("pe", lambda e, k=k: e.matmul(PB[pbq][:], lhsT=wA[:, k, hl * 128:(hl + 1) * 128], rhs=uT[ub][:, k, :],
                                                                         start=(k == 0), stop=(k == 7)), reads=[r_uT[ub], r_wA], writes=[rPB[pbq]])
                                    pbd = []
                                    for d in range(2):
                                        pb = bank_rot.next(); pbd.append(pb)
                                        cb = 256 * (1 + d) + hl * 128
                                        for k in range(8):
                                            op("pe", lambda e, k=k, pb=pb, cb=cb: e.matmul(PB[pb][:], lhsT=wA[:, k, cb:cb + 128], rhs=uT[ub][:, k, :],
                                                                                             start=(k == 0), stop=(k == 7)), reads=[r_uT[ub], r_wA], writes=[rPB[pb]])
                                    op("act", lambda e: e.activation(out=qs[hl][:], in_=PB[pbq][:], func=AF.Silu), reads=[rPB[pbq]], writes=[r_qs[hl]])
                                    TM = [tmps[0], tmps[1]]; RT = [r_tmps[0], r_tmps[1]]
                                    for d in range(2):
                                        op("act", lambda e, d=d: e.activation(out=TM[d]["sig"][:], in_=PB[pbd[d]][:], func=AF.Sigmoid), reads=[rPB[pbd[d]]], writes=[RT[d]["sig"]])
                                    for d in range(2):
                                        lb_c = lbt[:, 0, d, h:h + 1]; om_c = lbt[:, 1, d, h:h + 1]
                                        op("act", lambda e, d=d, lb_c=lb_c, om_c=om_c: e.activation(out=TM[d]["lg"][:], in_=TM[d]["sig"][:], func=AF.Ln, scale=om_c, bias=lb_c),
                                           reads=[RT[d]["sig"], r_lbt], writes=[RT[d]["lg"]])
                                    irefs = []
                                    for d in range(2):
                                        tm, rt = TM[d], RT[d]
                                        om_c = lbt[:, 1, d, h:h + 1]; nom_c = lbt[:, 2, d, h:h + 1]
                                        op("dve", lambda e, tm=tm, om_c=om_c, nom_c=nom_c: e.tensor_scalar(out=tm["kk"][:], in0=tm["sig"][:], scalar1=nom_c, scalar2=om_c, op0=ALU.mult, op1=ALU.add),
                                           reads=[rt["sig"], r_lbt], writes=[rt["kk"]])
                                        op("dve", lambda e, tm=tm: e.tensor_tensor_scan(out=tm["Bf"][:], data0=rmask[:], data1=tm["lg"][:], initial=0.0, op0=ALU.mult, op1=ALU.add),
                                           reads=[r_rmask, rt["lg"]], writes=[rt["Bf"]])
                                        B3 = tm["Bf"][:].rearrange("p (c t) -> p c t", t=64)
                                        if d == 1:
                                            lg3 = tm["lg"][:].rearrange("p (c t) -> p c t", t=64)
                                            op("dve", lambda e, B3=B3, lg3=lg3: e.tensor_tensor(out=lg3, in0=lg3, in1=B3, op=ALU.subtract), reads=[rt["lg"], rt["Bf"]], writes=[rt["lg"]])
                                            op("dve", lambda e, B3=B3, lg3=lg3: e.tensor_tensor(out=B3, in0=lg3, in1=B3[:, :, 63:64].broadcast_to([128, 8, 64]), op=ALU.add),
                                               reads=[rt["lg"], rt["Bf"]], writes=[rt["Bf"]])
                                            iref, iend = 32, 0
                                        else:
                                            iref, iend = 31, 63
                                        irefs.append(iref)
                                        op("act", lambda e, B3=B3, iref=iref, d=d: e.activation(out=Bref[:, d, hl, g * 8:(g + 1) * 8], in_=B3[:, :, iref], func=AF.Copy), reads=[rt["Bf"]], writes=[r_B])
                                        op("act", lambda e, B3=B3, iend=iend, d=d: e.activation(out=Bend[:, d, hl, g * 8:(g + 1) * 8], in_=B3[:, :, iend], func=AF.Copy), reads=[rt["Bf"]], writes=[r_B])
                                        op("dve", lambda e, B3=B3, tm=tm, iref=iref: e.tensor_tensor(out=tm["dd"][:].rearrange("p (c t) -> p c t", t=64), in0=B3,
                                                                                                       in1=B3[:, :, iref:iref + 1].broadcast_to([128, 8, 64]), op=ALU.subtract),
                                           reads=[rt["Bf"]], writes=[rt["dd"]])
                                    for d in range(2):
                                        tm, rt = TM[d], RT[d]
                                        op("act", lambda e, tm=tm: e.activation(out=tm["Eq"][:], in_=tm["dd"][:], func=AF.Exp), reads=[rt["dd"]], writes=[rt["Eq"]])
                                        op("act", lambda e, tm=tm: e.activation(out=tm["Ek"][:], in_=tm["dd"][:], func=AF.Exp, scale=-1.0), reads=[rt["dd"]], writes=[rt["Ek"]])
                                    for d in range(2):
                                        tm, rt = TM[d], RT[d]
                                        op("dve", lambda e, tm=tm, d=d: e.tensor_tensor(out=KT[d][:, hl, gs], in0=tm["kk"][:], in1=tm["Ek"][:], op=ALU.mult),
                                           reads=[rt["kk"], rt["Ek"]], writes=[r_QK[d][hl][g]])
                                        op("dve", lambda e, tm=tm, d=d: e.tensor_tensor(out=QT[d][:, hl, gs], in0=qs[hl][:], in1=tm["Eq"][:], op=ALU.mult),
                                           reads=[r_qs[hl], rt["Eq"]], writes=[r_QK[d][hl][g]])
                                if hp == 0:
                                    pb1 = bank_rot.next()
                                    pb2 = bank_rot.next()
                                    for (pbx, cb) in ((pb1, 1024), (pb2, 1152)):
                                        for k in range(8):
                                            op("pe", lambda e, k=k, pbx=pbx, cb=cb: e.matmul(PB[pbx][:], lhsT=wA[:, k, cb:cb + 128], rhs=uT[ub][:, k, :], start=(k == 0), stop=(k == 7)),
                                               reads=[r_uT[ub], r_wA], writes=[rPB[pbx]])
                                    tm = tmps[tctr % NTMP]; rt = r_tmps[tctr % NTMP]; tctr += 1
                                    op("dve", lambda e, tm=tm: e.tensor_tensor(out=tm["sig"][:], in0=PB[pb1][:], in1=cosT[:, gs], op=ALU.mult), reads=[rPB[pb1], r_cs], writes=[rt["sig"]])
                                    op("dve", lambda e, tm=tm: e.tensor_tensor(out=tm["lg"][:], in0=PB[pb2][:], in1=sinT[:, gs], op=ALU.mult), reads=[rPB[pb2], r_cs], writes=[rt["lg"]])
                                    op("dve", lambda e, tm=tm: e.tensor_tensor(out=kTt[:, gs], in0=tm["sig"][:], in1=tm["lg"][:], op=ALU.add), reads=[rt["sig"], rt["lg"]], writes=[r_kT[g]])
                                    for t in range(4):
                                        pb = bank_rot.next()
                                        for k in range(8):
                                            op("pe", lambda e, k=k, pb=pb, t=t: e.matmul(PB[pb][:, 0:128], lhsT=uT[ub][:, k, t * 128:(t + 1) * 128], rhs=wA[:, k, 1280:1408],
                                                                                          start=(k == 0), stop=(k == 7)), reads=[r_uT[ub], r_wA], writes=[rPB[pb]])
                                        op("act", lambda e, pb=pb, t=t: e.activation(out=vtk[:, g * 4 + t, :], in_=PB[pb][:, 0:128], func=AF.Copy), reads=[rPB[pb]], writes=[r_v[g * 4 + t]])
                            op("dve", lambda e: e.tensor_tensor(out=Gx[:], in0=Bend[:], in1=Bref[:], op=ALU.subtract), reads=[r_B], writes=[r_G])
                            if NCH > 1:
                                op("dve", lambda e: e.tensor_tensor(out=Gx[:, 0, :, 0:NCH - 1], in0=Gx[:, 0, :, 0:NCH - 1], in1=Bref[:, 0, :, 1:NCH], op=ALU.add), reads=[r_B, r_G], writes=[r_G])
                                op("dve", lambda e: e.tensor_tensor(out=Gx[:, 1, :, 1:NCH], in0=Gx[:, 1, :, 1:NCH], in1=Bref[:, 1, :, 0:NCH - 1], op=ALU.add), reads=[r_B, r_G], writes=[r_G])
                            op("act", lambda e: e.activation(out=Gx[:], in_=Gx[:], func=AF.Exp), reads=[r_G], writes=[r_G])
                            if dbg and s == 0 and hp == 0:
                                for d in range(2):
                                    dma("sp", ds_dbg, dbg_d["QT"][d], QT[d][:], reads=[r_QK[d][hl][g] for hl in range(2) for g in range(NG)])
                                    dma("sp", ds_dbg, dbg_d["KT"][d], KT[d][:], reads=[r_QK[d][hl][g] for hl in range(2) for g in range(NG)])
                                dma("sp", ds_dbg, dbg_d["V"], Vt[:], reads=r_V)
                                dma("sp", ds_dbg, dbg_d["kT"], kTt[:], reads=r_kT)
                        T.barrier()

                        with ExitStack() as pb_:
                            Sm32 = SB(pb_, "Sm32", [128, 4, 128], F32); Smb = SB(pb_, "Smb", [128, 4, 128], BF16); tmp32 = SB(pb_, "tmp32", [128, 4, 128], F32)
                            r_Sm32 = [Res() for _ in range(4)]; r_Smb = [Res() for _ in range(4)]; r_tmp32 = [Res() for _ in range(4)]
                            scm = SB(pb_, "scm", [64, 2, 4, 64], BF16); r_scm = [[Res() for _ in range(4)] for _ in range(2)]
                            ktok = SB(pb_, "ktok", [64, 2, 4, 128], BF16); r_ktok = [[Res() for _ in range(4)] for _ in range(2)]
                            rB = [[None] * 4 for _ in range(2)]
                            for _p in range(2):
                                for _c in range(4):
                                    _r = Res()
                                    rB[_p][_c] = {n: _r for n in ("sc", "kt", "o", "dl")}
                            op("pool", lambda e: e.memset(Sm32[:], 0.0), writes=r_Sm32)
                            op("pool", lambda e: e.memset(Smb[:], 0.0), writes=r_Smb)
                            allQK = lambda d, hl: [r_QK[d][hl][g] for g in range(NG)]

                            def chunk_of(j, d):
                                return j if d == 0 else NCH - 1 - j

                            def stage1(j):
                                par = j % 2
                                for ci in range(4):
                                    hl, d = ci // 2, ci % 2
                                    c = chunk_of(j, d); tk = slice(c * 64, (c + 1) * 64)
                                    bk = ci + 4 * par
                                    op("pe", lambda e, bk=bk, d=d, hl=hl, tk=tk: e.matmul(PB[bk][0:64, 0:64], lhsT=KT[d][:, hl, tk], rhs=QT[d][:, hl, tk], start=True, stop=True),
                                       reads=allQK(d, hl), writes=[rB[par][ci]["sc"]])
                                    op("pe", lambda e, bk=bk, d=d, hl=hl, tk=tk: e.transpose(out=PB[bk][:].bitcast(BF16)[0:64, 128:256], in_=KT[d][:, hl, tk], identity=identb[:]),
                                       reads=allQK(d, hl) + [r_identb], writes=[rB[par][ci]["kt"]])

                            def stage2(j):
                                par = j % 2
                                for ci in range(4):
                                    hl, d = ci // 2, ci % 2
                                    bk = ci + 4 * par
                                    mk = maskf if d == 0 else maskb
                                    op("pool", lambda e, ci=ci: e.memset(scm[:, par, ci, :], 0.0), writes=[r_scm[par][ci]])
                                    op("dve", lambda e, bk=bk, ci=ci, mk=mk: e.copy_predicated(out=scm[:, par, ci, :], mask=mk[:], data=PB[bk][0:64, 0:64]),
                                       reads=[rB[par][ci]["sc"], r_mask], writes=[r_scm[par][ci]])
                                    op("act", lambda e, bk=bk, ci=ci: e.activation(out=ktok[:, par, ci, :], in_=PB[bk][:].bitcast(BF16)[0:64, 128:256], func=AF.Copy),
                                       reads=[rB[par][ci]["kt"]], writes=[r_ktok[par][ci]])

                            def stage3(j):
                                par = j % 2
                                for ci in range(4):
                                    hl, d = ci // 2, ci % 2
                                    c = chunk_of(j, d); tk = slice(c * 64, (c + 1) * 64)
                                    bk = ci + 4 * par
                                    vv = Vt[:, c, hl * 128:(hl + 1) * 128]
                                    op("pe", lambda e, bk=bk, vv=vv, ci=ci: e.matmul(PB[bk][:, 128:192], lhsT=vv, rhs=scm[:, par, ci, :], start=True, stop=False),
                                       reads=[r_V[c // 8], r_scm[par][ci]], writes=[rB[par][ci]["o"]])
                                    op("pe", lambda e, bk=bk, ci=ci, d=d, hl=hl, tk=tk: e.matmul(PB[bk][:, 128:192], lhsT=Smb[:, ci, :], rhs=QT[d][:, hl, tk], start=False, stop=True),
                                       reads=[r_Smb[ci]] + allQK(d, hl), writes=[rB[par][ci]["o"]])
                                    op("pe", lambda e, bk=bk, vv=vv, ci=ci: e.matmul(PB[bk][:, 192:320], lhsT=ktok[:, par, ci, :], rhs=vv, start=True, stop=True),
                                       reads=[r_ktok[par][ci], r_V[c // 8]], writes=[rB[par][ci]["dl"]])

                            def stage4(j):
                                par = j % 2
                                for ci in range(4):
                                    hl, d = ci // 2, ci % 2
                                    h = hp * 2 + hl
                                    c = chunk_of(j, d); tk = slice(c * 64, (c + 1) * 64)
                                    bk = ci + 4 * par
                                    if j < NCH // 2:
                                        op("act", lambda e, bk=bk, h=h, tk=tk: e.activation(out=oT[:, h, tk], in_=PB[bk][:, 128:192], func=AF.Copy),
                                           reads=[rB[par][ci]["o"]], writes=[r_oT[h][c]])
                                    else:
                                        op("dve", lambda e, bk=bk, h=h, tk=tk: e.tensor_tensor(out=oT[:, h, tk], in0=PB[bk][:, 128:192], in1=oT[:, h, tk], op=ALU.add),
                                           reads=[rB[par][ci]["o"], r_oT[h][c]], writes=[r_oT[h][c]])
                                    if j < NCH - 1:
                                        gcol = Gx[:, d, hl, c:c + 1]
                                        op("dve", lambda e, bk=bk, ci=ci: e.tensor_tensor(out=tmp32[:, ci, :], in0=PB[bk][:, 192:320], in1=Sm32[:, ci, :], op=ALU.add),
                                           reads=[rB[par][ci]["dl"], r_Sm32[ci]], writes=[r_tmp32[ci]])
                                        op("act", lambda e, ci=ci, gcol=gcol: e.activation(out=Sm32[:, ci, :], in_=tmp32[:, ci, :], func=AF.Identity, scale=gcol),
                                           reads=[r_tmp32[ci], r_G], writes=[r_Sm32[ci]])
                                        op("pool", lambda e, ci=ci: e.tensor_copy(out=Smb[:, ci, :], in_=Sm32[:, ci, :]),
                                           reads=[r_Sm32[ci]], writes=[r_Smb[ci]])

                            stage1(0)
                            stage2(0)
                            for j in range(NCH):
                                if j + 1 < NCH:
                                    stage1(j + 1)
                                stage3(j)
                                if j + 1 < NCH:
                                    stage2(j + 1)
                                stage4(j)
                        T.barrier()
                if dbg and s == 0:
                    dma("sp", ds_dbg, dbg_d["oT"], oT[:], reads=[r for rr in r_oT for r in rr])

                ogA = SB(pab, "ogA", [128, 4, S], BF16); atA = SB(pab, "atA", [128, 4, S], BF16)
                r_ogA = [Res() for _ in range(NG)]; r_atA = [Res() for _ in range(NT)]
                with ExitStack() as pc:
                    wC = SB(pc, "wC", [128, 8, 1536], BF16); r_wC = Res()
                    dma("pool", ds_w, wC[:, :, 0:512], w_in_k[:, :, O_RG:O_RG + 512], writes=[r_wC])
                    for m in range(4):
                        for hf in range(2):
                            hq = m + 4 * hf
                            cdst = 512 + m * 128 + hf * 64
                            dma("pool", ds_w, wC[:, :, cdst:cdst + 64], w_in_k[:, :, O_AQ + hq * 64:O_AQ + hq * 64 + 64], writes=[r_wC])
                            dma("pool", ds_w, wC[:, :, 512 + cdst:512 + cdst + 32], w_in_k[:, :, O_AQ + hq * 64 + 32:O_AQ + hq * 64 + 64], writes=[r_wC])
                            dma("pool", ds_w, wC[:, :, 512 + cdst + 32:512 + cdst + 64], w_in_k[:, :, O_AQ + hq * 64:O_AQ + hq * 64 + 32], writes=[r_wC])
                    xs = [SB(pc, "cxs%d" % i, [128, D], F32) for i in range(2)]; r_xs = [Res() for _ in range(2)]
                    uT = SB(pc, "cuT", [128, 8, 512], BF16); r_uT = Res()
                    qT = SB(pc, "qT", [128, 4, 512], BF16); r_qT = Res()
                    srg = SB(pc, "srg", [128, 4, 512], F32); r_srg = [Res() for _ in range(4)]
                    ta = [SB(pc, "cta%d" % i, [128, 512], F32) for i in range(4)]; r_ta = [Res() for _ in range(4)]
                    msc = [SB(pc, "msc%d" % i, [128, 384], F32) for i in range(2)]; r_msc = [Res(), Res()]
                    pex = [SB(pc, "pex%d" % i, [128, 384], BF16) for i in range(2)]; r_pex = [Res(), Res()]
                    pT = [SB(pc, "pT%d" % i, [128, 3, 128], BF16) for i in range(2)]; r_pT = [Res(), Res()]
                    sm = [SB(pc, "smx%d" % i, [128, 8], F32) for i in range(2)]; r_sm = [Res(), Res()]
                    sm2 = [SB(pc, "smy%d" % i, [128, 8], F32) for i in range(2)]; r_sm2 = [Res(), Res()]
                    ao = [SB(pc, "ao%d" % i, [128, 512], BF16) for i in range(2)]; r_ao = [Res(), Res()]
                    hctr = 0
                    for g in range(NG):
                        gs = slice(g * 512, (g + 1) * 512)
                        for t in range(4):
                            tile_i = g * 4 + t
                            sl = tile_i % 2
                            dma("sp", ds_x[sl], xs[sl][:], x_d[s, tile_i * 128:(tile_i + 1) * 128, :], writes=[r_xs[sl]])
                            make_uT(xs[sl], r_xs[sl], uT[:, :, t * 128:(t + 1) * 128], r_uT, 1, 0)
                        for m in range(4):
                            for (pbx, cb) in ((2, 512 + m * 128), (3, 1024 + m * 128)):
                                for k in range(8):
                                    op("pe", lambda e, k=k, pbx=pbx, cb=cb: e.matmul(PB[pbx][:], lhsT=wC[:, k, cb:cb + 128], rhs=uT[:, k, :], start=(k == 0), stop=(k == 7)),
                                       reads=[r_uT, r_wC], writes=[rPB[pbx]])
                            op("dve", lambda e: e.tensor_tensor(out=ta[0][:], in0=PB[2][:], in1=cosT[:, gs], op=ALU.mult), reads=[rPB[2], r_cs], writes=[r_ta[0]])
                            op("dve", lambda e: e.tensor_tensor(out=ta[1][:], in0=PB[3][:], in1=sinT[:, gs], op=ALU.mult), reads=[rPB[3], r_cs], writes=[r_ta[1]])
                            op("dve", lambda e, m=m: e.tensor_tensor(out=qT[:, m, :], in0=ta[0][:], in1=ta[1][:], op=ALU.add), reads=[r_ta[0], r_ta[1]], writes=[r_qT])
                        for h in range(4):
                            pbr = 4 + (h % 2)
                            for k in range(8):
                                op("pe", lambda e, k=k, h=h: e.matmul(PB[pbr][:], lhsT=wC[:, k, h * 128:(h + 1) * 128], rhs=uT[:, k, :], start=(k == 0), stop=(k == 7)),
                                   reads=[r_uT, r_wC], writes=[rPB[pbr]])
                            op("act", lambda e, h=h: e.activation(out=srg[:, h, :], in_=PB[pbr][:], func=AF.Silu), reads=[rPB[pbr]], writes=[r_srg[h]])
                        for h in range(4):
                            pbq_ = 6 + (h % 2)
                            t2_, r_t2 = ta[2 * (h % 2)], r_ta[2 * (h % 2)]
                            t3_, r_t3 = ta[2 * (h % 2) + 1], r_ta[2 * (h % 2) + 1]
                            rr = [r_oT[h][c] for c in range(g * 8, (g + 1) * 8)]
                            op("dve", lambda e, h=h: e.tensor_tensor(out=t2_[:], in0=oT[:, h, gs], in1=oT[:, h, gs], op=ALU.mult), reads=rr, writes=[r_t2])
                            op("pe", lambda e: e.matmul(PB[pbq_][:], lhsT=onesf[:], rhs=t2_[:], start=True, stop=True), reads=[r_onesf, r_t2], writes=[rPB[pbq_]])
                            op("act", lambda e: e.activation(out=t3_[:], in_=PB[pbq_][:], func=AF.Ln, scale=1.0 / 128.0, bias=epsc[:, 1:2]), reads=[rPB[pbq_], r_eps], writes=[r_t3])
                            op("act", lambda e: e.activation(out=t3_[:], in_=t3_[:], func=AF.Exp, scale=-0.5), reads=[r_t3], writes=[r_t3])
                            op("dve", lambda e, h=h: e.tensor_tensor(out=t2_[:], in0=oT[:, h, gs], in1=t3_[:], op=ALU.mult), reads=rr + [r_t3], writes=[r_t2])
                            op("dve", lambda e, h=h: e.scalar_tensor_tensor(out=ogA[:, h, gs], in0=srg[:, h, :], scalar=ngc[:, 0:1], in1=t2_[:], op0=ALU.mult, op1=ALU.mult),
                               reads=[r_srg[h], r_ngc, r_t2], writes=[r_ogA[g]])
                        def att_geom(t):
                            j = g * 4 + t
                            k0 = max(0, j - 1); k1 = min(NT, j + 2)
                            return j, k0, k1, k1 - k0, (k1 - k0) * 128, (128 if j == 0 else 0)

                        def att_s1_pe(idx):
                            t, h = idx // 8, idx % 8
                            j, k0, k1, nkb, nk, mc0 = att_geom(t)
                            m, hf = h % 4, h // 4
                            ps = slice(hf * 64, (hf + 1) * 64)
                            ib = idx % 2
                            pbs = 2 + ib
                            kgs = [r_kT[gg] for gg in range((k0 * 128) // 512, ((k1 * 128) - 1) // 512 + 1)]
                            op("pe", lambda e: e.matmul(PB[pbs][:, 0:nk], lhsT=qT[ps, m, t * 128:(t + 1) * 128], rhs=kTt[ps, k0 * 128:k1 * 128], start=True, stop=True),
                               reads=[r_qT] + kgs, writes=[rPB[pbs]])

                        def att_s1_ew(idx):
                            t, h = idx // 8, idx % 8
                            j, k0, k1, nkb, nk, mc0 = att_geom(t)
                            ib = idx % 2
                            pbs = 2 + ib
                            op("dve", lambda e: e.tensor_tensor(out=msc[ib][:, 0:nk], in0=PB[pbs][:, 0:nk], in1=amask[:, mc0:mc0 + nk], op=ALU.add),
                               reads=[rPB[pbs], r_amask], writes=[r_msc[ib]])
                            op("dve", lambda e: e.reduce_max(out=sm[ib][:, 0:1], in_=msc[ib][:, 0:nk], axis=AX.X), reads=[r_msc[ib]], writes=[r_sm[ib]])
                            op("dve", lambda e: e.tensor_scalar(out=sm[ib][:, 1:2], in0=sm[ib][:, 0:1], scalar1=-0.125, scalar2=sinkb[:, 1, h:h + 1], op0=ALU.mult, op1=ALU.min),
                               reads=[r_sm[ib], r_sink], writes=[r_sm[ib]])
                            op("act", lambda e: e.activation(out=pex[ib][:, 0:nk], in_=msc[ib][:, 0:nk], func=AF.Exp, scale=0.125, bias=sm[ib][:, 1:2], accum_out=sm2[ib][:, 0:1]),
                               reads=[r_msc[ib], r_sm[ib]], writes=[r_pex[ib], r_sm2[ib]])
                            op("act", lambda e: e.activation(out=sm2[ib][:, 1:2], in_=sinkb[:, 0, h:h + 1], func=AF.Exp, bias=sm[ib][:, 1:2], scale=1.0),
                               reads=[r_sm[ib], r_sink], writes=[r_sm2[ib]])

                        def att_s2a(idx):
                            t, h = idx // 8, idx % 8
                            j, k0, k1, nkb, nk, mc0 = att_geom(t)
                            ib = idx % 2
                            pbt = 4 + ib
                            for b in range(nkb):
                                op("pe", lambda e, b=b: e.transpose(out=PB[pbt][:].bitcast(BF16)[:, b * 128:(b + 1) * 128], in_=pex[ib][:, b * 128:(b + 1) * 128], identity=identb[:]),
                                   reads=[r_pex[ib], r_identb], writes=[rPB[pbt]])
                            if ib == 0:
                                op("act", lambda e: e.activation(out=pT[ib][:, 0:nkb, :], in_=PB[pbt][:].bitcast(BF16)[:, 0:nk].rearrange("p (b q) -> p b q", q=128), func=AF.Copy),
                                   reads=[rPB[pbt]], writes=[r_pT[ib]])
                            else:
                                op("dve", lambda e: e.tensor_copy(out=pT[ib][:, 0:nkb, :], in_=PB[pbt][:].bitcast(BF16)[:, 0:nk].rearrange("p (b q) -> p b q", q=128)),
                                   reads=[rPB[pbt]], writes=[r_pT[ib]])

                        def att_s2b(idx):
                            t, h = idx // 8, idx % 8
                            j, k0, k1, nkb, nk, mc0 = att_geom(t)
                            hf = h // 4
                            ib = idx % 2
                            op("dve", lambda e: e.tensor_tensor(out=sm2[ib][:, 2:3], in0=sm2[ib][:, 0:1], in1=sm2[ib][:, 1:2], op=ALU.add), reads=[r_sm2[ib]], writes=[r_sm2[ib]])
                            op("dve", lambda e: e.reciprocal(out=sm2[ib][:, 3:4], in_=sm2[ib][:, 2:3]), reads=[r_sm2[ib]], writes=[r_sm2[ib]])
                            pbo = 6 + ib
                            for b in range(nkb):
                                op("pe", lambda e, b=b: e.matmul(PB[pbo][:, 0:64], lhsT=pT[ib][:, b, :], rhs=vtk[:, k0 + b, hf * 64:(hf + 1) * 64], start=(b == 0), stop=(b == nkb - 1)),
                                   reads=[r_pT[ib], r_v[k0 + b]], writes=[rPB[pbo]])
                            aot = ao[t % 2]; r_aot = r_ao[t % 2]
                            op("act", lambda e: e.activation(out=aot[:, h * 64:(h + 1) * 64], in_=PB[pbo][:, 0:64], func=AF.Identity, scale=sm2[ib][:, 3:4]),
                               reads=[rPB[pbo], r_sm2[ib]], writes=[r_aot])
                            if h == 7:
                                pba = t % 2
                                for m in range(4):
                                    op("pe", lambda e, m=m: e.transpose(out=PB[pba][:].bitcast(BF16)[:, m * 128:(m + 1) * 128], in_=aot[:, m * 128:(m + 1) * 128], identity=identb[:]),
                                       reads=[r_aot, r_identb], writes=[rPB[pba]])
                                op("dve", lambda e: e.tensor_copy(out=atA[:, :, j * 128:(j + 1) * 128], in_=PB[pba][:].bitcast(BF16)[:, 0:512].rearrange("p (m q) -> p m q", q=128)),
                                   reads=[rPB[pba]], writes=[r_atA[j]])

                        att_s1_pe(0)
                        att_s1_ew(0)
                        for idx in range(32):
                            if idx + 1 < 32:
                                att_s1_pe(idx + 1)
                            att_s2a(idx)
                            if idx + 1 < 32:
                                att_s1_ew(idx + 1)
                            att_s2b(idx)
                T.barrier()

                with ExitStack() as pc:
                    if S >= 2048:
                        wG = oT[:].rearrange("p h s -> p (h s)").bitcast(BF16)
                        wGv = wG[:, 0:8 * 2048].rearrange("p (k n) -> p k n", n=2048)
                    else:
                        wGv = SB(pc, "wGt", [128, 8, 2048], BF16)[:]
                    r_wG = Res()
                    wR = SB(pc, "wR", [128, 4, D], BF16); wT = SB(pc, "wT", [128, 4, D], BF16); wO = SB(pc, "wO", [128, 8, D], BF16)
                    lnB = SB(pc, "lnB1", [128, 2, D], F32); r_lnB = Res()
                    r_wRT = Res(); r_wGc = [Res() for _ in range(8)]; r_wOr = Res()
                    dma("pool", ds_wc[8], wR[:], w_rec_k, writes=[r_wRT])
                    dma("pool", ds_wc[8], wT[:], w_att_k, writes=[r_wRT])
                    for dcq in range(8):
                        dma("pool", ds_wc[dcq], wGv[:, :, dcq * 128:(dcq + 1) * 128], w_in_k[:, :, O_GR + dcq * 128:O_GR + (dcq + 1) * 128], writes=[r_wGc[dcq]])
                        dma("pool", ds_wc[dcq], wGv[:, :, 1024 + dcq * 128:1024 + (dcq + 1) * 128], w_in_k[:, :, O_GA + dcq * 128:O_GA + (dcq + 1) * 128], writes=[r_wGc[dcq]])
                    for q2 in range(2):
                        dma("pool", ds_wc[9], wO[:, :, q2 * 512:(q2 + 1) * 512], w_out_k[:, :, q2 * 512:(q2 + 1) * 512], writes=[r_wOr])
                    dma("sp", ds_const, lnB[:, 0, :], ln1g_d[0:1, :].broadcast_to([128, D]), writes=[r_lnB])
                    dma("sp", ds_const, lnB[:, 1, :], ln1b_d[0:1, :].broadcast_to([128, D]), writes=[r_lnB])
                    zb = zer[:, 0:8].bitcast(BF16)[:, 0:8].unsqueeze(2)
                    dma("sp", ds_misc[3], u2T_d[s, :, 0:1].rearrange("(j p) o -> p j o", p=128), zb, reads=[r_zer], allow_slow_non_contiguous=True)
                    dma("sp", ds_misc[3], u2T_d[s, :, S + 1:S + 2].rearrange("(j p) o -> p j o", p=128), zb, reads=[r_zer], allow_slow_non_contiguous=True)
                    r_x1d = [Res() for _ in range(NT)]
                    r_u2d = [Res() for _ in range(NT)]
                    xs = [SB(pc, "bxs%d" % i, [128, D], F32) for i in range(4)]; r_xs = [Res() for _ in range(4)]
                    uT = SB(pc, "buT", [128, 8, 512], BF16); r_uT = Res()
                    mT = SB(pc, "mT", [128, 8, 512], BF16); r_mT = Res()
                    ta = [SB(pc, "bta%d" % i, [128, 512], F32) for i in range(4)]; r_ta = [Res() for _ in range(4)]
                    yt = [SB(pc, "yt%d" % i, [128, D], F32) for i in range(2)]; r_yt = [Res(), Res()]
                    st6 = [SB(pc, "st6%d" % i, [128, 20], F32) for i in range(2)]; r_st6 = [Res(), Res()]
                    u2t = [SB(pc, "u2t%d" % i, [128, 8, 128], BF16) for i in range(2)]; r_u2t = [Res(), Res()]
                    for g in range(NG):
                        gs = slice(g * 512, (g + 1) * 512)
                        for t in range(4):
                            tile_i = g * 4 + t
                            dma("sp", ds_x[t], xs[t][:], x_d[s, tile_i * 128:(tile_i + 1) * 128, :], writes=[r_xs[t]])
                            make_uT(xs[t], r_xs[t], uT[:, :, t * 128:(t + 1) * 128], r_uT, 1, 0)
                        BSETS = ((2, 3, 4, 5), (0, 1, 6, 7))

                        def dc_pe(dc):
                            b0, b1, b2, b3 = BSETS[dc % 2]
                            dcs = slice(dc * 128, (dc + 1) * 128)
                            for k in range(8):
                                op("pe", lambda e, k=k: e.matmul(PB[b0][:], lhsT=wGv[:, k, dc * 128:(dc + 1) * 128], rhs=uT[:, k, :], start=(k == 0), stop=(k == 7)),
                                   reads=[r_uT, r_wGc[dc]], writes=[rPB[b0]])
                            for k in range(8):
                                op("pe", lambda e, k=k: e.matmul(PB[b1][:], lhsT=wGv[:, k, 1024 + dc * 128:1024 + (dc + 1) * 128], rhs=uT[:, k, :], start=(k == 0), stop=(k == 7)),
                                   reads=[r_uT, r_wGc[dc]], writes=[rPB[b1]])
                            for h in range(4):
                                op("pe", lambda e, h=h: e.matmul(PB[b2][:], lhsT=wR[:, h, dcs], rhs=ogA[:, h, gs], start=(h == 0), stop=(h == 3)),
                                   reads=[r_wRT, r_ogA[g]], writes=[rPB[b2]])
                            for m in range(4):
                                op("pe", lambda e, m=m: e.matmul(PB[b3][:], lhsT=wT[:, m, dcs], rhs=atA[:, m, gs], start=(m == 0), stop=(m == 3)),
                                   reads=[r_wRT] + r_atA[g * 4:(g + 1) * 4], writes=[rPB[b3]])

                        def dc_ew(dc):
                            b0, b1, b2, b3 = BSETS[dc % 2]
                            a0, a1 = ta[2 * (dc % 2)], ta[2 * (dc % 2) + 1]
                            ra0, ra1 = r_ta[2 * (dc % 2)], r_ta[2 * (dc % 2) + 1]
                            op("act", lambda e: e.activation(out=a0[:], in_=PB[b0][:], func=AF.Sigmoid), reads=[rPB[b0]], writes=[ra0])
                            op("act", lambda e: e.activation(out=a1[:], in_=PB[b1][:], func=AF.Sigmoid), reads=[rPB[b1]], writes=[ra1])
                            op("dve", lambda e: e.tensor_tensor(out=a0[:], in0=PB[b2][:], in1=a0[:], op=ALU.mult), reads=[rPB[b2], ra0], writes=[ra0])
                            op("dve", lambda e: e.tensor_tensor(out=a1[:], in0=PB[b3][:], in1=a1[:], op=ALU.mult), reads=[rPB[b3], ra1], writes=[ra1])
                            op("dve", lambda e: e.tensor_tensor(out=mT[:, dc, :], in0=a0[:], in1=a1[:], op=ALU.add), reads=[ra0, ra1], writes=[r_mT])

                        dc_pe(0)
                        for dc in range(8):
                            if dc + 1 < 8:
                                dc_pe(dc + 1)
                            dc_ew(dc)
                        def wout_pe(t):
                            tile_i = g * 4 + t
                            wb = 2 + 2 * (tile_i % 2)
                            for hh in range(2):
                                for k in range(8):
                                    op("pe", lambda e, k=k, hh=hh: e.matmul(PB[wb + hh][:], lhsT=mT[:, k, t * 128:(t + 1) * 128], rhs=wO[:, k, hh * 512:(hh + 1) * 512], start=(k == 0), stop=(k == 7)),
                                       reads=[r_mT, r_wOr], writes=[rPB[wb + hh]])

                        wout_pe(0)
                        for t in range(4):
                            tile_i = g * 4 + t
                            ib = tile_i % 2
                            wb = 2 + 2 * (tile_i % 2)
                            ln_part1((wb, wb + 1), (rPB[wb], rPB[wb + 1]), 0, yt[ib], r_yt[ib])
                            if t + 1 < 4:
                                wout_pe(t + 1)
                            ln_part2(xs[t][:], r_xs[t], lnB[:, 0, :], lnB[:, 1, :], r_lnB, yt[ib], r_yt[ib], st6[ib], r_st6[ib])
                            dma("sp", ds_o[ib], x1_d[s, tile_i * 128:(tile_i + 1) * 128, :], yt[ib][:], reads=[r_yt[ib]], writes=[r_x1d[tile_i]])
                            make_uT(yt[ib], r_yt[ib], u2t[ib][:], r_u2t[ib], 3, 2, banks=(6, 7))
                            dma("sp", ds_o[2 + ib], u2T_d[s, :, 1 + tile_i * 128:1 + (tile_i + 1) * 128].rearrange("(j p) q -> p j q", p=128), u2t[ib][:],
                                reads=[r_u2t[ib]], writes=[r_u2d[tile_i]])
                T.barrier()
            T.barrier()

            with ExitStack() as pf:
                wU = SB(pf, "wU", [128, 8, 2 * DFF], BF16); wD = SB(pf, "wD", [128, NFC, D], BF16); r_wU = Res()
                lnB = SB(pf, "lnB2", [128, 2, D], F32); r_lnB = Res()
                cvw = SB(pf, "cvw", [128, 4, 2 * NFC], F32); r_cvw = r_lnB
                r_wc = [Res() for _ in range(11)]
                for jq in range(11):
                    dma("pool", ds_wc[jq], wU[:, :, jq * 256:(jq + 1) * 256], w_up_k[:, :, jq * 256:(jq + 1) * 256], writes=[r_wc[jq]])
                    dma("pool", ds_wc[jq], wU[:, :, DFF + jq * 256:DFF + (jq + 1) * 256], w_up_k[:, :, DFF + jq * 256:DFF + (jq + 1) * 256], writes=[r_wc[jq]])
                    dma("pool", ds_wc[jq], wD[:, 2 * jq:2 * jq + 2, :], w_dn_k[:, 2 * jq:2 * jq + 2, :], writes=[r_wc[jq]])
                dma("sp", ds_const, lnB[:, 0, :], ln2g_d[0:1, :].broadcast_to([128, D]), writes=[r_lnB])
                dma("sp", ds_const, lnB[:, 1, :], ln2b_d[0:1, :].broadcast_to([128, D]), writes=[r_lnB])
                dma("sp", ds_const, cvw[:, 0:3, :], cw_d.rearrange("k (j p) -> p k j", p=128), writes=[r_cvw], allow_slow_non_contiguous=True)
                dma("sp", ds_const, cvw[:, 3, :], cb_d.rearrange("o (j p) -> p (o j)", p=128), writes=[r_cvw], allow_slow_non_contiguous=True)
                NSG = S // 256
                u2 = [SB(pf, "fu2%d" % i, [128, 8, 258], BF16) for i in range(2)]; r_u2 = [Res(), Res()]
                x1t = [SB(pf, "fx1%d" % i, [128, D], F32) for i in range(4)]; r_x1t = [Res() for _ in range(4)]
                cv = [SB(pf, "fcv%d" % i, [128, 256], F32) for i in range(2)]; r_cv = [Res(), Res()]
                cgt = [SB(pf, "fcg%d" % i, [128, 256], F32) for i in range(2)]; r_cg = [Res(), Res()]
                gz = [SB(pf, "fgz%d" % i, [128, 256], F32) for i in range(2)]; r_gz = [Res(), Res()]
                actb = [SB(pf, "fact%d" % i, [128, 256], BF16) for i in range(2)]; r_act = [Res(), Res()]
                yt = [SB(pf, "fyt%d" % i, [128, D], F32) for i in range(2)]; r_yt = [Res(), Res()]
                st6 = [SB(pf, "fst6%d" % i, [128, 20], F32) for i in range(2)]; r_st6 = [Res(), Res()]
                def c2_up(sg, c):
                    ub = sg % 2; ib = c % 2
                    pv, pg = 4 + 2 * ib, 5 + 2 * ib
                    for (pbx, cb) in ((pv, c * 128), (pg, DFF + c * 128)):
                        for k in range(8):
                            op("pe", lambda e, k=k, pbx=pbx, cb=cb: e.matmul(PB[pbx][:, 0:258], lhsT=wU[:, k, cb:cb + 128], rhs=u2[ub][:, k, :], start=(k == 0), stop=(k == 7)),
                               reads=[r_wc[c // 2], r_u2[ub]], writes=[rPB[pbx]])

                def c2_s1(c):
                    ib = c % 2
                    pv, pg = 4 + 2 * ib, 5 + 2 * ib
                    for (pbx, dst, r_dst, col) in ((pv, cv[ib], r_cv[ib], c), (pg, cgt[ib], r_cg[ib], NFC + c)):
                        op("act", lambda e, pbx=pbx, dst=dst, col=col: e.activation(out=dst[:], in_=PB[pbx][:, 0:256], func=AF.Identity, scale=cvw[:, 0, col:col + 1], bias=cvw[:, 3, col:col + 1]),
                           reads=[rPB[pbx], r_cvw], writes=[r_dst])

                def c2_s2(c):
                    ib = c % 2
                    pv, pg = 4 + 2 * ib, 5 + 2 * ib
                    for (pbx, dst, r_dst, col) in ((pg, cgt[ib], r_cg[ib], NFC + c), (pv, cv[ib], r_cv[ib], c)):
                        op("dve", lambda e, pbx=pbx, dst=dst, col=col: e.scalar_tensor_tensor(out=dst[:], in0=PB[pbx][:, 1:257], scalar=cvw[:, 1, col:col + 1], in1=dst[:], op0=ALU.mult, op1=ALU.add),
                           reads=[rPB[pbx], r_cvw, r_dst], writes=[r_dst])
                        op("dve", lambda e, pbx=pbx, dst=dst, col=col: e.scalar_tensor_tensor(out=dst[:], in0=PB[pbx][:, 2:258], scalar=cvw[:, 2, col:col + 1], in1=dst[:], op0=ALU.mult, op1=ALU.add),
                           reads=[rPB[pbx], r_cvw, r_dst], writes=[r_dst])

                def c2_s3(c):
                    ib = c % 2
                    op("act", lambda e: e.activation(out=gz[ib][:], in_=cgt[ib][:], func=AF.Gelu_apprx_tanh), reads=[r_cg[ib]], writes=[r_gz[ib]])

                def c2_s4(c):
                    ib = c % 2
                    op("dve", lambda e: e.tensor_tensor(out=actb[ib][:], in0=gz[ib][:], in1=cv[ib][:], op=ALU.mult), reads=[r_gz[ib], r_cv[ib]], writes=[r_act[ib]])

                def c2_down(c):
                    ib = c % 2
                    for tt in range(2):
                        for hh in range(2):
                            pa_ = tt * 2 + hh
                            op("pe", lambda e, tt=tt, hh=hh, pa_=pa_: e.matmul(PB[pa_][:], lhsT=actb[ib][:, tt * 128:(tt + 1) * 128], rhs=wD[:, c, hh * 512:(hh + 1) * 512],
                                                                             start=(c == 0), stop=(c == NFC - 1)),
                               reads=[r_act[ib], r_wc[c // 2]], writes=[rPB[pa_]])

                def c2_load(sg):
                    a = sg * 256
                    ub = sg % 2
                    dma("sp", ds_misc[ub], u2[ub][:], u2T_d[s, :, a:a + 258].rearrange("(j p) q -> p j q", p=128), writes=[r_u2[ub]])
                    for tt in range(2):
                        sl = (sg * 2 + tt) % 4
                        dma("sp", ds_x[sl], x1t[sl][:], x1_d[s, a + tt * 128:a + (tt + 1) * 128, :], writes=[r_x1t[sl]])

                def c2_prologue(sg):
                    c2_up(sg, 0); c2_up(sg, 1); c2_s1(0); c2_s2(0); c2_s1(1)

                def c2_ln2(sg):
                    for tt in range(2):
                        tile_i = sg * 2 + tt
                        sl = tile_i % 4
                        ib = tile_i % 2
                        ln_part2(x1t[sl][:], r_x1t[sl], lnB[:, 0, :], lnB[:, 1, :], r_lnB, yt[ib], r_yt[ib], st6[ib], r_st6[ib])
                        dma("sp", ds_o[ib], out_d[s, tile_i * 128:(tile_i + 1) * 128, :], yt[ib][:], reads=[r_yt[ib]])

                c2_load(0)
                if NSG > 1:
                    c2_load(1)
                c2_prologue(0)
                for sg in range(NSG):
                    for c in range(NFC):
                        c2_s3(c)
                        if c + 1 < NFC:
                            c2_s2(c + 1)
                        c2_s4(c)
                        if c + 2 < NFC:
                            c2_up(sg, c + 2)
                            c2_s1(c + 2)
                        c2_down(c)
                    for tt in range(2):
                        tile_i = sg * 2 + tt
                        ib = tile_i % 2
                        ln_part1((tt * 2, tt * 2 + 1), (rPB[tt * 2], rPB[tt * 2 + 1]), 1, yt[ib], r_yt[ib])
                    if sg + 1 < NSG:
                        c2_prologue(sg + 1)
                    c2_ln2(sg)
                    if sg + 2 < NSG:
                        c2_load(sg + 2)
            T.barrier()
        print("build: instructions=%d waits=%d" % (T.n_ins, T.n_wait))
    return nc


def _consts():
    ident = np.eye(128, dtype=np.float32)
    i = np.arange(64)
    maskf = (i[:, None] <= i[None, :]).astype(np.uint8)
    maskb = (i[:, None] >= i[None, :]).astype(np.uint8)
    q = np.arange(128)[:, None]
    cc = np.arange(128)[None, :]
    NEG = -30000.0
    am = np.zeros((128, 384), np.float32)
    am[:, 0:128] = np.where(cc >= q, 0.0, NEG)
    am[:, 256:384] = np.where(cc <= q, 0.0, NEG)
    half = 32
    inv = (np.float32(10000.0) ** (-np.arange(half, dtype=np.float32) / np.float32(half))).astype(np.float32)
    p = np.arange(128)
    rope = np.stack([inv[p % 32], np.where((p % 64) < 32, -1.0, 1.0).astype(np.float32)], axis=1).astype(np.float32)
    return dict(k_ident=ident, k_maskf=maskf, k_maskb=maskb, k_amask=am, k_rope=rope)


def make_in_maps(inputs, n_cores, nseq):
    f = lambda a: np.ascontiguousarray(np.asarray(a))
    shared = dict(
        w_ada=f(inputs["w_ada"][0]), b_ada=f(inputs["b_ada"][0:1]), w_in=f(inputs["w_in"][0]),
        rec_lower_bound=f(inputs["rec_lower_bound"]), rec_norm_g=f(inputs["rec_norm_g"][0:1]),
        w_rec_branch=f(inputs["w_rec_branch"][0]), attn_sink=f(inputs["attn_sink"][0:1]),
        w_attn_branch=f(inputs["w_attn_branch"][0]), w_out=f(inputs["w_out"][0]),
        ln1_g=f(inputs["ln1_g"][0:1]), ln1_b=f(inputs["ln1_b"][0:1]), w_up=f(inputs["w_up"][0]),
        conv_w=f(inputs["conv_w"][0]), conv_b=f(inputs["conv_b"][0:1]), w_down=f(inputs["w_down"][0]),
        ln2_g=f(inputs["ln2_g"][0:1]), ln2_b=f(inputs["ln2_b"][0:1]))
    shared.update(_consts())
    maps = []
    for i in range(n_cores):
        m = dict(shared)
        m["x"] = f(inputs["x"][i * nseq:(i + 1) * nseq])
        m["c"] = f(inputs["c"][i * nseq:(i + 1) * nseq])
        m["positions"] = f(inputs["positions"][i * nseq:(i + 1) * nseq]).astype(np.int32)
        maps.append(m)
    return maps


def kernel(**inputs):
    B, S, _ = inputs["x"].shape
    nseq = B // N_CORES
    nc = build_nc(S=S, NSEQ=nseq)
    in_maps = make_in_maps(inputs, N_CORES, nseq)
    res = run_bass_kernel_spmd(nc, in_maps, core_ids=list(range(N_CORES)))
    return np.concatenate([np.asarray(r["out"]) for r in res.results], axis=0).astype(np.float32)
```
